# Optimizing a Trainium2 kernel written in Bass

```python
import math
import jax, jax.numpy as jnp
from jax import lax
import numpy as np

D_MODEL = 1024
BATCH = 8
SEQ = 2048
DEPTH = 4
DEC_BATCH = 128
DEC_SEQ = 8
PAST_LEN = 16384
PAGE_SIZE = 128

N_META = 16
N_EVEN = (DEPTH + 1) // 2
N_ODD = DEPTH // 2
S5_WIDTH = D_MODEL // 2
S5_GROUP = 16
S5_GROUPS = S5_WIDTH // S5_GROUP
S5_STATE = 64
RET_HEADS = 4
RET_DK = D_MODEL // 16
RET_DV = 2 * RET_DK
RET_QK = RET_HEADS * RET_DK
RET_WIDTH = RET_HEADS * RET_DV
AB_IN = S5_WIDTH + 2 * RET_QK + 2 * RET_WIDTH
AB_SPLITS = [S5_WIDTH, S5_WIDTH + RET_QK, S5_WIDTH + 2 * RET_QK, S5_WIDTH + 2 * RET_QK + RET_WIDTH]
AB_OUT = S5_WIDTH + RET_WIDTH
GLA_HEADS = 4
GLA_DK = D_MODEL // (2 * GLA_HEADS)
GLA_DV = D_MODEL // GLA_HEADS
GLA_QK = GLA_HEADS * GLA_DK
GLA_V = GLA_HEADS * GLA_DV
GLA_LOWRANK = 16
GLA_TAU = 16.0
GLA_IN = 2 * GLA_QK + 2 * GLA_V + GLA_LOWRANK
GLA_SPLITS = [GLA_QK, 2 * GLA_QK, 2 * GLA_QK + GLA_V, 2 * GLA_QK + 2 * GLA_V]
CHUNK = 16
D_FF = 128 * ((8 * D_MODEL // 3 + 127) // 128)
EPS = 1e-6
ROPE_BASE = 10000.0

kernel_name = 's5_retention_gla_macaron_hybrid'


def rmsnorm(x, g):
    xf = x.astype(jnp.float32)
    y = xf * lax.rsqrt(jnp.mean(xf * xf, axis=-1, keepdims=True) + EPS)
    return (y * g.astype(jnp.float32)).astype(x.dtype)


def swiglu(x, w_gu, w_down):
    gate, up = jnp.split(x @ w_gu, 2, axis=-1)
    return (jax.nn.silu(gate) * up) @ w_down


def rotary(x, pos):
    half = x.shape[-1] // 2
    inv_freq = 1.0 / (ROPE_BASE ** (jnp.arange(half, dtype=jnp.float32) / half))
    ang = pos.astype(jnp.float32)[:, None] * inv_freq[None, :]
    cos = jnp.cos(ang)[None, :, None, :]
    sin = jnp.sin(ang)[None, :, None, :]
    xf = x.astype(jnp.float32)
    x1, x2 = xf[..., :half], xf[..., half:]
    return jnp.concatenate([x1 * cos - x2 * sin, x1 * sin + x2 * cos], axis=-1).astype(x.dtype)


def chunked_gated_linear_attn(q, k, v, log_a, s0):
    f32 = jnp.float32
    bsz, L, H, _ = q.shape
    dv = v.shape[-1]
    pad = (-L) % CHUNK

    def to_chunks(t):
        t = jnp.pad(t.astype(f32), ((0, 0), (0, pad), (0, 0), (0, 0)))
        n = t.shape[1] // CHUNK
        return t.reshape(bsz, n, CHUNK, H, t.shape[-1]).transpose(1, 0, 2, 3, 4)

    qc, kc, vc, lac = to_chunks(q), to_chunks(k), to_chunks(v), to_chunks(log_a)
    causal = jnp.tril(jnp.ones((CHUNK, CHUNK), dtype=bool))[None, :, :, None, None]

    def step(S, inp):
        qi, ki, vi, lai = inp
        b = jnp.cumsum(lai, axis=1)
        o_inter = jnp.einsum('bthk,bhkv->bthv', qi * jnp.exp(b), S)
        diff = jnp.where(causal, b[:, :, None] - b[:, None, :], -jnp.inf)
        scores = jnp.einsum('bthk,bshk,btshk->bhts', qi, ki, jnp.exp(diff))
        o_intra = jnp.einsum('bhts,bshv->bthv', scores, vi)
        b_last = b[:, -1]
        S_new = jnp.exp(b_last)[..., None] * S + jnp.einsum(
            'bshk,bshv->bhkv', ki * jnp.exp(b_last[:, None] - b), vi)
        return S_new, o_inter + o_intra

    s_final, o = lax.scan(step, s0.astype(f32), (qc, kc, vc, lac))
    o = o.transpose(1, 0, 2, 3, 4).reshape(bsz, -1, H, dv)[:, :L]
    return o.astype(v.dtype), s_final


def _complex_affine_combine(e1, e2):
    a1r, a1i, b1r, b1i = e1
    a2r, a2i, b2r, b2i = e2
    return (a2r * a1r - a2i * a1i, a2r * a1i + a2i * a1r,
            a2r * b1r - a2i * b1i + b2r, a2r * b1i + a2i * b1r + b2i)


def s5_mixer(u, h0, a_re, a_im, log_dt, b_re, b_im, c_re, c_im, d_skip, w_glu):
    f32 = jnp.float32
    bsz, L, _ = u.shape
    uf = u.astype(f32).reshape(bsz, L, S5_GROUPS, S5_GROUP)
    ar, ai = a_re.astype(f32), a_im.astype(f32)
    dt = jnp.exp(log_dt.astype(f32))[:, None]
    mag = jnp.exp(dt * ar)
    abar_re, abar_im = mag * jnp.cos(dt * ai), mag * jnp.sin(dt * ai)
    den = ar * ar + ai * ai
    num_re = abar_re - 1.0
    f_re = (num_re * ar + abar_im * ai) / den
    f_im = (abar_im * ar - num_re * ai) / den
    br, bi = b_re.astype(f32), b_im.astype(f32)
    bbar_re = f_re[..., None] * br - f_im[..., None] * bi
    bbar_im = f_re[..., None] * bi + f_im[..., None] * br
    x_re = jnp.einsum('gpn,blgn->blgp', bbar_re, uf)
    x_im = jnp.einsum('gpn,blgn->blgp', bbar_im, uf)
    if h0 is not None:
        h0_re, h0_im = h0[0].astype(f32), h0[1].astype(f32)
        x_re = x_re.at[:, 0].add(abar_re * h0_re - abar_im * h0_im)
        x_im = x_im.at[:, 0].add(abar_re * h0_im + abar_im * h0_re)
    a_seq_re = jnp.broadcast_to(abar_re[None, None], (1, L, S5_GROUPS, S5_STATE))
    a_seq_im = jnp.broadcast_to(abar_im[None, None], (1, L, S5_GROUPS, S5_STATE))
    _, _, h_re, h_im = lax.associative_scan(
        _complex_affine_combine, (a_seq_re, a_seq_im, x_re, x_im), axis=1)
    y = (jnp.einsum('gnp,blgp->blgn', c_re.astype(f32), h_re)
         - jnp.einsum('gnp,blgp->blgn', c_im.astype(f32), h_im))
    y = y.reshape(bsz, L, S5_WIDTH) + d_skip.astype(f32) * u.astype(f32)
    z = jax.nn.gelu(y).astype(u.dtype)
    out = z * jax.nn.sigmoid(z @ w_glu)
    return out, (h_re[:, -1], h_im[:, -1])


def mixer_ab(h, pos, s5_h0, ret_s0, w_in, w_out, a_re, a_im, log_dt, b_re, b_im, c_re, c_im, d_skip, w_glu):
    f32 = jnp.float32
    bsz, L, _ = h.shape
    u, q, k, v, g = jnp.split(h @ w_in, AB_SPLITS, axis=-1)
    s5_out, s5_state = s5_mixer(u, s5_h0, a_re, a_im, log_dt, b_re, b_im, c_re, c_im, d_skip, w_glu)
    q = rotary(q.reshape(bsz, L, RET_HEADS, RET_DK), pos)
    k = rotary(k.reshape(bsz, L, RET_HEADS, RET_DK), pos) * (RET_DK ** -0.5)
    v = v.reshape(bsz, L, RET_HEADS, RET_DV)
    log_gamma = jnp.log(1.0 - 2.0 ** (-5.0 - jnp.arange(RET_HEADS, dtype=f32)))
    log_a = jnp.broadcast_to(log_gamma[None, None, :, None], (bsz, L, RET_HEADS, RET_DK))
    if ret_s0 is None:
        ret_s0 = jnp.zeros((bsz, RET_HEADS, RET_DK, RET_DV), f32)
    o, ret_state = chunked_gated_linear_attn(q, k, v, log_a, ret_s0)
    of = o.astype(f32)
    mu = jnp.mean(of, axis=-1, keepdims=True)
    var = jnp.mean(jnp.square(of - mu), axis=-1, keepdims=True)
    on = ((of - mu) * lax.rsqrt(var + EPS)).reshape(bsz, L, RET_WIDTH).astype(h.dtype)
    ret_out = on * jax.nn.silu(g)
    mixed = jnp.concatenate([s5_out, ret_out], axis=-1) @ w_out
    return mixed, s5_state, ret_state


def mixer_gla(h, gla_s0, w_in, w_alpha2, b_alpha, norm_g, w_out):
    f32 = jnp.float32
    bsz, L, _ = h.shape
    q, k, v, r, lr = jnp.split(h @ w_in, GLA_SPLITS, axis=-1)
    log_a = jax.nn.log_sigmoid((lr @ w_alpha2 + b_alpha).astype(f32)) / GLA_TAU
    log_a = log_a.reshape(bsz, L, GLA_HEADS, GLA_DK)
    q = q.reshape(bsz, L, GLA_HEADS, GLA_DK)
    k = k.reshape(bsz, L, GLA_HEADS, GLA_DK) * (GLA_DK ** -0.5)
    v = v.reshape(bsz, L, GLA_HEADS, GLA_DV)
    if gla_s0 is None:
        gla_s0 = jnp.zeros((bsz, GLA_HEADS, GLA_DK, GLA_DV), f32)
    o, gla_state = chunked_gated_linear_attn(q, k, v, log_a, gla_s0)
    of = o.astype(f32)
    on = of * lax.rsqrt(jnp.mean(of * of, axis=-1, keepdims=True) + EPS) * norm_g.astype(f32)
    on = on.reshape(bsz, L, GLA_V).astype(h.dtype)
    return (on * jax.nn.silu(r)) @ w_out, gla_state


def run_trunk(x, pos, s5_re0, s5_im0, ret0, gla0, w):
    s5_re_new, s5_im_new, ret_new, gla_new = [], [], [], []
    for layer in range(DEPTH):
        x = x + 0.5 * swiglu(rmsnorm(x, w['norm_ffn1'][layer]), w['ffn1_w_gu'][layer], w['ffn1_w_down'][layer])
        h = rmsnorm(x, w['norm_mix'][layer])
        if layer % 2 == 0:
            i = layer // 2
            h0 = None if s5_re0 is None else (s5_re0[i], s5_im0[i])
            r0 = None if ret0 is None else ret0[i]
            mixed, (hr, hi), rs = mixer_ab(
                h, pos, h0, r0, w['ab_w_in'][i], w['ab_w_out'][i], w['s5_a_re'][i], w['s5_a_im'][i],
                w['s5_log_dt'][i], w['s5_b_re'][i], w['s5_b_im'][i], w['s5_c_re'][i], w['s5_c_im'][i],
                w['s5_d'][i], w['s5_w_glu'][i])
            s5_re_new.append(hr)
            s5_im_new.append(hi)
            ret_new.append(rs)
        else:
            i = layer // 2
            g0 = None if gla0 is None else gla0[i]
            mixed, gs = mixer_gla(h, g0, w['gla_w_in'][i], w['gla_w_alpha2'][i], w['gla_b_alpha'][i],
                                  w['gla_norm'][i], w['gla_w_out'][i])
            gla_new.append(gs)
        x = x + mixed
        x = x + 0.5 * swiglu(rmsnorm(x, w['norm_ffn2'][layer]), w['ffn2_w_gu'][layer], w['ffn2_w_down'][layer])
    y = rmsnorm(x, w['norm_final'])
    return y, jnp.stack(s5_re_new), jnp.stack(s5_im_new), jnp.stack(ret_new), jnp.stack(gla_new)


def setup_inputs(seed: int = 0) -> dict:
    key = jax.random.key(seed)
    ks = jax.random.split(key, 40)
    f32 = jnp.float32

    def nrm(k, shape, scale):
        return scale * jax.random.normal(k, shape, f32)

    n_idx = jnp.arange(S5_STATE, dtype=f32)
    a_re = -0.5 + nrm(ks[10], (N_EVEN, S5_GROUPS, S5_STATE), 0.01)
    a_im = math.pi * n_idx[None, None, :] + nrm(ks[11], (N_EVEN, S5_GROUPS, S5_STATE), 0.01)
    log_dt = jax.random.uniform(ks[12], (N_EVEN, S5_GROUPS), f32, math.log(1e-3), math.log(1e-1))
    return {
        'x_prompt': nrm(ks[0], (BATCH, SEQ, D_MODEL), 1.0),
        'x_sample': nrm(ks[1], (DEC_BATCH, DEC_SEQ, D_MODEL), 1.0),
        'state_s5_re': nrm(ks[2], (N_EVEN, DEC_BATCH, S5_GROUPS, S5_STATE), 0.1),
        'state_s5_im': nrm(ks[3], (N_EVEN, DEC_BATCH, S5_GROUPS, S5_STATE), 0.1),
        'state_ret': nrm(ks[4], (N_EVEN, DEC_BATCH, RET_HEADS, RET_DK, RET_DV), 0.5),
        'state_gla': nrm(ks[5], (N_ODD, DEC_BATCH, GLA_HEADS, GLA_DK, GLA_DV), 0.5),
        'meta_tokens': nrm(ks[6], (N_META, D_MODEL), 1.0),
        'norm_ffn1': 1.0 + nrm(ks[7], (DEPTH, D_MODEL), 0.01),
        'norm_mix': 1.0 + nrm(ks[8], (DEPTH, D_MODEL), 0.01),
        'norm_ffn2': 1.0 + nrm(ks[9], (DEPTH, D_MODEL), 0.01),
        'norm_final': 1.0 + nrm(ks[13], (D_MODEL,), 0.01),
        'ffn1_w_gu': nrm(ks[14], (DEPTH, D_MODEL, 2 * D_FF), D_MODEL ** -0.5),
        'ffn1_w_down': nrm(ks[15], (DEPTH, D_FF, D_MODEL), D_FF ** -0.5),
        'ffn2_w_gu': nrm(ks[16], (DEPTH, D_MODEL, 2 * D_FF), D_MODEL ** -0.5),
        'ffn2_w_down': nrm(ks[17], (DEPTH, D_FF, D_MODEL), D_FF ** -0.5),
        'ab_w_in': nrm(ks[18], (N_EVEN, D_MODEL, AB_IN), D_MODEL ** -0.5),
        'ab_w_out': nrm(ks[19], (N_EVEN, AB_OUT, D_MODEL), AB_OUT ** -0.5),
        's5_a_re': a_re,
        's5_a_im': a_im,
        's5_log_dt': log_dt,
        's5_b_re': nrm(ks[20], (N_EVEN, S5_GROUPS, S5_STATE, S5_GROUP), (2 * S5_GROUP) ** -0.5),
        's5_b_im': nrm(ks[21], (N_EVEN, S5_GROUPS, S5_STATE, S5_GROUP), (2 * S5_GROUP) ** -0.5),
        's5_c_re': nrm(ks[22], (N_EVEN, S5_GROUPS, S5_GROUP, S5_STATE), (2 * S5_STATE) ** -0.5),
        's5_c_im': nrm(ks[23], (N_EVEN, S5_GROUPS, S5_GROUP, S5_STATE), (2 * S5_STATE) ** -0.5),
        's5_d': nrm(ks[24], (N_EVEN, S5_WIDTH), 1.0),
        's5_w_glu': nrm(ks[25], (N_EVEN, S5_WIDTH, S5_WIDTH), S5_WIDTH ** -0.5),
        'gla_w_in': nrm(ks[26], (N_ODD, D_MODEL, GLA_IN), D_MODEL ** -0.5),
        'gla_w_alpha2': nrm(ks[27], (N_ODD, GLA_LOWRANK, GLA_QK), GLA_LOWRANK ** -0.5),
        'gla_b_alpha': nrm(ks[28], (N_ODD, GLA_QK), 0.1),
        'gla_norm': 1.0 + nrm(ks[29], (N_ODD, GLA_DV), 0.01),
        'gla_w_out': nrm(ks[30], (N_ODD, GLA_V, D_MODEL), GLA_V ** -0.5),
    }


def reference(x_prompt, x_sample, state_s5_re, state_s5_im, state_ret, state_gla, meta_tokens,
              norm_ffn1, norm_mix, norm_ffn2, norm_final, ffn1_w_gu, ffn1_w_down, ffn2_w_gu, ffn2_w_down,
              ab_w_in, ab_w_out, s5_a_re, s5_a_im, s5_log_dt, s5_b_re, s5_b_im, s5_c_re, s5_c_im, s5_d,
              s5_w_glu, gla_w_in, gla_w_alpha2, gla_b_alpha, gla_norm, gla_w_out):
    w = dict(norm_ffn1=norm_ffn1, norm_mix=norm_mix, norm_ffn2=norm_ffn2, norm_final=norm_final,
             ffn1_w_gu=ffn1_w_gu, ffn1_w_down=ffn1_w_down, ffn2_w_gu=ffn2_w_gu, ffn2_w_down=ffn2_w_down,
             ab_w_in=ab_w_in, ab_w_out=ab_w_out, s5_a_re=s5_a_re, s5_a_im=s5_a_im, s5_log_dt=s5_log_dt,
             s5_b_re=s5_b_re, s5_b_im=s5_b_im, s5_c_re=s5_c_re, s5_c_im=s5_c_im, s5_d=s5_d,
             s5_w_glu=s5_w_glu, gla_w_in=gla_w_in, gla_w_alpha2=gla_w_alpha2, gla_b_alpha=gla_b_alpha,
             gla_norm=gla_norm, gla_w_out=gla_w_out)
    bsz = x_prompt.shape[0]
    meta = jnp.broadcast_to(meta_tokens.astype(x_prompt.dtype)[None], (bsz, N_META, D_MODEL))
    xp = jnp.concatenate([meta, x_prompt], axis=1)
    pos_p = jnp.arange(N_META + x_prompt.shape[1], dtype=jnp.int32)
    yp, p_s5_re, p_s5_im, p_ret, p_gla = run_trunk(xp, pos_p, None, None, None, None, w)
    y_prompt = yp[:, N_META:]
    pos_s = PAST_LEN + jnp.arange(x_sample.shape[1], dtype=jnp.int32)
    y_sample, s_s5_re, s_s5_im, s_ret, s_gla = run_trunk(
        x_sample, pos_s, state_s5_re, state_s5_im, state_ret, state_gla, w)
    return (y_prompt, y_sample, p_s5_re, p_s5_im, p_ret, p_gla, s_s5_re, s_s5_im, s_ret, s_gla)
```

```python
import os
import numpy as np
import concourse.bass as bass
import concourse.mybir as mybir
from concourse.bass_utils import run_bass_kernel_spmd

F32 = mybir.dt.float32
BF16 = mybir.dt.bfloat16
AF = mybir.ActivationFunctionType
ALU = mybir.AluOpType

D = 1024
NCH = 8
DEPTH = 4
SEQ = 2048
N_META = 16
LP = SEQ + N_META
NSS = 16
DEC_SEQ = 8
NSAMP = NSS * DEC_SEQ
T = LP + NSAMP
PAST_LEN = 16384
D_FF = 2816
NFF = 22
EPS = 1e-6
TILES = [(0, 512), (512, 1024), (1024, 1536), (1536, 2048), (2048, 2192)]
NT = len(TILES)
SLOT = 3072
NSLOT = 5
SEM_LIMIT = 30000


class Sem:
    __slots__ = ("h", "count", "dma")

    def __init__(self, h, dma):
        self.h = h
        self.count = 0
        self.dma = dma


class Region:
    __slots__ = ("name", "w", "r")

    def __init__(self, name):
        self.name = name
        self.w = None
        self.r = {}


class Sched:
    ENG = ("pe", "act", "dve", "pool", "sp")

    def __init__(self, nc):
        self.nc = nc
        self.rec = {e: [] for e in self.ENG}
        self.cur = {}
        self.waited = {e: {} for e in self.ENG}
        self.nsem = 0
        for e in self.ENG:
            self.cur[e] = self.new_sem(e, False)
        self.all_dma_sems = []
        self.pe_pending = False

    def new_sem(self, name, dma):
        self.nsem += 1
        s = Sem(self.nc.alloc_semaphore("s_%s_%d" % (name, self.nsem)), dma)
        if dma:
            self.all_dma_sems.append(s)
        return s

    def dma_sem(self, name):
        s = Sem(self.nc.alloc_semaphore("d_%s_%d" % (name, self.nsem)), True)
        self.nsem += 1
        return s

    def _deps(self, reads, writes):
        deps = {}

        def add(tok):
            if tok is None:
                return
            s, v = tok
            if s.dma:
                v = s.count
            k = id(s)
            if k not in deps or deps[k][1] < v:
                deps[k] = (s, v)

        for r in reads:
            add(r.w)
        for w in writes:
            add(w.w)
            for tok in w.r.values():
                add(tok)
        return deps

    def op(self, eng, fn, reads=(), writes=(), dsem=None, signal=True):
        if eng != "pe":
            assert not self.pe_pending, "non-PE op recorded inside an unsignalled PE group"
        else:
            self.pe_pending = not signal
        deps = self._deps(reads, writes)
        waits = []
        wd = self.waited[eng]
        mysem = self.cur[eng]
        for k, (s, v) in deps.items():
            if eng == "pe" and s is mysem:
                continue
            if wd.get(k, 0) >= v:
                continue
            wd[k] = v
            waits.append((s.h, v))
        if dsem is not None:
            dsem.count += 16
            tok = (dsem, dsem.count)
            inc = (dsem.h, 16)
        elif signal:
            if mysem.count >= SEM_LIMIT:
                mysem = self.new_sem(eng, False)
                self.cur[eng] = mysem
            mysem.count += 1
            tok = (mysem, mysem.count)
            inc = (mysem.h, 1)
        else:
            tok = (mysem, mysem.count + 1)
            inc = None
        self.rec[eng].append((waits, fn, inc))
        for w in writes:
            w.w = tok
            w.r = {}
        for r in reads:
            k = id(tok[0])
            if k not in r.r or r.r[k][1] < tok[1]:
                r.r[k] = tok
        return tok

    def barrier(self, io_sems=()):
        toks = [(self.cur[e], self.cur[e].count) for e in ("pe", "act", "dve", "pool")]
        toks += [(s, s.count) for s in io_sems]
        for e in ("pe", "act", "dve", "pool", "sp"):
            waits = []
            for (s, v) in toks:
                if v == 0:
                    continue
                if self.waited[e].get(id(s), 0) >= v:
                    continue
                self.waited[e][id(s)] = v
                waits.append((s.h, v))
            if waits:
                self.rec[e].append((waits, None, None))

    def replay(self, final_waits):
        nc = self.nc
        handles = {"pe": "tensor", "act": "scalar", "dve": "vector", "pool": "gpsimd", "sp": "sync"}
        with nc.Block() as block:
            for e in self.ENG:
                rec = self.rec[e]
                fw = final_waits if e == "sp" else []

                def body(h, rec=rec, fw=fw):
                    for (waits, fn, inc) in rec:
                        for (sh, v) in waits:
                            h.wait_ge(sh, v)
                        if fn is not None:
                            ins = fn(h)
                            if inc is not None:
                                ins.then_inc(inc[0], inc[1])
                    for (sh, v) in fw:
                        h.wait_ge(sh, v)

                getattr(block, handles[e])(body)


class Prog:
    def __init__(self):
        nc = bass.Bass("TRN2", target_bir_lowering=False)
        self.nc = nc
        self.S = Sched(nc)
        self.loads = []
        self.wcols = 0
        self.dram_in = {}
        self.dram_out = {}

    def din(self, name, shape):
        t = self.nc.dram_tensor(name, list(shape), F32, kind="ExternalInput").ap()
        self.dram_in[name] = t
        return t

    def dout(self, name, shape):
        t = self.nc.dram_tensor(name, list(shape), F32, kind="ExternalOutput").ap()
        self.dram_out[name] = t
        return t

    def sb(self, name, shape, dt):
        return self.nc.alloc_sbuf_tensor(name, list(shape), dt)

    def init_psum(self):
        self.banks = [self.nc.alloc_psum_tensor("ps%d" % i, [128, 512], F32) for i in range(8)]
        self.bank_r = [Region("ps%d" % i) for i in range(8)]
        self.bank_i = 0
        self.rot = list(range(8))

    def bank(self):
        i = self.rot[self.bank_i % len(self.rot)]
        self.bank_i += 1
        return self.banks[i], self.bank_r[i]

    def init_wstream(self, wts_ap):
        self.wts = wts_ap
        self.slots = [self.sb("wslot%d" % i, [128, SLOT], BF16) for i in range(NSLOT)]
        self.slot_r = [Region("wslot%d" % i) for i in range(NSLOT)]
        self.slot_sem = [self.S.dma_sem("w%d" % i) for i in range(NSLOT)]
        self.issued = 0
        self.released = set()
        self.next_get = 0

    def _pump(self):
        while self.issued < len(self.loads):
            j = self.issued
            if j >= NSLOT and (j - NSLOT) not in self.released:
                break
            off, n = self.loads[j]
            s = j % NSLOT
            dst = self.slots[s][:, 0:n]
            src = self.wts[:, off:off + n]
            self.S.op("pool", lambda h, dst=dst, src=src: h.dma_start(out=dst, in_=src),
                      writes=[self.slot_r[s]], dsem=self.slot_sem[s])
            self.issued += 1

    def wget(self):
        j = self.next_get
        self.next_get += 1
        self._pump()
        assert j < self.issued, "weight stream stalled (load %d not issuable: release missing)" % j
        return self.slots[j % NSLOT], self.slot_r[j % NSLOT], j

    def wrelease(self, j):
        self.released.add(j)
        self._pump()


KAB = os.environ.get("KAB", "rs")


def parse_subs():
    spec = os.environ.get("KSUBS")
    if spec:
        out = []
        for tok in spec.split(","):
            layer = int(tok[0])
            kind = {"f1": "ffn1", "m": "mix", "f2": "ffn2"}[tok[1:]]
            out.append((layer, kind))
        return out
    return [(l, k) for l in range(DEPTH) for k in ("ffn1", "mix", "ffn2")]


def make_plan(subs):
    plan = []
    for (layer, kind) in subs:
        i = layer // 2
        if kind == "ffn1" or kind == "ffn2":
            for j in range(NFF):
                plan.append((("ffn", layer, 0 if kind == "ffn1" else 1, j), SLOT))
        elif layer % 2 == 1:
            for hd in range(4):
                plan.append((("gla", i, hd, 0), 2048 + 128))
                plan.append((("gla", i, hd, 1), 3072))
                plan.append((("gla", i, hd, 2), 3072))
        else:
            if "r" in KAB:
                for pr in range(2):
                    plan.append((("ret", i, pr, 0), 3072))
                    plan.append((("ret", i, pr, 1), 3072))
                    plan.append((("ret", i, pr, 2), 2048))
                    plan.append((("ret", i, pr, 3), 2048))
            if "s" in KAB:
                for yc in range(4):
                    plan.append((("s5u", i, yc), 1024))
                plan.append((("s5g", i), 2048))
                plan.append((("s5o", i, 0), 2048))
                plan.append((("s5o", i, 1), 2048))
    return plan


class Layout:
    def __init__(self):
        self.off = {}
        self.n = 0

    def add(self, name, ncols):
        self.off[name] = (self.n, ncols)
        self.n += ncols

    def __getitem__(self, name):
        return self.off[name]


def cst_layout():
    L = Layout()
    L.add("gains", 13 * NCH)
    L.add("gla_b", 2 * 4)
    L.add("gla_g", 2 * 2)
    L.add("s5_d", 2 * 4)
    L.add("one", 1)
    L.add("hm", 2)
    L.add("cmask", 512)
    L.add("cmask4", 144)
    L.add("ident32", 128)
    L.add("nident32", 128)
    L.add("causal", 128)
    L.add("samp", 128)
    L.add("seqm", 16)
    L.add("ident", 128)
    L.add("w2", 2 * 512)
    L.add("jvec", 256)
    L.add("ret_gam", 18)
    L.add("s5sm", 2 * 3 * 16)
    return L


def build_program(subs):
    P = Prog()
    nc, S = P.nc, P.S
    plan = make_plan(subs)
    CL = cst_layout()
    has_gla = any(k == "mix" and l % 2 == 1 for (l, k) in subs)
    has_ab = any(k == "mix" and l % 2 == 0 for (l, k) in subs)
    WCOLS = max(sum(n for (_, n) in plan), 1)
    xT = P.din("xT", [D, T])
    cst = P.din("cst", [128, CL.n])
    wts = P.din("wts", [128, WCOLS])
    yT = P.dout("yT", [D, T])
    if has_gla:
        st_gla = P.din("st_gla", [2, 4, 128, NSS, 256])
        o_pgla = P.dout("o_pgla", [2, 4, 128, 256])
        o_sgla = P.dout("o_sgla", [2, 4, 128, NSS, 256])
    if has_ab:
        rot_d = P.din("rot_d", [2, NT, 128, 4, 512])
        s5w = P.din("s5w", [2, 4, 128, 7 * 512])
        st_s5 = P.din("st_s5", [2, 2, 128, 16, NSS])
        st_ret = P.din("st_ret", [2, 2, 128, NSS, 128])
        o_pret = P.dout("o_pret", [2, 2, 128, 128])
        o_sret = P.dout("o_sret", [2, 2, 128, NSS, 128])
        o_ps5 = P.dout("o_ps5", [2, 2, 128, 16])
        o_ss5 = P.dout("o_ss5", [2, 2, 128, 16, NSS])

    x = P.sb("x", [128, NCH, T], F32)
    hb = P.sb("hb", [128, NCH, T], BF16)
    NCF = CL["w2"][0]
    NCF = CL["causal"][0]
    cf = P.sb("cf", [128, NCF], F32)
    jv = P.sb("jv", [128, 256 + 18 + 96], F32)
    cb = P.sb("cb", [128, 128 + 128 + 16 + 128 + 1024], BF16)
    ones_bf = P.sb("ones_bf", [128, 128], BF16)
    eps_t = P.sb("eps_t", [128, 1], F32)
    ones32 = P.sb("ones32", [128, 128], F32)
    ARENA = 16384
    arena = P.sb("arena", [128, ARENA], F32)
    x_r = [[Region("x%d_%d" % (c, t)) for t in range(NT)] for c in range(NCH)]
    h_r = [[Region("h%d_%d" % (c, t)) for t in range(NT)] for c in range(NCH)]
    cst_r = Region("cst")
    P.init_psum()
    P.init_wstream(wts)
    off = 0
    for (_, n) in plan:
        P.loads.append((off, n))
        off += n

    ar = {"off": 0, "cache": None}

    def aalloc(ncols_f32, dt=F32, name="a"):
        o = ar["off"]
        ar["off"] = o + (ncols_f32 + 15) // 16 * 16
        assert ar["off"] <= ARENA, "arena overflow %d" % ar["off"]
        if ar["cache"] is not None:
            key = (o, ncols_f32, dt)
            if key not in ar["cache"]:
                ap = arena[:, o:o + ncols_f32]
                if dt == BF16:
                    ap = ap.bitcast(BF16)
                ar["cache"][key] = (ap, Region(name))
            return ar["cache"][key]
        ap = arena[:, o:o + ncols_f32]
        if dt == BF16:
            ap = ap.bitcast(BF16)
        return ap, Region(name)

    io_sems = []

    def areset():
        S.barrier(io_sems)
        ar["off"] = 0
        ar["cache"] = None

    def areset_passes(first):
        if first:
            areset()
            ar["cache"] = {}
        else:
            ar["off"] = 0

    def ACT(out, in_, func, reads, writes, bias=None, scale=None):
        kw = {}
        if bias is not None:
            kw["bias"] = bias
        if scale is not None:
            kw["scale"] = scale
        return S.op("act", lambda h: h.activation(out=out, in_=in_, func=func, **kw), reads=reads, writes=writes)

    def TT(eng, out, in0, in1, op, reads, writes):
        return S.op(eng, lambda h: h.tensor_tensor(out=out, in0=in0, in1=in1, op=op), reads=reads, writes=writes)

    def TS(eng, out, in0, s1, op0, reads, writes, s2=None, op1=None):
        if op1 is None:
            return S.op(eng, lambda h: h.tensor_scalar(out=out, in0=in0, scalar1=s1, scalar2=None, op0=op0),
                        reads=reads, writes=writes)
        return S.op(eng, lambda h: h.tensor_scalar(out=out, in0=in0, scalar1=s1, scalar2=s2, op0=op0, op1=op1),
                    reads=reads, writes=writes)

    def STT(eng, out, in0, scalar, in1, op0, op1, reads, writes):
        return S.op(eng, lambda h: h.scalar_tensor_tensor(out=out, in0=in0, scalar=scalar, in1=in1, op0=op0, op1=op1),
                    reads=reads, writes=writes)

    def COPY(eng, out, in_, reads, writes):
        if eng == "act":
            return S.op("act", lambda h: h.copy(out=out, in_=in_), reads=reads, writes=writes)
        return S.op(eng, lambda h: h.tensor_copy(out=out, in_=in_), reads=reads, writes=writes)

    def MM(out, lhsT, rhs, start, stop, reads, writes, signal=None):
        if signal is None:
            signal = stop
        return S.op("pe", lambda h: h.matmul(out, lhsT=lhsT, rhs=rhs, start=start, stop=stop),
                    reads=reads, writes=writes, signal=signal)

    def DMA(eng, out, in_, reads, writes, sem):
        return S.op(eng, lambda h: h.dma_start(out=out, in_=in_), reads=reads, writes=writes, dsem=sem)

    def SCAN(out, data0, data1, reads, writes, initial=0.0):
        return S.op("dve", lambda h: h.tensor_tensor_scan(out=out, data0=data0, data1=data1, initial=initial,
                                                          op0=ALU.mult, op1=ALU.add), reads=reads, writes=writes)

    def RECIP(out, in_, reads, writes):
        return S.op("dve", lambda h: h.reciprocal(out=out, in_=in_), reads=reads, writes=writes)

    def TRANSP(out, in_, reads, writes):
        return S.op("pe", lambda h: h.transpose(out=out, in_=in_, identity=ident_bf), reads=reads, writes=writes)

    def MEMSET(eng, out, val, writes):
        return S.op(eng, lambda h: h.memset(out, val), writes=writes)

    def ccol(name, j=0, n=1):
        o, _ = CL[name]
        return cf[:, o + j:o + j + n]

    in_sem = S.dma_sem("in")
    io_sems.append(in_sem)
    DMA("sp", cf[:], cst[:, 0:NCF], [], [cst_r], in_sem)
    for c in range(NCH):
        DMA("sp", x[:, c, :], xT[c * 128:(c + 1) * 128, :], [], [x_r[c][t] for t in range(NT)], in_sem)
    nb16 = 128 + 128 + 16 + 128 + 1024
    stg, stg_r = aalloc(nb16 + 256 + 18 + 96, F32, "stg")
    o_c = CL["causal"][0]
    DMA("sp", stg, cst[:, o_c:o_c + nb16 + 256 + 18 + 96], [], [stg_r], in_sem)
    COPY("dve", cb[:], stg[:, 0:nb16], [stg_r], [cst_r])
    COPY("dve", jv[:], stg[:, nb16:nb16 + 256 + 18 + 96], [stg_r], [cst_r])
    S.op("dve", lambda h: h.memset(ones_bf[:], 1.0), writes=[cst_r])
    S.op("dve", lambda h: h.memset(eps_t[:], EPS), writes=[cst_r])
    S.op("dve", lambda h: h.memset(ones32[:], 1.0), writes=[cst_r])
    causal_bf = cb[:, 0:128]
    samp_bf = cb[:, 128:256]
    seqm_bf = cb[:, 256:272]
    ident_bf = cb[:, 272:400]
    w2_bf = cb[:, 400:1424]
    jvec = jv[:, 0:256]
    areset()

    cnt = {"sq": 0, "rstd": 0, "sg": 0}
    nt_ = {}

    def norm_alloc():
        nt_["sq"] = [aalloc(256, BF16, "sq%d" % i) for i in range(4)]
        nt_["rstd"] = [aalloc(512, F32, "rstd%d" % i) for i in range(2)]

    def norm_tile(t, gidx, final=False):
        sq = [a for (a, _) in nt_["sq"]]
        sq_r = [r for (_, r) in nt_["sq"]]
        rstd = [a for (a, _) in nt_["rstd"]]
        rstd_r = [r for (_, r) in nt_["rstd"]]
        t0, t1 = TILES[t]
        n = t1 - t0
        pb, pr = P.bank()
        for c in range(NCH):
            i = cnt["sq"] % 4
            cnt["sq"] += 1
            ACT(sq[i][:, 0:n], x[:, c, t0:t1], AF.Square, [x_r[c][t]], [sq_r[i]])
            MM(pb[:, 0:n], ones_bf[:], sq[i][:, 0:n], c == 0, c == NCH - 1, [sq_r[i], cst_r], [pr], signal=True)
        k = cnt["rstd"] % 2
        cnt["rstd"] += 1
        ACT(rstd[k][:, 0:n], pb[:, 0:n], AF.Sqrt, [pr, cst_r], [rstd_r[k]], bias=eps_t[:, 0:1], scale=1.0 / D)
        RECIP(rstd[k][:, 0:n], rstd[k][:, 0:n], [rstd_r[k]], [rstd_r[k]])
        for c in range(NCH):
            g = ccol("gains", gidx * NCH + c)
            if final:
                STT("dve", x[:, c, t0:t1], x[:, c, t0:t1], g, rstd[k][:, 0:n], ALU.mult, ALU.mult,
                    [x_r[c][t], rstd_r[k], cst_r], [x_r[c][t]])
            else:
                STT("dve", hb[:, c, t0:t1], x[:, c, t0:t1], g, rstd[k][:, 0:n], ALU.mult, ALU.mult,
                    [x_r[c][t], rstd_r[k], cst_r], [h_r[c][t]])

    def ffn(gidx):
        areset()
        norm_alloc()
        sg_a = [aalloc(256, BF16, "sg%d" % i) for i in range(2)]
        hid_a = [[aalloc(256, BF16, "hid%d_%d" % (i, j)) for j in range(2)] for i in range(2)]
        sg = [a for (a, _) in sg_a]
        sg_r = [r for (_, r) in sg_a]
        hid = [[a for (a, _) in row] for row in hid_a]
        hid_r = [[r for (_, r) in row] for row in hid_a]
        for t in range(NT):
            norm_tile(t, gidx)
        steps = [(g, t) for g in range(NFF // 2) for t in range(NT)]
        slots = {}

        def A(si, jj):
            g, t = steps[si]
            t0, t1 = TILES[t]
            n = t1 - t0
            if t == 0:
                slots[(g, jj)] = P.wget()
            ws, wr, _ = slots[(g, jj)]
            pg, pgr = P.bank()
            pu, pur = P.bank()
            for (pb, pr, off) in ((pg, pgr, 0), (pu, pur, 1024)):
                for kc in range(NCH):
                    MM(pb[:, 0:n], ws[:, off + kc * 128: off + (kc + 1) * 128], hb[:, kc, t0:t1],
                       kc == 0, kc == NCH - 1, [wr, h_r[kc][t]], [pr])
            i = cnt["sg"] % 2
            cnt["sg"] += 1
            ACT(sg[i][:, 0:n], pg[:, 0:n], AF.Silu, [pgr], [sg_r[i]])
            TT("dve", hid[si % 2][jj][:, 0:n], sg[i][:, 0:n], pu[:, 0:n], ALU.mult, [sg_r[i], pur], [hid_r[si % 2][jj]])

        def B(si, half):
            g, t = steps[si]
            t0, t1 = TILES[t]
            n = t1 - t0
            for m in range(half * 4, half * 4 + 4):
                pb, pr = P.bank()
                for jj in range(2):
                    ws, wr, _ = slots[(g, jj)]
                    MM(pb[:, 0:n], ws[:, 2048 + m * 128: 2048 + (m + 1) * 128], hid[si % 2][jj][:, 0:n],
                       jj == 0, jj == 1, [wr, hid_r[si % 2][jj]], [pr])
                STT("dve", x[:, m, t0:t1], pb[:, 0:n], 0.5, x[:, m, t0:t1], ALU.mult, ALU.add,
                    [pr, x_r[m][t]], [x_r[m][t]])
            if half == 1 and t == NT - 1:
                for jj in range(2):
                    P.wrelease(slots[(g, jj)][2])

        ns = len(steps)
        A(0, 0)
        A(0, 1)
        for si in range(1, ns):
            A(si, 0)
            B(si - 1, 0)
            A(si, 1)
            B(si - 1, 1)
        B(ns - 1, 0)
        B(ns - 1, 1)

    def tile_blocks(t):
        if t < NT - 1:
            return [(b * 128, 128, "p") for b in range(4)]
        return [(0, 16, "p"), (16, 128, "s")]

    XOFF = int(os.environ.get("KXOFF", "0"))

    def gla_pass(i, hd, st_sem, out_sem):
        DK = 128
        w1, w1r, j1 = P.wget()
        w2s, w2r, j2 = P.wget()
        w3, w3r, j3 = P.wget()
        Wq = lambda kc: w1[:, kc * 128:(kc + 1) * 128]
        Wk = lambda kc: w1[:, 1024 + kc * 128:1024 + (kc + 1) * 128]
        Wlr = lambda kc: w1[:, 2048 + kc * 16:2048 + (kc + 1) * 16]
        Wv = lambda kc: w2s[:, kc * 256:(kc + 1) * 256]
        Wr = lambda cc, kc: (w2s[:, 2048 + kc * 128:2048 + (kc + 1) * 128] if cc == 0
                             else w3[:, kc * 128:(kc + 1) * 128])
        Wrr = lambda cc: (w2r if cc == 0 else w3r)
        Wo = lambda cc, m: w3[:, 1024 + cc * 1024 + m * 128:1024 + cc * 1024 + (m + 1) * 128]
        SETS = []
        for sidx in range(2):
            d = {}
            for nm, ncol, dt in (("lrb", 256, BF16), ("ee", 512, F32), ("Bc", 512, F32), ("E", 512, F32), ("Ei", 512, F32),
                                 ("qt", 256, BF16), ("kt", 256, BF16)):
                d[nm], d[nm + "_r"] = aalloc(ncol, dt, nm + str(sidx))
            for b in range(4):
                d["vtok%d" % b], d["vtok%d_r" % b] = aalloc(128, BF16, "vtok")
                d["ktok%d" % b], d["ktok%d_r" % b] = aalloc(64, BF16, "ktok")
                d["Am%d" % b], d["Am%d_r" % b] = aalloc(64, BF16, "Am")
            SETS.append(d)
        SR = []
        for q in range(3):
            SR.append([aalloc(256, BF16, "sr%d_%d" % (q, cc)) for cc in range(2)])
        S32, S32_r = aalloc(256, F32, "S32")
        Sbf, Sbf_r = aalloc(128, BF16, "Sbf")
        sqo, sqo_r = [], []
        og, og_r, og2, og2_r = [], [], [], []
        for cc in range(2):
            a, r = aalloc(256, BF16, "sqo%d" % cc)
            sqo.append(a)
            sqo_r.append(r)
            a, r = aalloc(512, F32, "og%d" % cc)
            og.append(a)
            og_r.append(r)
            a, r = aalloc(256, BF16, "og2%d" % cc)
            og2.append(a)
            og2_r.append(r)
        rs, rs_r = aalloc(512, F32, "rs")
        S0, S0_r = aalloc(2048, F32, "S0")
        S0b, S0b_r = aalloc(1024, BF16, "S0b")
        kmask, kmask_r = aalloc(512, BF16, "kmask")
        negb, negb_r = aalloc(1, F32, "negb")
        xtmp = [aalloc(512, F32, "xtmp%d" % q) for q in range(2)] if XOFF else None

        TS("dve", negb, ccol("gla_b", i * 4 + hd), -1.0, ALU.mult, [cst_r], [negb_r])
        MEMSET("dve", S32, 0.0, [S32_r])
        MEMSET("dve", Sbf, 0.0, [Sbf_r])
        PO = [[P.banks[0], P.banks[1]], [P.banks[2], P.banks[3]]]
        PO_r = [[P.bank_r[0], P.bank_r[1]], [P.bank_r[2], P.bank_r[3]]]

        def stage1(t):
            d = SETS[t % 2]
            t0, t1 = TILES[t]
            n = t1 - t0
            blocks = tile_blocks(t)
            sl = []

            def s_decay():
                pb, pr = P.bank()
                for kc in range(NCH):
                    MM(pb[0:16, 0:n], Wlr(kc), hb[:, kc, t0:t1], kc == 0, kc == NCH - 1, [w1r, h_r[kc][t]], [pr])
                COPY("act", d["lrb"][0:16, 0:n], pb[0:16, 0:n], [pr], [d["lrb_r"]])
                pz, pzr = P.bank()
                MM(pz[:, 0:n], w2_bf[0:16, i * 512 + hd * 128:i * 512 + (hd + 1) * 128], d["lrb"][0:16, 0:n], True, True,
                   [d["lrb_r"], cst_r], [pzr])
                ACT(d["ee"][:, 0:n], pz[:, 0:n], AF.Exp, [pzr, negb_r], [d["ee_r"]], bias=negb[:, 0:1], scale=-1.0)
                ACT(d["ee"][:, 0:n], d["ee"][:, 0:n], AF.Ln, [d["ee_r"], cst_r], [d["ee_r"]], bias=ccol("one"), scale=1.0)
                cm = ccol("cmask", 0, n) if t < NT - 1 else ccol("cmask4", 0, n)
                SCAN(d["Bc"][:, 0:n], cm, d["ee"][:, 0:n], [d["ee_r"], cst_r], [d["Bc_r"]])
                ACT(d["E"][:, 0:n], d["Bc"][:, 0:n], AF.Exp, [d["Bc_r"]], [d["E_r"]], scale=-1.0 / 16.0)
                ACT(d["Ei"][:, 0:n], d["Bc"][:, 0:n], AF.Exp, [d["Bc_r"]], [d["Ei_r"]], scale=1.0 / 16.0)
            sl.append(s_decay)

            def s_q():
                pq, pqr = P.bank()
                for kc in range(NCH):
                    MM(pq[:, 0:n], Wq(kc), hb[:, kc, t0:t1], kc == 0, kc == NCH - 1, [w1r, h_r[kc][t]], [pqr])
                TT("dve", d["qt"][:, 0:n], pq[:, 0:n], d["E"][:, 0:n], ALU.mult, [pqr, d["E_r"]], [d["qt_r"]])
            sl.append(s_q)

            def s_k():
                pk, pkr = P.bank()
                for kc in range(NCH):
                    MM(pk[:, 0:n], Wk(kc), hb[:, kc, t0:t1], kc == 0, kc == NCH - 1, [w1r, h_r[kc][t]], [pkr])
                STT("dve", d["kt"][:, 0:n], pk[:, 0:n], DK ** -0.5, d["Ei"][:, 0:n], ALU.mult, ALU.mult,
                    [pkr, d["Ei_r"]], [d["kt_r"]])
            sl.append(s_k)

            def mk_gate(cc):
                def s_gate():
                    pg, pgr = P.bank()
                    for kc in range(NCH):
                        MM(pg[:, 0:n], Wr(cc, kc), hb[:, kc, t0:t1], kc == 0, kc == NCH - 1, [Wrr(cc), h_r[kc][t]], [pgr])
                    ACT(SR[t % 3][cc][0][:, 0:n], pg[:, 0:n], AF.Silu, [pgr], [SR[t % 3][cc][1]])
                return s_gate
            sl.append(mk_gate(0))
            sl.append(mk_gate(1))

            def mk_tok(b, c0, nb):
                def s_tok():
                    pv, pvr = P.bank()
                    for kc in range(NCH):
                        MM(pv[0:nb, 0:256], hb[:, kc, t0 + c0:t0 + c0 + nb], Wv(kc), kc == 0, kc == NCH - 1,
                           [w2r, h_r[kc][t]], [pvr])
                    COPY("act", d["vtok%d" % b][0:nb, 0:256], pv[0:nb, 0:256], [pvr], [d["vtok%d_r" % b]])
                    pt, ptr = P.bank()
                    ptb = pt[:, 0:64].bitcast(BF16)
                    TRANSP(ptb[0:nb, 0:128], d["kt"][:, c0:c0 + nb], [d["kt_r"], cst_r], [ptr])
                    COPY("dve", d["ktok%d" % b][0:nb, 0:128], ptb[0:nb, 0:128], [ptr], [d["ktok%d_r" % b]])
                    ps_, psr = P.bank()
                    MM(ps_[0:nb, 0:nb], d["kt"][:, c0:c0 + nb], d["qt"][:, c0:c0 + nb], True, True,
                       [d["kt_r"], d["qt_r"]], [psr])
                    mk = causal_bf if blocks[b][2] == "p" else samp_bf
                    TT("dve", d["Am%d" % b][0:nb, 0:nb], ps_[0:nb, 0:nb], mk[0:nb, 0:nb], ALU.mult, [psr, cst_r],
                       [d["Am%d_r" % b]])
                return s_tok
            for b, (c0, nb, kind) in enumerate(blocks):
                sl.append(mk_tok(b, c0, nb))
            return sl

        def chain_steps(t):
            d = SETS[t % 2]
            po, po_r = PO[t % 2], PO_r[t % 2]
            t0, t1 = TILES[t]
            blocks = tile_blocks(t)
            steps = []

            def mk_p(b, c0, nb):
                def step():
                    vt_, vr_ = d["vtok%d" % b], d["vtok%d_r" % b]
                    kt_, kr_ = d["ktok%d" % b], d["ktok%d_r" % b]
                    am_, ar_ = d["Am%d" % b], d["Am%d_r" % b]
                    pU, pUr = P.bank()
                    MM(pU[:, 0:256], kt_[0:nb, 0:128], vt_[0:nb, 0:256], True, True, [kr_, vr_], [pUr])
                    for cc in range(2):
                        MM(po[cc][:, c0:c0 + nb], vt_[0:nb, cc * 128:(cc + 1) * 128], am_[0:nb, 0:nb], True, False,
                           [vr_, ar_], [po_r[cc]], signal=False)
                        MM(po[cc][:, c0:c0 + nb], Sbf[:, cc * 128:(cc + 1) * 128], d["qt"][:, c0:c0 + nb], False, True,
                           [Sbf_r, d["qt_r"]], [po_r[cc]], signal=True)
                    e = d["E"][:, c0 + nb - 1:c0 + nb]
                    TS("dve", S32, S32, e, ALU.mult, [S32_r, d["E_r"]], [S32_r])
                    STT("dve", S32, pU[:, 0:256], e, S32, ALU.mult, ALU.add, [pUr, d["E_r"], S32_r], [S32_r])
                    COPY("act", Sbf, S32, [S32_r], [Sbf_r])
                    if t == NT - 1:
                        DMA("sp", o_pgla[i, hd], S32, [S32_r], [], out_sem)
                return step

            def mk_s(b, c0, nb, half):
                def step():
                    vt_, vr_ = d["vtok%d" % b], d["vtok%d_r" % b]
                    kt_, kr_ = d["ktok%d" % b], d["ktok%d_r" % b]
                    am_, ar_ = d["Am%d" % b], d["Am%d_r" % b]
                    DMA("sp", S0.rearrange("p (s v) -> p s v", v=256), st_gla[i, hd, :, half * 8:(half + 1) * 8, :],
                        [], [S0_r], st_sem)
                    COPY("act", S0b, S0, [S0_r], [S0b_r])
                    TT("dve", kmask.rearrange("p (s k) -> p s k", k=128),
                       kt_[:, 0:128].unsqueeze(1).to_broadcast([128, 8, 128]),
                       seqm_bf[:, half * 8:(half + 1) * 8].unsqueeze(2).to_broadcast([128, 8, 128]),
                       ALU.mult, [kr_, cst_r], [kmask_r])
                    for s_ in range(8):
                        sq_i = half * 8 + s_
                        cs = c0 + sq_i * 8
                        for cc in range(2):
                            MM(po[cc][:, cs:cs + 8], vt_[0:nb, cc * 128:(cc + 1) * 128], am_[0:nb, sq_i * 8:sq_i * 8 + 8],
                               True, False, [vr_, ar_], [po_r[cc]], signal=False)
                            MM(po[cc][:, cs:cs + 8], S0b[:, s_ * 256 + cc * 128:s_ * 256 + (cc + 1) * 128],
                               d["qt"][:, cs:cs + 8], False, True, [S0b_r, d["qt_r"]], [po_r[cc]], signal=True)
                    for s_ in range(8):
                        sq_i = half * 8 + s_
                        cs = c0 + sq_i * 8
                        pU, pUr = P.bank()
                        MM(pU[:, 0:256], kmask[:, s_ * 128:(s_ + 1) * 128], vt_[0:nb, 0:256], True, True,
                           [kmask_r, vr_], [pUr])
                        e = d["E"][:, cs + 7:cs + 8]
                        S0s = S0[:, s_ * 256:(s_ + 1) * 256]
                        TS("dve", S0s, S0s, e, ALU.mult, [S0_r, d["E_r"]], [S0_r])
                        STT("dve", S0s, pU[:, 0:256], e, S0s, ALU.mult, ALU.add, [pUr, d["E_r"], S0_r], [S0_r])
                    DMA("sp", o_sgla[i, hd, :, half * 8:(half + 1) * 8, :], S0.rearrange("p (s v) -> p s v", v=256),
                        [S0_r], [], out_sem)
                return step

            for b, (c0, nb, kind) in enumerate(blocks):
                if kind == "p":
                    steps.append(mk_p(b, c0, nb))
                else:
                    steps.append(mk_s(b, c0, nb, 0))
                    steps.append(mk_s(b, c0, nb, 1))
            return steps

        def tail_steps(t):
            d = SETS[t % 2]
            t0, t1 = TILES[t]
            n = t1 - t0
            sr = [SR[t % 3][cc][0] for cc in range(2)]
            sr_r = [SR[t % 3][cc][1] for cc in range(2)]
            po, po_r = PO[t % 2], PO_r[t % 2]
            st = {}

            def T1():
                pn, pnr = P.bank()
                st["pn"] = (pn, pnr)
                for cc in range(2):
                    ACT(sqo[cc][:, 0:n], po[cc][:, 0:n], AF.Square, [po_r[cc]], [sqo_r[cc]])
                    MM(pn[:, 0:n], ones_bf[:], sqo[cc][:, 0:n], cc == 0, cc == 1, [sqo_r[cc], cst_r], [pnr], signal=True)

            def T2():
                pn, pnr = st["pn"]
                ACT(rs[:, 0:n], pn[:, 0:n], AF.Sqrt, [pnr, cst_r], [rs_r], bias=eps_t[:, 0:1], scale=1.0 / 256.0)
                RECIP(rs[:, 0:n], rs[:, 0:n], [rs_r], [rs_r])

            def T3():
                for cc in range(2):
                    TT("dve", og[cc][:, 0:n], po[cc][:, 0:n], rs[:, 0:n], ALU.mult, [po_r[cc], rs_r], [og_r[cc]])
                    STT("dve", og2[cc][:, 0:n], og[cc][:, 0:n], ccol("gla_g", i * 2 + cc), sr[cc][:, 0:n], ALU.mult, ALU.mult,
                        [og_r[cc], sr_r[cc], cst_r], [og2_r[cc]])

            def mk_T4(m0):
                def T4():
                    for m in range(m0, m0 + 4):
                        pm, pmr = P.bank()
                        for cc in range(2):
                            MM(pm[:, 0:n], Wo(cc, m), og2[cc][:, 0:n], cc == 0, cc == 1, [w3r, og2_r[cc]], [pmr])
                        TT("dve", x[:, m, t0:t1], pm[:, 0:n], x[:, m, t0:t1], ALU.add, [pmr, x_r[m][t]], [x_r[m][t]])
                return T4
            def T12():
                T1()
                T2()
            return [T12, T3, mk_T4(0), mk_T4(4)]

        for f in stage1(0):
            f()
        prev_tail = []
        for t in range(NT):
            fill = stage1(t + 1) if t + 1 < NT else []
            steps = chain_steps(t)
            merged = []
            a, b = list(fill), list(prev_tail)
            while a or b:
                if a:
                    merged.append(a.pop(0))
                if b:
                    merged.append(b.pop(0))
                if a:
                    merged.append(a.pop(0))
            nst = len(steps)
            fi = 0
            for si, st in enumerate(steps):
                st()
                want = (len(merged) * (si + 1) + nst - 1) // nst
                while fi < min(want, len(merged)):
                    merged[fi]()
                    fi += 1
            while fi < len(merged):
                merged[fi]()
                fi += 1
            prev_tail = tail_steps(t)
        for f in prev_tail:
            f()
        for j in (j1, j2, j3):
            P.wrelease(j)

    def gla_mixer(layer):
        i = layer // 2
        areset()
        norm_alloc()
        for t in range(NT):
            norm_tile(t, 4 + layer)
        P.rot = [4, 5, 6, 7]
        for hd in range(4):
            areset_passes(hd == 0)
            gla_pass(i, hd, st_sem, out_sem)
        areset()
        P.rot = list(range(8))

    KRET = int(os.environ.get("KRET", "9"))

    def ret_pass(i, pr, st_sem, out_sem):
        w1, w1r, j1 = P.wget()
        w2s, w2r, j2 = P.wget()
        w3, w3r, j3 = P.wget()
        w4, w4r, j4 = P.wget()
        Wq = lambda kc: w1[:, kc * 128:(kc + 1) * 128]
        Wqs = lambda kc: w1[:, 1024 + kc * 128:1024 + (kc + 1) * 128]
        Wk = lambda kc: w1[:, 2048 + kc * 128:2048 + (kc + 1) * 128]
        Wks = lambda kc: w2s[:, kc * 128:(kc + 1) * 128]
        Wv = lambda kc: w2s[:, 1024 + kc * 256:1024 + (kc + 1) * 256]
        Wg = lambda cc, kc: w3[:, cc * 1024 + kc * 128:cc * 1024 + (kc + 1) * 128]
        Wo = lambda cc, m: w4[:, cc * 1024 + m * 128:cc * 1024 + (m + 1) * 128]
        rot, rot_r = aalloc(2048, F32, "rot")
        tA, tA_r = aalloc(512, F32, "tA")
        tB, tB_r = aalloc(512, F32, "tB")
        kt, kt_r = aalloc(256, BF16, "kt")
        SETS = []
        for sidx in range(2):
            d = {}
            for hh in range(2):
                d["qth%d" % hh], d["qth%d_r" % hh] = aalloc(256, BF16, "qth")
            for b in range(4):
                d["vtok%d" % b], d["vtok%d_r" % b] = aalloc(128, BF16, "vtok")
                d["ktok%d" % b], d["ktok%d_r" % b] = aalloc(64, BF16, "ktok")
                d["Am%d" % b], d["Am%d_r" % b] = aalloc(128, BF16, "Am")
            SETS.append(d)
        SG = [[aalloc(256, BF16, "sgt%d_%d" % (q, cc)) for cc in range(2)] for q in range(3)]
        S32, S32_r = aalloc(128, F32, "S32")
        Sbf, Sbf_r = aalloc(64, BF16, "Sbf")
        obf, obf_r, o32, o32_r, csq, csq_r, og2, og2_r = [], [], [], [], [], [], [], []
        for cc in range(2):
            a, r = aalloc(256, BF16, "obf%d" % cc)
            obf.append(a)
            obf_r.append(r)
            a, r = aalloc(512, F32, "o32%d" % cc)
            o32.append(a)
            o32_r.append(r)
            a, r = aalloc(256, BF16, "csq%d" % cc)
            csq.append(a)
            csq_r.append(r)
            a, r = aalloc(256, BF16, "og2%d" % cc)
            og2.append(a)
            og2_r.append(r)
        rsh, rsh_r = [], []
        for hh in range(2):
            a, r = aalloc(512, F32, "rs%d" % hh)
            rsh.append(a)
            rsh_r.append(r)
        S0, S0_r = aalloc(1024, F32, "S0")
        S0b, S0b_r = aalloc(512, BF16, "S0b")
        kmask, kmask_r = aalloc(512, BF16, "kmask")
        MEMSET("dve", S32, 0.0, [S32_r])
        MEMSET("dve", Sbf, 0.0, [Sbf_r])
        PO = [[P.banks[0], P.banks[1]], [P.banks[2], P.banks[3]]]
        PO_r = [[P.bank_r[0], P.bank_r[1]], [P.bank_r[2], P.bank_r[3]]]
        gam = lambda kind, v: jv[:, 256 + (pr * 3 + kind) * 3 + v:256 + (pr * 3 + kind) * 3 + v + 1]

        def stage1(t):
            d = SETS[t % 2]
            t0, t1 = TILES[t]
            n = t1 - t0
            blocks = tile_blocks(t)
            sl = []

            def s_q():
                DMA("sp", rot.rearrange("p (a c) -> p a c", c=512), rot_d[pr, t], [], [rot_r], st_sem)
                cq, sq_ = rot[:, 0:n], rot[:, 512:512 + n]
                pa, par = P.bank()
                for kc in range(NCH):
                    MM(pa[:, 0:n], Wq(kc), hb[:, kc, t0:t1], kc == 0, kc == NCH - 1, [w1r, h_r[kc][t]], [par])
                pb, pbr = P.bank()
                for kc in range(NCH):
                    MM(pb[:, 0:n], Wqs(kc), hb[:, kc, t0:t1], kc == 0, kc == NCH - 1, [w1r, h_r[kc][t]], [pbr])
                for hh in range(2):
                    STT("dve", tA[:, 0:n], pa[:, 0:n], ccol("hm", hh), cq, ALU.mult, ALU.mult, [par, rot_r, cst_r], [tA_r])
                    STT("dve", tB[:, 0:n], pb[:, 0:n], ccol("hm", hh), sq_, ALU.mult, ALU.mult, [pbr, rot_r, cst_r], [tB_r])
                    TT("pool", d["qth%d" % hh][:, 0:n], tA[:, 0:n], tB[:, 0:n], ALU.add, [tA_r, tB_r], [d["qth%d_r" % hh]])
            sl.append(s_q)

            def s_k():
                ck, sk = rot[:, 1024:1024 + n], rot[:, 1536:1536 + n]
                pa, par = P.bank()
                for kc in range(NCH):
                    MM(pa[:, 0:n], Wk(kc), hb[:, kc, t0:t1], kc == 0, kc == NCH - 1, [w1r, h_r[kc][t]], [par])
                pb, pbr = P.bank()
                for kc in range(NCH):
                    MM(pb[:, 0:n], Wks(kc), hb[:, kc, t0:t1], kc == 0, kc == NCH - 1, [w2r, h_r[kc][t]], [pbr])
                TT("dve", tA[:, 0:n], pa[:, 0:n], ck, ALU.mult, [par, rot_r], [tA_r])
                TT("dve", tB[:, 0:n], pb[:, 0:n], sk, ALU.mult, [pbr, rot_r], [tB_r])
                TT("pool", kt[:, 0:n], tA[:, 0:n], tB[:, 0:n], ALU.add, [tA_r, tB_r], [kt_r])
            sl.append(s_k)

            def mk_gate(cc):
                def s_gate():
                    pg, pgr = P.bank()
                    for kc in range(NCH):
                        MM(pg[:, 0:n], Wg(cc, kc), hb[:, kc, t0:t1], kc == 0, kc == NCH - 1, [w3r, h_r[kc][t]], [pgr])
                    ACT(SG[t % 3][cc][0][:, 0:n], pg[:, 0:n], AF.Silu, [pgr], [SG[t % 3][cc][1]])
                return s_gate
            sl.append(mk_gate(0))
            sl.append(mk_gate(1))

            def mk_tok(b, c0, nb, kind):
                def s_tok():
                    pv, pvr = P.bank()
                    for kc in range(NCH):
                        MM(pv[0:nb, 0:256], hb[:, kc, t0 + c0:t0 + c0 + nb], Wv(kc), kc == 0, kc == NCH - 1,
                           [w2r, h_r[kc][t]], [pvr])
                    COPY("act", d["vtok%d" % b][0:nb, 0:256], pv[0:nb, 0:256], [pvr], [d["vtok%d_r" % b]])
                    pt, ptr = P.bank()
                    ptb = pt[:, 0:64].bitcast(BF16)
                    TRANSP(ptb[0:nb, 0:128], kt[:, c0:c0 + nb], [kt_r, cst_r], [ptr])
                    COPY("dve", d["ktok%d" % b][0:nb, 0:128], ptb[0:nb, 0:128], [ptr], [d["ktok%d_r" % b]])
                    ps_, psr = P.bank()
                    for hh in range(2):
                        MM(ps_[0:nb, hh * 128:hh * 128 + nb], kt[:, c0:c0 + nb], d["qth%d" % hh][:, c0:c0 + nb], True, True,
                           [kt_r, d["qth%d_r" % hh]], [psr], signal=True)
                    mk = causal_bf if kind == "p" else samp_bf
                    for hh in range(2):
                        TT("dve", d["Am%d" % b][0:nb, hh * 128:hh * 128 + nb], ps_[0:nb, hh * 128:hh * 128 + nb],
                           mk[0:nb, 0:nb], ALU.mult, [psr, cst_r], [d["Am%d_r" % b]])
                return s_tok
            for b, (c0, nb, kind) in enumerate(blocks):
                sl.append(mk_tok(b, c0, nb, kind))
            return sl

        def chain_steps(t):
            d = SETS[t % 2]
            po, po_r = PO[t % 2], PO_r[t % 2]
            blocks = tile_blocks(t)
            steps = []

            def mk_p(b, c0, nb):
                def step():
                    vt_, vr_ = d["vtok%d" % b], d["vtok%d_r" % b]
                    kt_, kr_ = d["ktok%d" % b], d["ktok%d_r" % b]
                    am_, ar_ = d["Am%d" % b], d["Am%d_r" % b]
                    pU, pUr = P.bank()
                    for hh in range(2):
                        MM(pU[:, hh * 128:(hh + 1) * 128], kt_[0:nb, 0:128], vt_[0:nb, hh * 128:(hh + 1) * 128], True, True,
                           [kr_, vr_], [pUr], signal=True)
                    for hh in range(2):
                        MM(po[hh][:, c0:c0 + nb], vt_[0:nb, hh * 128:(hh + 1) * 128], am_[0:nb, hh * 128:hh * 128 + nb],
                           True, False, [vr_, ar_], [po_r[hh]], signal=False)
                        MM(po[hh][:, c0:c0 + nb], Sbf[:, 0:128], d["qth%d" % hh][:, c0:c0 + nb],
                           False, True, [Sbf_r, d["qth%d_r" % hh]], [po_r[hh]], signal=True)
                    kd = 0 if nb == 128 else 1
                    TS("dve", S32, S32, gam(kd, 0), ALU.mult, [S32_r, cst_r], [S32_r])
                    for hh in range(2):
                        STT("dve", S32, pU[:, hh * 128:(hh + 1) * 128], gam(kd, 1 + hh), S32,
                            ALU.mult, ALU.add, [pUr, S32_r, cst_r], [S32_r])
                    COPY("act", Sbf, S32, [S32_r], [Sbf_r])
                    if t == NT - 1:
                        DMA("sp", o_pret[i, pr], S32, [S32_r], [], out_sem)
                return step

            def mk_s(b, c0, nb, half):
                def step():
                    vt_, vr_ = d["vtok%d" % b], d["vtok%d_r" % b]
                    kt_, kr_ = d["ktok%d" % b], d["ktok%d_r" % b]
                    am_, ar_ = d["Am%d" % b], d["Am%d_r" % b]
                    DMA("sp", S0.rearrange("p (s v) -> p s v", v=128), st_ret[i, pr, :, half * 8:(half + 1) * 8, :],
                        [], [S0_r], st_sem)
                    COPY("act", S0b, S0, [S0_r], [S0b_r])
                    TT("dve", kmask.rearrange("p (s k) -> p s k", k=128),
                       kt_[:, 0:128].unsqueeze(1).to_broadcast([128, 8, 128]),
                       seqm_bf[:, half * 8:(half + 1) * 8].unsqueeze(2).to_broadcast([128, 8, 128]),
                       ALU.mult, [kr_, cst_r], [kmask_r])
                    for s_ in range(8):
                        sq_i = half * 8 + s_
                        cs = c0 + sq_i * 8
                        for hh in range(2):
                            MM(po[hh][:, cs:cs + 8], vt_[0:nb, hh * 128:(hh + 1) * 128],
                               am_[0:nb, hh * 128 + sq_i * 8:hh * 128 + sq_i * 8 + 8],
                               True, False, [vr_, ar_], [po_r[hh]], signal=False)
                            MM(po[hh][:, cs:cs + 8], S0b[:, s_ * 128:(s_ + 1) * 128], d["qth%d" % hh][:, cs:cs + 8],
                               False, True, [S0b_r, d["qth%d_r" % hh]], [po_r[hh]], signal=True)
                    for s_ in range(8):
                        pU, pUr = P.bank()
                        for hh in range(2):
                            MM(pU[:, hh * 128:(hh + 1) * 128], kmask[:, s_ * 128:(s_ + 1) * 128],
                               vt_[0:nb, hh * 128:(hh + 1) * 128], True, True, [kmask_r, vr_], [pUr], signal=True)
                        S0s = S0[:, s_ * 128:(s_ + 1) * 128]
                        TS("dve", S0s, S0s, gam(2, 0), ALU.mult, [S0_r, cst_r], [S0_r])
                        for hh in range(2):
                            STT("dve", S0s, pU[:, hh * 128:(hh + 1) * 128], gam(2, 1 + hh), S0s,
                                ALU.mult, ALU.add, [pUr, S0_r, cst_r], [S0_r])
                    DMA("sp", o_sret[i, pr, :, half * 8:(half + 1) * 8, :], S0.rearrange("p (s v) -> p s v", v=128),
                        [S0_r], [], out_sem)
                return step

            for b, (c0, nb, kind) in enumerate(blocks):
                if kind == "p":
                    steps.append(mk_p(b, c0, nb))
                else:
                    steps.append(mk_s(b, c0, nb, 0))
                    steps.append(mk_s(b, c0, nb, 1))
            return steps

        def tail_steps(t):
            t0, t1 = TILES[t]
            n = t1 - t0
            po, po_r = PO[t % 2], PO_r[t % 2]
            sgt = [SG[t % 3][cc][0] for cc in range(2)]
            sgt_r = [SG[t % 3][cc][1] for cc in range(2)]
            st = {}
            H2 = range(2)

            def T1():
                for hh in H2:
                    COPY("act", obf[hh][:, 0:n], po[hh][:, 0:n], [po_r[hh]], [obf_r[hh]])
                    COPY("act", o32[hh][:, 0:n], po[hh][:, 0:n], [po_r[hh]], [o32_r[hh]])
                st["pmn"] = []
                for hh in H2:
                    pm_, pmr_ = P.bank()
                    MM(pm_[:, 0:n], ones_bf[:], obf[hh][:, 0:n], True, True, [obf_r[hh], cst_r], [pmr_])
                    st["pmn"].append((pm_, pmr_))

            def T2():
                for hh in H2:
                    pm_, pmr_ = st["pmn"][hh]
                    STT("dve", o32[hh][:, 0:n], pm_[:, 0:n], -1.0 / 128.0, o32[hh][:, 0:n], ALU.mult, ALU.add,
                        [pmr_, o32_r[hh]], [o32_r[hh]])
                for hh in H2:
                    ACT(csq[hh][:, 0:n], o32[hh][:, 0:n], AF.Square, [o32_r[hh]], [csq_r[hh]])
                st["pvn"] = []
                for hh in H2:
                    pv_, pvr_ = P.bank()
                    MM(pv_[:, 0:n], ones_bf[:], csq[hh][:, 0:n], True, True, [csq_r[hh], cst_r], [pvr_])
                    st["pvn"].append((pv_, pvr_))

            def T3():
                for hh in H2:
                    pv_, pvr_ = st["pvn"][hh]
                    ACT(rsh[hh][:, 0:n], pv_[:, 0:n], AF.Sqrt, [pvr_, cst_r], [rsh_r[hh]], bias=eps_t[:, 0:1],
                        scale=1.0 / 128.0)
                for hh in H2:
                    RECIP(rsh[hh][:, 0:n], rsh[hh][:, 0:n], [rsh_r[hh]], [rsh_r[hh]])

            def T4():
                for hh in H2:
                    TT("dve", o32[hh][:, 0:n], o32[hh][:, 0:n], rsh[hh][:, 0:n], ALU.mult, [o32_r[hh], rsh_r[hh]], [o32_r[hh]])
                for hh in H2:
                    TT("pool", og2[hh][:, 0:n], o32[hh][:, 0:n], sgt[hh][:, 0:n], ALU.mult, [o32_r[hh], sgt_r[hh]],
                       [og2_r[hh]])

            def mk_T5(m0):
                def T5():
                    for m in range(m0, m0 + 4):
                        pm, pmr = P.bank()
                        for cc in range(2):
                            MM(pm[:, 0:n], Wo(cc, m), og2[cc][:, 0:n], cc == 0, cc == 1, [w4r, og2_r[cc]], [pmr])
                        TT("dve", x[:, m, t0:t1], pm[:, 0:n], x[:, m, t0:t1], ALU.add, [pmr, x_r[m][t]], [x_r[m][t]])
                return T5
            def Tn():
                T1()
                T2()
                T3()
            return [Tn, T4, mk_T5(0), mk_T5(4)]

        for f in stage1(0):
            f()
        prev_tail = []
        for t in range(NT):
            fill = stage1(t + 1) if t + 1 < NT else []
            steps = chain_steps(t)
            merged = []
            a, b = list(fill), list(prev_tail)
            while a or b:
                if a:
                    merged.append(a.pop(0))
                if b:
                    merged.append(b.pop(0))
                if a:
                    merged.append(a.pop(0))
            nst = len(steps)
            fi = 0
            for si, st_ in enumerate(steps):
                st_()
                want = (len(merged) * (si + 1) + nst - 1) // nst
                while fi < min(want, len(merged)):
                    merged[fi]()
                    fi += 1
            while fi < len(merged):
                merged[fi]()
                fi += 1
            prev_tail = tail_steps(t)
        for f in prev_tail:
            f()
        for j in (j1, j2, j3, j4):
            P.wrelease(j)

    I32 = mybir.dt.int32
    TWO_PI = 2.0 * np.pi
    S5T = [(k * 256, 256, "p") for k in range(8)] + [(2048, 16, "p"), (2064, 128, "s")]

    def frac_sincos(v, W, tmp_f, tmp_i, sin_out, cos_out, v_r, t_r, so_r, co_r):
        COPY("dve", tmp_i[:, 0:W], v[:, 0:W], [v_r], [t_r])
        COPY("dve", tmp_f[:, 0:W], tmp_i[:, 0:W], [t_r], [t_r])
        TT("dve", v[:, 0:W], v[:, 0:W], tmp_f[:, 0:W], ALU.subtract, [v_r, t_r], [v_r])
        ACT(sin_out[:, 0:W], v[:, 0:W], AF.Sin, [v_r], [so_r], scale=TWO_PI)
        TS("dve", tmp_f[:, 0:W], v[:, 0:W], 0.25, ALU.add, [v_r, t_r], [t_r])
        COPY("dve", tmp_i[:, 0:W], tmp_f[:, 0:W], [t_r], [t_r])
        COPY("dve", cos_out[:, 0:W], tmp_i[:, 0:W], [t_r], [co_r])
        TT("dve", tmp_f[:, 0:W], tmp_f[:, 0:W], cos_out[:, 0:W], ALU.subtract, [t_r, co_r], [t_r])
        ACT(cos_out[:, 0:W], tmp_f[:, 0:W], AF.Sin, [t_r], [co_r], scale=TWO_PI)

    def s5_params(ar, ai, ldt, W, in_r, want_f, tag):
        bufs = {}

        def nb(name, dt=F32):
            a, r = aalloc(W, F32, tag + name)
            if dt is I32:
                a = a.bitcast(I32)
            bufs[name] = (a, r)
            return a, r

        dt_, dt_r = nb("dt")
        mag, mag_r = nb("mag")
        rfr, rfr_r = nb("rfr")
        tf, tf_r = nb("tf")
        ti, ti_r = nb("ti", I32)
        sn, sn_r = nb("sn")
        cs, cs_r = nb("cs")
        ACT(dt_, ldt, AF.Exp, [in_r], [dt_r])
        TT("dve", mag, dt_, ar, ALU.mult, [dt_r, in_r], [mag_r])
        ACT(mag, mag, AF.Exp, [mag_r], [mag_r])
        TT("dve", rfr, dt_, ai, ALU.mult, [dt_r, in_r], [rfr_r])
        TS("dve", rfr, rfr, 1.0 / TWO_PI, ALU.mult, [rfr_r], [rfr_r])
        frac_sincos(rfr, W, tf, ti, sn, cs, rfr_r, tf_r, sn_r, cs_r)
        TT("dve", cs, cs, mag, ALU.mult, [cs_r, mag_r], [cs_r])
        TT("dve", sn, sn, mag, ALU.mult, [sn_r, mag_r], [sn_r])
        out = {"mag": (mag, mag_r), "abr": (cs, cs_r), "abi": (sn, sn_r), "rfr": (rfr, rfr_r)}
        if want_f:
            fre, fre_r = nb("fre")
            fim, fim_r = nb("fim")
            den, den_r = dt_, dt_r
            TT("dve", den, ar, ar, ALU.mult, [in_r, dt_r], [den_r])
            TT("dve", tf, ai, ai, ALU.mult, [in_r, tf_r], [tf_r])
            TT("dve", den, den, tf, ALU.add, [den_r, tf_r], [den_r])
            RECIP(den, den, [den_r], [den_r])
            nre, nre_r = tf, tf_r
            TS("dve", nre, cs, -1.0, ALU.add, [cs_r, tf_r], [nre_r])
            TT("dve", fre, nre, ar, ALU.mult, [nre_r, in_r], [fre_r])
            TT("dve", fim, sn, ai, ALU.mult, [sn_r, in_r], [fim_r])
            TT("dve", fre, fre, fim, ALU.add, [fre_r, fim_r], [fre_r])
            TT("dve", fre, fre, den, ALU.mult, [fre_r, den_r], [fre_r])
            TT("dve", fim, sn, ar, ALU.mult, [sn_r, in_r, fre_r], [fim_r])
            TT("dve", nre, nre, ai, ALU.mult, [nre_r, in_r], [nre_r])
            TT("dve", fim, fim, nre, ALU.subtract, [fim_r, nre_r], [fim_r])
            TT("dve", fim, fim, den, ALU.mult, [fim_r, den_r], [fim_r])
            out["fre"] = (fre, fre_r)
            out["fim"] = (fim, fim_r)
        return out

    KS5 = int(os.environ.get("KS5", "9"))
    PENG = os.environ.get("KPENG", "pool")
    PENG2 = os.environ.get("KPENG2", "pool")

    def s5_pass(i, yc, zbuf, z_r, mark, st_sem, out_sem):
        wu, wur, ju = P.wget()
        id32_ = ccol("ident32", 0, 128)
        Bbr, Bbr_r = aalloc(256, BF16, "Bbr")
        Bbi, Bbi_r = aalloc(256, BF16, "Bbi")
        Cre, Cre_r = aalloc(256, BF16, "Cre")
        nCre, nCre_r = aalloc(256, BF16, "nCre")
        nCim, nCim_r = aalloc(256, BF16, "nCim")
        smo = 256 + 18 + (i * 3) * 16 + yc * 4
        sm = s5_params(jv[:, smo:smo + 4], jv[:, smo + 16:smo + 20], jv[:, smo + 32:smo + 36], 4, cst_r, True, "sm")
        mag, mag_r = sm["mag"]
        abr, abr_r = sm["abr"]
        abi, abi_r = sm["abi"]
        rfr, rfr_r = sm["rfr"]
        fre, fre_r = sm["fre"]
        fim, fim_r = sm["fim"]
        mark_b = ar["off"]
        stg, stg_r = aalloc(4 * 512, F32, "s5stg")
        DMA("sp", stg, s5w[i, yc][:, 0:4 * 512], [], [stg_r], st_sem)
        Bre_p, Bim_p, Cre_p, Cim_p = (stg[:, q * 512:(q + 1) * 512] for q in range(4))
        dg = [aalloc(128, F32, "dg%d" % q) for q in range(2)]
        pf, pf_r = [], []
        for q, (fq, fq_r) in enumerate(((fre, fre_r), (fim, fim_r))):
            pb, pr = P.bank()
            for sc in range(4):
                dgt, dgt_r = dg[sc % 2]
                TS("dve", dgt, id32_, fq[:, sc:sc + 1], ALU.mult, [cst_r, fq_r], [dgt_r])
                MM(pb[:, sc * 128:(sc + 1) * 128], ones32[:], dgt, True, True, [dgt_r, cst_r], [pr], signal=True)
            pf.append(pb)
            pf_r.append(pr)
        m1, m1_r = aalloc(512, F32, "m1")
        m2, m2_r = aalloc(512, F32, "m2")
        TT("dve", m1, pf[0][:, 0:512], Bre_p, ALU.mult, [pf_r[0], stg_r], [m1_r])
        TT("dve", m2, pf[1][:, 0:512], Bim_p, ALU.mult, [pf_r[1], stg_r], [m2_r])
        TT("dve", Bbr, m1, m2, ALU.subtract, [m1_r, m2_r], [Bbr_r])
        TT("dve", m1, pf[0][:, 0:512], Bim_p, ALU.mult, [pf_r[0], stg_r, m1_r], [m1_r])
        TT("dve", m2, pf[1][:, 0:512], Bre_p, ALU.mult, [pf_r[1], stg_r, m2_r], [m2_r])
        TT("dve", Bbi, m1, m2, ALU.add, [m1_r, m2_r], [Bbi_r])
        COPY("act", Cre, Cre_p, [stg_r], [Cre_r])
        TS("dve", nCre, Cre_p, -1.0, ALU.mult, [stg_r], [nCre_r])
        TS("dve", nCim, Cim_p, -1.0, ALU.mult, [stg_r], [nCim_r])
        S.barrier(io_sems)
        ar["off"] = mark_b
        cosT, cosT_r = aalloc(1024, F32, "cosT")
        sinT, sinT_r = aalloc(1024, F32, "sinT")
        ttf, ttf_r = aalloc(1024, F32, "ttf")
        tti, tti_r = aalloc(1024, F32, "tti")
        tti = tti.bitcast(I32)
        vt, vt_r = ttf, ttf_r
        vt, vt_r = aalloc(1024, F32, "vt")
        for sc in range(4):
            TS("dve", vt[:, sc * 256:(sc + 1) * 256], jvec, rfr[:, sc:sc + 1], ALU.mult, [cst_r, rfr_r], [vt_r])
        frac_sincos(vt, 1024, ttf, tti, sinT, cosT, vt_r, ttf_r, sinT_r, cosT_r)
        tt4 = tti.bitcast(BF16)
        x2, x2_r = aalloc(1024, F32, "x2")
        g2, g2_r = aalloc(512, F32, "g2")
        t2, t2_r = aalloc(512, BF16, "t2")
        AB, AB_r, GG, GG_r, TQ, TQ_r = [], [], [], [], [], []
        for st, (xb, gb, tb) in enumerate(((vt, ttf[:, 0:512], tt4[:, 0:1024]), (x2, g2, t2))):
            AB.append([xb[:, q * 256:(q + 1) * 256] for q in range(4)])
            AB_r.append([Region("ab%d_%d" % (st, q)) for q in range(4)])
            GG.append([gb[:, q * 256:(q + 1) * 256] for q in range(2)])
            GG_r.append([Region("g%d_%d" % (st, q)) for q in range(2)])
            TQ.append([tb[:, q * 256:(q + 1) * 256] for q in range(4)])
            TQ_r.append([Region("tq%d_%d" % (st, q)) for q in range(4)])
        decs, decs_r = ttf[:, 512:640], Region("decs")
        u2, u2_r = aalloc(256, F32, "u2")
        U32 = [ttf[:, 768:1024], u2]
        U32_r = [Region("u32a"), Region("u32b")]
        UBF = [tt4[:, 1024:1280], tt4[:, 1280:1536]]
        UBF_r = [Region("ubfa"), Region("ubfb")]
        ysb, ysb_r = aalloc(256, F32, "ysb")
        yt2, yt2_r = aalloc(256, F32, "yt2")
        small, small_r = aalloc(256, F32, "small")
        gL = small[:, 0:8]
        tmp4 = small[:, 8:40]
        car = small[:, 40:48]
        hL = small[:, 48:56]
        car_r = Region("car")
        gL_r = Region("gL")
        h0, h0_r = aalloc(128, F32, "h0")
        csm, csm_r = aalloc(128, F32, "csm")
        hs, hs_r = aalloc(128, F32, "hs")
        tm64, tm64_r = aalloc(256, F32, "tm64")
        S.barrier(io_sems)
        py, pyr = P.banks[0], P.bank_r[0]
        dcol = ccol("s5_d", i * 4 + yc)
        v3 = lambda a: a.rearrange("p (s j) -> p s j", j=8)

        def crot(dst_re, dst_im, g_re, g_im, cs_, sn_, W3, rd, wr):
            tA = tmp4[:, 0:W3]
            tB = tmp4[:, 8:8 + W3]
            TT("dve", tA, g_re, cs_, ALU.mult, rd, [small_r])
            TT("dve", tB, g_im, sn_, ALU.mult, rd, [small_r])
            TT("dve", dst_re, tA, tB, ALU.subtract, [small_r], wr)
            TT("dve", tA, g_re, sn_, ALU.mult, rd + [small_r], [small_r])
            TT("dve", tB, g_im, cs_, ALU.mult, rd + [small_r], [small_r])
            TT("dve", dst_im, tA, tB, ALU.add, [small_r], wr)

        w255 = small[:, 56:64]
        w255_r = Region("w255")
        crot(w255[:, 0:4], w255[:, 4:8], cosT.rearrange("p (c j) -> p c j", j=256)[:, :, 255],
             sinT.rearrange("p (c j) -> p c j", j=256)[:, :, 255], cosT.rearrange("p (c j) -> p c j", j=256)[:, :, 1],
             sinT.rearrange("p (c j) -> p c j", j=256)[:, :, 1], 4,
             [cosT_r, sinT_r, small_r], [w255_r])

        def tabs(sc, n, kind):
            if kind == "s":
                return (cosT[:, sc * 256:sc * 256 + 8].unsqueeze(1).to_broadcast([128, 16, 8]),
                        sinT[:, sc * 256:sc * 256 + 8].unsqueeze(1).to_broadcast([128, 16, 8]))
            return cosT[:, sc * 256:sc * 256 + n], sinT[:, sc * 256:sc * 256 + n]

        def vw(a, n, kind):
            return v3(a[:, 0:n]) if kind == "s" else a[:, 0:n]

        def PRE(k):
            c0, n, kind = S5T[k]
            tix = min(c0 // 512, NT - 1)
            u32, u32_r, ubf, ubf_r = U32[k % 2], U32_r[k % 2], UBF[k % 2], UBF_r[k % 2]
            pu, pur = P.bank()
            for kc in range(NCH):
                MM(pu[:, 0:n], wu[:, kc * 128:(kc + 1) * 128], hb[:, kc, c0:c0 + n], kc == 0, kc == NCH - 1,
                   [wur, h_r[kc][tix]], [pur])
            COPY("act", u32[:, 0:n], pu[:, 0:n], [pur], [u32_r])
            COPY("act", ubf[:, 0:n], u32[:, 0:n], [u32_r], [ubf_r])
            if kind == "s":
                DMA("sp", h0.rearrange("p (q c s) -> p q c s", q=2, c=4), st_s5[i, :, :, yc * 4:(yc + 1) * 4, :].rearrange(
                    "q p c s -> p q c s"), [], [h0_r], st_sem)
                h0v = h0.rearrange("p (q c s) -> p q c s", q=2, c=4)
                csv = csm.rearrange("p (q c s) -> p q c s", q=2, c=4)
                tmv = tm64.rearrange("p (q c s) -> p q c s", q=4, c=4)
                abr_b = abr.unsqueeze(2).to_broadcast([128, 4, 16])
                abi_b = abi.unsqueeze(2).to_broadcast([128, 4, 16])
                TT("dve", tmv[:, 0], h0v[:, 0], abr_b, ALU.mult, [h0_r, abr_r], [tm64_r])
                TT("dve", tmv[:, 1], h0v[:, 1], abi_b, ALU.mult, [h0_r, abi_r], [tm64_r])
                TT("dve", csv[:, 0], tmv[:, 0], tmv[:, 1], ALU.subtract, [tm64_r], [csm_r])
                TT("dve", tmv[:, 2], h0v[:, 1], abr_b, ALU.mult, [h0_r, abr_r], [tm64_r])
                TT("dve", tmv[:, 3], h0v[:, 0], abi_b, ALU.mult, [h0_r, abi_r], [tm64_r])
                TT("dve", csv[:, 1], tmv[:, 2], tmv[:, 3], ALU.add, [tm64_r], [csm_r])

        pxs = {}
        pzs = {}
        id32 = ccol("ident32", 0, 128)
        nid32 = ccol("nident32", 0, 128)

        def Fx(k, sc):
            c0, n, kind = S5T[k]
            ubf, ubf_r = UBF[k % 2], UBF_r[k % 2]
            px, pxr = P.bank()
            MM(px[:, 0:n], Bbr[:, sc * 128:(sc + 1) * 128], ubf[:, 0:n], True, True, [Bbr_r, ubf_r], [pxr], signal=True)
            MM(px[:, 256:256 + n], Bbi[:, sc * 128:(sc + 1) * 128], ubf[:, 0:n], True, True, [Bbi_r, ubf_r], [pxr],
               signal=True)
            pxs[(k, sc)] = (px, pxr)

        def Fr(k, sc):
            c0, n, kind = S5T[k]
            st = sc % 2
            a1, a2, b1, b2 = AB[st]
            a1_r, a2_r, b1_r, b2_r = AB_r[st]
            cs_, sn_ = tabs(sc, n, kind)
            px, pxr = pxs.pop((k, sc))
            xr, xi = vw(px, n, kind), vw(px[:, 256:512], n, kind)
            TT("dve", vw(a1, n, kind), xr, cs_, ALU.mult, [pxr, cosT_r], [a1_r])
            TT("dve", vw(a2, n, kind), xi, sn_, ALU.mult, [pxr, sinT_r], [a2_r])
            TT("dve", vw(b1, n, kind), xi, cs_, ALU.mult, [pxr, cosT_r], [b1_r])
            TT("dve", vw(b2, n, kind), xr, sn_, ALU.mult, [pxr, sinT_r], [b2_r])
            pz, pzr = P.bank()
            MM(pz[:, 0:n], id32, a1[:, 0:n], True, False, [a1_r, cst_r], [pzr], signal=True)
            MM(pz[:, 0:n], id32, a2[:, 0:n], False, True, [a2_r, cst_r], [pzr], signal=True)
            MM(pz[:, 256:256 + n], id32, b1[:, 0:n], True, False, [b1_r, cst_r], [pzr], signal=True)
            MM(pz[:, 256:256 + n], nid32, b2[:, 0:n], False, True, [b2_r, cst_r], [pzr], signal=True)
            pzs[(k, sc)] = (pz, pzr)

        def Gs(k, sc, first):
            c0, n, kind = S5T[k]
            st = sc % 2
            gre, gim = GG[st]
            gre_r, gim_r = GG_r[st]
            tq, tq_r = TQ[st], TQ_r[st]
            cs_, sn_ = tabs(sc, n, kind)
            pz, pzr = pzs.pop((k, sc))
            xre, xim = pz[:, 0:n], pz[:, 256:256 + n]
            if kind == "s":
                csv = csm.rearrange("p (q c s) -> p q c s", q=2, c=4)
                TT("dve", pz[:, 0:n:8], pz[:, 0:n:8], csv[:, 0, sc], ALU.add, [pzr, csm_r], [pzr])
                TT("dve", pz[:, 256:256 + n:8], pz[:, 256:256 + n:8], csv[:, 1, sc], ALU.add, [pzr, csm_r], [pzr])
                TS("dve", decs[:, 0:n], ccol("cmask4", 16, 128), mag[:, sc:sc + 1], ALU.mult, [cst_r, mag_r], [decs_r])
                dec = decs[:, 0:n]
                dec_rd = [decs_r]
            else:
                dec = mag[:, sc:sc + 1].to_broadcast([128, n])
                dec_rd = [mag_r]
            if kind == "p" and not first:
                SCAN(gre[:, 0:n], dec, xre, [pzr, car_r] + dec_rd, [gre_r], initial=car[:, sc:sc + 1])
                SCAN(gim[:, 0:n], dec, xim, [pzr, car_r] + dec_rd, [gim_r], initial=car[:, 4 + sc:5 + sc])
            else:
                SCAN(gre[:, 0:n], dec, xre, [pzr] + dec_rd, [gre_r])
                SCAN(gim[:, 0:n], dec, xim, [pzr] + dec_rd, [gim_r])
            g_re3, g_im3 = vw(gre, n, kind), vw(gim, n, kind)
            TT(PENG2, vw(tq[0], n, kind), g_re3, cs_, ALU.mult, [gre_r, cosT_r], [tq_r[0]])
            TT(PENG2, vw(tq[1], n, kind), g_im3, sn_, ALU.mult, [gim_r, sinT_r], [tq_r[1]])
            TT(PENG, vw(tq[2], n, kind), g_re3, sn_, ALU.mult, [gre_r, sinT_r], [tq_r[2]])
            TT(PENG, vw(tq[3], n, kind), g_im3, cs_, ALU.mult, [gim_r, cosT_r], [tq_r[3]])
            if kind == "p":
                COPY("act", gL[:, sc:sc + 1], gre[:, n - 1:n], [gre_r], [gL_r])
                COPY("act", gL[:, 4 + sc:5 + sc], gim[:, n - 1:n], [gim_r], [gL_r])
            else:
                tmv = tm64.rearrange("p (q c s) -> p q c s", q=4, c=4)
                COPY("act", tmv[:, 0, sc], gre[:, 7:n:8], [gre_r], [tm64_r])
                COPY("act", tmv[:, 1, sc], gim[:, 7:n:8], [gim_r], [tm64_r])

        def Gp(k, sc):
            c0, n, kind = S5T[k]
            tq, tq_r = TQ[sc % 2], TQ_r[sc % 2]
            for q, wmat, wreg in ((0, Cre, Cre_r), (1, nCre, nCre_r), (2, nCim, nCim_r), (3, nCim, nCim_r)):
                MM(py[:, 0:n], wmat[:, sc * 128:(sc + 1) * 128], tq[q][:, 0:n], sc == 0 and q == 0, sc == 3 and q == 3,
                   [wreg, tq_r[q]], [pyr], signal=True)

        def POSTC(k):
            c0, n, kind = S5T[k]
            if kind == "p" and n == 256:
                crot(car[:, 0:4], car[:, 4:8], gL[:, 0:4], gL[:, 4:8], w255[:, 0:4], w255[:, 4:8], 4,
                     [gL_r, small_r, w255_r], [car_r])
            elif kind == "p":
                L = n - 1
                csL = cosT.rearrange("p (c j) -> p c j", j=256)[:, :, L]
                snL = sinT.rearrange("p (c j) -> p c j", j=256)[:, :, L]
                crot(hL[:, 0:4], hL[:, 4:8], gL[:, 0:4], gL[:, 4:8], csL, snL, 4, [gL_r, small_r, cosT_r, sinT_r], [small_r])
                DMA("sp", o_ps5[i, :, :, yc * 4:(yc + 1) * 4].rearrange("q p c -> p q c"),
                    hL.rearrange("p (q c) -> p q c", q=2), [small_r], [], out_sem)
            else:
                hsv = hs.rearrange("p (q c s) -> p q c s", q=2, c=4)
                tmv = tm64.rearrange("p (q c s) -> p q c s", q=4, c=4)
                cs7 = cosT.rearrange("p (c j) -> p c j", j=256)[:, :, 7:8].to_broadcast([128, 4, 16])
                sn7 = sinT.rearrange("p (c j) -> p c j", j=256)[:, :, 7:8].to_broadcast([128, 4, 16])
                TT("dve", tmv[:, 2], tmv[:, 0], cs7, ALU.mult, [tm64_r, cosT_r], [tm64_r])
                TT("dve", tmv[:, 3], tmv[:, 1], sn7, ALU.mult, [tm64_r, sinT_r], [tm64_r])
                TT("dve", hsv[:, 0], tmv[:, 2], tmv[:, 3], ALU.subtract, [tm64_r], [hs_r])
                TT("dve", tmv[:, 2], tmv[:, 0], sn7, ALU.mult, [tm64_r, sinT_r], [tm64_r])
                TT("dve", tmv[:, 3], tmv[:, 1], cs7, ALU.mult, [tm64_r, cosT_r], [tm64_r])
                TT("dve", hsv[:, 1], tmv[:, 2], tmv[:, 3], ALU.add, [tm64_r], [hs_r])
                DMA("sp", o_ss5[i, :, :, yc * 4:(yc + 1) * 4, :].rearrange("q p c s -> p q c s"), hsv, [hs_r], [], out_sem)

        def POST(k):
            c0, n, kind = S5T[k]
            u32, u32_r = U32[k % 2], U32_r[k % 2]
            STT("dve", ysb[:, 0:n], u32[:, 0:n], dcol, py[:, 0:n], ALU.mult, ALU.add, [u32_r, pyr, cst_r], [ysb_r])
            TT("pool", yt2[:, 0:n], ysb[:, 0:n], ysb[:, 0:n], ALU.mult, [ysb_r], [yt2_r])
            TS("dve", yt2[:, 0:n], yt2[:, 0:n], 0.044715, ALU.mult, [yt2_r], [yt2_r], s2=1.0, op1=ALU.add)
            TT("pool", yt2[:, 0:n], yt2[:, 0:n], ysb[:, 0:n], ALU.mult, [yt2_r, ysb_r], [yt2_r])
            ACT(yt2[:, 0:n], yt2[:, 0:n], AF.Sigmoid, [yt2_r], [yt2_r], scale=2.0 * 0.7978845608028654)
            TT("dve", zbuf[:, yc * T + c0:yc * T + c0 + n], ysb[:, 0:n], yt2[:, 0:n], ALU.mult, [ysb_r, yt2_r], [z_r[yc]])

        items = [(k, sc) for k in range(len(S5T)) for sc in range(4)]
        NI = len(items)

        def emit_fx(j):
            if j < NI:
                k2, sc2 = items[j]
                if sc2 == 0:
                    PRE(k2)
                Fx(k2, sc2)

        emit_fx(0)
        emit_fx(1)
        Fr(*items[0])
        for idx in range(NI + 1):
            emit_fx(idx + 2)
            if idx + 1 < NI:
                Fr(*items[idx + 1])
            if idx < NI:
                k, sc = items[idx]
                Gs(k, sc, k == 0)
                if sc == 3:
                    POSTC(k)
            if idx >= 1:
                kp, scp = items[idx - 1]
                Gp(kp, scp)
                if scp == 3:
                    POST(kp)
        P.wrelease(ju)

    def s5_final(i, zbuf, z_r):
        wg, wgr, jg = P.wget()
        wo0, wo0r, jo0 = P.wget()
        wo1, wo1r, jo1 = P.wget()
        s5o, s5o_r = [], []
        for m in range(4):
            a, r = aalloc(256, BF16, "s5o%d" % m)
            s5o.append(a)
            s5o_r.append(r)
        sgm, sgm_r = [], []
        for m in range(2):
            a, r = aalloc(512, F32, "sgm%d" % m)
            sgm.append(a)
            sgm_r.append(r)
        for t in range(NT):
            t0, t1 = TILES[t]
            n = t1 - t0
            for m in range(4):
                pg, pgr = P.bank()
                for kc in range(4):
                    MM(pg[:, 0:n], wg[:, kc * 512 + m * 128:kc * 512 + (m + 1) * 128], zbuf[:, kc * T + t0:kc * T + t1],
                       kc == 0, kc == 3, [wgr, z_r[kc]], [pgr])
                ACT(sgm[m % 2][:, 0:n], pg[:, 0:n], AF.Sigmoid, [pgr], [sgm_r[m % 2]])
                TT("dve", s5o[m][:, 0:n], zbuf[:, m * T + t0:m * T + t1], sgm[m % 2][:, 0:n], ALU.mult,
                   [z_r[m], sgm_r[m % 2]], [s5o_r[m]])
            for m8 in range(NCH):
                pm, pmr = P.bank()
                for kc in range(4):
                    wsl, wslr = (wo0, wo0r) if kc < 2 else (wo1, wo1r)
                    MM(pm[:, 0:n], wsl[:, (kc % 2) * 1024 + m8 * 128:(kc % 2) * 1024 + (m8 + 1) * 128], s5o[kc][:, 0:n],
                       kc == 0, kc == 3, [wslr, s5o_r[kc]], [pmr])
                TT("dve", x[:, m8, t0:t1], pm[:, 0:n], x[:, m8, t0:t1], ALU.add, [pmr, x_r[m8][t]], [x_r[m8][t]])
        for j in (jg, jo0, jo1):
            P.wrelease(j)

    def ab_mixer(layer):
        i = layer // 2
        areset()
        norm_alloc()
        for t in range(NT):
            norm_tile(t, 4 + layer)
        P.rot = [4, 5, 6, 7]
        if "r" in KAB:
            for pr in range(2):
                areset_passes(pr == 0)
                ret_pass(i, pr, st_sem, out_sem)
        areset()
        if "s" in KAB:
            P.rot = [1, 2, 3, 4, 5, 6, 7]
            zbuf, _ = aalloc(4 * T // 2, BF16, "zbuf")
            z_r = [Region("z%d" % m) for m in range(4)]
            mark = ar["off"]
            for yc in range(4):
                s5_pass(i, yc, zbuf, z_r, mark, st_sem, out_sem)
                S.barrier(io_sems)
                ar["off"] = mark
            P.rot = list(range(8))
            s5_final(i, zbuf, z_r)
            areset()
        P.rot = list(range(8))

    st_sem = S.dma_sem("st")
    out_sem = S.dma_sem("out")
    io_sems.append(st_sem)
    io_sems.append(out_sem)
    for (layer, kind) in subs:
        if kind == "ffn1":
            ffn(layer)
        elif kind == "ffn2":
            ffn(8 + layer)
        elif layer % 2 == 1:
            gla_mixer(layer)
        else:
            ab_mixer(layer)
    assert P.next_get == len(P.loads) == P.issued, (P.next_get, len(P.loads), P.issued)

    areset()
    norm_alloc()
    for t in range(NT):
        norm_tile(t, 12, final=True)
    for c in range(NCH):
        DMA("sp", yT[c * 128:(c + 1) * 128, :], x[:, c, :], [x_r[c][t] for t in range(NT)], [], out_sem)
    finals = [(s.h, s.count) for s in (out_sem, st_sem, in_sem) if s.count > 0]
    S.replay(finals)
    return P


def _ffn_pack(w_gu, w_down):
    a = w_gu.reshape(NCH, 128, 2, NFF, 128).transpose(3, 1, 2, 0, 4).reshape(NFF, 128, 2048)
    b = w_down.reshape(NFF, 128, D)
    return np.concatenate([a, b], axis=2)


def _inproj(w, cols):
    m = len(cols)
    return w[:, cols].reshape(NCH, 128, m).transpose(1, 0, 2).reshape(128, NCH * m)


def pack_weights(plan, inp):
    f = np.float32
    blocks = []
    ffn_cache = {}
    for (key, n) in plan:
        kind = key[0]
        if kind == "ffn":
            _, layer, which, j = key
            ck = (layer, which)
            if ck not in ffn_cache:
                ffn_cache.clear()
                nm = "ffn1" if which == 0 else "ffn2"
                ffn_cache[ck] = _ffn_pack(inp[nm + "_w_gu"][layer], inp[nm + "_w_down"][layer])
            blk = ffn_cache[ck][j]
        elif kind == "gla":
            _, i, hd, part = key
            w = inp["gla_w_in"][i]
            wo = inp["gla_w_out"][i]
            ar = np.arange
            if part == 0:
                blk = np.concatenate([_inproj(w, hd * 128 + ar(128)), _inproj(w, 512 + hd * 128 + ar(128)),
                                      _inproj(w, 3072 + ar(16))], axis=1)
            elif part == 1:
                blk = np.concatenate([_inproj(w, 1024 + hd * 256 + ar(256)), _inproj(w, 2048 + hd * 256 + ar(128))], axis=1)
            else:
                rows = wo[hd * 256:(hd + 1) * 256].reshape(2, 128, D).transpose(1, 0, 2).reshape(128, 2 * D)
                blk = np.concatenate([_inproj(w, 2048 + hd * 256 + 128 + ar(128)), rows], axis=1)
        else:
            blk = pack_ab(key, inp)
        assert blk.shape == (128, n), (key, blk.shape, n)
        blocks.append(blk.astype(f, copy=False))
    if not blocks:
        return np.zeros((128, 1), f)
    return np.ascontiguousarray(np.concatenate(blocks, axis=1))


def build_cst(inp):
    f = np.float32
    CL = cst_layout()
    c = np.zeros((128, CL.n), f)

    def put(name, arr):
        o, n = CL[name]
        arr = np.asarray(arr, f)
        assert arr.shape[1] == n, (name, arr.shape, n)
        c[:arr.shape[0], o:o + n] = arr

    gains = np.concatenate([inp["norm_ffn1"], inp["norm_mix"], inp["norm_ffn2"], inp["norm_final"][None]], axis=0)
    put("gains", gains.reshape(13, NCH, 128).transpose(2, 0, 1).reshape(128, 13 * NCH))
    put("gla_b", inp["gla_b_alpha"].reshape(2, 4, 128).transpose(2, 0, 1).reshape(128, 8))
    put("gla_g", inp["gla_norm"].reshape(2, 2, 128).transpose(2, 0, 1).reshape(128, 4))
    put("s5_d", inp["s5_d"].reshape(2, 4, 128).transpose(2, 0, 1).reshape(128, 8))
    put("one", np.ones((128, 1), f))
    put("hm", (np.arange(128)[:, None] // 64 == np.arange(2)[None, :]).astype(f))
    cm = np.ones((128, 512), f)
    cm[:, 0::128] = 0.0
    put("cmask", cm)
    cm4 = np.ones((128, 144), f)
    cm4[:, 0] = 0.0
    cm4[:, 16::8] = 0.0
    put("cmask4", cm4)
    put("ident32", np.eye(128, dtype=f))
    put("nident32", -np.eye(128, dtype=f))
    s_ = np.arange(128)[:, None]
    t_ = np.arange(128)[None, :]
    put("causal", (t_ >= s_).astype(f))
    put("samp", ((t_ >= s_) & (t_ // 8 == s_ // 8)).astype(f))
    put("seqm", (np.arange(128)[:, None] // 8 == np.arange(16)[None, :]).astype(f))
    put("ident", np.eye(128, dtype=f))
    w2 = np.zeros((16, 1024), f)
    w2[:, 0:512] = inp["gla_w_alpha2"][0]
    w2[:, 512:1024] = inp["gla_w_alpha2"][1]
    put("w2", w2)
    put("jvec", np.broadcast_to(np.arange(256, dtype=f)[None], (128, 256)))
    put("ret_gam", ret_gam_table())
    sm = np.zeros((128, 96), f)
    for i in range(2):
        for q, nm in enumerate(("s5_a_re", "s5_a_im")):
            sm[:, (i * 3 + q) * 16:(i * 3 + q + 1) * 16] = inp[nm][i].reshape(16, 128).T
        sm[:, (i * 3 + 2) * 16:(i * 3 + 3) * 16] = np.repeat(inp["s5_log_dt"][i], 64).reshape(16, 128).T
    put("s5sm", sm)
    return c


def ret_gam_table():
    g = np.zeros((128, 18), np.float64)
    for pr in range(2):
        for p in range(128):
            h = 2 * pr + p // 64
            gamma = 1.0 - 2.0 ** (-5.0 - h)
            for k, n in enumerate((128, 16, 8)):
                base = (pr * 3 + k) * 3
                g[p, base] = gamma ** n
                g[p, base + 1 + p // 64] = gamma ** n
    return g.astype(np.float32)


def pack_ab(key, inp):
    ar = np.arange
    kind = key[0]
    if kind == "ret":
        _, i, pr, part = key
        w = inp["ab_w_in"][i]
        sw = np.concatenate([hh * 64 + (ar(64) + 32) % 64 for hh in range(2)])
        if part == 0:
            return np.concatenate([_inproj(w, 512 + pr * 128 + ar(128)), _inproj(w, 512 + pr * 128 + sw),
                                   _inproj(w, 768 + pr * 128 + ar(128))], axis=1)
        if part == 1:
            return np.concatenate([_inproj(w, 768 + pr * 128 + sw), _inproj(w, 1024 + pr * 256 + ar(256))], axis=1)
        if part == 2:
            return np.concatenate([_inproj(w, 1536 + pr * 256 + ar(128)), _inproj(w, 1536 + pr * 256 + 128 + ar(128))], axis=1)
        wo = inp["ab_w_out"][i]
        return wo[512 + pr * 256:512 + (pr + 1) * 256].reshape(2, 128, D).transpose(1, 0, 2).reshape(128, 2 * D)
    if kind == "s5u":
        _, i, yc = key
        return _inproj(inp["ab_w_in"][i], yc * 128 + ar(128))
    if kind == "s5g":
        _, i = key
        return inp["s5_w_glu"][i].reshape(4, 128, 512).transpose(1, 0, 2).reshape(128, 2048)
    if kind == "s5o":
        _, i, half = key
        wo = inp["ab_w_out"][i]
        return wo[half * 256:(half + 1) * 256].reshape(2, 128, D).transpose(1, 0, 2).reshape(128, 2 * D)
    raise KeyError(key)


def rot_tables():
    half = 32
    inv_freq = (1.0 / (np.float32(10000.0) ** (np.arange(half, dtype=np.float32) / np.float32(half)))).astype(np.float32)
    out = np.zeros((2, NT, 128, 4, 512), np.float32)
    p = np.arange(128)
    d = p % 64
    fi = d % 32
    sign = np.where(d < 32, -1.0, 1.0)
    for t, (t0, t1) in enumerate(TILES):
        n = t1 - t0
        col = np.arange(n)
        if t < NT - 1:
            pos = t0 + col
            j = col % 128
        else:
            pos = np.where(col < 16, 2048 + col, PAST_LEN + (col - 16) % 8)
            j = np.where(col < 16, col, (col - 16) % 8)
        ang = (pos.astype(np.float32)[None, :] * inv_freq[fi][:, None]).astype(np.float32).astype(np.float64)
        cs = np.cos(ang)
        sn = np.sin(ang) * sign[:, None]
        for pr in range(2):
            h = 2 * pr + p // 64
            gamma = 1.0 - 2.0 ** (-5.0 - h)
            lg = np.log(gamma)[:, None] * (j[None, :] + 1.0)
            Eq = np.exp(lg)
            Ek = np.exp(-lg) * (64.0 ** -0.5)
            out[pr, t, :, 0, :n] = cs * Eq
            out[pr, t, :, 1, :n] = sn * Eq
            out[pr, t, :, 2, :n] = cs * Ek
            out[pr, t, :, 3, :n] = sn * Ek
    return out


def s5w_pack(inp):
    out = np.zeros((2, 4, 128, 7, 512), np.float32)
    s = np.arange(128)
    for i in range(2):
        for yc in range(4):
            for sc in range(4):
                c = yc * 4 + sc
                g = 2 * c + s // 64
                p_ = s % 64
                for n in range(16):
                    r = 32 * sc + 16 * (s // 64) + n
                    out[i, yc, r, 0, sc * 128 + s] = inp["s5_b_re"][i][g, p_, n]
                    out[i, yc, r, 1, sc * 128 + s] = inp["s5_b_im"][i][g, p_, n]
                    out[i, yc, s, 2, sc * 128 + r] = inp["s5_c_re"][i][g, n, p_]
                    out[i, yc, s, 3, sc * 128 + r] = inp["s5_c_im"][i][g, n, p_]
                out[i, yc, :, 4, sc * 128:(sc + 1) * 128] = inp["s5_a_re"][i][g, p_][None, :]
                out[i, yc, :, 5, sc * 128:(sc + 1) * 128] = inp["s5_a_im"][i][g, p_][None, :]
                out[i, yc, :, 6, sc * 128:(sc + 1) * 128] = inp["s5_log_dt"][i][g][None, :]
    return out.reshape(2, 4, 128, 7 * 512)


def unpack_ab(o, core, sl, p_s5re, p_s5im, p_ret, s_s5re, s_s5im, s_ret):
    p_ret[:, core] = o["o_pret"].reshape(2, 4, 64, 128)
    s_ret[:, sl] = o["o_sret"].reshape(2, 4, 64, NSS, 128).transpose(0, 3, 1, 2, 4)
    ps5 = o["o_ps5"].transpose(0, 1, 3, 2).reshape(2, 2, 32, 64)
    p_s5re[:, core] = ps5[:, 0]
    p_s5im[:, core] = ps5[:, 1]
    ss5 = o["o_ss5"].transpose(0, 1, 4, 3, 2).reshape(2, 2, NSS, 32, 64)
    s_s5re[:, sl] = ss5[:, 0]
    s_s5im[:, sl] = ss5[:, 1]


def prepare_inputs(inp, subs):
    f = np.float32
    plan = make_plan(subs)
    wts = pack_weights(plan, inp)
    cst = build_cst(inp)
    has_gla = any(k == "mix" and l % 2 == 1 for (l, k) in subs)
    has_ab = any(k == "mix" and l % 2 == 0 for (l, k) in subs)
    if has_ab:
        rot = rot_tables()
        s5w = s5w_pack(inp)
    maps = []
    for core in range(8):
        xp = np.concatenate([inp["meta_tokens"], inp["x_prompt"][core]], axis=0)
        xs = inp["x_sample"][core * NSS:(core + 1) * NSS].reshape(NSAMP, D)
        xT = np.ascontiguousarray(np.concatenate([xp, xs], axis=0).T).astype(f)
        m = {"xT": xT, "cst": cst, "wts": wts}
        sl = slice(core * NSS, (core + 1) * NSS)
        if has_gla:
            m["st_gla"] = np.ascontiguousarray(inp["state_gla"][:, sl].transpose(0, 2, 3, 1, 4)).astype(f)
        if has_ab:
            m["rot_d"] = rot
            m["s5w"] = s5w
            re = inp["state_s5_re"][:, sl].reshape(2, NSS, 16, 128).transpose(0, 3, 2, 1)
            im = inp["state_s5_im"][:, sl].reshape(2, NSS, 16, 128).transpose(0, 3, 2, 1)
            m["st_s5"] = np.ascontiguousarray(np.stack([re, im], axis=1)).astype(f)
            m["st_ret"] = np.ascontiguousarray(
                inp["state_ret"][:, sl].reshape(2, NSS, 2, 128, 128).transpose(0, 2, 3, 1, 4)).astype(f)
        maps.append(m)
    return maps


_CACHE = {}
DEBUG_OUT = {}


def kernel(**inputs):
    inputs = {k: np.asarray(v) for k, v in inputs.items()}
    ncores = int(os.environ.get("KCORES", "8"))
    subs = parse_subs()
    maps = prepare_inputs(inputs, subs)[:ncores]
    key = tuple(subs)
    if key not in _CACHE:
        _CACHE[key] = build_program(subs)
    P = _CACHE[key]
    if os.environ.get("KTRACE"):
        res = run_bass_kernel_spmd(P.nc, maps, core_ids=list(range(ncores)), trace=True)
        print("KTRACE exec_time_ns", res.exec_time_ns)
    else:
        res = run_bass_kernel_spmd(P.nc, maps, core_ids=list(range(ncores)))
    outs = res.results
    yp = np.zeros((8, SEQ, D), np.float32)
    ys = np.zeros((128, DEC_SEQ, D), np.float32)
    p_s5re = np.zeros((2, 8, 32, 64), np.float32)
    p_s5im = np.zeros((2, 8, 32, 64), np.float32)
    p_ret = np.zeros((2, 8, 4, 64, 128), np.float32)
    p_gla = np.zeros((2, 8, 4, 128, 256), np.float32)
    s_s5re = np.zeros((2, 128, 32, 64), np.float32)
    s_s5im = np.zeros((2, 128, 32, 64), np.float32)
    s_ret = np.zeros((2, 128, 4, 64, 128), np.float32)
    s_gla = np.zeros((2, 128, 4, 128, 256), np.float32)
    for core in range(ncores):
        o = outs[core]
        y = o["yT"].T
        yp[core] = y[N_META:LP]
        sl = slice(core * NSS, (core + 1) * NSS)
        ys[sl] = y[LP:].reshape(NSS, DEC_SEQ, D)
        if "o_pgla" in o:
            p_gla[:, core] = o["o_pgla"]
            s_gla[:, sl] = o["o_sgla"].transpose(0, 3, 1, 2, 4)
        if "o_pret" in o:
            unpack_ab(o, core, sl, p_s5re, p_s5im, p_ret, s_s5re, s_s5im, s_ret)
    DEBUG_OUT.clear()
    DEBUG_OUT.update({"p_gla": p_gla, "s_gla": s_gla, "p_ret": p_ret, "s_ret": s_ret,
                      "p_s5re": p_s5re, "p_s5im": p_s5im, "s_s5re": s_s5re, "s_s5im": s_s5im})
    return (yp, ys, p_s5re, p_s5im, p_ret, p_gla, s_s5re, s_s5im, s_ret, s_gla)
```

```python
import os
import numpy as np
import concourse.bass as bass
import concourse.mybir as mybir
from concourse.bass_utils import run_bass_kernel_spmd

F32 = mybir.dt.float32
BF16 = mybir.dt.bfloat16
AF = mybir.ActivationFunctionType
ALU = mybir.AluOpType

D = 1024
NCH = 8
DEPTH = 4
SEQ = 2048
N_META = 16
LP = SEQ + N_META
NSS = 16
DEC_SEQ = 8
NSAMP = NSS * DEC_SEQ
T = LP + NSAMP
PAST_LEN = 16384
D_FF = 2816
NFF = 22
EPS = 1e-6
TILES = [(0, 512), (512, 1024), (1024, 1536), (1536, 2048), (2048, 2192)]
NT = len(TILES)
SLOT = 3072
NSLOT = 5
SEM_LIMIT = 30000


class Sem:
    __slots__ = ("h", "count", "dma")

    def __init__(self, h, dma):
        self.h = h
        self.count = 0
        self.dma = dma


class Region:
    __slots__ = ("name", "w", "r")

    def __init__(self, name):
        self.name = name
        self.w = None
        self.r = {}


class Sched:
    ENG = ("pe", "act", "dve", "pool", "sp")

    def __init__(self, nc):
        self.nc = nc
        self.rec = {e: [] for e in self.ENG}
        self.cur = {}
        self.waited = {e: {} for e in self.ENG}
        self.nsem = 0
        for e in self.ENG:
            self.cur[e] = self.new_sem(e, False)
        self.all_dma_sems = []
        self.pe_pending = False

    def new_sem(self, name, dma):
        self.nsem += 1
        s = Sem(self.nc.alloc_semaphore("s_%s_%d" % (name, self.nsem)), dma)
        if dma:
            self.all_dma_sems.append(s)
        return s

    def dma_sem(self, name):
        s = Sem(self.nc.alloc_semaphore("d_%s_%d" % (name, self.nsem)), True)
        self.nsem += 1
        return s

    def _deps(self, reads, writes):
        deps = {}

        def add(tok):
            if tok is None:
                return
            s, v = tok
            if s.dma:
                v = s.count
            k = id(s)
            if k not in deps or deps[k][1] < v:
                deps[k] = (s, v)

        for r in reads:
            add(r.w)
        for w in writes:
            add(w.w)
            for tok in w.r.values():
                add(tok)
        return deps

    def op(self, eng, fn, reads=(), writes=(), dsem=None, signal=True):
        if eng != "pe":
            assert not self.pe_pending, "non-PE op recorded inside an unsignalled PE group"
        else:
            self.pe_pending = not signal
        deps = self._deps(reads, writes)
        waits = []
        wd = self.waited[eng]
        mysem = self.cur[eng]
        for k, (s, v) in deps.items():
            if eng == "pe" and s is mysem:
                continue
            if wd.get(k, 0) >= v:
                continue
            wd[k] = v
            waits.append((s.h, v))
        if dsem is not None:
            dsem.count += 16
            tok = (dsem, dsem.count)
            inc = (dsem.h, 16)
        elif signal:
            if mysem.count >= SEM_LIMIT:
                mysem = self.new_sem(eng, False)
                self.cur[eng] = mysem
            mysem.count += 1
            tok = (mysem, mysem.count)
            inc = (mysem.h, 1)
        else:
            tok = (mysem, mysem.count + 1)
            inc = None
        self.rec[eng].append((waits, fn, inc))
        for w in writes:
            w.w = tok
            w.r = {}
        for r in reads:
            k = id(tok[0])
            if k not in r.r or r.r[k][1] < tok[1]:
                r.r[k] = tok
        return tok

    def barrier(self, io_sems=()):
        toks = [(self.cur[e], self.cur[e].count) for e in ("pe", "act", "dve", "pool")]
        toks += [(s, s.count) for s in io_sems]
        for e in ("pe", "act", "dve", "pool", "sp"):
            waits = []
            for (s, v) in toks:
                if v == 0:
                    continue
                if self.waited[e].get(id(s), 0) >= v:
                    continue
                self.waited[e][id(s)] = v
                waits.append((s.h, v))
            if waits:
                self.rec[e].append((waits, None, None))

    def replay(self, final_waits):
        nc = self.nc
        handles = {"pe": "tensor", "act": "scalar", "dve": "vector", "pool": "gpsimd", "sp": "sync"}
        with nc.Block() as block:
            for e in self.ENG:
                rec = self.rec[e]
                fw = final_waits if e == "sp" else []

                def body(h, rec=rec, fw=fw):
                    for (waits, fn, inc) in rec:
                        for (sh, v) in waits:
                            h.wait_ge(sh, v)
                        if fn is not None:
                            ins = fn(h)
                            if inc is not None:
                                ins.then_inc(inc[0], inc[1])
                    for (sh, v) in fw:
                        h.wait_ge(sh, v)

                getattr(block, handles[e])(body)


class Prog:
    def __init__(self):
        nc = bass.Bass("TRN2", target_bir_lowering=False)
        self.nc = nc
        self.S = Sched(nc)
        self.loads = []
        self.wcols = 0
        self.dram_in = {}
        self.dram_out = {}

    def din(self, name, shape):
        t = self.nc.dram_tensor(name, list(shape), F32, kind="ExternalInput").ap()
        self.dram_in[name] = t
        return t

    def dout(self, name, shape):
        t = self.nc.dram_tensor(name, list(shape), F32, kind="ExternalOutput").ap()
        self.dram_out[name] = t
        return t

    def sb(self, name, shape, dt):
        return self.nc.alloc_sbuf_tensor(name, list(shape), dt)

    def init_psum(self):
        self.banks = [self.nc.alloc_psum_tensor("ps%d" % i, [128, 512], F32) for i in range(8)]
        self.bank_r = [Region("ps%d" % i) for i in range(8)]
        self.bank_i = 0
        self.rot = list(range(8))

    def bank(self):
        i = self.rot[self.bank_i % len(self.rot)]
        self.bank_i += 1
        return self.banks[i], self.bank_r[i]

    def init_wstream(self, wts_ap):
        self.wts = wts_ap
        self.slots = [self.sb("wslot%d" % i, [128, SLOT], BF16) for i in range(NSLOT)]
        self.slot_r = [Region("wslot%d" % i) for i in range(NSLOT)]
        self.slot_sem = [self.S.dma_sem("w%d" % i) for i in range(NSLOT)]
        self.issued = 0
        self.released = set()
        self.next_get = 0

    def _pump(self):
        while self.issued < len(self.loads):
            j = self.issued
            if j >= NSLOT and (j - NSLOT) not in self.released:
                break
            off, n = self.loads[j]
            s = j % NSLOT
            dst = self.slots[s][:, 0:n]
            src = self.wts[:, off:off + n]
            self.S.op("pool", lambda h, dst=dst, src=src: h.dma_start(out=dst, in_=src),
                      writes=[self.slot_r[s]], dsem=self.slot_sem[s])
            self.issued += 1

    def wget(self):
        j = self.next_get
        self.next_get += 1
        self._pump()
        assert j < self.issued, "weight stream stalled (load %d not issuable: release missing)" % j
        return self.slots[j % NSLOT], self.slot_r[j % NSLOT], j

    def wrelease(self, j):
        self.released.add(j)
        self._pump()


KAB = os.environ.get("KAB", "rs")


def parse_subs():
    spec = os.environ.get("KSUBS")
    if spec:
        out = []
        for tok in spec.split(","):
            layer = int(tok[0])
            kind = {"f1": "ffn1", "m": "mix", "f2": "ffn2"}[tok[1:]]
            out.append((layer, kind))
        return out
    return [(l, k) for l in range(DEPTH) for k in ("ffn1", "mix", "ffn2")]


def make_plan(subs):
    plan = []
    for (layer, kind) in subs:
        i = layer // 2
        if kind == "ffn1" or kind == "ffn2":
            for j in range(NFF):
                plan.append((("ffn", layer, 0 if kind == "ffn1" else 1, j), SLOT))
        elif layer % 2 == 1:
            for hd in range(4):
                plan.append((("gla", i, hd, 0), 2048 + 128))
                plan.append((("gla", i, hd, 1), 3072))
                plan.append((("gla", i, hd, 2), 3072))
        else:
            if "r" in KAB:
                for pr in range(2):
                    plan.append((("ret", i, pr, 0), 3072))
                    plan.append((("ret", i, pr, 1), 3072))
                    plan.append((("ret", i, pr, 2), 2048))
                    plan.append((("ret", i, pr, 3), 2048))
            if "s" in KAB:
                for yc in range(4):
                    plan.append((("s5u", i, yc), 1024))
                plan.append((("s5g", i), 2048))
                plan.append((("s5o", i, 0), 2048))
                plan.append((("s5o", i, 1), 2048))
    return plan


class Layout:
    def __init__(self):
        self.off = {}
        self.n = 0

    def add(self, name, ncols):
        self.off[name] = (self.n, ncols)
        self.n += ncols

    def __getitem__(self, name):
        return self.off[name]


def cst_layout():
    L = Layout()
    L.add("gains", 13 * NCH)
    L.add("gla_b", 2 * 4)
    L.add("gla_g", 2 * 2)
    L.add("s5_d", 2 * 4)
    L.add("one", 1)
    L.add("hm", 2)
    L.add("cmask", 512)
    L.add("cmask4", 144)
    L.add("ident32", 128)
    L.add("nident32", 128)
    L.add("causal", 128)
    L.add("samp", 128)
    L.add("seqm", 16)
    L.add("ident", 128)
    L.add("w2", 2 * 512)
    L.add("jvec", 256)
    L.add("ret_gam", 18)
    L.add("s5sm", 2 * 3 * 16)
    return L


def build_program(subs):
    P = Prog()
    nc, S = P.nc, P.S
    plan = make_plan(subs)
    CL = cst_layout()
    has_gla = any(k == "mix" and l % 2 == 1 for (l, k) in subs)
    has_ab = any(k == "mix" and l % 2 == 0 for (l, k) in subs)
    WCOLS = max(sum(n for (_, n) in plan), 1)
    xT = P.din("xT", [D, T])
    cst = P.din("cst", [128, CL.n])
    wts = P.din("wts", [128, WCOLS])
    yT = P.dout("yT", [D, T])
    if has_gla:
        st_gla = P.din("st_gla", [2, 4, 128, NSS, 256])
        o_pgla = P.dout("o_pgla", [2, 4, 128, 256])
        o_sgla = P.dout("o_sgla", [2, 4, 128, NSS, 256])
    if has_ab:
        rot_d = P.din("rot_d", [2, NT, 128, 4, 512])
        s5w = P.din("s5w", [2, 4, 128, 7 * 512])
        st_s5 = P.din("st_s5", [2, 2, 128, 16, NSS])
        st_ret = P.din("st_ret", [2, 2, 128, NSS, 128])
        o_pret = P.dout("o_pret", [2, 2, 128, 128])
        o_sret = P.dout("o_sret", [2, 2, 128, NSS, 128])
        o_ps5 = P.dout("o_ps5", [2, 2, 128, 16])
        o_ss5 = P.dout("o_ss5", [2, 2, 128, 16, NSS])

    x = P.sb("x", [128, NCH, T], F32)
    hb = P.sb("hb", [128, NCH, T], BF16)
    NCF = CL["w2"][0]
    NCF = CL["causal"][0]
    cf = P.sb("cf", [128, NCF], F32)
    jv = P.sb("jv", [128, 256 + 18 + 96], F32)
    cb = P.sb("cb", [128, 128 + 128 + 16 + 128 + 1024], BF16)
    ones_bf = P.sb("ones_bf", [128, 128], BF16)
    eps_t = P.sb("eps_t", [128, 1], F32)
    ones32 = P.sb("ones32", [128, 128], F32)
    ARENA = 16384
    arena = P.sb("arena", [128, ARENA], F32)
    x_r = [[Region("x%d_%d" % (c, t)) for t in range(NT)] for c in range(NCH)]
    h_r = [[Region("h%d_%d" % (c, t)) for t in range(NT)] for c in range(NCH)]
    cst_r = Region("cst")
    P.init_psum()
    P.init_wstream(wts)
    off = 0
    for (_, n) in plan:
        P.loads.append((off, n))
        off += n

    ar = {"off": 0, "cache": None}

    def aalloc(ncols_f32, dt=F32, name="a"):
        o = ar["off"]
        ar["off"] = o + (ncols_f32 + 15) // 16 * 16
        assert ar["off"] <= ARENA, "arena overflow %d" % ar["off"]
        if ar["cache"] is not None:
            key = (o, ncols_f32, dt)
            if key not in ar["cache"]:
                ap = arena[:, o:o + ncols_f32]
                if dt == BF16:
                    ap = ap.bitcast(BF16)
                ar["cache"][key] = (ap, Region(name))
            return ar["cache"][key]
        ap = arena[:, o:o + ncols_f32]
        if dt == BF16:
            ap = ap.bitcast(BF16)
        return ap, Region(name)

    io_sems = []

    def areset():
        S.barrier(io_sems)
        ar["off"] = 0
        ar["cache"] = None

    def areset_passes(first):
        if first:
            areset()
            ar["cache"] = {}
        else:
            ar["off"] = 0

    def ACT(out, in_, func, reads, writes, bias=None, scale=None):
        kw = {}
        if bias is not None:
            kw["bias"] = bias
        if scale is not None:
            kw["scale"] = scale
        return S.op("act", lambda h: h.activation(out=out, in_=in_, func=func, **kw), reads=reads, writes=writes)

    def TT(eng, out, in0, in1, op, reads, writes):
        return S.op(eng, lambda h: h.tensor_tensor(out=out, in0=in0, in1=in1, op=op), reads=reads, writes=writes)

    def TS(eng, out, in0, s1, op0, reads, writes, s2=None, op1=None):
        if op1 is None:
            return S.op(eng, lambda h: h.tensor_scalar(out=out, in0=in0, scalar1=s1, scalar2=None, op0=op0),
                        reads=reads, writes=writes)
        return S.op(eng, lambda h: h.tensor_scalar(out=out, in0=in0, scalar1=s1, scalar2=s2, op0=op0, op1=op1),
                    reads=reads, writes=writes)

    def STT(eng, out, in0, scalar, in1, op0, op1, reads, writes):
        return S.op(eng, lambda h: h.scalar_tensor_tensor(out=out, in0=in0, scalar=scalar, in1=in1, op0=op0, op1=op1),
                    reads=reads, writes=writes)

    def COPY(eng, out, in_, reads, writes):
        if eng == "act":
            return S.op("act", lambda h: h.copy(out=out, in_=in_), reads=reads, writes=writes)
        return S.op(eng, lambda h: h.tensor_copy(out=out, in_=in_), reads=reads, writes=writes)

    def MM(out, lhsT, rhs, start, stop, reads, writes, signal=None):
        if signal is None:
            signal = stop
        return S.op("pe", lambda h: h.matmul(out, lhsT=lhsT, rhs=rhs, start=start, stop=stop),
                    reads=reads, writes=writes, signal=signal)

    def DMA(eng, out, in_, reads, writes, sem):
        return S.op(eng, lambda h: h.dma_start(out=out, in_=in_), reads=reads, writes=writes, dsem=sem)

    def SCAN(out, data0, data1, reads, writes, initial=0.0):
        return S.op("dve", lambda h: h.tensor_tensor_scan(out=out, data0=data0, data1=data1, initial=initial,
                                                          op0=ALU.mult, op1=ALU.add), reads=reads, writes=writes)

    def RECIP(out, in_, reads, writes):
        return S.op("dve", lambda h: h.reciprocal(out=out, in_=in_), reads=reads, writes=writes)

    def TRANSP(out, in_, reads, writes):
        return S.op("pe", lambda h: h.transpose(out=out, in_=in_, identity=ident_bf), reads=reads, writes=writes)

    def MEMSET(eng, out, val, writes):
        return S.op(eng, lambda h: h.memset(out, val), writes=writes)

    def ccol(name, j=0, n=1):
        o, _ = CL[name]
        return cf[:, o + j:o + j + n]

    in_sem = S.dma_sem("in")
    io_sems.append(in_sem)
    DMA("sp", cf[:], cst[:, 0:NCF], [], [cst_r], in_sem)
    for c in range(NCH):
        DMA("sp", x[:, c, :], xT[c * 128:(c + 1) * 128, :], [], [x_r[c][t] for t in range(NT)], in_sem)
    nb16 = 128 + 128 + 16 + 128 + 1024
    stg, stg_r = aalloc(nb16 + 256 + 18 + 96, F32, "stg")
    o_c = CL["causal"][0]
    DMA("sp", stg, cst[:, o_c:o_c + nb16 + 256 + 18 + 96], [], [stg_r], in_sem)
    COPY("dve", cb[:], stg[:, 0:nb16], [stg_r], [cst_r])
    COPY("dve", jv[:], stg[:, nb16:nb16 + 256 + 18 + 96], [stg_r], [cst_r])
    S.op("dve", lambda h: h.memset(ones_bf[:], 1.0), writes=[cst_r])
    S.op("dve", lambda h: h.memset(eps_t[:], EPS), writes=[cst_r])
    S.op("dve", lambda h: h.memset(ones32[:], 1.0), writes=[cst_r])
    causal_bf = cb[:, 0:128]
    samp_bf = cb[:, 128:256]
    seqm_bf = cb[:, 256:272]
    ident_bf = cb[:, 272:400]
    w2_bf = cb[:, 400:1424]
    jvec = jv[:, 0:256]
    areset()

    cnt = {"sq": 0, "rstd": 0, "sg": 0}
    nt_ = {}

    def norm_alloc():
        nt_["sq"] = [aalloc(256, BF16, "sq%d" % i) for i in range(4)]
        nt_["rstd"] = [aalloc(512, F32, "rstd%d" % i) for i in range(2)]

    def norm_tile(t, gidx, final=False):
        sq = [a for (a, _) in nt_["sq"]]
        sq_r = [r for (_, r) in nt_["sq"]]
        rstd = [a for (a, _) in nt_["rstd"]]
        rstd_r = [r for (_, r) in nt_["rstd"]]
        t0, t1 = TILES[t]
        n = t1 - t0
        pb, pr = P.bank()
        for c in range(NCH):
            i = cnt["sq"] % 4
            cnt["sq"] += 1
            ACT(sq[i][:, 0:n], x[:, c, t0:t1], AF.Square, [x_r[c][t]], [sq_r[i]])
            MM(pb[:, 0:n], ones_bf[:], sq[i][:, 0:n], c == 0, c == NCH - 1, [sq_r[i], cst_r], [pr], signal=True)
        k = cnt["rstd"] % 2
        cnt["rstd"] += 1
        ACT(rstd[k][:, 0:n], pb[:, 0:n], AF.Sqrt, [pr, cst_r], [rstd_r[k]], bias=eps_t[:, 0:1], scale=1.0 / D)
        RECIP(rstd[k][:, 0:n], rstd[k][:, 0:n], [rstd_r[k]], [rstd_r[k]])
        for c in range(NCH):
            g = ccol("gains", gidx * NCH + c)
            if final:
                STT("dve", x[:, c, t0:t1], x[:, c, t0:t1], g, rstd[k][:, 0:n], ALU.mult, ALU.mult,
                    [x_r[c][t], rstd_r[k], cst_r], [x_r[c][t]])
            else:
                STT("dve", hb[:, c, t0:t1], x[:, c, t0:t1], g, rstd[k][:, 0:n], ALU.mult, ALU.mult,
                    [x_r[c][t], rstd_r[k], cst_r], [h_r[c][t]])

    def ffn(gidx):
        areset()
        norm_alloc()
        sg_a = [aalloc(256, BF16, "sg%d" % i) for i in range(2)]
        hid_a = [[aalloc(256, BF16, "hid%d_%d" % (i, j)) for j in range(2)] for i in range(2)]
        sg = [a for (a, _) in sg_a]
        sg_r = [r for (_, r) in sg_a]
        hid = [[a for (a, _) in row] for row in hid_a]
        hid_r = [[r for (_, r) in row] for row in hid_a]
        for t in range(NT):
            norm_tile(t, gidx)
        steps = [(g, t) for g in range(NFF // 2) for t in range(NT)]
        slots = {}

        def A(si, jj):
            g, t = steps[si]
            t0, t1 = TILES[t]
            n = t1 - t0
            if t == 0:
                slots[(g, jj)] = P.wget()
            ws, wr, _ = slots[(g, jj)]
            pg, pgr = P.bank()
            pu, pur = P.bank()
            for (pb, pr, off) in ((pg, pgr, 0), (pu, pur, 1024)):
                for kc in range(NCH):
                    MM(pb[:, 0:n], ws[:, off + kc * 128: off + (kc + 1) * 128], hb[:, kc, t0:t1],
                       kc == 0, kc == NCH - 1, [wr, h_r[kc][t]], [pr])
            i = cnt["sg"] % 2
            cnt["sg"] += 1
            ACT(sg[i][:, 0:n], pg[:, 0:n], AF.Silu, [pgr], [sg_r[i]])
            TT("dve", hid[si % 2][jj][:, 0:n], sg[i][:, 0:n], pu[:, 0:n], ALU.mult, [sg_r[i], pur], [hid_r[si % 2][jj]])

        def B(si, half):
            g, t = steps[si]
            t0, t1 = TILES[t]
            n = t1 - t0
            for m in range(half * 4, half * 4 + 4):
                pb, pr = P.bank()
                for jj in range(2):
                    ws, wr, _ = slots[(g, jj)]
                    MM(pb[:, 0:n], ws[:, 2048 + m * 128: 2048 + (m + 1) * 128], hid[si % 2][jj][:, 0:n],
                       jj == 0, jj == 1, [wr, hid_r[si % 2][jj]], [pr])
                STT("dve", x[:, m, t0:t1], pb[:, 0:n], 0.5, x[:, m, t0:t1], ALU.mult, ALU.add,
                    [pr, x_r[m][t]], [x_r[m][t]])
            if half == 1 and t == NT - 1:
                for jj in range(2):
                    P.wrelease(slots[(g, jj)][2])

        ns = len(steps)
        A(0, 0)
        A(0, 1)
        for si in range(1, ns):
            A(si, 0)
            B(si - 1, 0)
            A(si, 1)
            B(si - 1, 1)
        B(ns - 1, 0)
        B(ns - 1, 1)

    def tile_blocks(t):
        if t < NT - 1:
            return [(b * 128, 128, "p") for b in range(4)]
        return [(0, 16, "p"), (16, 128, "s")]

    XOFF = int(os.environ.get("KXOFF", "0"))

    def gla_pass(i, hd, st_sem, out_sem):
        DK = 128
        w1, w1r, j1 = P.wget()
        w2s, w2r, j2 = P.wget()
        w3, w3r, j3 = P.wget()
        Wq = lambda kc: w1[:, kc * 128:(kc + 1) * 128]
        Wk = lambda kc: w1[:, 1024 + kc * 128:1024 + (kc + 1) * 128]
        Wlr = lambda kc: w1[:, 2048 + kc * 16:2048 + (kc + 1) * 16]
        Wv = lambda kc: w2s[:, kc * 256:(kc + 1) * 256]
        Wr = lambda cc, kc: (w2s[:, 2048 + kc * 128:2048 + (kc + 1) * 128] if cc == 0
                             else w3[:, kc * 128:(kc + 1) * 128])
        Wrr = lambda cc: (w2r if cc == 0 else w3r)
        Wo = lambda cc, m: w3[:, 1024 + cc * 1024 + m * 128:1024 + cc * 1024 + (m + 1) * 128]
        SETS = []
        for sidx in range(2):
            d = {}
            for nm, ncol, dt in (("lrb", 256, BF16), ("ee", 512, F32), ("Bc", 512, F32), ("E", 512, F32), ("Ei", 512, F32),
                                 ("qt", 256, BF16), ("kt", 256, BF16)):
                d[nm], d[nm + "_r"] = aalloc(ncol, dt, nm + str(sidx))
            for b in range(4):
                d["vtok%d" % b], d["vtok%d_r" % b] = aalloc(128, BF16, "vtok")
                d["ktok%d" % b], d["ktok%d_r" % b] = aalloc(64, BF16, "ktok")
                d["Am%d" % b], d["Am%d_r" % b] = aalloc(64, BF16, "Am")
            SETS.append(d)
        SR = []
        for q in range(3):
            SR.append([aalloc(256, BF16, "sr%d_%d" % (q, cc)) for cc in range(2)])
        S32, S32_r = aalloc(256, F32, "S32")
        Sbf, Sbf_r = aalloc(128, BF16, "Sbf")
        sqo, sqo_r = [], []
        og, og_r, og2, og2_r = [], [], [], []
        for cc in range(2):
            a, r = aalloc(256, BF16, "sqo%d" % cc)
            sqo.append(a)
            sqo_r.append(r)
            a, r = aalloc(512, F32, "og%d" % cc)
            og.append(a)
            og_r.append(r)
            a, r = aalloc(256, BF16, "og2%d" % cc)
            og2.append(a)
            og2_r.append(r)
        rs, rs_r = aalloc(512, F32, "rs")
        S0, S0_r = aalloc(2048, F32, "S0")
        S0b, S0b_r = aalloc(1024, BF16, "S0b")
        kmask, kmask_r = aalloc(512, BF16, "kmask")
        negb, negb_r = aalloc(1, F32, "negb")
        xtmp = [aalloc(512, F32, "xtmp%d" % q) for q in range(2)] if XOFF else None

        TS("dve", negb, ccol("gla_b", i * 4 + hd), -1.0, ALU.mult, [cst_r], [negb_r])
        MEMSET("dve", S32, 0.0, [S32_r])
        MEMSET("dve", Sbf, 0.0, [Sbf_r])
        PO = [[P.banks[0], P.banks[1]], [P.banks[2], P.banks[3]]]
        PO_r = [[P.bank_r[0], P.bank_r[1]], [P.bank_r[2], P.bank_r[3]]]

        def stage1(t):
            d = SETS[t % 2]
            t0, t1 = TILES[t]
            n = t1 - t0
            blocks = tile_blocks(t)
            sl = []

            def s_decay():
                pb, pr = P.bank()
                for kc in range(NCH):
                    MM(pb[0:16, 0:n], Wlr(kc), hb[:, kc, t0:t1], kc == 0, kc == NCH - 1, [w1r, h_r[kc][t]], [pr])
                COPY("act", d["lrb"][0:16, 0:n], pb[0:16, 0:n], [pr], [d["lrb_r"]])
                pz, pzr = P.bank()
                MM(pz[:, 0:n], w2_bf[0:16, i * 512 + hd * 128:i * 512 + (hd + 1) * 128], d["lrb"][0:16, 0:n], True, True,
                   [d["lrb_r"], cst_r], [pzr])
                ACT(d["ee"][:, 0:n], pz[:, 0:n], AF.Exp, [pzr, negb_r], [d["ee_r"]], bias=negb[:, 0:1], scale=-1.0)
                ACT(d["ee"][:, 0:n], d["ee"][:, 0:n], AF.Ln, [d["ee_r"], cst_r], [d["ee_r"]], bias=ccol("one"), scale=1.0)
                cm = ccol("cmask", 0, n) if t < NT - 1 else ccol("cmask4", 0, n)
                SCAN(d["Bc"][:, 0:n], cm, d["ee"][:, 0:n], [d["ee_r"], cst_r], [d["Bc_r"]])
                ACT(d["E"][:, 0:n], d["Bc"][:, 0:n], AF.Exp, [d["Bc_r"]], [d["E_r"]], scale=-1.0 / 16.0)
                ACT(d["Ei"][:, 0:n], d["Bc"][:, 0:n], AF.Exp, [d["Bc_r"]], [d["Ei_r"]], scale=1.0 / 16.0)
            sl.append(s_decay)

            def s_q():
                pq, pqr = P.bank()
                for kc in range(NCH):
                    MM(pq[:, 0:n], Wq(kc), hb[:, kc, t0:t1], kc == 0, kc == NCH - 1, [w1r, h_r[kc][t]], [pqr])
                TT("dve", d["qt"][:, 0:n], pq[:, 0:n], d["E"][:, 0:n], ALU.mult, [pqr, d["E_r"]], [d["qt_r"]])
            sl.append(s_q)

            def s_k():
                pk, pkr = P.bank()
                for kc in range(NCH):
                    MM(pk[:, 0:n], Wk(kc), hb[:, kc, t0:t1], kc == 0, kc == NCH - 1, [w1r, h_r[kc][t]], [pkr])
                STT("dve", d["kt"][:, 0:n], pk[:, 0:n], DK ** -0.5, d["Ei"][:, 0:n], ALU.mult, ALU.mult,
                    [pkr, d["Ei_r"]], [d["kt_r"]])
            sl.append(s_k)

            def mk_gate(cc):
                def s_gate():
                    pg, pgr = P.bank()
                    for kc in range(NCH):
                        MM(pg[:, 0:n], Wr(cc, kc), hb[:, kc, t0:t1], kc == 0, kc == NCH - 1, [Wrr(cc), h_r[kc][t]], [pgr])
                    ACT(SR[t % 3][cc][0][:, 0:n], pg[:, 0:n], AF.Silu, [pgr], [SR[t % 3][cc][1]])
                return s_gate
            sl.append(mk_gate(0))
            sl.append(mk_gate(1))

            def mk_tok(b, c0, nb):
                def s_tok():
                    pv, pvr = P.bank()
                    for kc in range(NCH):
                        MM(pv[0:nb, 0:256], hb[:, kc, t0 + c0:t0 + c0 + nb], Wv(kc), kc == 0, kc == NCH - 1,
                           [w2r, h_r[kc][t]], [pvr])
                    COPY("act", d["vtok%d" % b][0:nb, 0:256], pv[0:nb, 0:256], [pvr], [d["vtok%d_r" % b]])
                    pt, ptr = P.bank()
                    ptb = pt[:, 0:64].bitcast(BF16)
                    TRANSP(ptb[0:nb, 0:128], d["kt"][:, c0:c0 + nb], [d["kt_r"], cst_r], [ptr])
                    COPY("act", d["ktok%d" % b][0:nb, 0:128], ptb[0:nb, 0:128], [ptr], [d["ktok%d_r" % b]])
                    ps_, psr = P.bank()
                    MM(ps_[0:nb, 0:nb], d["kt"][:, c0:c0 + nb], d["qt"][:, c0:c0 + nb], True, True,
                       [d["kt_r"], d["qt_r"]], [psr])
                    mk = causal_bf if blocks[b][2] == "p" else samp_bf
                    TT("dve", d["Am%d" % b][0:nb, 0:nb], ps_[0:nb, 0:nb], mk[0:nb, 0:nb], ALU.mult, [psr, cst_r],
                       [d["Am%d_r" % b]])
                return s_tok
            for b, (c0, nb, kind) in enumerate(blocks):
                sl.append(mk_tok(b, c0, nb))
            return sl

        def chain_steps(t):
            d = SETS[t % 2]
            po, po_r = PO[t % 2], PO_r[t % 2]
            t0, t1 = TILES[t]
            blocks = tile_blocks(t)
            steps = []

            def mk_p(b, c0, nb):
                def step():
                    vt_, vr_ = d["vtok%d" % b], d["vtok%d_r" % b]
                    kt_, kr_ = d["ktok%d" % b], d["ktok%d_r" % b]
                    am_, ar_ = d["Am%d" % b], d["Am%d_r" % b]
                    pU, pUr = P.bank()
                    MM(pU[:, 0:256], kt_[0:nb, 0:128], vt_[0:nb, 0:256], True, True, [kr_, vr_], [pUr])
                    for cc in range(2):
                        MM(po[cc][:, c0:c0 + nb], vt_[0:nb, cc * 128:(cc + 1) * 128], am_[0:nb, 0:nb], True, False,
                           [vr_, ar_], [po_r[cc]], signal=False)
                        MM(po[cc][:, c0:c0 + nb], Sbf[:, cc * 128:(cc + 1) * 128], d["qt"][:, c0:c0 + nb], False, True,
                           [Sbf_r, d["qt_r"]], [po_r[cc]], signal=True)
                    e = d["E"][:, c0 + nb - 1:c0 + nb]
                    TS("dve", S32, S32, e, ALU.mult, [S32_r, d["E_r"]], [S32_r])
                    STT("dve", S32, pU[:, 0:256], e, S32, ALU.mult, ALU.add, [pUr, d["E_r"], S32_r], [S32_r])
                    COPY("act", Sbf, S32, [S32_r], [Sbf_r])
                    if t == NT - 1:
                        DMA("sp", o_pgla[i, hd], S32, [S32_r], [], out_sem)
                return step

            def mk_s(b, c0, nb, half):
                def step():
                    vt_, vr_ = d["vtok%d" % b], d["vtok%d_r" % b]
                    kt_, kr_ = d["ktok%d" % b], d["ktok%d_r" % b]
                    am_, ar_ = d["Am%d" % b], d["Am%d_r" % b]
                    DMA("sp", S0.rearrange("p (s v) -> p s v", v=256), st_gla[i, hd, :, half * 8:(half + 1) * 8, :],
                        [], [S0_r], st_sem)
                    COPY("act", S0b, S0, [S0_r], [S0b_r])
                    TT("dve", kmask.rearrange("p (s k) -> p s k", k=128),
                       kt_[:, 0:128].unsqueeze(1).to_broadcast([128, 8, 128]),
                       seqm_bf[:, half * 8:(half + 1) * 8].unsqueeze(2).to_broadcast([128, 8, 128]),
                       ALU.mult, [kr_, cst_r], [kmask_r])
                    for s_ in range(8):
                        sq_i = half * 8 + s_
                        cs = c0 + sq_i * 8
                        for cc in range(2):
                            MM(po[cc][:, cs:cs + 8], vt_[0:nb, cc * 128:(cc + 1) * 128], am_[0:nb, sq_i * 8:sq_i * 8 + 8],
                               True, False, [vr_, ar_], [po_r[cc]], signal=False)
                            MM(po[cc][:, cs:cs + 8], S0b[:, s_ * 256 + cc * 128:s_ * 256 + (cc + 1) * 128],
                               d["qt"][:, cs:cs + 8], False, True, [S0b_r, d["qt_r"]], [po_r[cc]], signal=True)
                    for s_ in range(8):
                        sq_i = half * 8 + s_
                        cs = c0 + sq_i * 8
                        pU, pUr = P.bank()
                        MM(pU[:, 0:256], kmask[:, s_ * 128:(s_ + 1) * 128], vt_[0:nb, 0:256], True, True,
                           [kmask_r, vr_], [pUr])
                        e = d["E"][:, cs + 7:cs + 8]
                        S0s = S0[:, s_ * 256:(s_ + 1) * 256]
                        TS("dve", S0s, S0s, e, ALU.mult, [S0_r, d["E_r"]], [S0_r])
                        STT("dve", S0s, pU[:, 0:256], e, S0s, ALU.mult, ALU.add, [pUr, d["E_r"], S0_r], [S0_r])
                    DMA("sp", o_sgla[i, hd, :, half * 8:(half + 1) * 8, :], S0.rearrange("p (s v) -> p s v", v=256),
                        [S0_r], [], out_sem)
                return step

            for b, (c0, nb, kind) in enumerate(blocks):
                if kind == "p":
                    steps.append(mk_p(b, c0, nb))
                else:
                    steps.append(mk_s(b, c0, nb, 0))
                    steps.append(mk_s(b, c0, nb, 1))
            return steps

        def tail_steps(t):
            d = SETS[t % 2]
            t0, t1 = TILES[t]
            n = t1 - t0
            sr = [SR[t % 3][cc][0] for cc in range(2)]
            sr_r = [SR[t % 3][cc][1] for cc in range(2)]
            po, po_r = PO[t % 2], PO_r[t % 2]
            st = {}

            def T1():
                pn, pnr = P.bank()
                st["pn"] = (pn, pnr)
                for cc in range(2):
                    ACT(sqo[cc][:, 0:n], po[cc][:, 0:n], AF.Square, [po_r[cc]], [sqo_r[cc]])
                    MM(pn[:, 0:n], ones_bf[:], sqo[cc][:, 0:n], cc == 0, cc == 1, [sqo_r[cc], cst_r], [pnr], signal=True)

            def T2():
                pn, pnr = st["pn"]
                ACT(rs[:, 0:n], pn[:, 0:n], AF.Sqrt, [pnr, cst_r], [rs_r], bias=eps_t[:, 0:1], scale=1.0 / 256.0)
                RECIP(rs[:, 0:n], rs[:, 0:n], [rs_r], [rs_r])

            def T3():
                for cc in range(2):
                    TT("dve", og[cc][:, 0:n], po[cc][:, 0:n], rs[:, 0:n], ALU.mult, [po_r[cc], rs_r], [og_r[cc]])
                    STT("dve", og2[cc][:, 0:n], og[cc][:, 0:n], ccol("gla_g", i * 2 + cc), sr[cc][:, 0:n], ALU.mult, ALU.mult,
                        [og_r[cc], sr_r[cc], cst_r], [og2_r[cc]])

            def mk_T4(m0):
                def T4():
                    for m in range(m0, m0 + 4):
                        pm, pmr = P.bank()
                        for cc in range(2):
                            MM(pm[:, 0:n], Wo(cc, m), og2[cc][:, 0:n], cc == 0, cc == 1, [w3r, og2_r[cc]], [pmr])
                        TT("dve", x[:, m, t0:t1], pm[:, 0:n], x[:, m, t0:t1], ALU.add, [pmr, x_r[m][t]], [x_r[m][t]])
                return T4
            def T12():
                T1()
                T2()
            return [T12, T3, mk_T4(0), mk_T4(4)]

        for f in stage1(0):
            f()
        prev_tail = []
        for t in range(NT):
            fill = stage1(t + 1) if t + 1 < NT else []
            steps = chain_steps(t)
            merged = []
            a, b = list(fill), list(prev_tail)
            while a or b:
                if a:
                    merged.append(a.pop(0))
                if b:
                    merged.append(b.pop(0))
                if a:
                    merged.append(a.pop(0))
            nst = len(steps)
            fi = 0
            for si, st in enumerate(steps):
                st()
                want = (len(merged) * (si + 1) + nst - 1) // nst
                while fi < min(want, len(merged)):
                    merged[fi]()
                    fi += 1
            while fi < len(merged):
                merged[fi]()
                fi += 1
            prev_tail = tail_steps(t)
        for f in prev_tail:
            f()
        for j in (j1, j2, j3):
            P.wrelease(j)

    def gla_mixer(layer):
        i = layer // 2
        areset()
        norm_alloc()
        for t in range(NT):
            norm_tile(t, 4 + layer)
        P.rot = [4, 5, 6, 7]
        for hd in range(4):
            areset_passes(hd == 0)
            gla_pass(i, hd, st_sem, out_sem)
        areset()
        P.rot = list(range(8))

    KRET = int(os.environ.get("KRET", "9"))

    def ret_pass(i, pr, st_sem, out_sem):
        w1, w1r, j1 = P.wget()
        w2s, w2r, j2 = P.wget()
        w3, w3r, j3 = P.wget()
        w4, w4r, j4 = P.wget()
        Wq = lambda kc: w1[:, kc * 128:(kc + 1) * 128]
        Wqs = lambda kc: w1[:, 1024 + kc * 128:1024 + (kc + 1) * 128]
        Wk = lambda kc: w1[:, 2048 + kc * 128:2048 + (kc + 1) * 128]
        Wks = lambda kc: w2s[:, kc * 128:(kc + 1) * 128]
        Wv = lambda kc: w2s[:, 1024 + kc * 256:1024 + (kc + 1) * 256]
        Wg = lambda cc, kc: w3[:, cc * 1024 + kc * 128:cc * 1024 + (kc + 1) * 128]
        Wo = lambda cc, m: w4[:, cc * 1024 + m * 128:cc * 1024 + (m + 1) * 128]
        rot, rot_r = aalloc(2048, F32, "rot")
        tA, tA_r = aalloc(512, F32, "tA")
        tB, tB_r = aalloc(512, F32, "tB")
        kt, kt_r = aalloc(256, BF16, "kt")
        SETS = []
        for sidx in range(2):
            d = {}
            for hh in range(2):
                d["qth%d" % hh], d["qth%d_r" % hh] = aalloc(256, BF16, "qth")
            for b in range(4):
                d["vtok%d" % b], d["vtok%d_r" % b] = aalloc(128, BF16, "vtok")
                d["ktok%d" % b], d["ktok%d_r" % b] = aalloc(64, BF16, "ktok")
                d["Am%d" % b], d["Am%d_r" % b] = aalloc(128, BF16, "Am")
            SETS.append(d)
        SG = [[aalloc(256, BF16, "sgt%d_%d" % (q, cc)) for cc in range(2)] for q in range(3)]
        S32, S32_r = aalloc(128, F32, "S32")
        Sbf, Sbf_r = aalloc(64, BF16, "Sbf")
        obf, obf_r, o32, o32_r, csq, csq_r, og2, og2_r = [], [], [], [], [], [], [], []
        for cc in range(2):
            a, r = aalloc(256, BF16, "obf%d" % cc)
            obf.append(a)
            obf_r.append(r)
            a, r = aalloc(512, F32, "o32%d" % cc)
            o32.append(a)
            o32_r.append(r)
            a, r = aalloc(256, BF16, "csq%d" % cc)
            csq.append(a)
            csq_r.append(r)
            a, r = aalloc(256, BF16, "og2%d" % cc)
            og2.append(a)
            og2_r.append(r)
        rsh, rsh_r = [], []
        for hh in range(2):
            a, r = aalloc(512, F32, "rs%d" % hh)
            rsh.append(a)
            rsh_r.append(r)
        S0, S0_r = aalloc(1024, F32, "S0")
        S0b, S0b_r = aalloc(512, BF16, "S0b")
        kmask, kmask_r = aalloc(512, BF16, "kmask")
        MEMSET("dve", S32, 0.0, [S32_r])
        MEMSET("dve", Sbf, 0.0, [Sbf_r])
        PO = [[P.banks[0], P.banks[1]], [P.banks[2], P.banks[3]]]
        PO_r = [[P.bank_r[0], P.bank_r[1]], [P.bank_r[2], P.bank_r[3]]]
        gam = lambda kind, v: jv[:, 256 + (pr * 3 + kind) * 3 + v:256 + (pr * 3 + kind) * 3 + v + 1]

        def stage1(t):
            d = SETS[t % 2]
            t0, t1 = TILES[t]
            n = t1 - t0
            blocks = tile_blocks(t)
            sl = []

            def s_q():
                DMA("sp", rot.rearrange("p (a c) -> p a c", c=512), rot_d[pr, t], [], [rot_r], st_sem)
                cq, sq_ = rot[:, 0:n], rot[:, 512:512 + n]
                pa, par = P.bank()
                for kc in range(NCH):
                    MM(pa[:, 0:n], Wq(kc), hb[:, kc, t0:t1], kc == 0, kc == NCH - 1, [w1r, h_r[kc][t]], [par])
                pb, pbr = P.bank()
                for kc in range(NCH):
                    MM(pb[:, 0:n], Wqs(kc), hb[:, kc, t0:t1], kc == 0, kc == NCH - 1, [w1r, h_r[kc][t]], [pbr])
                for hh in range(2):
                    STT("dve", tA[:, 0:n], pa[:, 0:n], ccol("hm", hh), cq, ALU.mult, ALU.mult, [par, rot_r, cst_r], [tA_r])
                    STT("dve", tB[:, 0:n], pb[:, 0:n], ccol("hm", hh), sq_, ALU.mult, ALU.mult, [pbr, rot_r, cst_r], [tB_r])
                    TT("pool", d["qth%d" % hh][:, 0:n], tA[:, 0:n], tB[:, 0:n], ALU.add, [tA_r, tB_r], [d["qth%d_r" % hh]])
            sl.append(s_q)

            def s_k():
                ck, sk = rot[:, 1024:1024 + n], rot[:, 1536:1536 + n]
                pa, par = P.bank()
                for kc in range(NCH):
                    MM(pa[:, 0:n], Wk(kc), hb[:, kc, t0:t1], kc == 0, kc == NCH - 1, [w1r, h_r[kc][t]], [par])
                pb, pbr = P.bank()
                for kc in range(NCH):
                    MM(pb[:, 0:n], Wks(kc), hb[:, kc, t0:t1], kc == 0, kc == NCH - 1, [w2r, h_r[kc][t]], [pbr])
                TT("dve", tA[:, 0:n], pa[:, 0:n], ck, ALU.mult, [par, rot_r], [tA_r])
                TT("dve", tB[:, 0:n], pb[:, 0:n], sk, ALU.mult, [pbr, rot_r], [tB_r])
                TT("pool", kt[:, 0:n], tA[:, 0:n], tB[:, 0:n], ALU.add, [tA_r, tB_r], [kt_r])
            sl.append(s_k)

            def mk_gate(cc):
                def s_gate():
                    pg, pgr = P.bank()
                    for kc in range(NCH):
                        MM(pg[:, 0:n], Wg(cc, kc), hb[:, kc, t0:t1], kc == 0, kc == NCH - 1, [w3r, h_r[kc][t]], [pgr])
                    ACT(SG[t % 3][cc][0][:, 0:n], pg[:, 0:n], AF.Silu, [pgr], [SG[t % 3][cc][1]])
                return s_gate
            sl.append(mk_gate(0))
            sl.append(mk_gate(1))

            def mk_tok(b, c0, nb, kind):
                def s_tok():
                    pv, pvr = P.bank()
                    for kc in range(NCH):
                        MM(pv[0:nb, 0:256], hb[:, kc, t0 + c0:t0 + c0 + nb], Wv(kc), kc == 0, kc == NCH - 1,
                           [w2r, h_r[kc][t]], [pvr])
                    COPY("act", d["vtok%d" % b][0:nb, 0:256], pv[0:nb, 0:256], [pvr], [d["vtok%d_r" % b]])
                    pt, ptr = P.bank()
                    ptb = pt[:, 0:64].bitcast(BF16)
                    TRANSP(ptb[0:nb, 0:128], kt[:, c0:c0 + nb], [kt_r, cst_r], [ptr])
                    COPY("act", d["ktok%d" % b][0:nb, 0:128], ptb[0:nb, 0:128], [ptr], [d["ktok%d_r" % b]])
                    ps_, psr = P.bank()
                    for hh in range(2):
                        MM(ps_[0:nb, hh * 128:hh * 128 + nb], kt[:, c0:c0 + nb], d["qth%d" % hh][:, c0:c0 + nb], True, True,
                           [kt_r, d["qth%d_r" % hh]], [psr], signal=True)
                    mk = causal_bf if kind == "p" else samp_bf
                    for hh in range(2):
                        TT("dve", d["Am%d" % b][0:nb, hh * 128:hh * 128 + nb], ps_[0:nb, hh * 128:hh * 128 + nb],
                           mk[0:nb, 0:nb], ALU.mult, [psr, cst_r], [d["Am%d_r" % b]])
                return s_tok
            for b, (c0, nb, kind) in enumerate(blocks):
                sl.append(mk_tok(b, c0, nb, kind))
            return sl

        def chain_steps(t):
            d = SETS[t % 2]
            po, po_r = PO[t % 2], PO_r[t % 2]
            blocks = tile_blocks(t)
            steps = []

            def mk_p(b, c0, nb):
                def step():
                    vt_, vr_ = d["vtok%d" % b], d["vtok%d_r" % b]
                    kt_, kr_ = d["ktok%d" % b], d["ktok%d_r" % b]
                    am_, ar_ = d["Am%d" % b], d["Am%d_r" % b]
                    pU, pUr = P.bank()
                    for hh in range(2):
                        MM(pU[:, hh * 128:(hh + 1) * 128], kt_[0:nb, 0:128], vt_[0:nb, hh * 128:(hh + 1) * 128], True, True,
                           [kr_, vr_], [pUr], signal=True)
                    for hh in range(2):
                        MM(po[hh][:, c0:c0 + nb], vt_[0:nb, hh * 128:(hh + 1) * 128], am_[0:nb, hh * 128:hh * 128 + nb],
                           True, False, [vr_, ar_], [po_r[hh]], signal=False)
                        MM(po[hh][:, c0:c0 + nb], Sbf[:, 0:128], d["qth%d" % hh][:, c0:c0 + nb],
                           False, True, [Sbf_r, d["qth%d_r" % hh]], [po_r[hh]], signal=True)
                    kd = 0 if nb == 128 else 1
                    TS("dve", S32, S32, gam(kd, 0), ALU.mult, [S32_r, cst_r], [S32_r])
                    for hh in range(2):
                        STT("dve", S32, pU[:, hh * 128:(hh + 1) * 128], gam(kd, 1 + hh), S32,
                            ALU.mult, ALU.add, [pUr, S32_r, cst_r], [S32_r])
                    COPY("act", Sbf, S32, [S32_r], [Sbf_r])
                    if t == NT - 1:
                        DMA("sp", o_pret[i, pr], S32, [S32_r], [], out_sem)
                return step

            def mk_s(b, c0, nb, half):
                def step():
                    vt_, vr_ = d["vtok%d" % b], d["vtok%d_r" % b]
                    kt_, kr_ = d["ktok%d" % b], d["ktok%d_r" % b]
                    am_, ar_ = d["Am%d" % b], d["Am%d_r" % b]
                    DMA("sp", S0.rearrange("p (s v) -> p s v", v=128), st_ret[i, pr, :, half * 8:(half + 1) * 8, :],
                        [], [S0_r], st_sem)
                    COPY("act", S0b, S0, [S0_r], [S0b_r])
                    TT("dve", kmask.rearrange("p (s k) -> p s k", k=128),
                       kt_[:, 0:128].unsqueeze(1).to_broadcast([128, 8, 128]),
                       seqm_bf[:, half * 8:(half + 1) * 8].unsqueeze(2).to_broadcast([128, 8, 128]),
                       ALU.mult, [kr_, cst_r], [kmask_r])
                    for s_ in range(8):
                        sq_i = half * 8 + s_
                        cs = c0 + sq_i * 8
                        for hh in range(2):
                            MM(po[hh][:, cs:cs + 8], vt_[0:nb, hh * 128:(hh + 1) * 128],
                               am_[0:nb, hh * 128 + sq_i * 8:hh * 128 + sq_i * 8 + 8],
                               True, False, [vr_, ar_], [po_r[hh]], signal=False)
                            MM(po[hh][:, cs:cs + 8], S0b[:, s_ * 128:(s_ + 1) * 128], d["qth%d" % hh][:, cs:cs + 8],
                               False, True, [S0b_r, d["qth%d_r" % hh]], [po_r[hh]], signal=True)
                    for s_ in range(8):
                        pU, pUr = P.bank()
                        for hh in range(2):
                            MM(pU[:, hh * 128:(hh + 1) * 128], kmask[:, s_ * 128:(s_ + 1) * 128],
                               vt_[0:nb, hh * 128:(hh + 1) * 128], True, True, [kmask_r, vr_], [pUr], signal=True)
                        S0s = S0[:, s_ * 128:(s_ + 1) * 128]
                        TS("dve", S0s, S0s, gam(2, 0), ALU.mult, [S0_r, cst_r], [S0_r])
                        for hh in range(2):
                            STT("dve", S0s, pU[:, hh * 128:(hh + 1) * 128], gam(2, 1 + hh), S0s,
                                ALU.mult, ALU.add, [pUr, S0_r, cst_r], [S0_r])
                    DMA("sp", o_sret[i, pr, :, half * 8:(half + 1) * 8, :], S0.rearrange("p (s v) -> p s v", v=128),
                        [S0_r], [], out_sem)
                return step

            for b, (c0, nb, kind) in enumerate(blocks):
                if kind == "p":
                    steps.append(mk_p(b, c0, nb))
                else:
                    steps.append(mk_s(b, c0, nb, 0))
                    steps.append(mk_s(b, c0, nb, 1))
            return steps

        def tail_steps(t):
            t0, t1 = TILES[t]
            n = t1 - t0
            po, po_r = PO[t % 2], PO_r[t % 2]
            sgt = [SG[t % 3][cc][0] for cc in range(2)]
            sgt_r = [SG[t % 3][cc][1] for cc in range(2)]
            st = {}
            H2 = range(2)

            def T1():
                for hh in H2:
                    COPY("act", obf[hh][:, 0:n], po[hh][:, 0:n], [po_r[hh]], [obf_r[hh]])
                    COPY("act", o32[hh][:, 0:n], po[hh][:, 0:n], [po_r[hh]], [o32_r[hh]])
                st["pmn"] = []
                for hh in H2:
                    pm_, pmr_ = P.bank()
                    MM(pm_[:, 0:n], ones_bf[:], obf[hh][:, 0:n], True, True, [obf_r[hh], cst_r], [pmr_])
                    st["pmn"].append((pm_, pmr_))

            def T2():
                for hh in H2:
                    pm_, pmr_ = st["pmn"][hh]
                    STT("dve", o32[hh][:, 0:n], pm_[:, 0:n], -1.0 / 128.0, o32[hh][:, 0:n], ALU.mult, ALU.add,
                        [pmr_, o32_r[hh]], [o32_r[hh]])
                for hh in H2:
                    ACT(csq[hh][:, 0:n], o32[hh][:, 0:n], AF.Square, [o32_r[hh]], [csq_r[hh]])
                st["pvn"] = []
                for hh in H2:
                    pv_, pvr_ = P.bank()
                    MM(pv_[:, 0:n], ones_bf[:], csq[hh][:, 0:n], True, True, [csq_r[hh], cst_r], [pvr_])
                    st["pvn"].append((pv_, pvr_))

            def T3():
                for hh in H2:
                    pv_, pvr_ = st["pvn"][hh]
                    ACT(rsh[hh][:, 0:n], pv_[:, 0:n], AF.Sqrt, [pvr_, cst_r], [rsh_r[hh]], bias=eps_t[:, 0:1],
                        scale=1.0 / 128.0)
                for hh in H2:
                    RECIP(rsh[hh][:, 0:n], rsh[hh][:, 0:n], [rsh_r[hh]], [rsh_r[hh]])

            def T4():
                for hh in H2:
                    TT("dve", o32[hh][:, 0:n], o32[hh][:, 0:n], rsh[hh][:, 0:n], ALU.mult, [o32_r[hh], rsh_r[hh]], [o32_r[hh]])
                for hh in H2:
                    TT("pool", og2[hh][:, 0:n], o32[hh][:, 0:n], sgt[hh][:, 0:n], ALU.mult, [o32_r[hh], sgt_r[hh]],
                       [og2_r[hh]])

            def mk_T5(m0):
                def T5():
                    for m in range(m0, m0 + 4):
                        pm, pmr = P.bank()
                        for cc in range(2):
                            MM(pm[:, 0:n], Wo(cc, m), og2[cc][:, 0:n], cc == 0, cc == 1, [w4r, og2_r[cc]], [pmr])
                        TT("dve", x[:, m, t0:t1], pm[:, 0:n], x[:, m, t0:t1], ALU.add, [pmr, x_r[m][t]], [x_r[m][t]])
                return T5
            def Tn():
                T1()
                T2()
                T3()
            return [Tn, T4, mk_T5(0), mk_T5(4)]

        for f in stage1(0):
            f()
        prev_tail = []
        for t in range(NT):
            fill = stage1(t + 1) if t + 1 < NT else []
            steps = chain_steps(t)
            merged = []
            a, b = list(fill), list(prev_tail)
            while a or b:
                if a:
                    merged.append(a.pop(0))
                if b:
                    merged.append(b.pop(0))
                if a:
                    merged.append(a.pop(0))
            nst = len(steps)
            fi = 0
            for si, st_ in enumerate(steps):
                st_()
                want = (len(merged) * (si + 1) + nst - 1) // nst
                while fi < min(want, len(merged)):
                    merged[fi]()
                    fi += 1
            while fi < len(merged):
                merged[fi]()
                fi += 1
            prev_tail = tail_steps(t)
        for f in prev_tail:
            f()
        for j in (j1, j2, j3, j4):
            P.wrelease(j)

    I32 = mybir.dt.int32
    TWO_PI = 2.0 * np.pi
    S5T = [(k * 256, 256, "p") for k in range(8)] + [(2048, 16, "p"), (2064, 128, "s")]

    def frac_sincos(v, W, tmp_f, tmp_i, sin_out, cos_out, v_r, t_r, so_r, co_r):
        COPY("dve", tmp_i[:, 0:W], v[:, 0:W], [v_r], [t_r])
        COPY("dve", tmp_f[:, 0:W], tmp_i[:, 0:W], [t_r], [t_r])
        TT("dve", v[:, 0:W], v[:, 0:W], tmp_f[:, 0:W], ALU.subtract, [v_r, t_r], [v_r])
        ACT(sin_out[:, 0:W], v[:, 0:W], AF.Sin, [v_r], [so_r], scale=TWO_PI)
        TS("dve", tmp_f[:, 0:W], v[:, 0:W], 0.25, ALU.add, [v_r, t_r], [t_r])
        COPY("dve", tmp_i[:, 0:W], tmp_f[:, 0:W], [t_r], [t_r])
        COPY("dve", cos_out[:, 0:W], tmp_i[:, 0:W], [t_r], [co_r])
        TT("dve", tmp_f[:, 0:W], tmp_f[:, 0:W], cos_out[:, 0:W], ALU.subtract, [t_r, co_r], [t_r])
        ACT(cos_out[:, 0:W], tmp_f[:, 0:W], AF.Sin, [t_r], [co_r], scale=TWO_PI)

    def s5_params(ar, ai, ldt, W, in_r, want_f, tag):
        bufs = {}

        def nb(name, dt=F32):
            a, r = aalloc(W, F32, tag + name)
            if dt is I32:
                a = a.bitcast(I32)
            bufs[name] = (a, r)
            return a, r

        dt_, dt_r = nb("dt")
        mag, mag_r = nb("mag")
        rfr, rfr_r = nb("rfr")
        tf, tf_r = nb("tf")
        ti, ti_r = nb("ti", I32)
        sn, sn_r = nb("sn")
        cs, cs_r = nb("cs")
        ACT(dt_, ldt, AF.Exp, [in_r], [dt_r])
        TT("dve", mag, dt_, ar, ALU.mult, [dt_r, in_r], [mag_r])
        ACT(mag, mag, AF.Exp, [mag_r], [mag_r])
        TT("dve", rfr, dt_, ai, ALU.mult, [dt_r, in_r], [rfr_r])
        TS("dve", rfr, rfr, 1.0 / TWO_PI, ALU.mult, [rfr_r], [rfr_r])
        frac_sincos(rfr, W, tf, ti, sn, cs, rfr_r, tf_r, sn_r, cs_r)
        TT("dve", cs, cs, mag, ALU.mult, [cs_r, mag_r], [cs_r])
        TT("dve", sn, sn, mag, ALU.mult, [sn_r, mag_r], [sn_r])
        out = {"mag": (mag, mag_r), "abr": (cs, cs_r), "abi": (sn, sn_r), "rfr": (rfr, rfr_r)}
        if want_f:
            fre, fre_r = nb("fre")
            fim, fim_r = nb("fim")
            den, den_r = dt_, dt_r
            TT("dve", den, ar, ar, ALU.mult, [in_r, dt_r], [den_r])
            TT("dve", tf, ai, ai, ALU.mult, [in_r, tf_r], [tf_r])
            TT("dve", den, den, tf, ALU.add, [den_r, tf_r], [den_r])
            RECIP(den, den, [den_r], [den_r])
            nre, nre_r = tf, tf_r
            TS("dve", nre, cs, -1.0, ALU.add, [cs_r, tf_r], [nre_r])
            TT("dve", fre, nre, ar, ALU.mult, [nre_r, in_r], [fre_r])
            TT("dve", fim, sn, ai, ALU.mult, [sn_r, in_r], [fim_r])
            TT("dve", fre, fre, fim, ALU.add, [fre_r, fim_r], [fre_r])
            TT("dve", fre, fre, den, ALU.mult, [fre_r, den_r], [fre_r])
            TT("dve", fim, sn, ar, ALU.mult, [sn_r, in_r, fre_r], [fim_r])
            TT("dve", nre, nre, ai, ALU.mult, [nre_r, in_r], [nre_r])
            TT("dve", fim, fim, nre, ALU.subtract, [fim_r, nre_r], [fim_r])
            TT("dve", fim, fim, den, ALU.mult, [fim_r, den_r], [fim_r])
            out["fre"] = (fre, fre_r)
            out["fim"] = (fim, fim_r)
        return out

    KS5 = int(os.environ.get("KS5", "9"))
    PENG = os.environ.get("KPENG", "pool")
    PENG2 = os.environ.get("KPENG2", "pool")

    def s5_pass(i, yc, zbuf, z_r, mark, st_sem, out_sem):
        wu, wur, ju = P.wget()
        id32_ = ccol("ident32", 0, 128)
        Bbr, Bbr_r = aalloc(256, BF16, "Bbr")
        Bbi, Bbi_r = aalloc(256, BF16, "Bbi")
        Cre, Cre_r = aalloc(256, BF16, "Cre")
        nCre, nCre_r = aalloc(256, BF16, "nCre")
        nCim, nCim_r = aalloc(256, BF16, "nCim")
        smo = 256 + 18 + (i * 3) * 16 + yc * 4
        sm = s5_params(jv[:, smo:smo + 4], jv[:, smo + 16:smo + 20], jv[:, smo + 32:smo + 36], 4, cst_r, True, "sm")
        mag, mag_r = sm["mag"]
        abr, abr_r = sm["abr"]
        abi, abi_r = sm["abi"]
        rfr, rfr_r = sm["rfr"]
        fre, fre_r = sm["fre"]
        fim, fim_r = sm["fim"]
        mark_b = ar["off"]
        stg, stg_r = aalloc(4 * 512, F32, "s5stg")
        DMA("sp", stg, s5w[i, yc][:, 0:4 * 512], [], [stg_r], st_sem)
        Bre_p, Bim_p, Cre_p, Cim_p = (stg[:, q * 512:(q + 1) * 512] for q in range(4))
        dg = [aalloc(128, F32, "dg%d" % q) for q in range(2)]
        pf, pf_r = [], []
        for q, (fq, fq_r) in enumerate(((fre, fre_r), (fim, fim_r))):
            pb, pr = P.bank()
            for sc in range(4):
                dgt, dgt_r = dg[sc % 2]
                TS("dve", dgt, id32_, fq[:, sc:sc + 1], ALU.mult, [cst_r, fq_r], [dgt_r])
                MM(pb[:, sc * 128:(sc + 1) * 128], ones32[:], dgt, True, True, [dgt_r, cst_r], [pr], signal=True)
            pf.append(pb)
            pf_r.append(pr)
        m1, m1_r = aalloc(512, F32, "m1")
        m2, m2_r = aalloc(512, F32, "m2")
        TT("dve", m1, pf[0][:, 0:512], Bre_p, ALU.mult, [pf_r[0], stg_r], [m1_r])
        TT("dve", m2, pf[1][:, 0:512], Bim_p, ALU.mult, [pf_r[1], stg_r], [m2_r])
        TT("dve", Bbr, m1, m2, ALU.subtract, [m1_r, m2_r], [Bbr_r])
        TT("dve", m1, pf[0][:, 0:512], Bim_p, ALU.mult, [pf_r[0], stg_r, m1_r], [m1_r])
        TT("dve", m2, pf[1][:, 0:512], Bre_p, ALU.mult, [pf_r[1], stg_r, m2_r], [m2_r])
        TT("dve", Bbi, m1, m2, ALU.add, [m1_r, m2_r], [Bbi_r])
        COPY("act", Cre, Cre_p, [stg_r], [Cre_r])
        TS("dve", nCre, Cre_p, -1.0, ALU.mult, [stg_r], [nCre_r])
        TS("dve", nCim, Cim_p, -1.0, ALU.mult, [stg_r], [nCim_r])
        S.barrier(io_sems)
        ar["off"] = mark_b
        cosT, cosT_r = aalloc(1024, F32, "cosT")
        sinT, sinT_r = aalloc(1024, F32, "sinT")
        ttf, ttf_r = aalloc(1024, F32, "ttf")
        tti, tti_r = aalloc(1024, F32, "tti")
        tti = tti.bitcast(I32)
        vt, vt_r = ttf, ttf_r
        vt, vt_r = aalloc(1024, F32, "vt")
        for sc in range(4):
            TS("dve", vt[:, sc * 256:(sc + 1) * 256], jvec, rfr[:, sc:sc + 1], ALU.mult, [cst_r, rfr_r], [vt_r])
        frac_sincos(vt, 1024, ttf, tti, sinT, cosT, vt_r, ttf_r, sinT_r, cosT_r)
        tt4 = tti.bitcast(BF16)
        x2, x2_r = aalloc(1024, F32, "x2")
        g2, g2_r = aalloc(512, F32, "g2")
        t2, t2_r = aalloc(512, BF16, "t2")
        AB, AB_r, GG, GG_r, TQ, TQ_r = [], [], [], [], [], []
        for st, (xb, gb, tb) in enumerate(((vt, ttf[:, 0:512], tt4[:, 0:1024]), (x2, g2, t2))):
            AB.append([xb[:, q * 256:(q + 1) * 256] for q in range(4)])
            AB_r.append([Region("ab%d_%d" % (st, q)) for q in range(4)])
            GG.append([gb[:, q * 256:(q + 1) * 256] for q in range(2)])
            GG_r.append([Region("g%d_%d" % (st, q)) for q in range(2)])
            TQ.append([tb[:, q * 256:(q + 1) * 256] for q in range(4)])
            TQ_r.append([Region("tq%d_%d" % (st, q)) for q in range(4)])
        decs, decs_r = ttf[:, 512:640], Region("decs")
        u2, u2_r = aalloc(256, F32, "u2")
        U32 = [ttf[:, 768:1024], u2]
        U32_r = [Region("u32a"), Region("u32b")]
        UBF = [tt4[:, 1024:1280], tt4[:, 1280:1536]]
        UBF_r = [Region("ubfa"), Region("ubfb")]
        ysb, ysb_r = aalloc(256, F32, "ysb")
        yt2, yt2_r = aalloc(256, F32, "yt2")
        small, small_r = aalloc(256, F32, "small")
        gL = small[:, 0:8]
        tmp4 = small[:, 8:40]
        car = small[:, 40:48]
        hL = small[:, 48:56]
        car_r = Region("car")
        gL_r = Region("gL")
        h0, h0_r = aalloc(128, F32, "h0")
        csm, csm_r = aalloc(128, F32, "csm")
        hs, hs_r = aalloc(128, F32, "hs")
        tm64, tm64_r = aalloc(256, F32, "tm64")
        S.barrier(io_sems)
        py, pyr = P.banks[0], P.bank_r[0]
        dcol = ccol("s5_d", i * 4 + yc)
        v3 = lambda a: a.rearrange("p (s j) -> p s j", j=8)

        def crot(dst_re, dst_im, g_re, g_im, cs_, sn_, W3, rd, wr):
            tA = tmp4[:, 0:W3]
            tB = tmp4[:, 8:8 + W3]
            TT("dve", tA, g_re, cs_, ALU.mult, rd, [small_r])
            TT("dve", tB, g_im, sn_, ALU.mult, rd, [small_r])
            TT("dve", dst_re, tA, tB, ALU.subtract, [small_r], wr)
            TT("dve", tA, g_re, sn_, ALU.mult, rd + [small_r], [small_r])
            TT("dve", tB, g_im, cs_, ALU.mult, rd + [small_r], [small_r])
            TT("dve", dst_im, tA, tB, ALU.add, [small_r], wr)

        w255 = small[:, 56:64]
        w255_r = Region("w255")
        crot(w255[:, 0:4], w255[:, 4:8], cosT.rearrange("p (c j) -> p c j", j=256)[:, :, 255],
             sinT.rearrange("p (c j) -> p c j", j=256)[:, :, 255], cosT.rearrange("p (c j) -> p c j", j=256)[:, :, 1],
             sinT.rearrange("p (c j) -> p c j", j=256)[:, :, 1], 4,
             [cosT_r, sinT_r, small_r], [w255_r])

        def tabs(sc, n, kind):
            if kind == "s":
                return (cosT[:, sc * 256:sc * 256 + 8].unsqueeze(1).to_broadcast([128, 16, 8]),
                        sinT[:, sc * 256:sc * 256 + 8].unsqueeze(1).to_broadcast([128, 16, 8]))
            return cosT[:, sc * 256:sc * 256 + n], sinT[:, sc * 256:sc * 256 + n]

        def vw(a, n, kind):
            return v3(a[:, 0:n]) if kind == "s" else a[:, 0:n]

        def PRE(k):
            c0, n, kind = S5T[k]
            tix = min(c0 // 512, NT - 1)
            u32, u32_r, ubf, ubf_r = U32[k % 2], U32_r[k % 2], UBF[k % 2], UBF_r[k % 2]
            pu, pur = P.bank()
            for kc in range(NCH):
                MM(pu[:, 0:n], wu[:, kc * 128:(kc + 1) * 128], hb[:, kc, c0:c0 + n], kc == 0, kc == NCH - 1,
                   [wur, h_r[kc][tix]], [pur])
            COPY("act", u32[:, 0:n], pu[:, 0:n], [pur], [u32_r])
            COPY("act", ubf[:, 0:n], u32[:, 0:n], [u32_r], [ubf_r])
            if kind == "s":
                DMA("sp", h0.rearrange("p (q c s) -> p q c s", q=2, c=4), st_s5[i, :, :, yc * 4:(yc + 1) * 4, :].rearrange(
                    "q p c s -> p q c s"), [], [h0_r], st_sem)
                h0v = h0.rearrange("p (q c s) -> p q c s", q=2, c=4)
                csv = csm.rearrange("p (q c s) -> p q c s", q=2, c=4)
                tmv = tm64.rearrange("p (q c s) -> p q c s", q=4, c=4)
                abr_b = abr.unsqueeze(2).to_broadcast([128, 4, 16])
                abi_b = abi.unsqueeze(2).to_broadcast([128, 4, 16])
                TT("dve", tmv[:, 0], h0v[:, 0], abr_b, ALU.mult, [h0_r, abr_r], [tm64_r])
                TT("dve", tmv[:, 1], h0v[:, 1], abi_b, ALU.mult, [h0_r, abi_r], [tm64_r])
                TT("dve", csv[:, 0], tmv[:, 0], tmv[:, 1], ALU.subtract, [tm64_r], [csm_r])
                TT("dve", tmv[:, 2], h0v[:, 1], abr_b, ALU.mult, [h0_r, abr_r], [tm64_r])
                TT("dve", tmv[:, 3], h0v[:, 0], abi_b, ALU.mult, [h0_r, abi_r], [tm64_r])
                TT("dve", csv[:, 1], tmv[:, 2], tmv[:, 3], ALU.add, [tm64_r], [csm_r])

        pxs = {}
        pzs = {}
        id32 = ccol("ident32", 0, 128)
        nid32 = ccol("nident32", 0, 128)

        def Fx(k, sc):
            c0, n, kind = S5T[k]
            ubf, ubf_r = UBF[k % 2], UBF_r[k % 2]
            px, pxr = P.bank()
            MM(px[:, 0:n], Bbr[:, sc * 128:(sc + 1) * 128], ubf[:, 0:n], True, True, [Bbr_r, ubf_r], [pxr], signal=True)
            MM(px[:, 256:256 + n], Bbi[:, sc * 128:(sc + 1) * 128], ubf[:, 0:n], True, True, [Bbi_r, ubf_r], [pxr],
               signal=True)
            pxs[(k, sc)] = (px, pxr)

        def Fr(k, sc):
            c0, n, kind = S5T[k]
            st = sc % 2
            a1, a2, b1, b2 = AB[st]
            a1_r, a2_r, b1_r, b2_r = AB_r[st]
            cs_, sn_ = tabs(sc, n, kind)
            px, pxr = pxs.pop((k, sc))
            xr, xi = vw(px, n, kind), vw(px[:, 256:512], n, kind)
            TT("dve", vw(a1, n, kind), xr, cs_, ALU.mult, [pxr, cosT_r], [a1_r])
            TT("dve", vw(a2, n, kind), xi, sn_, ALU.mult, [pxr, sinT_r], [a2_r])
            TT("dve", vw(b1, n, kind), xi, cs_, ALU.mult, [pxr, cosT_r], [b1_r])
            TT("dve", vw(b2, n, kind), xr, sn_, ALU.mult, [pxr, sinT_r], [b2_r])
            pz, pzr = P.bank()
            MM(pz[:, 0:n], id32, a1[:, 0:n], True, False, [a1_r, cst_r], [pzr], signal=True)
            MM(pz[:, 0:n], id32, a2[:, 0:n], False, True, [a2_r, cst_r], [pzr], signal=True)
            MM(pz[:, 256:256 + n], id32, b1[:, 0:n], True, False, [b1_r, cst_r], [pzr], signal=True)
            MM(pz[:, 256:256 + n], nid32, b2[:, 0:n], False, True, [b2_r, cst_r], [pzr], signal=True)
            pzs[(k, sc)] = (pz, pzr)

        def Gs(k, sc, first):
            c0, n, kind = S5T[k]
            st = sc % 2
            gre, gim = GG[st]
            gre_r, gim_r = GG_r[st]
            tq, tq_r = TQ[st], TQ_r[st]
            cs_, sn_ = tabs(sc, n, kind)
            pz, pzr = pzs.pop((k, sc))
            xre, xim = pz[:, 0:n], pz[:, 256:256 + n]
            if kind == "s":
                csv = csm.rearrange("p (q c s) -> p q c s", q=2, c=4)
                TT("dve", pz[:, 0:n:8], pz[:, 0:n:8], csv[:, 0, sc], ALU.add, [pzr, csm_r], [pzr])
                TT("dve", pz[:, 256:256 + n:8], pz[:, 256:256 + n:8], csv[:, 1, sc], ALU.add, [pzr, csm_r], [pzr])
                TS("dve", decs[:, 0:n], ccol("cmask4", 16, 128), mag[:, sc:sc + 1], ALU.mult, [cst_r, mag_r], [decs_r])
                dec = decs[:, 0:n]
                dec_rd = [decs_r]
            else:
                dec = mag[:, sc:sc + 1].to_broadcast([128, n])
                dec_rd = [mag_r]
            if kind == "p" and not first:
                SCAN(gre[:, 0:n], dec, xre, [pzr, car_r] + dec_rd, [gre_r], initial=car[:, sc:sc + 1])
                SCAN(gim[:, 0:n], dec, xim, [pzr, car_r] + dec_rd, [gim_r], initial=car[:, 4 + sc:5 + sc])
            else:
                SCAN(gre[:, 0:n], dec, xre, [pzr] + dec_rd, [gre_r])
                SCAN(gim[:, 0:n], dec, xim, [pzr] + dec_rd, [gim_r])
            g_re3, g_im3 = vw(gre, n, kind), vw(gim, n, kind)
            TT(PENG2, vw(tq[0], n, kind), g_re3, cs_, ALU.mult, [gre_r, cosT_r], [tq_r[0]])
            TT(PENG2, vw(tq[1], n, kind), g_im3, sn_, ALU.mult, [gim_r, sinT_r], [tq_r[1]])
            TT(PENG, vw(tq[2], n, kind), g_re3, sn_, ALU.mult, [gre_r, sinT_r], [tq_r[2]])
            TT(PENG, vw(tq[3], n, kind), g_im3, cs_, ALU.mult, [gim_r, cosT_r], [tq_r[3]])
            if kind == "p":
                COPY("act", gL[:, sc:sc + 1], gre[:, n - 1:n], [gre_r], [gL_r])
                COPY("act", gL[:, 4 + sc:5 + sc], gim[:, n - 1:n], [gim_r], [gL_r])
            else:
                tmv = tm64.rearrange("p (q c s) -> p q c s", q=4, c=4)
                COPY("act", tmv[:, 0, sc], gre[:, 7:n:8], [gre_r], [tm64_r])
                COPY("act", tmv[:, 1, sc], gim[:, 7:n:8], [gim_r], [tm64_r])

        def Gp(k, sc):
            c0, n, kind = S5T[k]
            tq, tq_r = TQ[sc % 2], TQ_r[sc % 2]
            for q, wmat, wreg in ((0, Cre, Cre_r), (1, nCre, nCre_r), (2, nCim, nCim_r), (3, nCim, nCim_r)):
                MM(py[:, 0:n], wmat[:, sc * 128:(sc + 1) * 128], tq[q][:, 0:n], sc == 0 and q == 0, sc == 3 and q == 3,
                   [wreg, tq_r[q]], [pyr], signal=True)

        def POSTC(k):
            c0, n, kind = S5T[k]
            if kind == "p" and n == 256:
                crot(car[:, 0:4], car[:, 4:8], gL[:, 0:4], gL[:, 4:8], w255[:, 0:4], w255[:, 4:8], 4,
                     [gL_r, small_r, w255_r], [car_r])
            elif kind == "p":
                L = n - 1
                csL = cosT.rearrange("p (c j) -> p c j", j=256)[:, :, L]
                snL = sinT.rearrange("p (c j) -> p c j", j=256)[:, :, L]
                crot(hL[:, 0:4], hL[:, 4:8], gL[:, 0:4], gL[:, 4:8], csL, snL, 4, [gL_r, small_r, cosT_r, sinT_r], [small_r])
                DMA("sp", o_ps5[i, :, :, yc * 4:(yc + 1) * 4].rearrange("q p c -> p q c"),
                    hL.rearrange("p (q c) -> p q c", q=2), [small_r], [], out_sem)
            else:
                hsv = hs.rearrange("p (q c s) -> p q c s", q=2, c=4)
                tmv = tm64.rearrange("p (q c s) -> p q c s", q=4, c=4)
                cs7 = cosT.rearrange("p (c j) -> p c j", j=256)[:, :, 7:8].to_broadcast([128, 4, 16])
                sn7 = sinT.rearrange("p (c j) -> p c j", j=256)[:, :, 7:8].to_broadcast([128, 4, 16])
                TT("dve", tmv[:, 2], tmv[:, 0], cs7, ALU.mult, [tm64_r, cosT_r], [tm64_r])
                TT("dve", tmv[:, 3], tmv[:, 1], sn7, ALU.mult, [tm64_r, sinT_r], [tm64_r])
                TT("dve", hsv[:, 0], tmv[:, 2], tmv[:, 3], ALU.subtract, [tm64_r], [hs_r])
                TT("dve", tmv[:, 2], tmv[:, 0], sn7, ALU.mult, [tm64_r, sinT_r], [tm64_r])
                TT("dve", tmv[:, 3], tmv[:, 1], cs7, ALU.mult, [tm64_r, cosT_r], [tm64_r])
                TT("dve", hsv[:, 1], tmv[:, 2], tmv[:, 3], ALU.add, [tm64_r], [hs_r])
                DMA("sp", o_ss5[i, :, :, yc * 4:(yc + 1) * 4, :].rearrange("q p c s -> p q c s"), hsv, [hs_r], [], out_sem)

        def POST(k):
            c0, n, kind = S5T[k]
            u32, u32_r = U32[k % 2], U32_r[k % 2]
            STT("dve", ysb[:, 0:n], u32[:, 0:n], dcol, py[:, 0:n], ALU.mult, ALU.add, [u32_r, pyr, cst_r], [ysb_r])
            TT("pool", yt2[:, 0:n], ysb[:, 0:n], ysb[:, 0:n], ALU.mult, [ysb_r], [yt2_r])
            TS("dve", yt2[:, 0:n], yt2[:, 0:n], 0.044715, ALU.mult, [yt2_r], [yt2_r], s2=1.0, op1=ALU.add)
            TT("pool", yt2[:, 0:n], yt2[:, 0:n], ysb[:, 0:n], ALU.mult, [yt2_r, ysb_r], [yt2_r])
            ACT(yt2[:, 0:n], yt2[:, 0:n], AF.Sigmoid, [yt2_r], [yt2_r], scale=2.0 * 0.7978845608028654)
            TT("dve", zbuf[:, yc * T + c0:yc * T + c0 + n], ysb[:, 0:n], yt2[:, 0:n], ALU.mult, [ysb_r, yt2_r], [z_r[yc]])

        items = [(k, sc) for k in range(len(S5T)) for sc in range(4)]
        NI = len(items)

        def emit_fx(j):
            if j < NI:
                k2, sc2 = items[j]
                if sc2 == 0:
                    PRE(k2)
                Fx(k2, sc2)

        emit_fx(0)
        emit_fx(1)
        Fr(*items[0])
        for idx in range(NI + 1):
            emit_fx(idx + 2)
            if idx + 1 < NI:
                Fr(*items[idx + 1])
            if idx < NI:
                k, sc = items[idx]
                Gs(k, sc, k == 0)
                if sc == 3:
                    POSTC(k)
            if idx >= 1:
                kp, scp = items[idx - 1]
                Gp(kp, scp)
                if scp == 3:
                    POST(kp)
        P.wrelease(ju)

    def s5_final(i, zbuf, z_r):
        wg, wgr, jg = P.wget()
        wo0, wo0r, jo0 = P.wget()
        wo1, wo1r, jo1 = P.wget()
        s5o, s5o_r = [], []
        for m in range(4):
            a, r = aalloc(256, BF16, "s5o%d" % m)
            s5o.append(a)
            s5o_r.append(r)
        sgm, sgm_r = [], []
        for m in range(2):
            a, r = aalloc(512, F32, "sgm%d" % m)
            sgm.append(a)
            sgm_r.append(r)
        for t in range(NT):
            t0, t1 = TILES[t]
            n = t1 - t0
            for m in range(4):
                pg, pgr = P.bank()
                for kc in range(4):
                    MM(pg[:, 0:n], wg[:, kc * 512 + m * 128:kc * 512 + (m + 1) * 128], zbuf[:, kc * T + t0:kc * T + t1],
                       kc == 0, kc == 3, [wgr, z_r[kc]], [pgr])
                ACT(sgm[m % 2][:, 0:n], pg[:, 0:n], AF.Sigmoid, [pgr], [sgm_r[m % 2]])
                TT("dve", s5o[m][:, 0:n], zbuf[:, m * T + t0:m * T + t1], sgm[m % 2][:, 0:n], ALU.mult,
                   [z_r[m], sgm_r[m % 2]], [s5o_r[m]])
            for m8 in range(NCH):
                pm, pmr = P.bank()
                for kc in range(4):
                    wsl, wslr = (wo0, wo0r) if kc < 2 else (wo1, wo1r)
                    MM(pm[:, 0:n], wsl[:, (kc % 2) * 1024 + m8 * 128:(kc % 2) * 1024 + (m8 + 1) * 128], s5o[kc][:, 0:n],
                       kc == 0, kc == 3, [wslr, s5o_r[kc]], [pmr])
                TT("dve", x[:, m8, t0:t1], pm[:, 0:n], x[:, m8, t0:t1], ALU.add, [pmr, x_r[m8][t]], [x_r[m8][t]])
        for j in (jg, jo0, jo1):
            P.wrelease(j)

    def ab_mixer(layer):
        i = layer // 2
        areset()
        norm_alloc()
        for t in range(NT):
            norm_tile(t, 4 + layer)
        P.rot = [4, 5, 6, 7]
        if "r" in KAB:
            for pr in range(2):
                areset_passes(pr == 0)
                ret_pass(i, pr, st_sem, out_sem)
        areset()
        if "s" in KAB:
            P.rot = [1, 2, 3, 4, 5, 6, 7]
            zbuf, _ = aalloc(4 * T // 2, BF16, "zbuf")
            z_r = [Region("z%d" % m) for m in range(4)]
            mark = ar["off"]
            for yc in range(4):
                s5_pass(i, yc, zbuf, z_r, mark, st_sem, out_sem)
                S.barrier(io_sems)
                ar["off"] = mark
            P.rot = list(range(8))
            s5_final(i, zbuf, z_r)
            areset()
        P.rot = list(range(8))

    st_sem = S.dma_sem("st")
    out_sem = S.dma_sem("out")
    io_sems.append(st_sem)
    io_sems.append(out_sem)
    for (layer, kind) in subs:
        if kind == "ffn1":
            ffn(layer)
        elif kind == "ffn2":
            ffn(8 + layer)
        elif layer % 2 == 1:
            gla_mixer(layer)
        else:
            ab_mixer(layer)
    assert P.next_get == len(P.loads) == P.issued, (P.next_get, len(P.loads), P.issued)

    areset()
    norm_alloc()
    for t in range(NT):
        norm_tile(t, 12, final=True)
    for c in range(NCH):
        DMA("sp", yT[c * 128:(c + 1) * 128, :], x[:, c, :], [x_r[c][t] for t in range(NT)], [], out_sem)
    finals = [(s.h, s.count) for s in (out_sem, st_sem, in_sem) if s.count > 0]
    S.replay(finals)
    return P


def _ffn_pack(w_gu, w_down):
    a = w_gu.reshape(NCH, 128, 2, NFF, 128).transpose(3, 1, 2, 0, 4).reshape(NFF, 128, 2048)
    b = w_down.reshape(NFF, 128, D)
    return np.concatenate([a, b], axis=2)


def _inproj(w, cols):
    m = len(cols)
    return w[:, cols].reshape(NCH, 128, m).transpose(1, 0, 2).reshape(128, NCH * m)


def pack_weights(plan, inp):
    f = np.float32
    blocks = []
    ffn_cache = {}
    for (key, n) in plan:
        kind = key[0]
        if kind == "ffn":
            _, layer, which, j = key
            ck = (layer, which)
            if ck not in ffn_cache:
                ffn_cache.clear()
                nm = "ffn1" if which == 0 else "ffn2"
                ffn_cache[ck] = _ffn_pack(inp[nm + "_w_gu"][layer], inp[nm + "_w_down"][layer])
            blk = ffn_cache[ck][j]
        elif kind == "gla":
            _, i, hd, part = key
            w = inp["gla_w_in"][i]
            wo = inp["gla_w_out"][i]
            ar = np.arange
            if part == 0:
                blk = np.concatenate([_inproj(w, hd * 128 + ar(128)), _inproj(w, 512 + hd * 128 + ar(128)),
                                      _inproj(w, 3072 + ar(16))], axis=1)
            elif part == 1:
                blk = np.concatenate([_inproj(w, 1024 + hd * 256 + ar(256)), _inproj(w, 2048 + hd * 256 + ar(128))], axis=1)
            else:
                rows = wo[hd * 256:(hd + 1) * 256].reshape(2, 128, D).transpose(1, 0, 2).reshape(128, 2 * D)
                blk = np.concatenate([_inproj(w, 2048 + hd * 256 + 128 + ar(128)), rows], axis=1)
        else:
            blk = pack_ab(key, inp)
        assert blk.shape == (128, n), (key, blk.shape, n)
        blocks.append(blk.astype(f, copy=False))
    if not blocks:
        return np.zeros((128, 1), f)
    return np.ascontiguousarray(np.concatenate(blocks, axis=1))


def build_cst(inp):
    f = np.float32
    CL = cst_layout()
    c = np.zeros((128, CL.n), f)

    def put(name, arr):
        o, n = CL[name]
        arr = np.asarray(arr, f)
        assert arr.shape[1] == n, (name, arr.shape, n)
        c[:arr.shape[0], o:o + n] = arr

    gains = np.concatenate([inp["norm_ffn1"], inp["norm_mix"], inp["norm_ffn2"], inp["norm_final"][None]], axis=0)
    put("gains", gains.reshape(13, NCH, 128).transpose(2, 0, 1).reshape(128, 13 * NCH))
    put("gla_b", inp["gla_b_alpha"].reshape(2, 4, 128).transpose(2, 0, 1).reshape(128, 8))
    put("gla_g", inp["gla_norm"].reshape(2, 2, 128).transpose(2, 0, 1).reshape(128, 4))
    put("s5_d", inp["s5_d"].reshape(2, 4, 128).transpose(2, 0, 1).reshape(128, 8))
    put("one", np.ones((128, 1), f))
    put("hm", (np.arange(128)[:, None] // 64 == np.arange(2)[None, :]).astype(f))
    cm = np.ones((128, 512), f)
    cm[:, 0::128] = 0.0
    put("cmask", cm)
    cm4 = np.ones((128, 144), f)
    cm4[:, 0] = 0.0
    cm4[:, 16::8] = 0.0
    put("cmask4", cm4)
    put("ident32", np.eye(128, dtype=f))
    put("nident32", -np.eye(128, dtype=f))
    s_ = np.arange(128)[:, None]
    t_ = np.arange(128)[None, :]
    put("causal", (t_ >= s_).astype(f))
    put("samp", ((t_ >= s_) & (t_ // 8 == s_ // 8)).astype(f))
    put("seqm", (np.arange(128)[:, None] // 8 == np.arange(16)[None, :]).astype(f))
    put("ident", np.eye(128, dtype=f))
    w2 = np.zeros((16, 1024), f)
    w2[:, 0:512] = inp["gla_w_alpha2"][0]
    w2[:, 512:1024] = inp["gla_w_alpha2"][1]
    put("w2", w2)
    put("jvec", np.broadcast_to(np.arange(256, dtype=f)[None], (128, 256)))
    put("ret_gam", ret_gam_table())
    sm = np.zeros((128, 96), f)
    for i in range(2):
        for q, nm in enumerate(("s5_a_re", "s5_a_im")):
            sm[:, (i * 3 + q) * 16:(i * 3 + q + 1) * 16] = inp[nm][i].reshape(16, 128).T
        sm[:, (i * 3 + 2) * 16:(i * 3 + 3) * 16] = np.repeat(inp["s5_log_dt"][i], 64).reshape(16, 128).T
    put("s5sm", sm)
    return c


def ret_gam_table():
    g = np.zeros((128, 18), np.float64)
    for pr in range(2):
        for p in range(128):
            h = 2 * pr + p // 64
            gamma = 1.0 - 2.0 ** (-5.0 - h)
            for k, n in enumerate((128, 16, 8)):
                base = (pr * 3 + k) * 3
                g[p, base] = gamma ** n
                g[p, base + 1 + p // 64] = gamma ** n
    return g.astype(np.float32)


def pack_ab(key, inp):
    ar = np.arange
    kind = key[0]
    if kind == "ret":
        _, i, pr, part = key
        w = inp["ab_w_in"][i]
        sw = np.concatenate([hh * 64 + (ar(64) + 32) % 64 for hh in range(2)])
        if part == 0:
            return np.concatenate([_inproj(w, 512 + pr * 128 + ar(128)), _inproj(w, 512 + pr * 128 + sw),
                                   _inproj(w, 768 + pr * 128 + ar(128))], axis=1)
        if part == 1:
            return np.concatenate([_inproj(w, 768 + pr * 128 + sw), _inproj(w, 1024 + pr * 256 + ar(256))], axis=1)
        if part == 2:
            return np.concatenate([_inproj(w, 1536 + pr * 256 + ar(128)), _inproj(w, 1536 + pr * 256 + 128 + ar(128))], axis=1)
        wo = inp["ab_w_out"][i]
        return wo[512 + pr * 256:512 + (pr + 1) * 256].reshape(2, 128, D).transpose(1, 0, 2).reshape(128, 2 * D)
    if kind == "s5u":
        _, i, yc = key
        return _inproj(inp["ab_w_in"][i], yc * 128 + ar(128))
    if kind == "s5g":
        _, i = key
        return inp["s5_w_glu"][i].reshape(4, 128, 512).transpose(1, 0, 2).reshape(128, 2048)
    if kind == "s5o":
        _, i, half = key
        wo = inp["ab_w_out"][i]
        return wo[half * 256:(half + 1) * 256].reshape(2, 128, D).transpose(1, 0, 2).reshape(128, 2 * D)
    raise KeyError(key)


def rot_tables():
    half = 32
    inv_freq = (1.0 / (np.float32(10000.0) ** (np.arange(half, dtype=np.float32) / np.float32(half)))).astype(np.float32)
    out = np.zeros((2, NT, 128, 4, 512), np.float32)
    p = np.arange(128)
    d = p % 64
    fi = d % 32
    sign = np.where(d < 32, -1.0, 1.0)
    for t, (t0, t1) in enumerate(TILES):
        n = t1 - t0
        col = np.arange(n)
        if t < NT - 1:
            pos = t0 + col
            j = col % 128
        else:
            pos = np.where(col < 16, 2048 + col, PAST_LEN + (col - 16) % 8)
            j = np.where(col < 16, col, (col - 16) % 8)
        ang = (pos.astype(np.float32)[None, :] * inv_freq[fi][:, None]).astype(np.float32).astype(np.float64)
        cs = np.cos(ang)
        sn = np.sin(ang) * sign[:, None]
        for pr in range(2):
            h = 2 * pr + p // 64
            gamma = 1.0 - 2.0 ** (-5.0 - h)
            lg = np.log(gamma)[:, None] * (j[None, :] + 1.0)
            Eq = np.exp(lg)
            Ek = np.exp(-lg) * (64.0 ** -0.5)
            out[pr, t, :, 0, :n] = cs * Eq
            out[pr, t, :, 1, :n] = sn * Eq
            out[pr, t, :, 2, :n] = cs * Ek
            out[pr, t, :, 3, :n] = sn * Ek
    return out


def s5w_pack(inp):
    out = np.zeros((2, 4, 128, 7, 512), np.float32)
    s = np.arange(128)
    for i in range(2):
        for yc in range(4):
            for sc in range(4):
                c = yc * 4 + sc
                g = 2 * c + s // 64
                p_ = s % 64
                for n in range(16):
                    r = 32 * sc + 16 * (s // 64) + n
                    out[i, yc, r, 0, sc * 128 + s] = inp["s5_b_re"][i][g, p_, n]
                    out[i, yc, r, 1, sc * 128 + s] = inp["s5_b_im"][i][g, p_, n]
                    out[i, yc, s, 2, sc * 128 + r] = inp["s5_c_re"][i][g, n, p_]
                    out[i, yc, s, 3, sc * 128 + r] = inp["s5_c_im"][i][g, n, p_]
                out[i, yc, :, 4, sc * 128:(sc + 1) * 128] = inp["s5_a_re"][i][g, p_][None, :]
                out[i, yc, :, 5, sc * 128:(sc + 1) * 128] = inp["s5_a_im"][i][g, p_][None, :]
                out[i, yc, :, 6, sc * 128:(sc + 1) * 128] = inp["s5_log_dt"][i][g][None, :]
    return out.reshape(2, 4, 128, 7 * 512)


def unpack_ab(o, core, sl, p_s5re, p_s5im, p_ret, s_s5re, s_s5im, s_ret):
    p_ret[:, core] = o["o_pret"].reshape(2, 4, 64, 128)
    s_ret[:, sl] = o["o_sret"].reshape(2, 4, 64, NSS, 128).transpose(0, 3, 1, 2, 4)
    ps5 = o["o_ps5"].transpose(0, 1, 3, 2).reshape(2, 2, 32, 64)
    p_s5re[:, core] = ps5[:, 0]
    p_s5im[:, core] = ps5[:, 1]
    ss5 = o["o_ss5"].transpose(0, 1, 4, 3, 2).reshape(2, 2, NSS, 32, 64)
    s_s5re[:, sl] = ss5[:, 0]
    s_s5im[:, sl] = ss5[:, 1]


def prepare_inputs(inp, subs):
    f = np.float32
    plan = make_plan(subs)
    wts = pack_weights(plan, inp)
    cst = build_cst(inp)
    has_gla = any(k == "mix" and l % 2 == 1 for (l, k) in subs)
    has_ab = any(k == "mix" and l % 2 == 0 for (l, k) in subs)
    if has_ab:
        rot = rot_tables()
        s5w = s5w_pack(inp)
    maps = []
    for core in range(8):
        xp = np.concatenate([inp["meta_tokens"], inp["x_prompt"][core]], axis=0)
        xs = inp["x_sample"][core * NSS:(core + 1) * NSS].reshape(NSAMP, D)
        xT = np.ascontiguousarray(np.concatenate([xp, xs], axis=0).T).astype(f)
        m = {"xT": xT, "cst": cst, "wts": wts}
        sl = slice(core * NSS, (core + 1) * NSS)
        if has_gla:
            m["st_gla"] = np.ascontiguousarray(inp["state_gla"][:, sl].transpose(0, 2, 3, 1, 4)).astype(f)
        if has_ab:
            m["rot_d"] = rot
            m["s5w"] = s5w
            re = inp["state_s5_re"][:, sl].reshape(2, NSS, 16, 128).transpose(0, 3, 2, 1)
            im = inp["state_s5_im"][:, sl].reshape(2, NSS, 16, 128).transpose(0, 3, 2, 1)
            m["st_s5"] = np.ascontiguousarray(np.stack([re, im], axis=1)).astype(f)
            m["st_ret"] = np.ascontiguousarray(
                inp["state_ret"][:, sl].reshape(2, NSS, 2, 128, 128).transpose(0, 2, 3, 1, 4)).astype(f)
        maps.append(m)
    return maps


_CACHE = {}
DEBUG_OUT = {}


def kernel(**inputs):
    inputs = {k: np.asarray(v) for k, v in inputs.items()}
    ncores = int(os.environ.get("KCORES", "8"))
    subs = parse_subs()
    maps = prepare_inputs(inputs, subs)[:ncores]
    key = tuple(subs)
    if key not in _CACHE:
        _CACHE[key] = build_program(subs)
    P = _CACHE[key]
    if os.environ.get("KTRACE"):
        res = run_bass_kernel_spmd(P.nc, maps, core_ids=list(range(ncores)), trace=True)
        print("KTRACE exec_time_ns", res.exec_time_ns)
    else:
        res = run_bass_kernel_spmd(P.nc, maps, core_ids=list(range(ncores)))
    outs = res.results
    yp = np.zeros((8, SEQ, D), np.float32)
    ys = np.zeros((128, DEC_SEQ, D), np.float32)
    p_s5re = np.zeros((2, 8, 32, 64), np.float32)
    p_s5im = np.zeros((2, 8, 32, 64), np.float32)
    p_ret = np.zeros((2, 8, 4, 64, 128), np.float32)
    p_gla = np.zeros((2, 8, 4, 128, 256), np.float32)
    s_s5re = np.zeros((2, 128, 32, 64), np.float32)
    s_s5im = np.zeros((2, 128, 32, 64), np.float32)
    s_ret = np.zeros((2, 128, 4, 64, 128), np.float32)
    s_gla = np.zeros((2, 128, 4, 128, 256), np.float32)
    for core in range(ncores):
        o = outs[core]
        y = o["yT"].T
        yp[core] = y[N_META:LP]
        sl = slice(core * NSS, (core + 1) * NSS)
        ys[sl] = y[LP:].reshape(NSS, DEC_SEQ, D)
        if "o_pgla" in o:
            p_gla[:, core] = o["o_pgla"]
            s_gla[:, sl] = o["o_sgla"].transpose(0, 3, 1, 2, 4)
        if "o_pret" in o:
            unpack_ab(o, core, sl, p_s5re, p_s5im, p_ret, s_s5re, s_s5im, s_ret)
    DEBUG_OUT.clear()
    DEBUG_OUT.update({"p_gla": p_gla, "s_gla": s_gla, "p_ret": p_ret, "s_ret": s_ret,
                      "p_s5re": p_s5re, "p_s5im": p_s5im, "s_s5re": s_s5re, "s_s5im": s_s5im})
    return (yp, ys, p_s5re, p_s5im, p_ret, p_gla, s_s5re, s_s5im, s_ret, s_gla)
```

```python
import os
import numpy as np
import concourse.bass as bass
import concourse.mybir as mybir
from concourse.bass_utils import run_bass_kernel_spmd

F32 = mybir.dt.float32
BF16 = mybir.dt.bfloat16
AF = mybir.ActivationFunctionType
ALU = mybir.AluOpType

D = 1024
NCH = 8
DEPTH = 4
SEQ = 2048
N_META = 16
LP = SEQ + N_META
NSS = 16
DEC_SEQ = 8
NSAMP = NSS * DEC_SEQ
T = LP + NSAMP
PAST_LEN = 16384
D_FF = 2816
NFF = 22
EPS = 1e-6
TILES = [(0, 512), (512, 1024), (1024, 1536), (1536, 2048), (2048, 2192)]
NT = len(TILES)
SLOT = 3072
NSLOT = 5
SEM_LIMIT = 30000


class Sem:
    __slots__ = ("h", "count", "dma")

    def __init__(self, h, dma):
        self.h = h
        self.count = 0
        self.dma = dma


class Region:
    __slots__ = ("name", "w", "r")

    def __init__(self, name):
        self.name = name
        self.w = None
        self.r = {}


class Sched:
    ENG = ("pe", "act", "dve", "pool", "sp")

    def __init__(self, nc):
        self.nc = nc
        self.rec = {e: [] for e in self.ENG}
        self.cur = {}
        self.waited = {e: {} for e in self.ENG}
        self.nsem = 0
        for e in self.ENG:
            self.cur[e] = self.new_sem(e, False)
        self.all_dma_sems = []
        self.pe_pending = False

    def new_sem(self, name, dma):
        self.nsem += 1
        s = Sem(self.nc.alloc_semaphore("s_%s_%d" % (name, self.nsem)), dma)
        if dma:
            self.all_dma_sems.append(s)
        return s

    def dma_sem(self, name):
        s = Sem(self.nc.alloc_semaphore("d_%s_%d" % (name, self.nsem)), True)
        self.nsem += 1
        return s

    def _deps(self, reads, writes):
        deps = {}

        def add(tok):
            if tok is None:
                return
            s, v = tok
            if s.dma:
                v = s.count
            k = id(s)
            if k not in deps or deps[k][1] < v:
                deps[k] = (s, v)

        for r in reads:
            add(r.w)
        for w in writes:
            add(w.w)
            for tok in w.r.values():
                add(tok)
        return deps

    def op(self, eng, fn, reads=(), writes=(), dsem=None, signal=True):
        if eng != "pe":
            assert not self.pe_pending, "non-PE op recorded inside an unsignalled PE group"
        else:
            self.pe_pending = not signal
        deps = self._deps(reads, writes)
        waits = []
        wd = self.waited[eng]
        mysem = self.cur[eng]
        for k, (s, v) in deps.items():
            if eng == "pe" and s is mysem:
                continue
            if wd.get(k, 0) >= v:
                continue
            wd[k] = v
            waits.append((s.h, v))
        if dsem is not None:
            dsem.count += 16
            tok = (dsem, dsem.count)
            inc = (dsem.h, 16)
        elif signal:
            if mysem.count >= SEM_LIMIT:
                mysem = self.new_sem(eng, False)
                self.cur[eng] = mysem
            mysem.count += 1
            tok = (mysem, mysem.count)
            inc = (mysem.h, 1)
        else:
            tok = (mysem, mysem.count + 1)
            inc = None
        self.rec[eng].append((waits, fn, inc))
        for w in writes:
            w.w = tok
            w.r = {}
        for r in reads:
            k = id(tok[0])
            if k not in r.r or r.r[k][1] < tok[1]:
                r.r[k] = tok
        return tok

    def barrier(self, io_sems=()):
        toks = [(self.cur[e], self.cur[e].count) for e in ("pe", "act", "dve", "pool")]
        toks += [(s, s.count) for s in io_sems]
        for e in ("pe", "act", "dve", "pool", "sp"):
            waits = []
            for (s, v) in toks:
                if v == 0:
                    continue
                if self.waited[e].get(id(s), 0) >= v:
                    continue
                self.waited[e][id(s)] = v
                waits.append((s.h, v))
            if waits:
                self.rec[e].append((waits, None, None))

    def replay(self, final_waits):
        nc = self.nc
        handles = {"pe": "tensor", "act": "scalar", "dve": "vector", "pool": "gpsimd", "sp": "sync"}
        with nc.Block() as block:
            for e in self.ENG:
                rec = self.rec[e]
                fw = final_waits if e == "sp" else []

                def body(h, rec=rec, fw=fw):
                    for (waits, fn, inc) in rec:
                        for (sh, v) in waits:
                            h.wait_ge(sh, v)
                        if fn is not None:
                            ins = fn(h)
                            if inc is not None:
                                ins.then_inc(inc[0], inc[1])
                    for (sh, v) in fw:
                        h.wait_ge(sh, v)

                getattr(block, handles[e])(body)


class Prog:
    def __init__(self):
        nc = bass.Bass("TRN2", target_bir_lowering=False)
        self.nc = nc
        self.S = Sched(nc)
        self.loads = []
        self.wcols = 0
        self.dram_in = {}
        self.dram_out = {}

    def din(self, name, shape):
        t = self.nc.dram_tensor(name, list(shape), F32, kind="ExternalInput").ap()
        self.dram_in[name] = t
        return t

    def dout(self, name, shape):
        t = self.nc.dram_tensor(name, list(shape), F32, kind="ExternalOutput").ap()
        self.dram_out[name] = t
        return t

    def sb(self, name, shape, dt):
        return self.nc.alloc_sbuf_tensor(name, list(shape), dt)

    def init_psum(self):
        self.banks = [self.nc.alloc_psum_tensor("ps%d" % i, [128, 512], F32) for i in range(8)]
        self.bank_r = [Region("ps%d" % i) for i in range(8)]
        self.bank_i = 0
        self.rot = list(range(8))

    def bank(self):
        i = self.rot[self.bank_i % len(self.rot)]
        self.bank_i += 1
        return self.banks[i], self.bank_r[i]

    def init_wstream(self, wts_ap):
        self.wts = wts_ap
        self.slots = [self.sb("wslot%d" % i, [128, SLOT], BF16) for i in range(NSLOT)]
        self.slot_r = [Region("wslot%d" % i) for i in range(NSLOT)]
        self.slot_sem = [self.S.dma_sem("w%d" % i) for i in range(NSLOT)]
        self.issued = 0
        self.released = set()
        self.next_get = 0

    def _pump(self):
        while self.issued < len(self.loads):
            j = self.issued
            if j >= NSLOT and (j - NSLOT) not in self.released:
                break
            off, n = self.loads[j]
            s = j % NSLOT
            dst = self.slots[s][:, 0:n]
            src = self.wts[:, off:off + n]
            self.S.op("pool", lambda h, dst=dst, src=src: h.dma_start(out=dst, in_=src),
                      writes=[self.slot_r[s]], dsem=self.slot_sem[s])
            self.issued += 1

    def wget(self):
        j = self.next_get
        self.next_get += 1
        self._pump()
        assert j < self.issued, "weight stream stalled (load %d not issuable: release missing)" % j
        return self.slots[j % NSLOT], self.slot_r[j % NSLOT], j

    def wrelease(self, j):
        self.released.add(j)
        self._pump()


KAB = os.environ.get("KAB", "rs")


def parse_subs():
    spec = os.environ.get("KSUBS")
    if spec:
        out = []
        for tok in spec.split(","):
            layer = int(tok[0])
            kind = {"f1": "ffn1", "m": "mix", "f2": "ffn2"}[tok[1:]]
            out.append((layer, kind))
        return out
    return [(l, k) for l in range(DEPTH) for k in ("ffn1", "mix", "ffn2")]


def make_plan(subs):
    plan = []
    for (layer, kind) in subs:
        i = layer // 2
        if kind == "ffn1" or kind == "ffn2":
            for j in range(NFF):
                plan.append((("ffn", layer, 0 if kind == "ffn1" else 1, j), SLOT))
        elif layer % 2 == 1:
            for hd in range(4):
                plan.append((("gla", i, hd, 0), 2048 + 128))
                plan.append((("gla", i, hd, 1), 3072))
                plan.append((("gla", i, hd, 2), 3072))
        else:
            if "r" in KAB:
                for pr in range(2):
                    plan.append((("ret", i, pr, 0), 3072))
                    plan.append((("ret", i, pr, 1), 3072))
                    plan.append((("ret", i, pr, 2), 2048))
                    plan.append((("ret", i, pr, 3), 2048))
            if "s" in KAB:
                for yc in range(4):
                    plan.append((("s5u", i, yc), 1024))
                plan.append((("s5g", i), 2048))
                plan.append((("s5o", i, 0), 2048))
                plan.append((("s5o", i, 1), 2048))
    return plan


class Layout:
    def __init__(self):
        self.off = {}
        self.n = 0

    def add(self, name, ncols):
        self.off[name] = (self.n, ncols)
        self.n += ncols

    def __getitem__(self, name):
        return self.off[name]


def cst_layout():
    L = Layout()
    L.add("gains", 13 * NCH)
    L.add("gla_b", 2 * 4)
    L.add("gla_g", 2 * 2)
    L.add("s5_d", 2 * 4)
    L.add("one", 1)
    L.add("hm", 2)
    L.add("cmask", 512)
    L.add("cmask4", 144)
    L.add("ident32", 128)
    L.add("nident32", 128)
    L.add("causal", 128)
    L.add("samp", 128)
    L.add("seqm", 16)
    L.add("ident", 128)
    L.add("w2", 2 * 512)
    L.add("jvec", 256)
    L.add("ret_gam", 18)
    L.add("s5sm", 2 * 3 * 16)
    return L


def build_program(subs):
    P = Prog()
    nc, S = P.nc, P.S
    plan = make_plan(subs)
    CL = cst_layout()
    has_gla = any(k == "mix" and l % 2 == 1 for (l, k) in subs)
    has_ab = any(k == "mix" and l % 2 == 0 for (l, k) in subs)
    WCOLS = max(sum(n for (_, n) in plan), 1)
    xT = P.din("xT", [D, T])
    cst = P.din("cst", [128, CL.n])
    wts = P.din("wts", [128, WCOLS])
    yT = P.dout("yT", [D, T])
    if has_gla:
        st_gla = P.din("st_gla", [2, 4, 128, NSS, 256])
        o_pgla = P.dout("o_pgla", [2, 4, 128, 256])
        o_sgla = P.dout("o_sgla", [2, 4, 128, NSS, 256])
    if has_ab:
        rot_d = P.din("rot_d", [2, NT, 128, 4, 512])
        s5w = P.din("s5w", [2, 4, 128, 7 * 512])
        st_s5 = P.din("st_s5", [2, 2, 128, 16, NSS])
        st_ret = P.din("st_ret", [2, 2, 128, NSS, 128])
        o_pret = P.dout("o_pret", [2, 2, 128, 128])
        o_sret = P.dout("o_sret", [2, 2, 128, NSS, 128])
        o_ps5 = P.dout("o_ps5", [2, 2, 128, 16])
        o_ss5 = P.dout("o_ss5", [2, 2, 128, 16, NSS])

    x = P.sb("x", [128, NCH, T], F32)
    hb = P.sb("hb", [128, NCH, T], BF16)
    NCF = CL["w2"][0]
    NCF = CL["causal"][0]
    cf = P.sb("cf", [128, NCF], F32)
    jv = P.sb("jv", [128, 256 + 18 + 96], F32)
    cb = P.sb("cb", [128, 128 + 128 + 16 + 128 + 1024], BF16)
    ones_bf = P.sb("ones_bf", [128, 128], BF16)
    eps_t = P.sb("eps_t", [128, 1], F32)
    ones32 = P.sb("ones32", [128, 128], F32)
    ARENA = 16384
    arena = P.sb("arena", [128, ARENA], F32)
    x_r = [[Region("x%d_%d" % (c, t)) for t in range(NT)] for c in range(NCH)]
    h_r = [[Region("h%d_%d" % (c, t)) for t in range(NT)] for c in range(NCH)]
    cst_r = Region("cst")
    P.init_psum()
    P.init_wstream(wts)
    off = 0
    for (_, n) in plan:
        P.loads.append((off, n))
        off += n

    ar = {"off": 0, "cache": None}

    def aalloc(ncols_f32, dt=F32, name="a"):
        o = ar["off"]
        ar["off"] = o + (ncols_f32 + 15) // 16 * 16
        assert ar["off"] <= ARENA, "arena overflow %d" % ar["off"]
        if ar["cache"] is not None:
            key = (o, ncols_f32, dt)
            if key not in ar["cache"]:
                ap = arena[:, o:o + ncols_f32]
                if dt == BF16:
                    ap = ap.bitcast(BF16)
                ar["cache"][key] = (ap, Region(name))
            return ar["cache"][key]
        ap = arena[:, o:o + ncols_f32]
        if dt == BF16:
            ap = ap.bitcast(BF16)
        return ap, Region(name)

    io_sems = []

    def areset():
        S.barrier(io_sems)
        ar["off"] = 0
        ar["cache"] = None

    def areset_passes(first):
        if first:
            areset()
            ar["cache"] = {}
        else:
            ar["off"] = 0

    def ACT(out, in_, func, reads, writes, bias=None, scale=None):
        kw = {}
        if bias is not None:
            kw["bias"] = bias
        if scale is not None:
            kw["scale"] = scale
        return S.op("act", lambda h: h.activation(out=out, in_=in_, func=func, **kw), reads=reads, writes=writes)

    def TT(eng, out, in0, in1, op, reads, writes):
        return S.op(eng, lambda h: h.tensor_tensor(out=out, in0=in0, in1=in1, op=op), reads=reads, writes=writes)

    def TS(eng, out, in0, s1, op0, reads, writes, s2=None, op1=None):
        if op1 is None:
            return S.op(eng, lambda h: h.tensor_scalar(out=out, in0=in0, scalar1=s1, scalar2=None, op0=op0),
                        reads=reads, writes=writes)
        return S.op(eng, lambda h: h.tensor_scalar(out=out, in0=in0, scalar1=s1, scalar2=s2, op0=op0, op1=op1),
                    reads=reads, writes=writes)

    def STT(eng, out, in0, scalar, in1, op0, op1, reads, writes):
        return S.op(eng, lambda h: h.scalar_tensor_tensor(out=out, in0=in0, scalar=scalar, in1=in1, op0=op0, op1=op1),
                    reads=reads, writes=writes)

    def COPY(eng, out, in_, reads, writes):
        if eng == "act":
            return S.op("act", lambda h: h.copy(out=out, in_=in_), reads=reads, writes=writes)
        return S.op(eng, lambda h: h.tensor_copy(out=out, in_=in_), reads=reads, writes=writes)

    def MM(out, lhsT, rhs, start, stop, reads, writes, signal=None):
        if signal is None:
            signal = stop
        return S.op("pe", lambda h: h.matmul(out, lhsT=lhsT, rhs=rhs, start=start, stop=stop),
                    reads=reads, writes=writes, signal=signal)

    def DMA(eng, out, in_, reads, writes, sem):
        return S.op(eng, lambda h: h.dma_start(out=out, in_=in_), reads=reads, writes=writes, dsem=sem)

    def SCAN(out, data0, data1, reads, writes, initial=0.0):
        return S.op("dve", lambda h: h.tensor_tensor_scan(out=out, data0=data0, data1=data1, initial=initial,
                                                          op0=ALU.mult, op1=ALU.add), reads=reads, writes=writes)

    def RECIP(out, in_, reads, writes):
        return S.op("dve", lambda h: h.reciprocal(out=out, in_=in_), reads=reads, writes=writes)

    def TRANSP(out, in_, reads, writes):
        return S.op("pe", lambda h: h.transpose(out=out, in_=in_, identity=ident_bf), reads=reads, writes=writes)

    def MEMSET(eng, out, val, writes):
        return S.op(eng, lambda h: h.memset(out, val), writes=writes)

    def ccol(name, j=0, n=1):
        o, _ = CL[name]
        return cf[:, o + j:o + j + n]

    in_sem = S.dma_sem("in")
    io_sems.append(in_sem)
    DMA("sp", cf[:], cst[:, 0:NCF], [], [cst_r], in_sem)
    for c in range(NCH):
        DMA("sp", x[:, c, :], xT[c * 128:(c + 1) * 128, :], [], [x_r[c][t] for t in range(NT)], in_sem)
    nb16 = 128 + 128 + 16 + 128 + 1024
    stg, stg_r = aalloc(nb16 + 256 + 18 + 96, F32, "stg")
    o_c = CL["causal"][0]
    DMA("sp", stg, cst[:, o_c:o_c + nb16 + 256 + 18 + 96], [], [stg_r], in_sem)
    COPY("dve", cb[:], stg[:, 0:nb16], [stg_r], [cst_r])
    COPY("dve", jv[:], stg[:, nb16:nb16 + 256 + 18 + 96], [stg_r], [cst_r])
    S.op("dve", lambda h: h.memset(ones_bf[:], 1.0), writes=[cst_r])
    S.op("dve", lambda h: h.memset(eps_t[:], EPS), writes=[cst_r])
    S.op("dve", lambda h: h.memset(ones32[:], 1.0), writes=[cst_r])
    causal_bf = cb[:, 0:128]
    samp_bf = cb[:, 128:256]
    seqm_bf = cb[:, 256:272]
    ident_bf = cb[:, 272:400]
    w2_bf = cb[:, 400:1424]
    jvec = jv[:, 0:256]
    areset()

    cnt = {"sq": 0, "rstd": 0, "sg": 0}
    nt_ = {}

    def norm_alloc():
        nt_["sq"] = [aalloc(256, BF16, "sq%d" % i) for i in range(4)]
        nt_["rstd"] = [aalloc(512, F32, "rstd%d" % i) for i in range(2)]

    def norm_tile(t, gidx, final=False):
        sq = [a for (a, _) in nt_["sq"]]
        sq_r = [r for (_, r) in nt_["sq"]]
        rstd = [a for (a, _) in nt_["rstd"]]
        rstd_r = [r for (_, r) in nt_["rstd"]]
        t0, t1 = TILES[t]
        n = t1 - t0
        pb, pr = P.bank()
        for c in range(NCH):
            i = cnt["sq"] % 4
            cnt["sq"] += 1
            ACT(sq[i][:, 0:n], x[:, c, t0:t1], AF.Square, [x_r[c][t]], [sq_r[i]])
            MM(pb[:, 0:n], ones_bf[:], sq[i][:, 0:n], c == 0, c == NCH - 1, [sq_r[i], cst_r], [pr], signal=True)
        k = cnt["rstd"] % 2
        cnt["rstd"] += 1
        ACT(rstd[k][:, 0:n], pb[:, 0:n], AF.Sqrt, [pr, cst_r], [rstd_r[k]], bias=eps_t[:, 0:1], scale=1.0 / D)
        RECIP(rstd[k][:, 0:n], rstd[k][:, 0:n], [rstd_r[k]], [rstd_r[k]])
        for c in range(NCH):
            g = ccol("gains", gidx * NCH + c)
            if final:
                STT("dve", x[:, c, t0:t1], x[:, c, t0:t1], g, rstd[k][:, 0:n], ALU.mult, ALU.mult,
                    [x_r[c][t], rstd_r[k], cst_r], [x_r[c][t]])
            else:
                STT("dve", hb[:, c, t0:t1], x[:, c, t0:t1], g, rstd[k][:, 0:n], ALU.mult, ALU.mult,
                    [x_r[c][t], rstd_r[k], cst_r], [h_r[c][t]])

    def ffn(gidx):
        areset()
        norm_alloc()
        sg_a = [aalloc(256, BF16, "sg%d" % i) for i in range(2)]
        hid_a = [[aalloc(256, BF16, "hid%d_%d" % (i, j)) for j in range(2)] for i in range(2)]
        sg = [a for (a, _) in sg_a]
        sg_r = [r for (_, r) in sg_a]
        hid = [[a for (a, _) in row] for row in hid_a]
        hid_r = [[r for (_, r) in row] for row in hid_a]
        for t in range(NT):
            norm_tile(t, gidx)
        steps = [(g, t) for g in range(NFF // 2) for t in range(NT)]
        slots = {}

        def A(si, jj):
            g, t = steps[si]
            t0, t1 = TILES[t]
            n = t1 - t0
            if t == 0:
                slots[(g, jj)] = P.wget()
            ws, wr, _ = slots[(g, jj)]
            pg, pgr = P.bank()
            pu, pur = P.bank()
            for (pb, pr, off) in ((pg, pgr, 0), (pu, pur, 1024)):
                for kc in range(NCH):
                    MM(pb[:, 0:n], ws[:, off + kc * 128: off + (kc + 1) * 128], hb[:, kc, t0:t1],
                       kc == 0, kc == NCH - 1, [wr, h_r[kc][t]], [pr])
            i = cnt["sg"] % 2
            cnt["sg"] += 1
            ACT(sg[i][:, 0:n], pg[:, 0:n], AF.Silu, [pgr], [sg_r[i]])
            TT("dve", hid[si % 2][jj][:, 0:n], sg[i][:, 0:n], pu[:, 0:n], ALU.mult, [sg_r[i], pur], [hid_r[si % 2][jj]])

        def B(si, half):
            g, t = steps[si]
            t0, t1 = TILES[t]
            n = t1 - t0
            for m in range(half * 4, half * 4 + 4):
                pb, pr = P.bank()
                for jj in range(2):
                    ws, wr, _ = slots[(g, jj)]
                    MM(pb[:, 0:n], ws[:, 2048 + m * 128: 2048 + (m + 1) * 128], hid[si % 2][jj][:, 0:n],
                       jj == 0, jj == 1, [wr, hid_r[si % 2][jj]], [pr])
                STT("dve", x[:, m, t0:t1], pb[:, 0:n], 0.5, x[:, m, t0:t1], ALU.mult, ALU.add,
                    [pr, x_r[m][t]], [x_r[m][t]])
            if half == 1 and t == NT - 1:
                for jj in range(2):
                    P.wrelease(slots[(g, jj)][2])

        ns = len(steps)
        A(0, 0)
        A(0, 1)
        for si in range(1, ns):
            A(si, 0)
            B(si - 1, 0)
            A(si, 1)
            B(si - 1, 1)
        B(ns - 1, 0)
        B(ns - 1, 1)

    def tile_blocks(t):
        if t < NT - 1:
            return [(b * 128, 128, "p") for b in range(4)]
        return [(0, 16, "p"), (16, 128, "s")]

    XOFF = int(os.environ.get("KXOFF", "0"))

    def gla_pass(i, hd, st_sem, out_sem):
        DK = 128
        w1, w1r, j1 = P.wget()
        w2s, w2r, j2 = P.wget()
        w3, w3r, j3 = P.wget()
        Wq = lambda kc: w1[:, kc * 128:(kc + 1) * 128]
        Wk = lambda kc: w1[:, 1024 + kc * 128:1024 + (kc + 1) * 128]
        Wlr = lambda kc: w1[:, 2048 + kc * 16:2048 + (kc + 1) * 16]
        Wv = lambda kc: w2s[:, kc * 256:(kc + 1) * 256]
        Wr = lambda cc, kc: (w2s[:, 2048 + kc * 128:2048 + (kc + 1) * 128] if cc == 0
                             else w3[:, kc * 128:(kc + 1) * 128])
        Wrr = lambda cc: (w2r if cc == 0 else w3r)
        Wo = lambda cc, m: w3[:, 1024 + cc * 1024 + m * 128:1024 + cc * 1024 + (m + 1) * 128]
        SETS = []
        for sidx in range(2):
            d = {}
            for nm, ncol, dt in (("lrb", 256, BF16), ("ee", 512, F32), ("Bc", 512, F32), ("E", 512, F32), ("Ei", 512, F32),
                                 ("qt", 256, BF16), ("kt", 256, BF16)):
                d[nm], d[nm + "_r"] = aalloc(ncol, dt, nm + str(sidx))
            for b in range(4):
                d["vtok%d" % b], d["vtok%d_r" % b] = aalloc(128, BF16, "vtok")
                d["ktok%d" % b], d["ktok%d_r" % b] = aalloc(64, BF16, "ktok")
                d["Am%d" % b], d["Am%d_r" % b] = aalloc(64, BF16, "Am")
            SETS.append(d)
        SR = []
        for q in range(3):
            SR.append([aalloc(256, BF16, "sr%d_%d" % (q, cc)) for cc in range(2)])
        S32, S32_r = aalloc(256, F32, "S32")
        Sbf, Sbf_r = aalloc(128, BF16, "Sbf")
        sqo, sqo_r = [], []
        og, og_r, og2, og2_r = [], [], [], []
        for cc in range(2):
            a, r = aalloc(256, BF16, "sqo%d" % cc)
            sqo.append(a)
            sqo_r.append(r)
            a, r = aalloc(512, F32, "og%d" % cc)
            og.append(a)
            og_r.append(r)
            a, r = aalloc(256, BF16, "og2%d" % cc)
            og2.append(a)
            og2_r.append(r)
        rs, rs_r = aalloc(512, F32, "rs")
        S0, S0_r = aalloc(2048, F32, "S0")
        S0b, S0b_r = aalloc(1024, BF16, "S0b")
        kmask, kmask_r = aalloc(512, BF16, "kmask")
        negb, negb_r = aalloc(1, F32, "negb")
        xtmp = [aalloc(512, F32, "xtmp%d" % q) for q in range(2)] if XOFF else None

        TS("dve", negb, ccol("gla_b", i * 4 + hd), -1.0, ALU.mult, [cst_r], [negb_r])
        MEMSET("dve", S32, 0.0, [S32_r])
        MEMSET("dve", Sbf, 0.0, [Sbf_r])
        PO = [[P.banks[0], P.banks[1]], [P.banks[2], P.banks[3]]]
        PO_r = [[P.bank_r[0], P.bank_r[1]], [P.bank_r[2], P.bank_r[3]]]

        def stage1(t):
            d = SETS[t % 2]
            t0, t1 = TILES[t]
            n = t1 - t0
            blocks = tile_blocks(t)
            sl = []

            def s_decay():
                pb, pr = P.bank()
                for kc in range(NCH):
                    MM(pb[0:16, 0:n], Wlr(kc), hb[:, kc, t0:t1], kc == 0, kc == NCH - 1, [w1r, h_r[kc][t]], [pr])
                COPY("act", d["lrb"][0:16, 0:n], pb[0:16, 0:n], [pr], [d["lrb_r"]])
                pz, pzr = P.bank()
                MM(pz[:, 0:n], w2_bf[0:16, i * 512 + hd * 128:i * 512 + (hd + 1) * 128], d["lrb"][0:16, 0:n], True, True,
                   [d["lrb_r"], cst_r], [pzr])
                ACT(d["ee"][:, 0:n], pz[:, 0:n], AF.Exp, [pzr, negb_r], [d["ee_r"]], bias=negb[:, 0:1], scale=-1.0)
                ACT(d["ee"][:, 0:n], d["ee"][:, 0:n], AF.Ln, [d["ee_r"], cst_r], [d["ee_r"]], bias=ccol("one"), scale=1.0)
                cm = ccol("cmask", 0, n) if t < NT - 1 else ccol("cmask4", 0, n)
                SCAN(d["Bc"][:, 0:n], cm, d["ee"][:, 0:n], [d["ee_r"], cst_r], [d["Bc_r"]])
                ACT(d["E"][:, 0:n], d["Bc"][:, 0:n], AF.Exp, [d["Bc_r"]], [d["E_r"]], scale=-1.0 / 16.0)
                ACT(d["Ei"][:, 0:n], d["Bc"][:, 0:n], AF.Exp, [d["Bc_r"]], [d["Ei_r"]], scale=1.0 / 16.0)
            sl.append(s_decay)

            def s_q():
                pq, pqr = P.bank()
                for kc in range(NCH):
                    MM(pq[:, 0:n], Wq(kc), hb[:, kc, t0:t1], kc == 0, kc == NCH - 1, [w1r, h_r[kc][t]], [pqr])
                TT("dve", d["qt"][:, 0:n], pq[:, 0:n], d["E"][:, 0:n], ALU.mult, [pqr, d["E_r"]], [d["qt_r"]])
            sl.append(s_q)

            def s_k():
                pk, pkr = P.bank()
                for kc in range(NCH):
                    MM(pk[:, 0:n], Wk(kc), hb[:, kc, t0:t1], kc == 0, kc == NCH - 1, [w1r, h_r[kc][t]], [pkr])
                STT("dve", d["kt"][:, 0:n], pk[:, 0:n], DK ** -0.5, d["Ei"][:, 0:n], ALU.mult, ALU.mult,
                    [pkr, d["Ei_r"]], [d["kt_r"]])
            sl.append(s_k)

            def mk_gate(cc):
                def s_gate():
                    pg, pgr = P.bank()
                    for kc in range(NCH):
                        MM(pg[:, 0:n], Wr(cc, kc), hb[:, kc, t0:t1], kc == 0, kc == NCH - 1, [Wrr(cc), h_r[kc][t]], [pgr])
                    ACT(SR[t % 3][cc][0][:, 0:n], pg[:, 0:n], AF.Silu, [pgr], [SR[t % 3][cc][1]])
                return s_gate
            sl.append(mk_gate(0))
            sl.append(mk_gate(1))

            def mk_tok(b, c0, nb):
                def s_tok():
                    pv, pvr = P.bank()
                    for kc in range(NCH):
                        MM(pv[0:nb, 0:256], hb[:, kc, t0 + c0:t0 + c0 + nb], Wv(kc), kc == 0, kc == NCH - 1,
                           [w2r, h_r[kc][t]], [pvr])
                    COPY("act", d["vtok%d" % b][0:nb, 0:256], pv[0:nb, 0:256], [pvr], [d["vtok%d_r" % b]])
                    pt, ptr = P.bank()
                    ptb = pt[:, 0:64].bitcast(BF16)
                    TRANSP(ptb[0:nb, 0:128], d["kt"][:, c0:c0 + nb], [d["kt_r"], cst_r], [ptr])
                    COPY("act", d["ktok%d" % b][0:nb, 0:128], ptb[0:nb, 0:128], [ptr], [d["ktok%d_r" % b]])
                    ps_, psr = P.bank()
                    MM(ps_[0:nb, 0:nb], d["kt"][:, c0:c0 + nb], d["qt"][:, c0:c0 + nb], True, True,
                       [d["kt_r"], d["qt_r"]], [psr])
                    mk = causal_bf if blocks[b][2] == "p" else samp_bf
                    TT("dve", d["Am%d" % b][0:nb, 0:nb], ps_[0:nb, 0:nb], mk[0:nb, 0:nb], ALU.mult, [psr, cst_r],
                       [d["Am%d_r" % b]])
                return s_tok
            for b, (c0, nb, kind) in enumerate(blocks):
                sl.append(mk_tok(b, c0, nb))
            return sl

        def chain_steps(t):
            d = SETS[t % 2]
            po, po_r = PO[t % 2], PO_r[t % 2]
            t0, t1 = TILES[t]
            blocks = tile_blocks(t)
            steps = []

            def mk_p(b, c0, nb):
                def step():
                    vt_, vr_ = d["vtok%d" % b], d["vtok%d_r" % b]
                    kt_, kr_ = d["ktok%d" % b], d["ktok%d_r" % b]
                    am_, ar_ = d["Am%d" % b], d["Am%d_r" % b]
                    pU, pUr = P.bank()
                    MM(pU[:, 0:256], kt_[0:nb, 0:128], vt_[0:nb, 0:256], True, True, [kr_, vr_], [pUr])
                    for cc in range(2):
                        MM(po[cc][:, c0:c0 + nb], vt_[0:nb, cc * 128:(cc + 1) * 128], am_[0:nb, 0:nb], True, False,
                           [vr_, ar_], [po_r[cc]], signal=False)
                        MM(po[cc][:, c0:c0 + nb], Sbf[:, cc * 128:(cc + 1) * 128], d["qt"][:, c0:c0 + nb], False, True,
                           [Sbf_r, d["qt_r"]], [po_r[cc]], signal=True)
                    e = d["E"][:, c0 + nb - 1:c0 + nb]
                    TS("dve", S32, S32, e, ALU.mult, [S32_r, d["E_r"]], [S32_r])
                    STT("dve", S32, pU[:, 0:256], e, S32, ALU.mult, ALU.add, [pUr, d["E_r"], S32_r], [S32_r])
                    COPY("act", Sbf, S32, [S32_r], [Sbf_r])
                    if t == NT - 1:
                        DMA("sp", o_pgla[i, hd], S32, [S32_r], [], out_sem)
                return step

            def mk_s(b, c0, nb, half):
                def step():
                    vt_, vr_ = d["vtok%d" % b], d["vtok%d_r" % b]
                    kt_, kr_ = d["ktok%d" % b], d["ktok%d_r" % b]
                    am_, ar_ = d["Am%d" % b], d["Am%d_r" % b]
                    DMA("sp", S0.rearrange("p (s v) -> p s v", v=256), st_gla[i, hd, :, half * 8:(half + 1) * 8, :],
                        [], [S0_r], st_sem)
                    COPY("act", S0b, S0, [S0_r], [S0b_r])
                    TT("dve", kmask.rearrange("p (s k) -> p s k", k=128),
                       kt_[:, 0:128].unsqueeze(1).to_broadcast([128, 8, 128]),
                       seqm_bf[:, half * 8:(half + 1) * 8].unsqueeze(2).to_broadcast([128, 8, 128]),
                       ALU.mult, [kr_, cst_r], [kmask_r])
                    for s_ in range(8):
                        sq_i = half * 8 + s_
                        cs = c0 + sq_i * 8
                        for cc in range(2):
                            MM(po[cc][:, cs:cs + 8], vt_[0:nb, cc * 128:(cc + 1) * 128], am_[0:nb, sq_i * 8:sq_i * 8 + 8],
                               True, False, [vr_, ar_], [po_r[cc]], signal=False)
                            MM(po[cc][:, cs:cs + 8], S0b[:, s_ * 256 + cc * 128:s_ * 256 + (cc + 1) * 128],
                               d["qt"][:, cs:cs + 8], False, True, [S0b_r, d["qt_r"]], [po_r[cc]], signal=True)
                    for s_ in range(8):
                        sq_i = half * 8 + s_
                        cs = c0 + sq_i * 8
                        pU, pUr = P.bank()
                        MM(pU[:, 0:256], kmask[:, s_ * 128:(s_ + 1) * 128], vt_[0:nb, 0:256], True, True,
                           [kmask_r, vr_], [pUr])
                        e = d["E"][:, cs + 7:cs + 8]
                        S0s = S0[:, s_ * 256:(s_ + 1) * 256]
                        TS("dve", S0s, S0s, e, ALU.mult, [S0_r, d["E_r"]], [S0_r])
                        STT("dve", S0s, pU[:, 0:256], e, S0s, ALU.mult, ALU.add, [pUr, d["E_r"], S0_r], [S0_r])
                    DMA("sp", o_sgla[i, hd, :, half * 8:(half + 1) * 8, :], S0.rearrange("p (s v) -> p s v", v=256),
                        [S0_r], [], out_sem)
                return step

            for b, (c0, nb, kind) in enumerate(blocks):
                if kind == "p":
                    steps.append(mk_p(b, c0, nb))
                else:
                    steps.append(mk_s(b, c0, nb, 0))
                    steps.append(mk_s(b, c0, nb, 1))
            return steps

        def tail_steps(t):
            d = SETS[t % 2]
            t0, t1 = TILES[t]
            n = t1 - t0
            sr = [SR[t % 3][cc][0] for cc in range(2)]
            sr_r = [SR[t % 3][cc][1] for cc in range(2)]
            po, po_r = PO[t % 2], PO_r[t % 2]
            st = {}

            def T1():
                pn, pnr = P.bank()
                st["pn"] = (pn, pnr)
                for cc in range(2):
                    ACT(sqo[cc][:, 0:n], po[cc][:, 0:n], AF.Square, [po_r[cc]], [sqo_r[cc]])
                    MM(pn[:, 0:n], ones_bf[:], sqo[cc][:, 0:n], cc == 0, cc == 1, [sqo_r[cc], cst_r], [pnr], signal=True)

            def T2():
                pn, pnr = st["pn"]
                ACT(rs[:, 0:n], pn[:, 0:n], AF.Sqrt, [pnr, cst_r], [rs_r], bias=eps_t[:, 0:1], scale=1.0 / 256.0)
                RECIP(rs[:, 0:n], rs[:, 0:n], [rs_r], [rs_r])

            def T3():
                for cc in range(2):
                    TT("dve", og[cc][:, 0:n], po[cc][:, 0:n], rs[:, 0:n], ALU.mult, [po_r[cc], rs_r], [og_r[cc]])
                    STT("dve", og2[cc][:, 0:n], og[cc][:, 0:n], ccol("gla_g", i * 2 + cc), sr[cc][:, 0:n], ALU.mult, ALU.mult,
                        [og_r[cc], sr_r[cc], cst_r], [og2_r[cc]])

            def mk_T4(m0):
                def T4():
                    for m in range(m0, m0 + 4):
                        pm, pmr = P.bank()
                        for cc in range(2):
                            MM(pm[:, 0:n], Wo(cc, m), og2[cc][:, 0:n], cc == 0, cc == 1, [w3r, og2_r[cc]], [pmr])
                        TT("dve", x[:, m, t0:t1], pm[:, 0:n], x[:, m, t0:t1], ALU.add, [pmr, x_r[m][t]], [x_r[m][t]])
                return T4
            def T12():
                T1()
                T2()
            return [T12, T3, mk_T4(0), mk_T4(4)]

        for f in stage1(0):
            f()
        prev_tail = []
        for t in range(NT):
            fill = stage1(t + 1) if t + 1 < NT else []
            steps = chain_steps(t)
            merged = []
            a, b = list(fill), list(prev_tail)
            while a or b:
                if a:
                    merged.append(a.pop(0))
                if b:
                    merged.append(b.pop(0))
                if a:
                    merged.append(a.pop(0))
            nst = len(steps)
            fi = 0
            for si, st in enumerate(steps):
                st()
                want = (len(merged) * (si + 1) + nst - 1) // nst
                while fi < min(want, len(merged)):
                    merged[fi]()
                    fi += 1
            while fi < len(merged):
                merged[fi]()
                fi += 1
            prev_tail = tail_steps(t)
        for f in prev_tail:
            f()
        for j in (j1, j2, j3):
            P.wrelease(j)

    def gla_mixer(layer):
        i = layer // 2
        areset()
        norm_alloc()
        for t in range(NT):
            norm_tile(t, 4 + layer)
        P.rot = [4, 5, 6, 7]
        for hd in range(4):
            areset_passes(hd == 0)
            gla_pass(i, hd, st_sem, out_sem)
        areset()
        P.rot = list(range(8))

    KRET = int(os.environ.get("KRET", "9"))

    def ret_pass(i, pr, st_sem, out_sem):
        w1, w1r, j1 = P.wget()
        w2s, w2r, j2 = P.wget()
        w3, w3r, j3 = P.wget()
        w4, w4r, j4 = P.wget()
        Wq = lambda kc: w1[:, kc * 128:(kc + 1) * 128]
        Wqs = lambda kc: w1[:, 1024 + kc * 128:1024 + (kc + 1) * 128]
        Wk = lambda kc: w1[:, 2048 + kc * 128:2048 + (kc + 1) * 128]
        Wks = lambda kc: w2s[:, kc * 128:(kc + 1) * 128]
        Wv = lambda kc: w2s[:, 1024 + kc * 256:1024 + (kc + 1) * 256]
        Wg = lambda cc, kc: w3[:, cc * 1024 + kc * 128:cc * 1024 + (kc + 1) * 128]
        Wo = lambda cc, m: w4[:, cc * 1024 + m * 128:cc * 1024 + (m + 1) * 128]
        rot, rot_r = aalloc(2048, F32, "rot")
        tA, tA_r = aalloc(512, F32, "tA")
        tB, tB_r = aalloc(512, F32, "tB")
        kt, kt_r = aalloc(256, BF16, "kt")
        SETS = []
        for sidx in range(2):
            d = {}
            for hh in range(2):
                d["qth%d" % hh], d["qth%d_r" % hh] = aalloc(256, BF16, "qth")
            for b in range(4):
                d["vtok%d" % b], d["vtok%d_r" % b] = aalloc(128, BF16, "vtok")
                d["ktok%d" % b], d["ktok%d_r" % b] = aalloc(64, BF16, "ktok")
                d["Am%d" % b], d["Am%d_r" % b] = aalloc(128, BF16, "Am")
            SETS.append(d)
        SG = [[aalloc(256, BF16, "sgt%d_%d" % (q, cc)) for cc in range(2)] for q in range(3)]
        S32, S32_r = aalloc(128, F32, "S32")
        Sbf, Sbf_r = aalloc(64, BF16, "Sbf")
        obf, obf_r, o32, o32_r, csq, csq_r, og2, og2_r = [], [], [], [], [], [], [], []
        for cc in range(2):
            a, r = aalloc(256, BF16, "obf%d" % cc)
            obf.append(a)
            obf_r.append(r)
            a, r = aalloc(512, F32, "o32%d" % cc)
            o32.append(a)
            o32_r.append(r)
            a, r = aalloc(256, BF16, "csq%d" % cc)
            csq.append(a)
            csq_r.append(r)
            a, r = aalloc(256, BF16, "og2%d" % cc)
            og2.append(a)
            og2_r.append(r)
        rsh, rsh_r = [], []
        for hh in range(2):
            a, r = aalloc(512, F32, "rs%d" % hh)
            rsh.append(a)
            rsh_r.append(r)
        S0, S0_r = aalloc(1024, F32, "S0")
        S0b, S0b_r = aalloc(512, BF16, "S0b")
        kmask, kmask_r = aalloc(512, BF16, "kmask")
        MEMSET("dve", S32, 0.0, [S32_r])
        MEMSET("dve", Sbf, 0.0, [Sbf_r])
        PO = [[P.banks[0], P.banks[1]], [P.banks[2], P.banks[3]]]
        PO_r = [[P.bank_r[0], P.bank_r[1]], [P.bank_r[2], P.bank_r[3]]]
        gam = lambda kind, v: jv[:, 256 + (pr * 3 + kind) * 3 + v:256 + (pr * 3 + kind) * 3 + v + 1]

        def stage1(t):
            d = SETS[t % 2]
            t0, t1 = TILES[t]
            n = t1 - t0
            blocks = tile_blocks(t)
            sl = []

            def s_q():
                DMA("sp", rot.rearrange("p (a c) -> p a c", c=512), rot_d[pr, t], [], [rot_r], st_sem)
                cq, sq_ = rot[:, 0:n], rot[:, 512:512 + n]
                pa, par = P.bank()
                for kc in range(NCH):
                    MM(pa[:, 0:n], Wq(kc), hb[:, kc, t0:t1], kc == 0, kc == NCH - 1, [w1r, h_r[kc][t]], [par])
                pb, pbr = P.bank()
                for kc in range(NCH):
                    MM(pb[:, 0:n], Wqs(kc), hb[:, kc, t0:t1], kc == 0, kc == NCH - 1, [w1r, h_r[kc][t]], [pbr])
                for hh in range(2):
                    STT("dve", tA[:, 0:n], pa[:, 0:n], ccol("hm", hh), cq, ALU.mult, ALU.mult, [par, rot_r, cst_r], [tA_r])
                    STT("dve", tB[:, 0:n], pb[:, 0:n], ccol("hm", hh), sq_, ALU.mult, ALU.mult, [pbr, rot_r, cst_r], [tB_r])
                    TT("pool", d["qth%d" % hh][:, 0:n], tA[:, 0:n], tB[:, 0:n], ALU.add, [tA_r, tB_r], [d["qth%d_r" % hh]])
            sl.append(s_q)

            def s_k():
                ck, sk = rot[:, 1024:1024 + n], rot[:, 1536:1536 + n]
                pa, par = P.bank()
                for kc in range(NCH):
                    MM(pa[:, 0:n], Wk(kc), hb[:, kc, t0:t1], kc == 0, kc == NCH - 1, [w1r, h_r[kc][t]], [par])
                pb, pbr = P.bank()
                for kc in range(NCH):
                    MM(pb[:, 0:n], Wks(kc), hb[:, kc, t0:t1], kc == 0, kc == NCH - 1, [w2r, h_r[kc][t]], [pbr])
                TT("dve", tA[:, 0:n], pa[:, 0:n], ck, ALU.mult, [par, rot_r], [tA_r])
                TT("dve", tB[:, 0:n], pb[:, 0:n], sk, ALU.mult, [pbr, rot_r], [tB_r])
                TT("pool", kt[:, 0:n], tA[:, 0:n], tB[:, 0:n], ALU.add, [tA_r, tB_r], [kt_r])
            sl.append(s_k)

            def mk_gate(cc):
                def s_gate():
                    pg, pgr = P.bank()
                    for kc in range(NCH):
                        MM(pg[:, 0:n], Wg(cc, kc), hb[:, kc, t0:t1], kc == 0, kc == NCH - 1, [w3r, h_r[kc][t]], [pgr])
                    ACT(SG[t % 3][cc][0][:, 0:n], pg[:, 0:n], AF.Silu, [pgr], [SG[t % 3][cc][1]])
                return s_gate
            sl.append(mk_gate(0))
            sl.append(mk_gate(1))

            def mk_tok(b, c0, nb, kind):
                def s_tok():
                    pv, pvr = P.bank()
                    for kc in range(NCH):
                        MM(pv[0:nb, 0:256], hb[:, kc, t0 + c0:t0 + c0 + nb], Wv(kc), kc == 0, kc == NCH - 1,
                           [w2r, h_r[kc][t]], [pvr])
                    COPY("act", d["vtok%d" % b][0:nb, 0:256], pv[0:nb, 0:256], [pvr], [d["vtok%d_r" % b]])
                    pt, ptr = P.bank()
                    ptb = pt[:, 0:64].bitcast(BF16)
                    TRANSP(ptb[0:nb, 0:128], kt[:, c0:c0 + nb], [kt_r, cst_r], [ptr])
                    COPY("act", d["ktok%d" % b][0:nb, 0:128], ptb[0:nb, 0:128], [ptr], [d["ktok%d_r" % b]])
                    ps_, psr = P.bank()
                    for hh in range(2):
                        MM(ps_[0:nb, hh * 128:hh * 128 + nb], kt[:, c0:c0 + nb], d["qth%d" % hh][:, c0:c0 + nb], True, True,
                           [kt_r, d["qth%d_r" % hh]], [psr], signal=True)
                    mk = causal_bf if kind == "p" else samp_bf
                    for hh in range(2):
                        TT("dve", d["Am%d" % b][0:nb, hh * 128:hh * 128 + nb], ps_[0:nb, hh * 128:hh * 128 + nb],
                           mk[0:nb, 0:nb], ALU.mult, [psr, cst_r], [d["Am%d_r" % b]])
                return s_tok
            for b, (c0, nb, kind) in enumerate(blocks):
                sl.append(mk_tok(b, c0, nb, kind))
            return sl

        def chain_steps(t):
            d = SETS[t % 2]
            po, po_r = PO[t % 2], PO_r[t % 2]
            blocks = tile_blocks(t)
            steps = []

            def mk_p(b, c0, nb):
                def step():
                    vt_, vr_ = d["vtok%d" % b], d["vtok%d_r" % b]
                    kt_, kr_ = d["ktok%d" % b], d["ktok%d_r" % b]
                    am_, ar_ = d["Am%d" % b], d["Am%d_r" % b]
                    pU, pUr = P.bank()
                    for hh in range(2):
                        MM(pU[:, hh * 128:(hh + 1) * 128], kt_[0:nb, 0:128], vt_[0:nb, hh * 128:(hh + 1) * 128], True, True,
                           [kr_, vr_], [pUr], signal=True)
                    for hh in range(2):
                        MM(po[hh][:, c0:c0 + nb], vt_[0:nb, hh * 128:(hh + 1) * 128], am_[0:nb, hh * 128:hh * 128 + nb],
                           True, False, [vr_, ar_], [po_r[hh]], signal=False)
                        MM(po[hh][:, c0:c0 + nb], Sbf[:, 0:128], d["qth%d" % hh][:, c0:c0 + nb],
                           False, True, [Sbf_r, d["qth%d_r" % hh]], [po_r[hh]], signal=True)
                    kd = 0 if nb == 128 else 1
                    TS("dve", S32, S32, gam(kd, 0), ALU.mult, [S32_r, cst_r], [S32_r])
                    for hh in range(2):
                        STT("dve", S32, pU[:, hh * 128:(hh + 1) * 128], gam(kd, 1 + hh), S32,
                            ALU.mult, ALU.add, [pUr, S32_r, cst_r], [S32_r])
                    COPY("act", Sbf, S32, [S32_r], [Sbf_r])
                    if t == NT - 1:
                        DMA("sp", o_pret[i, pr], S32, [S32_r], [], out_sem)
                return step

            def mk_s(b, c0, nb, half):
                def step():
                    vt_, vr_ = d["vtok%d" % b], d["vtok%d_r" % b]
                    kt_, kr_ = d["ktok%d" % b], d["ktok%d_r" % b]
                    am_, ar_ = d["Am%d" % b], d["Am%d_r" % b]
                    DMA("sp", S0.rearrange("p (s v) -> p s v", v=128), st_ret[i, pr, :, half * 8:(half + 1) * 8, :],
                        [], [S0_r], st_sem)
                    COPY("act", S0b, S0, [S0_r], [S0b_r])
                    TT("dve", kmask.rearrange("p (s k) -> p s k", k=128),
                       kt_[:, 0:128].unsqueeze(1).to_broadcast([128, 8, 128]),
                       seqm_bf[:, half * 8:(half + 1) * 8].unsqueeze(2).to_broadcast([128, 8, 128]),
                       ALU.mult, [kr_, cst_r], [kmask_r])
                    for s_ in range(8):
                        sq_i = half * 8 + s_
                        cs = c0 + sq_i * 8
                        for hh in range(2):
                            MM(po[hh][:, cs:cs + 8], vt_[0:nb, hh * 128:(hh + 1) * 128],
                               am_[0:nb, hh * 128 + sq_i * 8:hh * 128 + sq_i * 8 + 8],
                               True, False, [vr_, ar_], [po_r[hh]], signal=False)
                            MM(po[hh][:, cs:cs + 8], S0b[:, s_ * 128:(s_ + 1) * 128], d["qth%d" % hh][:, cs:cs + 8],
                               False, True, [S0b_r, d["qth%d_r" % hh]], [po_r[hh]], signal=True)
                    for s_ in range(8):
                        pU, pUr = P.bank()
                        for hh in range(2):
                            MM(pU[:, hh * 128:(hh + 1) * 128], kmask[:, s_ * 128:(s_ + 1) * 128],
                               vt_[0:nb, hh * 128:(hh + 1) * 128], True, True, [kmask_r, vr_], [pUr], signal=True)
                        S0s = S0[:, s_ * 128:(s_ + 1) * 128]
                        TS("dve", S0s, S0s, gam(2, 0), ALU.mult, [S0_r, cst_r], [S0_r])
                        for hh in range(2):
                            STT("dve", S0s, pU[:, hh * 128:(hh + 1) * 128], gam(2, 1 + hh), S0s,
                                ALU.mult, ALU.add, [pUr, S0_r, cst_r], [S0_r])
                    DMA("sp", o_sret[i, pr, :, half * 8:(half + 1) * 8, :], S0.rearrange("p (s v) -> p s v", v=128),
                        [S0_r], [], out_sem)
                return step

            for b, (c0, nb, kind) in enumerate(blocks):
                if kind == "p":
                    steps.append(mk_p(b, c0, nb))
                else:
                    steps.append(mk_s(b, c0, nb, 0))
                    steps.append(mk_s(b, c0, nb, 1))
            return steps

        def tail_steps(t):
            t0, t1 = TILES[t]
            n = t1 - t0
            po, po_r = PO[t % 2], PO_r[t % 2]
            sgt = [SG[t % 3][cc][0] for cc in range(2)]
            sgt_r = [SG[t % 3][cc][1] for cc in range(2)]
            st = {}
            H2 = range(2)

            def T1():
                for hh in H2:
                    COPY("act", obf[hh][:, 0:n], po[hh][:, 0:n], [po_r[hh]], [obf_r[hh]])
                    COPY("act", o32[hh][:, 0:n], po[hh][:, 0:n], [po_r[hh]], [o32_r[hh]])
                st["pmn"] = []
                for hh in H2:
                    pm_, pmr_ = P.bank()
                    MM(pm_[:, 0:n], ones_bf[:], obf[hh][:, 0:n], True, True, [obf_r[hh], cst_r], [pmr_])
                    st["pmn"].append((pm_, pmr_))

            def T2():
                for hh in H2:
                    pm_, pmr_ = st["pmn"][hh]
                    STT("dve", o32[hh][:, 0:n], pm_[:, 0:n], -1.0 / 128.0, o32[hh][:, 0:n], ALU.mult, ALU.add,
                        [pmr_, o32_r[hh]], [o32_r[hh]])
                for hh in H2:
                    ACT(csq[hh][:, 0:n], o32[hh][:, 0:n], AF.Square, [o32_r[hh]], [csq_r[hh]])
                st["pvn"] = []
                for hh in H2:
                    pv_, pvr_ = P.bank()
                    MM(pv_[:, 0:n], ones_bf[:], csq[hh][:, 0:n], True, True, [csq_r[hh], cst_r], [pvr_])
                    st["pvn"].append((pv_, pvr_))

            def T3():
                for hh in H2:
                    pv_, pvr_ = st["pvn"][hh]
                    ACT(rsh[hh][:, 0:n], pv_[:, 0:n], AF.Sqrt, [pvr_, cst_r], [rsh_r[hh]], bias=eps_t[:, 0:1],
                        scale=1.0 / 128.0)
                for hh in H2:
                    RECIP(rsh[hh][:, 0:n], rsh[hh][:, 0:n], [rsh_r[hh]], [rsh_r[hh]])

            def T4():
                for hh in H2:
                    TT("dve", o32[hh][:, 0:n], o32[hh][:, 0:n], rsh[hh][:, 0:n], ALU.mult, [o32_r[hh], rsh_r[hh]], [o32_r[hh]])
                for hh in H2:
                    TT("pool", og2[hh][:, 0:n], o32[hh][:, 0:n], sgt[hh][:, 0:n], ALU.mult, [o32_r[hh], sgt_r[hh]],
                       [og2_r[hh]])

            def mk_T5(m0):
                def T5():
                    for m in range(m0, m0 + 4):
                        pm, pmr = P.bank()
                        for cc in range(2):
                            MM(pm[:, 0:n], Wo(cc, m), og2[cc][:, 0:n], cc == 0, cc == 1, [w4r, og2_r[cc]], [pmr])
                        TT("dve", x[:, m, t0:t1], pm[:, 0:n], x[:, m, t0:t1], ALU.add, [pmr, x_r[m][t]], [x_r[m][t]])
                return T5
            def Tn():
                T1()
                T2()
                T3()
            return [Tn, T4, mk_T5(0), mk_T5(4)]

        for f in stage1(0):
            f()
        prev_tail = []
        for t in range(NT):
            fill = stage1(t + 1) if t + 1 < NT else []
            steps = chain_steps(t)
            merged = []
            a, b = list(fill), list(prev_tail)
            while a or b:
                if a:
                    merged.append(a.pop(0))
                if b:
                    merged.append(b.pop(0))
                if a:
                    merged.append(a.pop(0))
            nst = len(steps)
            fi = 0
            for si, st_ in enumerate(steps):
                st_()
                want = (len(merged) * (si + 1) + nst - 1) // nst
                while fi < min(want, len(merged)):
                    merged[fi]()
                    fi += 1
            while fi < len(merged):
                merged[fi]()
                fi += 1
            prev_tail = tail_steps(t)
        for f in prev_tail:
            f()
        for j in (j1, j2, j3, j4):
            P.wrelease(j)

    I32 = mybir.dt.int32
    TWO_PI = 2.0 * np.pi
    S5T = [(k * 256, 256, "p") for k in range(8)] + [(2048, 16, "p"), (2064, 128, "s")]

    def frac_sincos(v, W, tmp_f, tmp_i, sin_out, cos_out, v_r, t_r, so_r, co_r):
        COPY("dve", tmp_i[:, 0:W], v[:, 0:W], [v_r], [t_r])
        COPY("dve", tmp_f[:, 0:W], tmp_i[:, 0:W], [t_r], [t_r])
        TT("dve", v[:, 0:W], v[:, 0:W], tmp_f[:, 0:W], ALU.subtract, [v_r, t_r], [v_r])
        ACT(sin_out[:, 0:W], v[:, 0:W], AF.Sin, [v_r], [so_r], scale=TWO_PI)
        TS("dve", tmp_f[:, 0:W], v[:, 0:W], 0.25, ALU.add, [v_r, t_r], [t_r])
        COPY("dve", tmp_i[:, 0:W], tmp_f[:, 0:W], [t_r], [t_r])
        COPY("dve", cos_out[:, 0:W], tmp_i[:, 0:W], [t_r], [co_r])
        TT("dve", tmp_f[:, 0:W], tmp_f[:, 0:W], cos_out[:, 0:W], ALU.subtract, [t_r, co_r], [t_r])
        ACT(cos_out[:, 0:W], tmp_f[:, 0:W], AF.Sin, [t_r], [co_r], scale=TWO_PI)

    def s5_params(ar, ai, ldt, W, in_r, want_f, tag):
        bufs = {}

        def nb(name, dt=F32):
            a, r = aalloc(W, F32, tag + name)
            if dt is I32:
                a = a.bitcast(I32)
            bufs[name] = (a, r)
            return a, r

        dt_, dt_r = nb("dt")
        mag, mag_r = nb("mag")
        rfr, rfr_r = nb("rfr")
        tf, tf_r = nb("tf")
        ti, ti_r = nb("ti", I32)
        sn, sn_r = nb("sn")
        cs, cs_r = nb("cs")
        ACT(dt_, ldt, AF.Exp, [in_r], [dt_r])
        TT("dve", mag, dt_, ar, ALU.mult, [dt_r, in_r], [mag_r])
        ACT(mag, mag, AF.Exp, [mag_r], [mag_r])
        TT("dve", rfr, dt_, ai, ALU.mult, [dt_r, in_r], [rfr_r])
        TS("dve", rfr, rfr, 1.0 / TWO_PI, ALU.mult, [rfr_r], [rfr_r])
        frac_sincos(rfr, W, tf, ti, sn, cs, rfr_r, tf_r, sn_r, cs_r)
        TT("dve", cs, cs, mag, ALU.mult, [cs_r, mag_r], [cs_r])
        TT("dve", sn, sn, mag, ALU.mult, [sn_r, mag_r], [sn_r])
        out = {"mag": (mag, mag_r), "abr": (cs, cs_r), "abi": (sn, sn_r), "rfr": (rfr, rfr_r)}
        if want_f:
            fre, fre_r = nb("fre")
            fim, fim_r = nb("fim")
            den, den_r = dt_, dt_r
            TT("dve", den, ar, ar, ALU.mult, [in_r, dt_r], [den_r])
            TT("dve", tf, ai, ai, ALU.mult, [in_r, tf_r], [tf_r])
            TT("dve", den, den, tf, ALU.add, [den_r, tf_r], [den_r])
            RECIP(den, den, [den_r], [den_r])
            nre, nre_r = tf, tf_r
            TS("dve", nre, cs, -1.0, ALU.add, [cs_r, tf_r], [nre_r])
            TT("dve", fre, nre, ar, ALU.mult, [nre_r, in_r], [fre_r])
            TT("dve", fim, sn, ai, ALU.mult, [sn_r, in_r], [fim_r])
            TT("dve", fre, fre, fim, ALU.add, [fre_r, fim_r], [fre_r])
            TT("dve", fre, fre, den, ALU.mult, [fre_r, den_r], [fre_r])
            TT("dve", fim, sn, ar, ALU.mult, [sn_r, in_r, fre_r], [fim_r])
            TT("dve", nre, nre, ai, ALU.mult, [nre_r, in_r], [nre_r])
            TT("dve", fim, fim, nre, ALU.subtract, [fim_r, nre_r], [fim_r])
            TT("dve", fim, fim, den, ALU.mult, [fim_r, den_r], [fim_r])
            out["fre"] = (fre, fre_r)
            out["fim"] = (fim, fim_r)
        return out

    KS5 = int(os.environ.get("KS5", "9"))
    PENG = os.environ.get("KPENG", "pool")
    PENG2 = os.environ.get("KPENG2", "pool")

    def s5_pass(i, yc, zbuf, z_r, mark, st_sem, out_sem, smL):
        wu, wur, ju = P.wget()
        id32_ = ccol("ident32", 0, 128)
        Bbr, Bbr_r = aalloc(256, BF16, "Bbr")
        Bbi, Bbi_r = aalloc(256, BF16, "Bbi")
        Cre, Cre_r = aalloc(256, BF16, "Cre")
        nCre, nCre_r = aalloc(256, BF16, "nCre")
        nCim, nCim_r = aalloc(256, BF16, "nCim")
        csl = slice(yc * 4, (yc + 1) * 4)
        mag, mag_r = smL["mag"][0][:, csl], smL["mag"][1]
        abr, abr_r = smL["abr"][0][:, csl], smL["abr"][1]
        abi, abi_r = smL["abi"][0][:, csl], smL["abi"][1]
        rfr, rfr_r = smL["rfr"][0][:, csl], smL["rfr"][1]
        fre, fre_r = smL["fre"][0][:, csl], smL["fre"][1]
        fim, fim_r = smL["fim"][0][:, csl], smL["fim"][1]
        mark_b = ar["off"]
        stg, stg_r = aalloc(4 * 512, F32, "s5stg")
        DMA("sp", stg, s5w[i, yc][:, 0:4 * 512], [], [stg_r], st_sem)
        Bre_p, Bim_p, Cre_p, Cim_p = (stg[:, q * 512:(q + 1) * 512] for q in range(4))
        dg = [aalloc(128, F32, "dg%d" % q) for q in range(2)]
        pf, pf_r = [], []
        for q, (fq, fq_r) in enumerate(((fre, fre_r), (fim, fim_r))):
            pb, pr = P.bank()
            for sc in range(4):
                dgt, dgt_r = dg[sc % 2]
                TS("dve", dgt, id32_, fq[:, sc:sc + 1], ALU.mult, [cst_r, fq_r], [dgt_r])
                MM(pb[:, sc * 128:(sc + 1) * 128], ones32[:], dgt, True, True, [dgt_r, cst_r], [pr], signal=True)
            pf.append(pb)
            pf_r.append(pr)
        m1, m1_r = aalloc(512, F32, "m1")
        m2, m2_r = aalloc(512, F32, "m2")
        TT("dve", m1, pf[0][:, 0:512], Bre_p, ALU.mult, [pf_r[0], stg_r], [m1_r])
        TT("dve", m2, pf[1][:, 0:512], Bim_p, ALU.mult, [pf_r[1], stg_r], [m2_r])
        TT("dve", Bbr, m1, m2, ALU.subtract, [m1_r, m2_r], [Bbr_r])
        TT("dve", m1, pf[0][:, 0:512], Bim_p, ALU.mult, [pf_r[0], stg_r, m1_r], [m1_r])
        TT("dve", m2, pf[1][:, 0:512], Bre_p, ALU.mult, [pf_r[1], stg_r, m2_r], [m2_r])
        TT("dve", Bbi, m1, m2, ALU.add, [m1_r, m2_r], [Bbi_r])
        COPY("act", Cre, Cre_p, [stg_r], [Cre_r])
        TS("dve", nCre, Cre_p, -1.0, ALU.mult, [stg_r], [nCre_r])
        TS("dve", nCim, Cim_p, -1.0, ALU.mult, [stg_r], [nCim_r])
        S.barrier(io_sems)
        ar["off"] = mark_b
        cosT, cosT_r = aalloc(1024, F32, "cosT")
        sinT, sinT_r = aalloc(1024, F32, "sinT")
        ttf, ttf_r = aalloc(1024, F32, "ttf")
        tti, tti_r = aalloc(1024, F32, "tti")
        tti = tti.bitcast(I32)
        vt, vt_r = ttf, ttf_r
        vt, vt_r = aalloc(1024, F32, "vt")
        for sc in range(4):
            TS("dve", vt[:, sc * 256:(sc + 1) * 256], jvec, rfr[:, sc:sc + 1], ALU.mult, [cst_r, rfr_r], [vt_r])
        frac_sincos(vt, 1024, ttf, tti, sinT, cosT, vt_r, ttf_r, sinT_r, cosT_r)
        tt4 = tti.bitcast(BF16)
        x2, x2_r = aalloc(1024, F32, "x2")
        g2, g2_r = aalloc(512, F32, "g2")
        t2, t2_r = aalloc(512, BF16, "t2")
        AB, AB_r, GG, GG_r, TQ, TQ_r = [], [], [], [], [], []
        for st, (xb, gb, tb) in enumerate(((vt, ttf[:, 0:512], tt4[:, 0:1024]), (x2, g2, t2))):
            AB.append([xb[:, q * 256:(q + 1) * 256] for q in range(4)])
            AB_r.append([Region("ab%d_%d" % (st, q)) for q in range(4)])
            GG.append([gb[:, q * 256:(q + 1) * 256] for q in range(2)])
            GG_r.append([Region("g%d_%d" % (st, q)) for q in range(2)])
            TQ.append([tb[:, q * 256:(q + 1) * 256] for q in range(4)])
            TQ_r.append([Region("tq%d_%d" % (st, q)) for q in range(4)])
        decs, decs_r = ttf[:, 512:640], Region("decs")
        u2, u2_r = aalloc(256, F32, "u2")
        U32 = [ttf[:, 768:1024], u2]
        U32_r = [Region("u32a"), Region("u32b")]
        UBF = [tt4[:, 1024:1280], tt4[:, 1280:1536]]
        UBF_r = [Region("ubfa"), Region("ubfb")]
        ysb, ysb_r = aalloc(256, F32, "ysb")
        yt2, yt2_r = aalloc(256, F32, "yt2")
        small, small_r = aalloc(256, F32, "small")
        gL = small[:, 0:8]
        tmp4 = small[:, 8:40]
        car = small[:, 40:48]
        hL = small[:, 48:56]
        car_r = Region("car")
        gL_r = Region("gL")
        h0, h0_r = aalloc(128, F32, "h0")
        csm, csm_r = aalloc(128, F32, "csm")
        hs, hs_r = aalloc(128, F32, "hs")
        tm64, tm64_r = aalloc(256, F32, "tm64")
        S.barrier(io_sems)
        py, pyr = P.banks[0], P.bank_r[0]
        dcol = ccol("s5_d", i * 4 + yc)
        v3 = lambda a: a.rearrange("p (s j) -> p s j", j=8)

        def crot(dst_re, dst_im, g_re, g_im, cs_, sn_, W3, rd, wr):
            tA = tmp4[:, 0:W3]
            tB = tmp4[:, 8:8 + W3]
            TT("dve", tA, g_re, cs_, ALU.mult, rd, [small_r])
            TT("dve", tB, g_im, sn_, ALU.mult, rd, [small_r])
            TT("dve", dst_re, tA, tB, ALU.subtract, [small_r], wr)
            TT("dve", tA, g_re, sn_, ALU.mult, rd + [small_r], [small_r])
            TT("dve", tB, g_im, cs_, ALU.mult, rd + [small_r], [small_r])
            TT("dve", dst_im, tA, tB, ALU.add, [small_r], wr)

        w255 = small[:, 56:64]
        w255_r = Region("w255")
        crot(w255[:, 0:4], w255[:, 4:8], cosT.rearrange("p (c j) -> p c j", j=256)[:, :, 255],
             sinT.rearrange("p (c j) -> p c j", j=256)[:, :, 255], cosT.rearrange("p (c j) -> p c j", j=256)[:, :, 1],
             sinT.rearrange("p (c j) -> p c j", j=256)[:, :, 1], 4,
             [cosT_r, sinT_r, small_r], [w255_r])

        def tabs(sc, n, kind):
            if kind == "s":
                return (cosT[:, sc * 256:sc * 256 + 8].unsqueeze(1).to_broadcast([128, 16, 8]),
                        sinT[:, sc * 256:sc * 256 + 8].unsqueeze(1).to_broadcast([128, 16, 8]))
            return cosT[:, sc * 256:sc * 256 + n], sinT[:, sc * 256:sc * 256 + n]

        def vw(a, n, kind):
            return v3(a[:, 0:n]) if kind == "s" else a[:, 0:n]

        def PRE(k):
            c0, n, kind = S5T[k]
            tix = min(c0 // 512, NT - 1)
            u32, u32_r, ubf, ubf_r = U32[k % 2], U32_r[k % 2], UBF[k % 2], UBF_r[k % 2]
            pu, pur = P.bank()
            for kc in range(NCH):
                MM(pu[:, 0:n], wu[:, kc * 128:(kc + 1) * 128], hb[:, kc, c0:c0 + n], kc == 0, kc == NCH - 1,
                   [wur, h_r[kc][tix]], [pur])
            COPY("act", u32[:, 0:n], pu[:, 0:n], [pur], [u32_r])
            COPY("act", ubf[:, 0:n], u32[:, 0:n], [u32_r], [ubf_r])
            if kind == "s":
                DMA("sp", h0.rearrange("p (q c s) -> p q c s", q=2, c=4), st_s5[i, :, :, yc * 4:(yc + 1) * 4, :].rearrange(
                    "q p c s -> p q c s"), [], [h0_r], st_sem)
                h0v = h0.rearrange("p (q c s) -> p q c s", q=2, c=4)
                csv = csm.rearrange("p (q c s) -> p q c s", q=2, c=4)
                tmv = tm64.rearrange("p (q c s) -> p q c s", q=4, c=4)
                abr_b = abr.unsqueeze(2).to_broadcast([128, 4, 16])
                abi_b = abi.unsqueeze(2).to_broadcast([128, 4, 16])
                TT("dve", tmv[:, 0], h0v[:, 0], abr_b, ALU.mult, [h0_r, abr_r], [tm64_r])
                TT("dve", tmv[:, 1], h0v[:, 1], abi_b, ALU.mult, [h0_r, abi_r], [tm64_r])
                TT("dve", csv[:, 0], tmv[:, 0], tmv[:, 1], ALU.subtract, [tm64_r], [csm_r])
                TT("dve", tmv[:, 2], h0v[:, 1], abr_b, ALU.mult, [h0_r, abr_r], [tm64_r])
                TT("dve", tmv[:, 3], h0v[:, 0], abi_b, ALU.mult, [h0_r, abi_r], [tm64_r])
                TT("dve", csv[:, 1], tmv[:, 2], tmv[:, 3], ALU.add, [tm64_r], [csm_r])

        pxs = {}
        pzs = {}
        id32 = ccol("ident32", 0, 128)
        nid32 = ccol("nident32", 0, 128)

        def Fx(k, sc):
            c0, n, kind = S5T[k]
            ubf, ubf_r = UBF[k % 2], UBF_r[k % 2]
            px, pxr = P.bank()
            MM(px[:, 0:n], Bbr[:, sc * 128:(sc + 1) * 128], ubf[:, 0:n], True, True, [Bbr_r, ubf_r], [pxr], signal=True)
            MM(px[:, 256:256 + n], Bbi[:, sc * 128:(sc + 1) * 128], ubf[:, 0:n], True, True, [Bbi_r, ubf_r], [pxr],
               signal=True)
            pxs[(k, sc)] = (px, pxr)

        def Fr(k, sc):
            c0, n, kind = S5T[k]
            st = sc % 2
            a1, a2, b1, b2 = AB[st]
            a1_r, a2_r, b1_r, b2_r = AB_r[st]
            cs_, sn_ = tabs(sc, n, kind)
            px, pxr = pxs.pop((k, sc))
            xr, xi = vw(px, n, kind), vw(px[:, 256:512], n, kind)
            TT("dve", vw(a1, n, kind), xr, cs_, ALU.mult, [pxr, cosT_r], [a1_r])
            TT("dve", vw(a2, n, kind), xi, sn_, ALU.mult, [pxr, sinT_r], [a2_r])
            TT("dve", vw(b1, n, kind), xi, cs_, ALU.mult, [pxr, cosT_r], [b1_r])
            TT("dve", vw(b2, n, kind), xr, sn_, ALU.mult, [pxr, sinT_r], [b2_r])
            pz, pzr = P.bank()
            MM(pz[:, 0:n], id32, a1[:, 0:n], True, False, [a1_r, cst_r], [pzr], signal=True)
            MM(pz[:, 0:n], id32, a2[:, 0:n], False, True, [a2_r, cst_r], [pzr], signal=True)
            MM(pz[:, 256:256 + n], id32, b1[:, 0:n], True, False, [b1_r, cst_r], [pzr], signal=True)
            MM(pz[:, 256:256 + n], nid32, b2[:, 0:n], False, True, [b2_r, cst_r], [pzr], signal=True)
            pzs[(k, sc)] = (pz, pzr)

        def Gs(k, sc, first):
            c0, n, kind = S5T[k]
            st = sc % 2
            gre, gim = GG[st]
            gre_r, gim_r = GG_r[st]
            tq, tq_r = TQ[st], TQ_r[st]
            cs_, sn_ = tabs(sc, n, kind)
            pz, pzr = pzs.pop((k, sc))
            xre, xim = pz[:, 0:n], pz[:, 256:256 + n]
            if kind == "s":
                csv = csm.rearrange("p (q c s) -> p q c s", q=2, c=4)
                TT("dve", pz[:, 0:n:8], pz[:, 0:n:8], csv[:, 0, sc], ALU.add, [pzr, csm_r], [pzr])
                TT("dve", pz[:, 256:256 + n:8], pz[:, 256:256 + n:8], csv[:, 1, sc], ALU.add, [pzr, csm_r], [pzr])
                TS("dve", decs[:, 0:n], ccol("cmask4", 16, 128), mag[:, sc:sc + 1], ALU.mult, [cst_r, mag_r], [decs_r])
                dec = decs[:, 0:n]
                dec_rd = [decs_r]
            else:
                dec = mag[:, sc:sc + 1].to_broadcast([128, n])
                dec_rd = [mag_r]
            if kind == "p" and not first:
                SCAN(gre[:, 0:n], dec, xre, [pzr, car_r] + dec_rd, [gre_r], initial=car[:, sc:sc + 1])
                SCAN(gim[:, 0:n], dec, xim, [pzr, car_r] + dec_rd, [gim_r], initial=car[:, 4 + sc:5 + sc])
            else:
                SCAN(gre[:, 0:n], dec, xre, [pzr] + dec_rd, [gre_r])
                SCAN(gim[:, 0:n], dec, xim, [pzr] + dec_rd, [gim_r])
            g_re3, g_im3 = vw(gre, n, kind), vw(gim, n, kind)
            TT(PENG2, vw(tq[0], n, kind), g_re3, cs_, ALU.mult, [gre_r, cosT_r], [tq_r[0]])
            TT(PENG2, vw(tq[1], n, kind), g_im3, sn_, ALU.mult, [gim_r, sinT_r], [tq_r[1]])
            TT(PENG, vw(tq[2], n, kind), g_re3, sn_, ALU.mult, [gre_r, sinT_r], [tq_r[2]])
            TT(PENG, vw(tq[3], n, kind), g_im3, cs_, ALU.mult, [gim_r, cosT_r], [tq_r[3]])
            if kind == "p":
                COPY("act", gL[:, sc:sc + 1], gre[:, n - 1:n], [gre_r], [gL_r])
                COPY("act", gL[:, 4 + sc:5 + sc], gim[:, n - 1:n], [gim_r], [gL_r])
            else:
                tmv = tm64.rearrange("p (q c s) -> p q c s", q=4, c=4)
                COPY("act", tmv[:, 0, sc], gre[:, 7:n:8], [gre_r], [tm64_r])
                COPY("act", tmv[:, 1, sc], gim[:, 7:n:8], [gim_r], [tm64_r])

        def Gp(k, sc):
            c0, n, kind = S5T[k]
            tq, tq_r = TQ[sc % 2], TQ_r[sc % 2]
            for q, wmat, wreg in ((0, Cre, Cre_r), (1, nCre, nCre_r), (2, nCim, nCim_r), (3, nCim, nCim_r)):
                MM(py[:, 0:n], wmat[:, sc * 128:(sc + 1) * 128], tq[q][:, 0:n], sc == 0 and q == 0, sc == 3 and q == 3,
                   [wreg, tq_r[q]], [pyr], signal=True)

        def POSTC(k):
            c0, n, kind = S5T[k]
            if kind == "p" and n == 256:
                crot(car[:, 0:4], car[:, 4:8], gL[:, 0:4], gL[:, 4:8], w255[:, 0:4], w255[:, 4:8], 4,
                     [gL_r, small_r, w255_r], [car_r])
            elif kind == "p":
                L = n - 1
                csL = cosT.rearrange("p (c j) -> p c j", j=256)[:, :, L]
                snL = sinT.rearrange("p (c j) -> p c j", j=256)[:, :, L]
                crot(hL[:, 0:4], hL[:, 4:8], gL[:, 0:4], gL[:, 4:8], csL, snL, 4, [gL_r, small_r, cosT_r, sinT_r], [small_r])
                DMA("sp", o_ps5[i, :, :, yc * 4:(yc + 1) * 4].rearrange("q p c -> p q c"),
                    hL.rearrange("p (q c) -> p q c", q=2), [small_r], [], out_sem)
            else:
                hsv = hs.rearrange("p (q c s) -> p q c s", q=2, c=4)
                tmv = tm64.rearrange("p (q c s) -> p q c s", q=4, c=4)
                cs7 = cosT.rearrange("p (c j) -> p c j", j=256)[:, :, 7:8].to_broadcast([128, 4, 16])
                sn7 = sinT.rearrange("p (c j) -> p c j", j=256)[:, :, 7:8].to_broadcast([128, 4, 16])
                TT("dve", tmv[:, 2], tmv[:, 0], cs7, ALU.mult, [tm64_r, cosT_r], [tm64_r])
                TT("dve", tmv[:, 3], tmv[:, 1], sn7, ALU.mult, [tm64_r, sinT_r], [tm64_r])
                TT("dve", hsv[:, 0], tmv[:, 2], tmv[:, 3], ALU.subtract, [tm64_r], [hs_r])
                TT("dve", tmv[:, 2], tmv[:, 0], sn7, ALU.mult, [tm64_r, sinT_r], [tm64_r])
                TT("dve", tmv[:, 3], tmv[:, 1], cs7, ALU.mult, [tm64_r, cosT_r], [tm64_r])
                TT("dve", hsv[:, 1], tmv[:, 2], tmv[:, 3], ALU.add, [tm64_r], [hs_r])
                DMA("sp", o_ss5[i, :, :, yc * 4:(yc + 1) * 4, :].rearrange("q p c s -> p q c s"), hsv, [hs_r], [], out_sem)

        def POST(k):
            c0, n, kind = S5T[k]
            u32, u32_r = U32[k % 2], U32_r[k % 2]
            STT("dve", ysb[:, 0:n], u32[:, 0:n], dcol, py[:, 0:n], ALU.mult, ALU.add, [u32_r, pyr, cst_r], [ysb_r])
            TT("pool", yt2[:, 0:n], ysb[:, 0:n], ysb[:, 0:n], ALU.mult, [ysb_r], [yt2_r])
            TS("dve", yt2[:, 0:n], yt2[:, 0:n], 0.044715, ALU.mult, [yt2_r], [yt2_r], s2=1.0, op1=ALU.add)
            TT("pool", yt2[:, 0:n], yt2[:, 0:n], ysb[:, 0:n], ALU.mult, [yt2_r, ysb_r], [yt2_r])
            ACT(yt2[:, 0:n], yt2[:, 0:n], AF.Sigmoid, [yt2_r], [yt2_r], scale=2.0 * 0.7978845608028654)
            TT("dve", zbuf[:, yc * T + c0:yc * T + c0 + n], ysb[:, 0:n], yt2[:, 0:n], ALU.mult, [ysb_r, yt2_r], [z_r[yc]])

        items = [(k, sc) for k in range(len(S5T)) for sc in range(4)]
        NI = len(items)

        def emit_fx(j):
            if j < NI:
                k2, sc2 = items[j]
                if sc2 == 0:
                    PRE(k2)
                Fx(k2, sc2)

        emit_fx(0)
        emit_fx(1)
        Fr(*items[0])
        for idx in range(NI + 1):
            emit_fx(idx + 2)
            if idx + 1 < NI:
                Fr(*items[idx + 1])
            if idx < NI:
                k, sc = items[idx]
                Gs(k, sc, k == 0)
                if sc == 3:
                    POSTC(k)
            if idx >= 1:
                kp, scp = items[idx - 1]
                Gp(kp, scp)
                if scp == 3:
                    POST(kp)
        P.wrelease(ju)

    def s5_final(i, zbuf, z_r):
        wg, wgr, jg = P.wget()
        wo0, wo0r, jo0 = P.wget()
        wo1, wo1r, jo1 = P.wget()
        s5o, s5o_r = [], []
        for m in range(4):
            a, r = aalloc(256, BF16, "s5o%d" % m)
            s5o.append(a)
            s5o_r.append(r)
        sgm, sgm_r = [], []
        for m in range(2):
            a, r = aalloc(512, F32, "sgm%d" % m)
            sgm.append(a)
            sgm_r.append(r)
        for t in range(NT):
            t0, t1 = TILES[t]
            n = t1 - t0
            for m in range(4):
                pg, pgr = P.bank()
                for kc in range(4):
                    MM(pg[:, 0:n], wg[:, kc * 512 + m * 128:kc * 512 + (m + 1) * 128], zbuf[:, kc * T + t0:kc * T + t1],
                       kc == 0, kc == 3, [wgr, z_r[kc]], [pgr])
                ACT(sgm[m % 2][:, 0:n], pg[:, 0:n], AF.Sigmoid, [pgr], [sgm_r[m % 2]])
                TT("dve", s5o[m][:, 0:n], zbuf[:, m * T + t0:m * T + t1], sgm[m % 2][:, 0:n], ALU.mult,
                   [z_r[m], sgm_r[m % 2]], [s5o_r[m]])
            for m8 in range(NCH):
                pm, pmr = P.bank()
                for kc in range(4):
                    wsl, wslr = (wo0, wo0r) if kc < 2 else (wo1, wo1r)
                    MM(pm[:, 0:n], wsl[:, (kc % 2) * 1024 + m8 * 128:(kc % 2) * 1024 + (m8 + 1) * 128], s5o[kc][:, 0:n],
                       kc == 0, kc == 3, [wslr, s5o_r[kc]], [pmr])
                TT("dve", x[:, m8, t0:t1], pm[:, 0:n], x[:, m8, t0:t1], ALU.add, [pmr, x_r[m8][t]], [x_r[m8][t]])
        for j in (jg, jo0, jo1):
            P.wrelease(j)

    def ab_mixer(layer):
        i = layer // 2
        areset()
        norm_alloc()
        for t in range(NT):
            norm_tile(t, 4 + layer)
        P.rot = [4, 5, 6, 7]
        if "r" in KAB:
            for pr in range(2):
                areset_passes(pr == 0)
                ret_pass(i, pr, st_sem, out_sem)
        areset()
        if "s" in KAB:
            P.rot = [1, 2, 3, 4, 5, 6, 7]
            zbuf, _ = aalloc(4 * T // 2, BF16, "zbuf")
            z_r = [Region("z%d" % m) for m in range(4)]
            smb = 256 + 18 + (i * 3) * 16
            smL = s5_params(jv[:, smb:smb + 16], jv[:, smb + 16:smb + 32], jv[:, smb + 32:smb + 48], 16, cst_r, True, "smL")
            mark = ar["off"]
            for yc in range(4):
                s5_pass(i, yc, zbuf, z_r, mark, st_sem, out_sem, smL)
                S.barrier(io_sems)
                ar["off"] = mark
            P.rot = list(range(8))
            s5_final(i, zbuf, z_r)
            areset()
        P.rot = list(range(8))

    st_sem = S.dma_sem("st")
    out_sem = S.dma_sem("out")
    io_sems.append(st_sem)
    io_sems.append(out_sem)
    for (layer, kind) in subs:
        if kind == "ffn1":
            ffn(layer)
        elif kind == "ffn2":
            ffn(8 + layer)
        elif layer % 2 == 1:
            gla_mixer(layer)
        else:
            ab_mixer(layer)
    assert P.next_get == len(P.loads) == P.issued, (P.next_get, len(P.loads), P.issued)

    areset()
    norm_alloc()
    for t in range(NT):
        norm_tile(t, 12, final=True)
    for c in range(NCH):
        DMA("sp", yT[c * 128:(c + 1) * 128, :], x[:, c, :], [x_r[c][t] for t in range(NT)], [], out_sem)
    finals = [(s.h, s.count) for s in (out_sem, st_sem, in_sem) if s.count > 0]
    S.replay(finals)
    return P


def _ffn_pack(w_gu, w_down):
    a = w_gu.reshape(NCH, 128, 2, NFF, 128).transpose(3, 1, 2, 0, 4).reshape(NFF, 128, 2048)
    b = w_down.reshape(NFF, 128, D)
    return np.concatenate([a, b], axis=2)


def _inproj(w, cols):
    m = len(cols)
    return w[:, cols].reshape(NCH, 128, m).transpose(1, 0, 2).reshape(128, NCH * m)


def pack_weights(plan, inp):
    f = np.float32
    blocks = []
    ffn_cache = {}
    for (key, n) in plan:
        kind = key[0]
        if kind == "ffn":
            _, layer, which, j = key
            ck = (layer, which)
            if ck not in ffn_cache:
                ffn_cache.clear()
                nm = "ffn1" if which == 0 else "ffn2"
                ffn_cache[ck] = _ffn_pack(inp[nm + "_w_gu"][layer], inp[nm + "_w_down"][layer])
            blk = ffn_cache[ck][j]
        elif kind == "gla":
            _, i, hd, part = key
            w = inp["gla_w_in"][i]
            wo = inp["gla_w_out"][i]
            ar = np.arange
            if part == 0:
                blk = np.concatenate([_inproj(w, hd * 128 + ar(128)), _inproj(w, 512 + hd * 128 + ar(128)),
                                      _inproj(w, 3072 + ar(16))], axis=1)
            elif part == 1:
                blk = np.concatenate([_inproj(w, 1024 + hd * 256 + ar(256)), _inproj(w, 2048 + hd * 256 + ar(128))], axis=1)
            else:
                rows = wo[hd * 256:(hd + 1) * 256].reshape(2, 128, D).transpose(1, 0, 2).reshape(128, 2 * D)
                blk = np.concatenate([_inproj(w, 2048 + hd * 256 + 128 + ar(128)), rows], axis=1)
        else:
            blk = pack_ab(key, inp)
        assert blk.shape == (128, n), (key, blk.shape, n)
        blocks.append(blk.astype(f, copy=False))
    if not blocks:
        return np.zeros((128, 1), f)
    return np.ascontiguousarray(np.concatenate(blocks, axis=1))


def build_cst(inp):
    f = np.float32
    CL = cst_layout()
    c = np.zeros((128, CL.n), f)

    def put(name, arr):
        o, n = CL[name]
        arr = np.asarray(arr, f)
        assert arr.shape[1] == n, (name, arr.shape, n)
        c[:arr.shape[0], o:o + n] = arr

    gains = np.concatenate([inp["norm_ffn1"], inp["norm_mix"], inp["norm_ffn2"], inp["norm_final"][None]], axis=0)
    put("gains", gains.reshape(13, NCH, 128).transpose(2, 0, 1).reshape(128, 13 * NCH))
    put("gla_b", inp["gla_b_alpha"].reshape(2, 4, 128).transpose(2, 0, 1).reshape(128, 8))
    put("gla_g", inp["gla_norm"].reshape(2, 2, 128).transpose(2, 0, 1).reshape(128, 4))
    put("s5_d", inp["s5_d"].reshape(2, 4, 128).transpose(2, 0, 1).reshape(128, 8))
    put("one", np.ones((128, 1), f))
    put("hm", (np.arange(128)[:, None] // 64 == np.arange(2)[None, :]).astype(f))
    cm = np.ones((128, 512), f)
    cm[:, 0::128] = 0.0
    put("cmask", cm)
    cm4 = np.ones((128, 144), f)
    cm4[:, 0] = 0.0
    cm4[:, 16::8] = 0.0
    put("cmask4", cm4)
    put("ident32", np.eye(128, dtype=f))
    put("nident32", -np.eye(128, dtype=f))
    s_ = np.arange(128)[:, None]
    t_ = np.arange(128)[None, :]
    put("causal", (t_ >= s_).astype(f))
    put("samp", ((t_ >= s_) & (t_ // 8 == s_ // 8)).astype(f))
    put("seqm", (np.arange(128)[:, None] // 8 == np.arange(16)[None, :]).astype(f))
    put("ident", np.eye(128, dtype=f))
    w2 = np.zeros((16, 1024), f)
    w2[:, 0:512] = inp["gla_w_alpha2"][0]
    w2[:, 512:1024] = inp["gla_w_alpha2"][1]
    put("w2", w2)
    put("jvec", np.broadcast_to(np.arange(256, dtype=f)[None], (128, 256)))
    put("ret_gam", ret_gam_table())
    sm = np.zeros((128, 96), f)
    for i in range(2):
        for q, nm in enumerate(("s5_a_re", "s5_a_im")):
            sm[:, (i * 3 + q) * 16:(i * 3 + q + 1) * 16] = inp[nm][i].reshape(16, 128).T
        sm[:, (i * 3 + 2) * 16:(i * 3 + 3) * 16] = np.repeat(inp["s5_log_dt"][i], 64).reshape(16, 128).T
    put("s5sm", sm)
    return c


def ret_gam_table():
    g = np.zeros((128, 18), np.float64)
    for pr in range(2):
        for p in range(128):
            h = 2 * pr + p // 64
            gamma = 1.0 - 2.0 ** (-5.0 - h)
            for k, n in enumerate((128, 16, 8)):
                base = (pr * 3 + k) * 3
                g[p, base] = gamma ** n
                g[p, base + 1 + p // 64] = gamma ** n
    return g.astype(np.float32)


def pack_ab(key, inp):
    ar = np.arange
    kind = key[0]
    if kind == "ret":
        _, i, pr, part = key
        w = inp["ab_w_in"][i]
        sw = np.concatenate([hh * 64 + (ar(64) + 32) % 64 for hh in range(2)])
        if part == 0:
            return np.concatenate([_inproj(w, 512 + pr * 128 + ar(128)), _inproj(w, 512 + pr * 128 + sw),
                                   _inproj(w, 768 + pr * 128 + ar(128))], axis=1)
        if part == 1:
            return np.concatenate([_inproj(w, 768 + pr * 128 + sw), _inproj(w, 1024 + pr * 256 + ar(256))], axis=1)
        if part == 2:
            return np.concatenate([_inproj(w, 1536 + pr * 256 + ar(128)), _inproj(w, 1536 + pr * 256 + 128 + ar(128))], axis=1)
        wo = inp["ab_w_out"][i]
        return wo[512 + pr * 256:512 + (pr + 1) * 256].reshape(2, 128, D).transpose(1, 0, 2).reshape(128, 2 * D)
    if kind == "s5u":
        _, i, yc = key
        return _inproj(inp["ab_w_in"][i], yc * 128 + ar(128))
    if kind == "s5g":
        _, i = key
        return inp["s5_w_glu"][i].reshape(4, 128, 512).transpose(1, 0, 2).reshape(128, 2048)
    if kind == "s5o":
        _, i, half = key
        wo = inp["ab_w_out"][i]
        return wo[half * 256:(half + 1) * 256].reshape(2, 128, D).transpose(1, 0, 2).reshape(128, 2 * D)
    raise KeyError(key)


def rot_tables():
    half = 32
    inv_freq = (1.0 / (np.float32(10000.0) ** (np.arange(half, dtype=np.float32) / np.float32(half)))).astype(np.float32)
    out = np.zeros((2, NT, 128, 4, 512), np.float32)
    p = np.arange(128)
    d = p % 64
    fi = d % 32
    sign = np.where(d < 32, -1.0, 1.0)
    for t, (t0, t1) in enumerate(TILES):
        n = t1 - t0
        col = np.arange(n)
        if t < NT - 1:
            pos = t0 + col
            j = col % 128
        else:
            pos = np.where(col < 16, 2048 + col, PAST_LEN + (col - 16) % 8)
            j = np.where(col < 16, col, (col - 16) % 8)
        ang = (pos.astype(np.float32)[None, :] * inv_freq[fi][:, None]).astype(np.float32).astype(np.float64)
        cs = np.cos(ang)
        sn = np.sin(ang) * sign[:, None]
        for pr in range(2):
            h = 2 * pr + p // 64
            gamma = 1.0 - 2.0 ** (-5.0 - h)
            lg = np.log(gamma)[:, None] * (j[None, :] + 1.0)
            Eq = np.exp(lg)
            Ek = np.exp(-lg) * (64.0 ** -0.5)
            out[pr, t, :, 0, :n] = cs * Eq
            out[pr, t, :, 1, :n] = sn * Eq
            out[pr, t, :, 2, :n] = cs * Ek
            out[pr, t, :, 3, :n] = sn * Ek
    return out


def s5w_pack(inp):
    out = np.zeros((2, 4, 128, 7, 512), np.float32)
    s = np.arange(128)
    for i in range(2):
        for yc in range(4):
            for sc in range(4):
                c = yc * 4 + sc
                g = 2 * c + s // 64
                p_ = s % 64
                for n in range(16):
                    r = 32 * sc + 16 * (s // 64) + n
                    out[i, yc, r, 0, sc * 128 + s] = inp["s5_b_re"][i][g, p_, n]
                    out[i, yc, r, 1, sc * 128 + s] = inp["s5_b_im"][i][g, p_, n]
                    out[i, yc, s, 2, sc * 128 + r] = inp["s5_c_re"][i][g, n, p_]
                    out[i, yc, s, 3, sc * 128 + r] = inp["s5_c_im"][i][g, n, p_]
                out[i, yc, :, 4, sc * 128:(sc + 1) * 128] = inp["s5_a_re"][i][g, p_][None, :]
                out[i, yc, :, 5, sc * 128:(sc + 1) * 128] = inp["s5_a_im"][i][g, p_][None, :]
                out[i, yc, :, 6, sc * 128:(sc + 1) * 128] = inp["s5_log_dt"][i][g][None, :]
    return out.reshape(2, 4, 128, 7 * 512)


def unpack_ab(o, core, sl, p_s5re, p_s5im, p_ret, s_s5re, s_s5im, s_ret):
    p_ret[:, core] = o["o_pret"].reshape(2, 4, 64, 128)
    s_ret[:, sl] = o["o_sret"].reshape(2, 4, 64, NSS, 128).transpose(0, 3, 1, 2, 4)
    ps5 = o["o_ps5"].transpose(0, 1, 3, 2).reshape(2, 2, 32, 64)
    p_s5re[:, core] = ps5[:, 0]
    p_s5im[:, core] = ps5[:, 1]
    ss5 = o["o_ss5"].transpose(0, 1, 4, 3, 2).reshape(2, 2, NSS, 32, 64)
    s_s5re[:, sl] = ss5[:, 0]
    s_s5im[:, sl] = ss5[:, 1]


def prepare_inputs(inp, subs):
    f = np.float32
    plan = make_plan(subs)
    wts = pack_weights(plan, inp)
    cst = build_cst(inp)
    has_gla = any(k == "mix" and l % 2 == 1 for (l, k) in subs)
    has_ab = any(k == "mix" and l % 2 == 0 for (l, k) in subs)
    if has_ab:
        rot = rot_tables()
        s5w = s5w_pack(inp)
    maps = []
    for core in range(8):
        xp = np.concatenate([inp["meta_tokens"], inp["x_prompt"][core]], axis=0)
        xs = inp["x_sample"][core * NSS:(core + 1) * NSS].reshape(NSAMP, D)
        xT = np.ascontiguousarray(np.concatenate([xp, xs], axis=0).T).astype(f)
        m = {"xT": xT, "cst": cst, "wts": wts}
        sl = slice(core * NSS, (core + 1) * NSS)
        if has_gla:
            m["st_gla"] = np.ascontiguousarray(inp["state_gla"][:, sl].transpose(0, 2, 3, 1, 4)).astype(f)
        if has_ab:
            m["rot_d"] = rot
            m["s5w"] = s5w
            re = inp["state_s5_re"][:, sl].reshape(2, NSS, 16, 128).transpose(0, 3, 2, 1)
            im = inp["state_s5_im"][:, sl].reshape(2, NSS, 16, 128).transpose(0, 3, 2, 1)
            m["st_s5"] = np.ascontiguousarray(np.stack([re, im], axis=1)).astype(f)
            m["st_ret"] = np.ascontiguousarray(
                inp["state_ret"][:, sl].reshape(2, NSS, 2, 128, 128).transpose(0, 2, 3, 1, 4)).astype(f)
        maps.append(m)
    return maps


_CACHE = {}
DEBUG_OUT = {}


def kernel(**inputs):
    inputs = {k: np.asarray(v) for k, v in inputs.items()}
    ncores = int(os.environ.get("KCORES", "8"))
    subs = parse_subs()
    maps = prepare_inputs(inputs, subs)[:ncores]
    key = tuple(subs)
    if key not in _CACHE:
        _CACHE[key] = build_program(subs)
    P = _CACHE[key]
    if os.environ.get("KTRACE"):
        res = run_bass_kernel_spmd(P.nc, maps, core_ids=list(range(ncores)), trace=True)
        print("KTRACE exec_time_ns", res.exec_time_ns)
    else:
        res = run_bass_kernel_spmd(P.nc, maps, core_ids=list(range(ncores)))
    outs = res.results
    yp = np.zeros((8, SEQ, D), np.float32)
    ys = np.zeros((128, DEC_SEQ, D), np.float32)
    p_s5re = np.zeros((2, 8, 32, 64), np.float32)
    p_s5im = np.zeros((2, 8, 32, 64), np.float32)
    p_ret = np.zeros((2, 8, 4, 64, 128), np.float32)
    p_gla = np.zeros((2, 8, 4, 128, 256), np.float32)
    s_s5re = np.zeros((2, 128, 32, 64), np.float32)
    s_s5im = np.zeros((2, 128, 32, 64), np.float32)
    s_ret = np.zeros((2, 128, 4, 64, 128), np.float32)
    s_gla = np.zeros((2, 128, 4, 128, 256), np.float32)
    for core in range(ncores):
        o = outs[core]
        y = o["yT"].T
        yp[core] = y[N_META:LP]
        sl = slice(core * NSS, (core + 1) * NSS)
        ys[sl] = y[LP:].reshape(NSS, DEC_SEQ, D)
        if "o_pgla" in o:
            p_gla[:, core] = o["o_pgla"]
            s_gla[:, sl] = o["o_sgla"].transpose(0, 3, 1, 2, 4)
        if "o_pret" in o:
            unpack_ab(o, core, sl, p_s5re, p_s5im, p_ret, s_s5re, s_s5im, s_ret)
    DEBUG_OUT.clear()
    DEBUG_OUT.update({"p_gla": p_gla, "s_gla": s_gla, "p_ret": p_ret, "s_ret": s_ret,
                      "p_s5re": p_s5re, "p_s5im": p_s5im, "s_s5re": s_s5re, "s_s5im": s_s5im})
    return (yp, ys, p_s5re, p_s5im, p_ret, p_gla, s_s5re, s_s5im, s_ret, s_gla)
```

```python
import os
import numpy as np
import concourse.bass as bass
import concourse.mybir as mybir
from concourse.bass_utils import run_bass_kernel_spmd

F32 = mybir.dt.float32
BF16 = mybir.dt.bfloat16
AF = mybir.ActivationFunctionType
ALU = mybir.AluOpType

D = 1024
NCH = 8
DEPTH = 4
SEQ = 2048
N_META = 16
LP = SEQ + N_META
NSS = 16
DEC_SEQ = 8
NSAMP = NSS * DEC_SEQ
T = LP + NSAMP
PAST_LEN = 16384
D_FF = 2816
NFF = 22
EPS = 1e-6
TILES = [(0, 512), (512, 1024), (1024, 1536), (1536, 2048), (2048, 2192)]
NT = len(TILES)
SLOT = 3072
NSLOT = 5
SEM_LIMIT = 30000


class Sem:
    __slots__ = ("h", "count", "dma")

    def __init__(self, h, dma):
        self.h = h
        self.count = 0
        self.dma = dma


class Region:
    __slots__ = ("name", "w", "r")

    def __init__(self, name):
        self.name = name
        self.w = None
        self.r = {}


class Sched:
    ENG = ("pe", "act", "dve", "pool", "sp")

    def __init__(self, nc):
        self.nc = nc
        self.rec = {e: [] for e in self.ENG}
        self.cur = {}
        self.waited = {e: {} for e in self.ENG}
        self.nsem = 0
        for e in self.ENG:
            self.cur[e] = self.new_sem(e, False)
        self.all_dma_sems = []
        self.pe_pending = False

    def new_sem(self, name, dma):
        self.nsem += 1
        s = Sem(self.nc.alloc_semaphore("s_%s_%d" % (name, self.nsem)), dma)
        if dma:
            self.all_dma_sems.append(s)
        return s

    def dma_sem(self, name):
        s = Sem(self.nc.alloc_semaphore("d_%s_%d" % (name, self.nsem)), True)
        self.nsem += 1
        return s

    def _deps(self, reads, writes):
        deps = {}

        def add(tok):
            if tok is None:
                return
            s, v = tok
            if s.dma:
                v = s.count
            k = id(s)
            if k not in deps or deps[k][1] < v:
                deps[k] = (s, v)

        for r in reads:
            add(r.w)
        for w in writes:
            add(w.w)
            for tok in w.r.values():
                add(tok)
        return deps

    def op(self, eng, fn, reads=(), writes=(), dsem=None, signal=True):
        if eng != "pe":
            assert not self.pe_pending, "non-PE op recorded inside an unsignalled PE group"
        else:
            self.pe_pending = not signal
        deps = self._deps(reads, writes)
        waits = []
        wd = self.waited[eng]
        mysem = self.cur[eng]
        for k, (s, v) in deps.items():
            if eng == "pe" and s is mysem:
                continue
            if wd.get(k, 0) >= v:
                continue
            wd[k] = v
            waits.append((s.h, v))
        if dsem is not None:
            dsem.count += 16
            tok = (dsem, dsem.count)
            inc = (dsem.h, 16)
        elif signal:
            if mysem.count >= SEM_LIMIT:
                mysem = self.new_sem(eng, False)
                self.cur[eng] = mysem
            mysem.count += 1
            tok = (mysem, mysem.count)
            inc = (mysem.h, 1)
        else:
            tok = (mysem, mysem.count + 1)
            inc = None
        self.rec[eng].append((waits, fn, inc))
        for w in writes:
            w.w = tok
            w.r = {}
        for r in reads:
            k = id(tok[0])
            if k not in r.r or r.r[k][1] < tok[1]:
                r.r[k] = tok
        return tok

    def barrier(self, io_sems=()):
        toks = [(self.cur[e], self.cur[e].count) for e in ("pe", "act", "dve", "pool")]
        toks += [(s, s.count) for s in io_sems]
        for e in ("pe", "act", "dve", "pool", "sp"):
            waits = []
            for (s, v) in toks:
                if v == 0:
                    continue
                if self.waited[e].get(id(s), 0) >= v:
                    continue
                self.waited[e][id(s)] = v
                waits.append((s.h, v))
            if waits:
                self.rec[e].append((waits, None, None))

    def replay(self, final_waits):
        nc = self.nc
        handles = {"pe": "tensor", "act": "scalar", "dve": "vector", "pool": "gpsimd", "sp": "sync"}
        with nc.Block() as block:
            for e in self.ENG:
                rec = self.rec[e]
                fw = final_waits if e == "sp" else []

                def body(h, rec=rec, fw=fw):
                    for (waits, fn, inc) in rec:
                        for (sh, v) in waits:
                            h.wait_ge(sh, v)
                        if fn is not None:
                            ins = fn(h)
                            if inc is not None:
                                ins.then_inc(inc[0], inc[1])
                    for (sh, v) in fw:
                        h.wait_ge(sh, v)

                getattr(block, handles[e])(body)


class Prog:
    def __init__(self):
        nc = bass.Bass("TRN2", target_bir_lowering=False)
        self.nc = nc
        self.S = Sched(nc)
        self.loads = []
        self.wcols = 0
        self.dram_in = {}
        self.dram_out = {}

    def din(self, name, shape):
        t = self.nc.dram_tensor(name, list(shape), F32, kind="ExternalInput").ap()
        self.dram_in[name] = t
        return t

    def dout(self, name, shape):
        t = self.nc.dram_tensor(name, list(shape), F32, kind="ExternalOutput").ap()
        self.dram_out[name] = t
        return t

    def sb(self, name, shape, dt):
        return self.nc.alloc_sbuf_tensor(name, list(shape), dt)

    def init_psum(self):
        self.banks = [self.nc.alloc_psum_tensor("ps%d" % i, [128, 512], F32) for i in range(8)]
        self.bank_r = [Region("ps%d" % i) for i in range(8)]
        self.bank_i = 0
        self.rot = list(range(8))

    def bank(self):
        i = self.rot[self.bank_i % len(self.rot)]
        self.bank_i += 1
        return self.banks[i], self.bank_r[i]

    def init_wstream(self, wts_ap):
        self.wts = wts_ap
        self.slots = [self.sb("wslot%d" % i, [128, SLOT], BF16) for i in range(NSLOT)]
        self.slot_r = [Region("wslot%d" % i) for i in range(NSLOT)]
        self.slot_sem = [self.S.dma_sem("w%d" % i) for i in range(NSLOT)]
        self.issued = 0
        self.released = set()
        self.next_get = 0

    def _pump(self):
        while self.issued < len(self.loads):
            j = self.issued
            if j >= NSLOT and (j - NSLOT) not in self.released:
                break
            off, n = self.loads[j]
            s = j % NSLOT
            dst = self.slots[s][:, 0:n]
            src = self.wts[:, off:off + n]
            self.S.op("pool", lambda h, dst=dst, src=src: h.dma_start(out=dst, in_=src),
                      writes=[self.slot_r[s]], dsem=self.slot_sem[s])
            self.issued += 1

    def wget(self):
        j = self.next_get
        self.next_get += 1
        self._pump()
        assert j < self.issued, "weight stream stalled (load %d not issuable: release missing)" % j
        return self.slots[j % NSLOT], self.slot_r[j % NSLOT], j

    def wrelease(self, j):
        self.released.add(j)
        self._pump()


KAB = os.environ.get("KAB", "rs")


def parse_subs():
    spec = os.environ.get("KSUBS")
    if spec:
        out = []
        for tok in spec.split(","):
            layer = int(tok[0])
            kind = {"f1": "ffn1", "m": "mix", "f2": "ffn2"}[tok[1:]]
            out.append((layer, kind))
        return out
    return [(l, k) for l in range(DEPTH) for k in ("ffn1", "mix", "ffn2")]


def make_plan(subs):
    plan = []
    for (layer, kind) in subs:
        i = layer // 2
        if kind == "ffn1" or kind == "ffn2":
            for j in range(NFF):
                plan.append((("ffn", layer, 0 if kind == "ffn1" else 1, j), SLOT))
        elif layer % 2 == 1:
            for hd in range(4):
                plan.append((("gla", i, hd, 0), 2048 + 128))
                plan.append((("gla", i, hd, 1), 3072))
                plan.append((("gla", i, hd, 2), 3072))
        else:
            if "r" in KAB:
                for pr in range(2):
                    plan.append((("ret", i, pr, 0), 3072))
                    plan.append((("ret", i, pr, 1), 3072))
                    plan.append((("ret", i, pr, 2), 2048))
                    plan.append((("ret", i, pr, 3), 2048))
            if "s" in KAB:
                for yc in range(4):
                    plan.append((("s5u", i, yc), 1024))
                plan.append((("s5g", i), 2048))
                plan.append((("s5o", i, 0), 2048))
                plan.append((("s5o", i, 1), 2048))
    return plan


class Layout:
    def __init__(self):
        self.off = {}
        self.n = 0

    def add(self, name, ncols):
        self.off[name] = (self.n, ncols)
        self.n += ncols

    def __getitem__(self, name):
        return self.off[name]


def cst_layout():
    L = Layout()
    L.add("gains", 13 * NCH)
    L.add("gla_b", 2 * 4)
    L.add("gla_g", 2 * 2)
    L.add("s5_d", 2 * 4)
    L.add("one", 1)
    L.add("hm", 2)
    L.add("cmask", 512)
    L.add("cmask4", 144)
    L.add("ident32", 128)
    L.add("nident32", 128)
    L.add("causal", 128)
    L.add("samp", 128)
    L.add("seqm", 16)
    L.add("ident", 128)
    L.add("w2", 2 * 512)
    L.add("jvec", 256)
    L.add("ret_gam", 18)
    L.add("s5sm", 2 * 3 * 16)
    return L


def build_program(subs):
    P = Prog()
    nc, S = P.nc, P.S
    plan = make_plan(subs)
    CL = cst_layout()
    has_gla = any(k == "mix" and l % 2 == 1 for (l, k) in subs)
    has_ab = any(k == "mix" and l % 2 == 0 for (l, k) in subs)
    WCOLS = max(sum(n for (_, n) in plan), 1)
    xT = P.din("xT", [D, T])
    cst = P.din("cst", [128, CL.n])
    wts = P.din("wts", [128, WCOLS])
    yT = P.dout("yT", [D, T])
    if has_gla:
        st_gla = P.din("st_gla", [2, 4, 128, NSS, 256])
        o_pgla = P.dout("o_pgla", [2, 4, 128, 256])
        o_sgla = P.dout("o_sgla", [2, 4, 128, NSS, 256])
    if has_ab:
        rot_d = P.din("rot_d", [2, NT, 128, 4, 512])
        s5w = P.din("s5w", [2, 4, 128, 7 * 512])
        st_s5 = P.din("st_s5", [2, 2, 128, 16, NSS])
        st_ret = P.din("st_ret", [2, 2, 128, NSS, 128])
        o_pret = P.dout("o_pret", [2, 2, 128, 128])
        o_sret = P.dout("o_sret", [2, 2, 128, NSS, 128])
        o_ps5 = P.dout("o_ps5", [2, 2, 128, 16])
        o_ss5 = P.dout("o_ss5", [2, 2, 128, 16, NSS])

    x = P.sb("x", [128, NCH, T], F32)
    hb = P.sb("hb", [128, NCH, T], BF16)
    NCF = CL["w2"][0]
    NCF = CL["causal"][0]
    cf = P.sb("cf", [128, NCF], F32)
    jv = P.sb("jv", [128, 256 + 18 + 96], F32)
    cb = P.sb("cb", [128, 128 + 128 + 16 + 128 + 1024], BF16)
    ones_bf = P.sb("ones_bf", [128, 128], BF16)
    eps_t = P.sb("eps_t", [128, 1], F32)
    ones32 = P.sb("ones32", [128, 128], F32)
    ARENA = 16384
    arena = P.sb("arena", [128, ARENA], F32)
    x_r = [[Region("x%d_%d" % (c, t)) for t in range(NT)] for c in range(NCH)]
    h_r = [[Region("h%d_%d" % (c, t)) for t in range(NT)] for c in range(NCH)]
    cst_r = Region("cst")
    P.init_psum()
    P.init_wstream(wts)
    off = 0
    for (_, n) in plan:
        P.loads.append((off, n))
        off += n

    ar = {"off": 0, "cache": None}

    def aalloc(ncols_f32, dt=F32, name="a"):
        o = ar["off"]
        ar["off"] = o + (ncols_f32 + 15) // 16 * 16
        assert ar["off"] <= ARENA, "arena overflow %d" % ar["off"]
        if ar["cache"] is not None:
            key = (o, ncols_f32, dt)
            if key not in ar["cache"]:
                ap = arena[:, o:o + ncols_f32]
                if dt == BF16:
                    ap = ap.bitcast(BF16)
                ar["cache"][key] = (ap, Region(name))
            return ar["cache"][key]
        ap = arena[:, o:o + ncols_f32]
        if dt == BF16:
            ap = ap.bitcast(BF16)
        return ap, Region(name)

    io_sems = []

    def areset():
        S.barrier(io_sems)
        ar["off"] = 0
        ar["cache"] = None

    def areset_passes(first):
        if first:
            areset()
            ar["cache"] = {}
        else:
            ar["off"] = 0

    def ACT(out, in_, func, reads, writes, bias=None, scale=None):
        kw = {}
        if bias is not None:
            kw["bias"] = bias
        if scale is not None:
            kw["scale"] = scale
        return S.op("act", lambda h: h.activation(out=out, in_=in_, func=func, **kw), reads=reads, writes=writes)

    def TT(eng, out, in0, in1, op, reads, writes):
        return S.op(eng, lambda h: h.tensor_tensor(out=out, in0=in0, in1=in1, op=op), reads=reads, writes=writes)

    def TS(eng, out, in0, s1, op0, reads, writes, s2=None, op1=None):
        if op1 is None:
            return S.op(eng, lambda h: h.tensor_scalar(out=out, in0=in0, scalar1=s1, scalar2=None, op0=op0),
                        reads=reads, writes=writes)
        return S.op(eng, lambda h: h.tensor_scalar(out=out, in0=in0, scalar1=s1, scalar2=s2, op0=op0, op1=op1),
                    reads=reads, writes=writes)

    def STT(eng, out, in0, scalar, in1, op0, op1, reads, writes):
        return S.op(eng, lambda h: h.scalar_tensor_tensor(out=out, in0=in0, scalar=scalar, in1=in1, op0=op0, op1=op1),
                    reads=reads, writes=writes)

    def COPY(eng, out, in_, reads, writes):
        if eng == "act":
            return S.op("act", lambda h: h.copy(out=out, in_=in_), reads=reads, writes=writes)
        return S.op(eng, lambda h: h.tensor_copy(out=out, in_=in_), reads=reads, writes=writes)

    def MM(out, lhsT, rhs, start, stop, reads, writes, signal=None):
        if signal is None:
            signal = stop
        return S.op("pe", lambda h: h.matmul(out, lhsT=lhsT, rhs=rhs, start=start, stop=stop),
                    reads=reads, writes=writes, signal=signal)

    def DMA(eng, out, in_, reads, writes, sem):
        return S.op(eng, lambda h: h.dma_start(out=out, in_=in_), reads=reads, writes=writes, dsem=sem)

    def SCAN(out, data0, data1, reads, writes, initial=0.0):
        return S.op("dve", lambda h: h.tensor_tensor_scan(out=out, data0=data0, data1=data1, initial=initial,
                                                          op0=ALU.mult, op1=ALU.add), reads=reads, writes=writes)

    def RECIP(out, in_, reads, writes):
        return S.op("dve", lambda h: h.reciprocal(out=out, in_=in_), reads=reads, writes=writes)

    def TRANSP(out, in_, reads, writes):
        return S.op("pe", lambda h: h.transpose(out=out, in_=in_, identity=ident_bf), reads=reads, writes=writes)

    def MEMSET(eng, out, val, writes):
        return S.op(eng, lambda h: h.memset(out, val), writes=writes)

    def ccol(name, j=0, n=1):
        o, _ = CL[name]
        return cf[:, o + j:o + j + n]

    in_sem = S.dma_sem("in")
    io_sems.append(in_sem)
    DMA("sp", cf[:], cst[:, 0:NCF], [], [cst_r], in_sem)
    for c in range(NCH):
        DMA("sp", x[:, c, :], xT[c * 128:(c + 1) * 128, :], [], [x_r[c][t] for t in range(NT)], in_sem)
    nb16 = 128 + 128 + 16 + 128 + 1024
    stg, stg_r = aalloc(nb16 + 256 + 18 + 96, F32, "stg")
    o_c = CL["causal"][0]
    DMA("sp", stg, cst[:, o_c:o_c + nb16 + 256 + 18 + 96], [], [stg_r], in_sem)
    COPY("dve", cb[:], stg[:, 0:nb16], [stg_r], [cst_r])
    COPY("dve", jv[:], stg[:, nb16:nb16 + 256 + 18 + 96], [stg_r], [cst_r])
    S.op("dve", lambda h: h.memset(ones_bf[:], 1.0), writes=[cst_r])
    S.op("dve", lambda h: h.memset(eps_t[:], EPS), writes=[cst_r])
    S.op("dve", lambda h: h.memset(ones32[:], 1.0), writes=[cst_r])
    causal_bf = cb[:, 0:128]
    samp_bf = cb[:, 128:256]
    seqm_bf = cb[:, 256:272]
    ident_bf = cb[:, 272:400]
    w2_bf = cb[:, 400:1424]
    jvec = jv[:, 0:256]
    areset()

    cnt = {"sq": 0, "rstd": 0, "sg": 0}
    nt_ = {}

    def norm_alloc():
        nt_["sq"] = [aalloc(256, BF16, "sq%d" % i) for i in range(4)]
        nt_["rstd"] = [aalloc(512, F32, "rstd%d" % i) for i in range(2)]

    def norm_tile(t, gidx, final=False):
        sq = [a for (a, _) in nt_["sq"]]
        sq_r = [r for (_, r) in nt_["sq"]]
        rstd = [a for (a, _) in nt_["rstd"]]
        rstd_r = [r for (_, r) in nt_["rstd"]]
        t0, t1 = TILES[t]
        n = t1 - t0
        pb, pr = P.bank()
        for c in range(NCH):
            i = cnt["sq"] % 4
            cnt["sq"] += 1
            ACT(sq[i][:, 0:n], x[:, c, t0:t1], AF.Square, [x_r[c][t]], [sq_r[i]])
            MM(pb[:, 0:n], ones_bf[:], sq[i][:, 0:n], c == 0, c == NCH - 1, [sq_r[i], cst_r], [pr], signal=True)
        k = cnt["rstd"] % 2
        cnt["rstd"] += 1
        ACT(rstd[k][:, 0:n], pb[:, 0:n], AF.Sqrt, [pr, cst_r], [rstd_r[k]], bias=eps_t[:, 0:1], scale=1.0 / D)
        RECIP(rstd[k][:, 0:n], rstd[k][:, 0:n], [rstd_r[k]], [rstd_r[k]])
        for c in range(NCH):
            g = ccol("gains", gidx * NCH + c)
            if final:
                STT("dve", x[:, c, t0:t1], x[:, c, t0:t1], g, rstd[k][:, 0:n], ALU.mult, ALU.mult,
                    [x_r[c][t], rstd_r[k], cst_r], [x_r[c][t]])
            else:
                STT("dve", hb[:, c, t0:t1], x[:, c, t0:t1], g, rstd[k][:, 0:n], ALU.mult, ALU.mult,
                    [x_r[c][t], rstd_r[k], cst_r], [h_r[c][t]])

    def ffn(gidx):
        areset()
        norm_alloc()
        sg_a = [aalloc(256, BF16, "sg%d" % i) for i in range(2)]
        hid_a = [[aalloc(256, BF16, "hid%d_%d" % (i, j)) for j in range(2)] for i in range(2)]
        sg = [a for (a, _) in sg_a]
        sg_r = [r for (_, r) in sg_a]
        hid = [[a for (a, _) in row] for row in hid_a]
        hid_r = [[r for (_, r) in row] for row in hid_a]
        for t in range(NT):
            norm_tile(t, gidx)
        steps = [(g, t) for g in range(NFF // 2) for t in range(NT)]
        slots = {}

        def A(si, jj):
            g, t = steps[si]
            t0, t1 = TILES[t]
            n = t1 - t0
            if t == 0:
                slots[(g, jj)] = P.wget()
            ws, wr, _ = slots[(g, jj)]
            pg, pgr = P.bank()
            pu, pur = P.bank()
            for (pb, pr, off) in ((pg, pgr, 0), (pu, pur, 1024)):
                for kc in range(NCH):
                    MM(pb[:, 0:n], ws[:, off + kc * 128: off + (kc + 1) * 128], hb[:, kc, t0:t1],
                       kc == 0, kc == NCH - 1, [wr, h_r[kc][t]], [pr])
            i = cnt["sg"] % 2
            cnt["sg"] += 1
            ACT(sg[i][:, 0:n], pg[:, 0:n], AF.Silu, [pgr], [sg_r[i]])
            TT("dve", hid[si % 2][jj][:, 0:n], sg[i][:, 0:n], pu[:, 0:n], ALU.mult, [sg_r[i], pur], [hid_r[si % 2][jj]])

        def B(si, half):
            g, t = steps[si]
            t0, t1 = TILES[t]
            n = t1 - t0
            for m in range(half * 4, half * 4 + 4):
                pb, pr = P.bank()
                for jj in range(2):
                    ws, wr, _ = slots[(g, jj)]
                    MM(pb[:, 0:n], ws[:, 2048 + m * 128: 2048 + (m + 1) * 128], hid[si % 2][jj][:, 0:n],
                       jj == 0, jj == 1, [wr, hid_r[si % 2][jj]], [pr])
                STT("dve", x[:, m, t0:t1], pb[:, 0:n], 0.5, x[:, m, t0:t1], ALU.mult, ALU.add,
                    [pr, x_r[m][t]], [x_r[m][t]])
            if half == 1 and t == NT - 1:
                for jj in range(2):
                    P.wrelease(slots[(g, jj)][2])

        ns = len(steps)
        A(0, 0)
        A(0, 1)
        for si in range(1, ns):
            A(si, 0)
            B(si - 1, 0)
            A(si, 1)
            B(si - 1, 1)
        B(ns - 1, 0)
        B(ns - 1, 1)

    def tile_blocks(t):
        if t < NT - 1:
            return [(b * 128, 128, "p") for b in range(4)]
        return [(0, 16, "p"), (16, 128, "s")]

    XOFF = int(os.environ.get("KXOFF", "0"))

    def gla_pass(i, hd, st_sem, out_sem):
        DK = 128
        w1, w1r, j1 = P.wget()
        w2s, w2r, j2 = P.wget()
        w3, w3r, j3 = P.wget()
        Wq = lambda kc: w1[:, kc * 128:(kc + 1) * 128]
        Wk = lambda kc: w1[:, 1024 + kc * 128:1024 + (kc + 1) * 128]
        Wlr = lambda kc: w1[:, 2048 + kc * 16:2048 + (kc + 1) * 16]
        Wv = lambda kc: w2s[:, kc * 256:(kc + 1) * 256]
        Wr = lambda cc, kc: (w2s[:, 2048 + kc * 128:2048 + (kc + 1) * 128] if cc == 0
                             else w3[:, kc * 128:(kc + 1) * 128])
        Wrr = lambda cc: (w2r if cc == 0 else w3r)
        Wo = lambda cc, m: w3[:, 1024 + cc * 1024 + m * 128:1024 + cc * 1024 + (m + 1) * 128]
        SETS = []
        for sidx in range(2):
            d = {}
            for nm, ncol, dt in (("lrb", 256, BF16), ("ee", 512, F32), ("Bc", 512, F32), ("E", 512, F32), ("Ei", 512, F32),
                                 ("qt", 256, BF16), ("kt", 256, BF16)):
                d[nm], d[nm + "_r"] = aalloc(ncol, dt, nm + str(sidx))
            for b in range(4):
                d["vtok%d" % b], d["vtok%d_r" % b] = aalloc(128, BF16, "vtok")
                d["ktok%d" % b], d["ktok%d_r" % b] = aalloc(64, BF16, "ktok")
                d["Am%d" % b], d["Am%d_r" % b] = aalloc(64, BF16, "Am")
            SETS.append(d)
        SR = []
        for q in range(3):
            SR.append([aalloc(256, BF16, "sr%d_%d" % (q, cc)) for cc in range(2)])
        S32, S32_r = aalloc(256, F32, "S32")
        Sbf, Sbf_r = aalloc(128, BF16, "Sbf")
        sqo, sqo_r = [], []
        og, og_r, og2, og2_r = [], [], [], []
        for cc in range(2):
            a, r = aalloc(256, BF16, "sqo%d" % cc)
            sqo.append(a)
            sqo_r.append(r)
            a, r = aalloc(512, F32, "og%d" % cc)
            og.append(a)
            og_r.append(r)
            a, r = aalloc(256, BF16, "og2%d" % cc)
            og2.append(a)
            og2_r.append(r)
        rs, rs_r = aalloc(512, F32, "rs")
        S0, S0_r = aalloc(2048, F32, "S0")
        S0b, S0b_r = aalloc(1024, BF16, "S0b")
        kmask, kmask_r = aalloc(512, BF16, "kmask")
        negb, negb_r = aalloc(1, F32, "negb")
        xtmp = [aalloc(512, F32, "xtmp%d" % q) for q in range(2)] if XOFF else None

        TS("dve", negb, ccol("gla_b", i * 4 + hd), -1.0, ALU.mult, [cst_r], [negb_r])
        MEMSET("dve", S32, 0.0, [S32_r])
        MEMSET("dve", Sbf, 0.0, [Sbf_r])
        PO = [[P.banks[0], P.banks[1]], [P.banks[2], P.banks[3]]]
        PO_r = [[P.bank_r[0], P.bank_r[1]], [P.bank_r[2], P.bank_r[3]]]

        def stage1(t):
            d = SETS[t % 2]
            t0, t1 = TILES[t]
            n = t1 - t0
            blocks = tile_blocks(t)
            sl = []

            def s_decay():
                pb, pr = P.bank()
                for kc in range(NCH):
                    MM(pb[0:16, 0:n], Wlr(kc), hb[:, kc, t0:t1], kc == 0, kc == NCH - 1, [w1r, h_r[kc][t]], [pr])
                COPY("act", d["lrb"][0:16, 0:n], pb[0:16, 0:n], [pr], [d["lrb_r"]])
                pz, pzr = P.bank()
                MM(pz[:, 0:n], w2_bf[0:16, i * 512 + hd * 128:i * 512 + (hd + 1) * 128], d["lrb"][0:16, 0:n], True, True,
                   [d["lrb_r"], cst_r], [pzr])
                ACT(d["ee"][:, 0:n], pz[:, 0:n], AF.Exp, [pzr, negb_r], [d["ee_r"]], bias=negb[:, 0:1], scale=-1.0)
                ACT(d["ee"][:, 0:n], d["ee"][:, 0:n], AF.Ln, [d["ee_r"], cst_r], [d["ee_r"]], bias=ccol("one"), scale=1.0)
                cm = ccol("cmask", 0, n) if t < NT - 1 else ccol("cmask4", 0, n)
                SCAN(d["Bc"][:, 0:n], cm, d["ee"][:, 0:n], [d["ee_r"], cst_r], [d["Bc_r"]])
                ACT(d["E"][:, 0:n], d["Bc"][:, 0:n], AF.Exp, [d["Bc_r"]], [d["E_r"]], scale=-1.0 / 16.0)
                ACT(d["Ei"][:, 0:n], d["Bc"][:, 0:n], AF.Exp, [d["Bc_r"]], [d["Ei_r"]], scale=1.0 / 16.0)
            sl.append(s_decay)

            def s_q():
                pq, pqr = P.bank()
                for kc in range(NCH):
                    MM(pq[:, 0:n], Wq(kc), hb[:, kc, t0:t1], kc == 0, kc == NCH - 1, [w1r, h_r[kc][t]], [pqr])
                TT("dve", d["qt"][:, 0:n], pq[:, 0:n], d["E"][:, 0:n], ALU.mult, [pqr, d["E_r"]], [d["qt_r"]])
            sl.append(s_q)

            def s_k():
                pk, pkr = P.bank()
                for kc in range(NCH):
                    MM(pk[:, 0:n], Wk(kc), hb[:, kc, t0:t1], kc == 0, kc == NCH - 1, [w1r, h_r[kc][t]], [pkr])
                STT("dve", d["kt"][:, 0:n], pk[:, 0:n], DK ** -0.5, d["Ei"][:, 0:n], ALU.mult, ALU.mult,
                    [pkr, d["Ei_r"]], [d["kt_r"]])
            sl.append(s_k)

            def mk_gate(cc):
                def s_gate():
                    pg, pgr = P.bank()
                    for kc in range(NCH):
                        MM(pg[:, 0:n], Wr(cc, kc), hb[:, kc, t0:t1], kc == 0, kc == NCH - 1, [Wrr(cc), h_r[kc][t]], [pgr])
                    ACT(SR[t % 3][cc][0][:, 0:n], pg[:, 0:n], AF.Silu, [pgr], [SR[t % 3][cc][1]])
                return s_gate
            sl.append(mk_gate(0))
            sl.append(mk_gate(1))

            def mk_tok(b, c0, nb):
                def s_tok():
                    pv, pvr = P.bank()
                    for kc in range(NCH):
                        MM(pv[0:nb, 0:256], hb[:, kc, t0 + c0:t0 + c0 + nb], Wv(kc), kc == 0, kc == NCH - 1,
                           [w2r, h_r[kc][t]], [pvr])
                    COPY("act", d["vtok%d" % b][0:nb, 0:256], pv[0:nb, 0:256], [pvr], [d["vtok%d_r" % b]])
                    pt, ptr = P.bank()
                    ptb = pt[:, 0:64].bitcast(BF16)
                    TRANSP(ptb[0:nb, 0:128], d["kt"][:, c0:c0 + nb], [d["kt_r"], cst_r], [ptr])
                    COPY("act", d["ktok%d" % b][0:nb, 0:128], ptb[0:nb, 0:128], [ptr], [d["ktok%d_r" % b]])
                    ps_, psr = P.bank()
                    MM(ps_[0:nb, 0:nb], d["kt"][:, c0:c0 + nb], d["qt"][:, c0:c0 + nb], True, True,
                       [d["kt_r"], d["qt_r"]], [psr])
                    mk = causal_bf if blocks[b][2] == "p" else samp_bf
                    TT("dve", d["Am%d" % b][0:nb, 0:nb], ps_[0:nb, 0:nb], mk[0:nb, 0:nb], ALU.mult, [psr, cst_r],
                       [d["Am%d_r" % b]])
                return s_tok
            for b, (c0, nb, kind) in enumerate(blocks):
                sl.append(mk_tok(b, c0, nb))
            return sl

        def chain_steps(t):
            d = SETS[t % 2]
            po, po_r = PO[t % 2], PO_r[t % 2]
            t0, t1 = TILES[t]
            blocks = tile_blocks(t)
            steps = []

            def mk_p(b, c0, nb):
                def step():
                    vt_, vr_ = d["vtok%d" % b], d["vtok%d_r" % b]
                    kt_, kr_ = d["ktok%d" % b], d["ktok%d_r" % b]
                    am_, ar_ = d["Am%d" % b], d["Am%d_r" % b]
                    pU, pUr = P.bank()
                    MM(pU[:, 0:256], kt_[0:nb, 0:128], vt_[0:nb, 0:256], True, True, [kr_, vr_], [pUr])
                    for cc in range(2):
                        MM(po[cc][:, c0:c0 + nb], vt_[0:nb, cc * 128:(cc + 1) * 128], am_[0:nb, 0:nb], True, False,
                           [vr_, ar_], [po_r[cc]], signal=False)
                        MM(po[cc][:, c0:c0 + nb], Sbf[:, cc * 128:(cc + 1) * 128], d["qt"][:, c0:c0 + nb], False, True,
                           [Sbf_r, d["qt_r"]], [po_r[cc]], signal=True)
                    e = d["E"][:, c0 + nb - 1:c0 + nb]
                    TS("dve", S32, S32, e, ALU.mult, [S32_r, d["E_r"]], [S32_r])
                    STT("dve", S32, pU[:, 0:256], e, S32, ALU.mult, ALU.add, [pUr, d["E_r"], S32_r], [S32_r])
                    COPY("act", Sbf, S32, [S32_r], [Sbf_r])
                    if t == NT - 1:
                        DMA("sp", o_pgla[i, hd], S32, [S32_r], [], out_sem)
                return step

            def mk_s(b, c0, nb, half):
                def step():
                    vt_, vr_ = d["vtok%d" % b], d["vtok%d_r" % b]
                    kt_, kr_ = d["ktok%d" % b], d["ktok%d_r" % b]
                    am_, ar_ = d["Am%d" % b], d["Am%d_r" % b]
                    DMA("sp", S0.rearrange("p (s v) -> p s v", v=256), st_gla[i, hd, :, half * 8:(half + 1) * 8, :],
                        [], [S0_r], st_sem)
                    COPY("act", S0b, S0, [S0_r], [S0b_r])
                    TT("dve", kmask.rearrange("p (s k) -> p s k", k=128),
                       kt_[:, 0:128].unsqueeze(1).to_broadcast([128, 8, 128]),
                       seqm_bf[:, half * 8:(half + 1) * 8].unsqueeze(2).to_broadcast([128, 8, 128]),
                       ALU.mult, [kr_, cst_r], [kmask_r])
                    for s_ in range(8):
                        sq_i = half * 8 + s_
                        cs = c0 + sq_i * 8
                        for cc in range(2):
                            MM(po[cc][:, cs:cs + 8], vt_[0:nb, cc * 128:(cc + 1) * 128], am_[0:nb, sq_i * 8:sq_i * 8 + 8],
                               True, False, [vr_, ar_], [po_r[cc]], signal=False)
                            MM(po[cc][:, cs:cs + 8], S0b[:, s_ * 256 + cc * 128:s_ * 256 + (cc + 1) * 128],
                               d["qt"][:, cs:cs + 8], False, True, [S0b_r, d["qt_r"]], [po_r[cc]], signal=True)
                    for s_ in range(8):
                        sq_i = half * 8 + s_
                        cs = c0 + sq_i * 8
                        pU, pUr = P.bank()
                        MM(pU[:, 0:256], kmask[:, s_ * 128:(s_ + 1) * 128], vt_[0:nb, 0:256], True, True,
                           [kmask_r, vr_], [pUr])
                        e = d["E"][:, cs + 7:cs + 8]
                        S0s = S0[:, s_ * 256:(s_ + 1) * 256]
                        TS("dve", S0s, S0s, e, ALU.mult, [S0_r, d["E_r"]], [S0_r])
                        STT("dve", S0s, pU[:, 0:256], e, S0s, ALU.mult, ALU.add, [pUr, d["E_r"], S0_r], [S0_r])
                    DMA("sp", o_sgla[i, hd, :, half * 8:(half + 1) * 8, :], S0.rearrange("p (s v) -> p s v", v=256),
                        [S0_r], [], out_sem)
                return step

            for b, (c0, nb, kind) in enumerate(blocks):
                if kind == "p":
                    steps.append(mk_p(b, c0, nb))
                else:
                    steps.append(mk_s(b, c0, nb, 0))
                    steps.append(mk_s(b, c0, nb, 1))
            return steps

        def tail_steps(t):
            d = SETS[t % 2]
            t0, t1 = TILES[t]
            n = t1 - t0
            sr = [SR[t % 3][cc][0] for cc in range(2)]
            sr_r = [SR[t % 3][cc][1] for cc in range(2)]
            po, po_r = PO[t % 2], PO_r[t % 2]
            st = {}

            def T1():
                pn, pnr = P.bank()
                st["pn"] = (pn, pnr)
                for cc in range(2):
                    ACT(sqo[cc][:, 0:n], po[cc][:, 0:n], AF.Square, [po_r[cc]], [sqo_r[cc]])
                    MM(pn[:, 0:n], ones_bf[:], sqo[cc][:, 0:n], cc == 0, cc == 1, [sqo_r[cc], cst_r], [pnr], signal=True)

            def T2():
                pn, pnr = st["pn"]
                ACT(rs[:, 0:n], pn[:, 0:n], AF.Sqrt, [pnr, cst_r], [rs_r], bias=eps_t[:, 0:1], scale=1.0 / 256.0)
                RECIP(rs[:, 0:n], rs[:, 0:n], [rs_r], [rs_r])

            def T3():
                for cc in range(2):
                    TT("dve", og[cc][:, 0:n], po[cc][:, 0:n], rs[:, 0:n], ALU.mult, [po_r[cc], rs_r], [og_r[cc]])
                    STT("dve", og2[cc][:, 0:n], og[cc][:, 0:n], ccol("gla_g", i * 2 + cc), sr[cc][:, 0:n], ALU.mult, ALU.mult,
                        [og_r[cc], sr_r[cc], cst_r], [og2_r[cc]])

            def mk_T4(m0):
                def T4():
                    for m in range(m0, m0 + 4):
                        pm, pmr = P.bank()
                        for cc in range(2):
                            MM(pm[:, 0:n], Wo(cc, m), og2[cc][:, 0:n], cc == 0, cc == 1, [w3r, og2_r[cc]], [pmr])
                        TT("dve", x[:, m, t0:t1], pm[:, 0:n], x[:, m, t0:t1], ALU.add, [pmr, x_r[m][t]], [x_r[m][t]])
                return T4
            def T12():
                T1()
                T2()
            return [T12, T3, mk_T4(0), mk_T4(4)]

        for f in stage1(0):
            f()
        prev_tail = []
        for t in range(NT):
            fill = stage1(t + 1) if t + 1 < NT else []
            steps = chain_steps(t)
            merged = []
            a, b = list(fill), list(prev_tail)
            while a or b:
                if b:
                    merged.append(b.pop(0))
                if a:
                    merged.append(a.pop(0))
                if a:
                    merged.append(a.pop(0))
            nst = len(steps)
            fi = 0
            for si, st in enumerate(steps):
                st()
                want = (len(merged) * (si + 1) + nst - 1) // nst
                while fi < min(want, len(merged)):
                    merged[fi]()
                    fi += 1
            while fi < len(merged):
                merged[fi]()
                fi += 1
            prev_tail = tail_steps(t)
        for f in prev_tail:
            f()
        for j in (j1, j2, j3):
            P.wrelease(j)

    def gla_mixer(layer):
        i = layer // 2
        areset()
        norm_alloc()
        for t in range(NT):
            norm_tile(t, 4 + layer)
        P.rot = [4, 5, 6, 7]
        for hd in range(4):
            areset_passes(hd == 0)
            gla_pass(i, hd, st_sem, out_sem)
        areset()
        P.rot = list(range(8))

    KRET = int(os.environ.get("KRET", "9"))

    def ret_pass(i, pr, st_sem, out_sem):
        w1, w1r, j1 = P.wget()
        w2s, w2r, j2 = P.wget()
        w3, w3r, j3 = P.wget()
        w4, w4r, j4 = P.wget()
        Wq = lambda kc: w1[:, kc * 128:(kc + 1) * 128]
        Wqs = lambda kc: w1[:, 1024 + kc * 128:1024 + (kc + 1) * 128]
        Wk = lambda kc: w1[:, 2048 + kc * 128:2048 + (kc + 1) * 128]
        Wks = lambda kc: w2s[:, kc * 128:(kc + 1) * 128]
        Wv = lambda kc: w2s[:, 1024 + kc * 256:1024 + (kc + 1) * 256]
        Wg = lambda cc, kc: w3[:, cc * 1024 + kc * 128:cc * 1024 + (kc + 1) * 128]
        Wo = lambda cc, m: w4[:, cc * 1024 + m * 128:cc * 1024 + (m + 1) * 128]
        rot, rot_r = aalloc(2048, F32, "rot")
        tA, tA_r = aalloc(512, F32, "tA")
        tB, tB_r = aalloc(512, F32, "tB")
        kt, kt_r = aalloc(256, BF16, "kt")
        SETS = []
        for sidx in range(2):
            d = {}
            for hh in range(2):
                d["qth%d" % hh], d["qth%d_r" % hh] = aalloc(256, BF16, "qth")
            for b in range(4):
                d["vtok%d" % b], d["vtok%d_r" % b] = aalloc(128, BF16, "vtok")
                d["ktok%d" % b], d["ktok%d_r" % b] = aalloc(64, BF16, "ktok")
                d["Am%d" % b], d["Am%d_r" % b] = aalloc(128, BF16, "Am")
            SETS.append(d)
        SG = [[aalloc(256, BF16, "sgt%d_%d" % (q, cc)) for cc in range(2)] for q in range(3)]
        S32, S32_r = aalloc(128, F32, "S32")
        Sbf, Sbf_r = aalloc(64, BF16, "Sbf")
        obf, obf_r, o32, o32_r, csq, csq_r, og2, og2_r = [], [], [], [], [], [], [], []
        for cc in range(2):
            a, r = aalloc(256, BF16, "obf%d" % cc)
            obf.append(a)
            obf_r.append(r)
            a, r = aalloc(512, F32, "o32%d" % cc)
            o32.append(a)
            o32_r.append(r)
            a, r = aalloc(256, BF16, "csq%d" % cc)
            csq.append(a)
            csq_r.append(r)
            a, r = aalloc(256, BF16, "og2%d" % cc)
            og2.append(a)
            og2_r.append(r)
        rsh, rsh_r = [], []
        for hh in range(2):
            a, r = aalloc(512, F32, "rs%d" % hh)
            rsh.append(a)
            rsh_r.append(r)
        S0, S0_r = aalloc(1024, F32, "S0")
        S0b, S0b_r = aalloc(512, BF16, "S0b")
        kmask, kmask_r = aalloc(512, BF16, "kmask")
        MEMSET("dve", S32, 0.0, [S32_r])
        MEMSET("dve", Sbf, 0.0, [Sbf_r])
        PO = [[P.banks[0], P.banks[1]], [P.banks[2], P.banks[3]]]
        PO_r = [[P.bank_r[0], P.bank_r[1]], [P.bank_r[2], P.bank_r[3]]]
        gam = lambda kind, v: jv[:, 256 + (pr * 3 + kind) * 3 + v:256 + (pr * 3 + kind) * 3 + v + 1]

        def stage1(t):
            d = SETS[t % 2]
            t0, t1 = TILES[t]
            n = t1 - t0
            blocks = tile_blocks(t)
            sl = []

            def s_q():
                DMA("sp", rot.rearrange("p (a c) -> p a c", c=512), rot_d[pr, t], [], [rot_r], st_sem)
                cq, sq_ = rot[:, 0:n], rot[:, 512:512 + n]
                pa, par = P.bank()
                for kc in range(NCH):
                    MM(pa[:, 0:n], Wq(kc), hb[:, kc, t0:t1], kc == 0, kc == NCH - 1, [w1r, h_r[kc][t]], [par])
                pb, pbr = P.bank()
                for kc in range(NCH):
                    MM(pb[:, 0:n], Wqs(kc), hb[:, kc, t0:t1], kc == 0, kc == NCH - 1, [w1r, h_r[kc][t]], [pbr])
                for hh in range(2):
                    STT("dve", tA[:, 0:n], pa[:, 0:n], ccol("hm", hh), cq, ALU.mult, ALU.mult, [par, rot_r, cst_r], [tA_r])
                    STT("dve", tB[:, 0:n], pb[:, 0:n], ccol("hm", hh), sq_, ALU.mult, ALU.mult, [pbr, rot_r, cst_r], [tB_r])
                    TT("pool", d["qth%d" % hh][:, 0:n], tA[:, 0:n], tB[:, 0:n], ALU.add, [tA_r, tB_r], [d["qth%d_r" % hh]])
            sl.append(s_q)

            def s_k():
                ck, sk = rot[:, 1024:1024 + n], rot[:, 1536:1536 + n]
                pa, par = P.bank()
                for kc in range(NCH):
                    MM(pa[:, 0:n], Wk(kc), hb[:, kc, t0:t1], kc == 0, kc == NCH - 1, [w1r, h_r[kc][t]], [par])
                pb, pbr = P.bank()
                for kc in range(NCH):
                    MM(pb[:, 0:n], Wks(kc), hb[:, kc, t0:t1], kc == 0, kc == NCH - 1, [w2r, h_r[kc][t]], [pbr])
                TT("dve", tA[:, 0:n], pa[:, 0:n], ck, ALU.mult, [par, rot_r], [tA_r])
                TT("dve", tB[:, 0:n], pb[:, 0:n], sk, ALU.mult, [pbr, rot_r], [tB_r])
                TT("pool", kt[:, 0:n], tA[:, 0:n], tB[:, 0:n], ALU.add, [tA_r, tB_r], [kt_r])
            sl.append(s_k)

            def mk_gate(cc):
                def s_gate():
                    pg, pgr = P.bank()
                    for kc in range(NCH):
                        MM(pg[:, 0:n], Wg(cc, kc), hb[:, kc, t0:t1], kc == 0, kc == NCH - 1, [w3r, h_r[kc][t]], [pgr])
                    ACT(SG[t % 3][cc][0][:, 0:n], pg[:, 0:n], AF.Silu, [pgr], [SG[t % 3][cc][1]])
                return s_gate
            sl.append(mk_gate(0))
            sl.append(mk_gate(1))

            def mk_tok(b, c0, nb, kind):
                def s_tok():
                    pv, pvr = P.bank()
                    for kc in range(NCH):
                        MM(pv[0:nb, 0:256], hb[:, kc, t0 + c0:t0 + c0 + nb], Wv(kc), kc == 0, kc == NCH - 1,
                           [w2r, h_r[kc][t]], [pvr])
                    COPY("act", d["vtok%d" % b][0:nb, 0:256], pv[0:nb, 0:256], [pvr], [d["vtok%d_r" % b]])
                    pt, ptr = P.bank()
                    ptb = pt[:, 0:64].bitcast(BF16)
                    TRANSP(ptb[0:nb, 0:128], kt[:, c0:c0 + nb], [kt_r, cst_r], [ptr])
                    COPY("act", d["ktok%d" % b][0:nb, 0:128], ptb[0:nb, 0:128], [ptr], [d["ktok%d_r" % b]])
                    ps_, psr = P.bank()
                    for hh in range(2):
                        MM(ps_[0:nb, hh * 128:hh * 128 + nb], kt[:, c0:c0 + nb], d["qth%d" % hh][:, c0:c0 + nb], True, True,
                           [kt_r, d["qth%d_r" % hh]], [psr], signal=True)
                    mk = causal_bf if kind == "p" else samp_bf
                    for hh in range(2):
                        TT("dve", d["Am%d" % b][0:nb, hh * 128:hh * 128 + nb], ps_[0:nb, hh * 128:hh * 128 + nb],
                           mk[0:nb, 0:nb], ALU.mult, [psr, cst_r], [d["Am%d_r" % b]])
                return s_tok
            for b, (c0, nb, kind) in enumerate(blocks):
                sl.append(mk_tok(b, c0, nb, kind))
            return sl

        def chain_steps(t):
            d = SETS[t % 2]
            po, po_r = PO[t % 2], PO_r[t % 2]
            blocks = tile_blocks(t)
            steps = []

            def mk_p(b, c0, nb):
                def step():
                    vt_, vr_ = d["vtok%d" % b], d["vtok%d_r" % b]
                    kt_, kr_ = d["ktok%d" % b], d["ktok%d_r" % b]
                    am_, ar_ = d["Am%d" % b], d["Am%d_r" % b]
                    pU, pUr = P.bank()
                    for hh in range(2):
                        MM(pU[:, hh * 128:(hh + 1) * 128], kt_[0:nb, 0:128], vt_[0:nb, hh * 128:(hh + 1) * 128], True, True,
                           [kr_, vr_], [pUr], signal=True)
                    for hh in range(2):
                        MM(po[hh][:, c0:c0 + nb], vt_[0:nb, hh * 128:(hh + 1) * 128], am_[0:nb, hh * 128:hh * 128 + nb],
                           True, False, [vr_, ar_], [po_r[hh]], signal=False)
                        MM(po[hh][:, c0:c0 + nb], Sbf[:, 0:128], d["qth%d" % hh][:, c0:c0 + nb],
                           False, True, [Sbf_r, d["qth%d_r" % hh]], [po_r[hh]], signal=True)
                    kd = 0 if nb == 128 else 1
                    TS("dve", S32, S32, gam(kd, 0), ALU.mult, [S32_r, cst_r], [S32_r])
                    for hh in range(2):
                        STT("dve", S32, pU[:, hh * 128:(hh + 1) * 128], gam(kd, 1 + hh), S32,
                            ALU.mult, ALU.add, [pUr, S32_r, cst_r], [S32_r])
                    COPY("act", Sbf, S32, [S32_r], [Sbf_r])
                    if t == NT - 1:
                        DMA("sp", o_pret[i, pr], S32, [S32_r], [], out_sem)
                return step

            def mk_s(b, c0, nb, half):
                def step():
                    vt_, vr_ = d["vtok%d" % b], d["vtok%d_r" % b]
                    kt_, kr_ = d["ktok%d" % b], d["ktok%d_r" % b]
                    am_, ar_ = d["Am%d" % b], d["Am%d_r" % b]
                    DMA("sp", S0.rearrange("p (s v) -> p s v", v=128), st_ret[i, pr, :, half * 8:(half + 1) * 8, :],
                        [], [S0_r], st_sem)
                    COPY("act", S0b, S0, [S0_r], [S0b_r])
                    TT("dve", kmask.rearrange("p (s k) -> p s k", k=128),
                       kt_[:, 0:128].unsqueeze(1).to_broadcast([128, 8, 128]),
                       seqm_bf[:, half * 8:(half + 1) * 8].unsqueeze(2).to_broadcast([128, 8, 128]),
                       ALU.mult, [kr_, cst_r], [kmask_r])
                    for s_ in range(8):
                        sq_i = half * 8 + s_
                        cs = c0 + sq_i * 8
                        for hh in range(2):
                            MM(po[hh][:, cs:cs + 8], vt_[0:nb, hh * 128:(hh + 1) * 128],
                               am_[0:nb, hh * 128 + sq_i * 8:hh * 128 + sq_i * 8 + 8],
                               True, False, [vr_, ar_], [po_r[hh]], signal=False)
                            MM(po[hh][:, cs:cs + 8], S0b[:, s_ * 128:(s_ + 1) * 128], d["qth%d" % hh][:, cs:cs + 8],
                               False, True, [S0b_r, d["qth%d_r" % hh]], [po_r[hh]], signal=True)
                    for s_ in range(8):
                        pU, pUr = P.bank()
                        for hh in range(2):
                            MM(pU[:, hh * 128:(hh + 1) * 128], kmask[:, s_ * 128:(s_ + 1) * 128],
                               vt_[0:nb, hh * 128:(hh + 1) * 128], True, True, [kmask_r, vr_], [pUr], signal=True)
                        S0s = S0[:, s_ * 128:(s_ + 1) * 128]
                        TS("dve", S0s, S0s, gam(2, 0), ALU.mult, [S0_r, cst_r], [S0_r])
                        for hh in range(2):
                            STT("dve", S0s, pU[:, hh * 128:(hh + 1) * 128], gam(2, 1 + hh), S0s,
                                ALU.mult, ALU.add, [pUr, S0_r, cst_r], [S0_r])
                    DMA("sp", o_sret[i, pr, :, half * 8:(half + 1) * 8, :], S0.rearrange("p (s v) -> p s v", v=128),
                        [S0_r], [], out_sem)
                return step

            for b, (c0, nb, kind) in enumerate(blocks):
                if kind == "p":
                    steps.append(mk_p(b, c0, nb))
                else:
                    steps.append(mk_s(b, c0, nb, 0))
                    steps.append(mk_s(b, c0, nb, 1))
            return steps

        def tail_steps(t):
            t0, t1 = TILES[t]
            n = t1 - t0
            po, po_r = PO[t % 2], PO_r[t % 2]
            sgt = [SG[t % 3][cc][0] for cc in range(2)]
            sgt_r = [SG[t % 3][cc][1] for cc in range(2)]
            st = {}
            H2 = range(2)

            def T1():
                for hh in H2:
                    COPY("act", obf[hh][:, 0:n], po[hh][:, 0:n], [po_r[hh]], [obf_r[hh]])
                    COPY("act", o32[hh][:, 0:n], po[hh][:, 0:n], [po_r[hh]], [o32_r[hh]])
                st["pmn"] = []
                for hh in H2:
                    pm_, pmr_ = P.bank()
                    MM(pm_[:, 0:n], ones_bf[:], obf[hh][:, 0:n], True, True, [obf_r[hh], cst_r], [pmr_])
                    st["pmn"].append((pm_, pmr_))

            def T2():
                for hh in H2:
                    pm_, pmr_ = st["pmn"][hh]
                    STT("dve", o32[hh][:, 0:n], pm_[:, 0:n], -1.0 / 128.0, o32[hh][:, 0:n], ALU.mult, ALU.add,
                        [pmr_, o32_r[hh]], [o32_r[hh]])
                for hh in H2:
                    ACT(csq[hh][:, 0:n], o32[hh][:, 0:n], AF.Square, [o32_r[hh]], [csq_r[hh]])
                st["pvn"] = []
                for hh in H2:
                    pv_, pvr_ = P.bank()
                    MM(pv_[:, 0:n], ones_bf[:], csq[hh][:, 0:n], True, True, [csq_r[hh], cst_r], [pvr_])
                    st["pvn"].append((pv_, pvr_))

            def T3():
                for hh in H2:
                    pv_, pvr_ = st["pvn"][hh]
                    ACT(rsh[hh][:, 0:n], pv_[:, 0:n], AF.Sqrt, [pvr_, cst_r], [rsh_r[hh]], bias=eps_t[:, 0:1],
                        scale=1.0 / 128.0)
                for hh in H2:
                    RECIP(rsh[hh][:, 0:n], rsh[hh][:, 0:n], [rsh_r[hh]], [rsh_r[hh]])

            def T4():
                for hh in H2:
                    TT("dve", o32[hh][:, 0:n], o32[hh][:, 0:n], rsh[hh][:, 0:n], ALU.mult, [o32_r[hh], rsh_r[hh]], [o32_r[hh]])
                for hh in H2:
                    TT("pool", og2[hh][:, 0:n], o32[hh][:, 0:n], sgt[hh][:, 0:n], ALU.mult, [o32_r[hh], sgt_r[hh]],
                       [og2_r[hh]])

            def mk_T5(m0):
                def T5():
                    for m in range(m0, m0 + 4):
                        pm, pmr = P.bank()
                        for cc in range(2):
                            MM(pm[:, 0:n], Wo(cc, m), og2[cc][:, 0:n], cc == 0, cc == 1, [w4r, og2_r[cc]], [pmr])
                        TT("dve", x[:, m, t0:t1], pm[:, 0:n], x[:, m, t0:t1], ALU.add, [pmr, x_r[m][t]], [x_r[m][t]])
                return T5
            def Tn():
                T1()
                T2()
                T3()
            return [Tn, T4, mk_T5(0), mk_T5(4)]

        for f in stage1(0):
            f()
        prev_tail = []
        for t in range(NT):
            fill = stage1(t + 1) if t + 1 < NT else []
            steps = chain_steps(t)
            merged = []
            a, b = list(fill), list(prev_tail)
            while a or b:
                if b:
                    merged.append(b.pop(0))
                if a:
                    merged.append(a.pop(0))
                if a:
                    merged.append(a.pop(0))
            nst = len(steps)
            fi = 0
            for si, st_ in enumerate(steps):
                st_()
                want = (len(merged) * (si + 1) + nst - 1) // nst
                while fi < min(want, len(merged)):
                    merged[fi]()
                    fi += 1
            while fi < len(merged):
                merged[fi]()
                fi += 1
            prev_tail = tail_steps(t)
        for f in prev_tail:
            f()
        for j in (j1, j2, j3, j4):
            P.wrelease(j)

    I32 = mybir.dt.int32
    TWO_PI = 2.0 * np.pi
    S5T = [(k * 256, 256, "p") for k in range(8)] + [(2048, 16, "p"), (2064, 128, "s")]

    def frac_sincos(v, W, tmp_f, tmp_i, sin_out, cos_out, v_r, t_r, so_r, co_r):
        COPY("dve", tmp_i[:, 0:W], v[:, 0:W], [v_r], [t_r])
        COPY("dve", tmp_f[:, 0:W], tmp_i[:, 0:W], [t_r], [t_r])
        TT("dve", v[:, 0:W], v[:, 0:W], tmp_f[:, 0:W], ALU.subtract, [v_r, t_r], [v_r])
        ACT(sin_out[:, 0:W], v[:, 0:W], AF.Sin, [v_r], [so_r], scale=TWO_PI)
        TS("dve", tmp_f[:, 0:W], v[:, 0:W], 0.25, ALU.add, [v_r, t_r], [t_r])
        COPY("dve", tmp_i[:, 0:W], tmp_f[:, 0:W], [t_r], [t_r])
        COPY("dve", cos_out[:, 0:W], tmp_i[:, 0:W], [t_r], [co_r])
        TT("dve", tmp_f[:, 0:W], tmp_f[:, 0:W], cos_out[:, 0:W], ALU.subtract, [t_r, co_r], [t_r])
        ACT(cos_out[:, 0:W], tmp_f[:, 0:W], AF.Sin, [t_r], [co_r], scale=TWO_PI)

    def s5_params(ar, ai, ldt, W, in_r, want_f, tag):
        bufs = {}

        def nb(name, dt=F32):
            a, r = aalloc(W, F32, tag + name)
            if dt is I32:
                a = a.bitcast(I32)
            bufs[name] = (a, r)
            return a, r

        dt_, dt_r = nb("dt")
        mag, mag_r = nb("mag")
        rfr, rfr_r = nb("rfr")
        tf, tf_r = nb("tf")
        ti, ti_r = nb("ti", I32)
        sn, sn_r = nb("sn")
        cs, cs_r = nb("cs")
        ACT(dt_, ldt, AF.Exp, [in_r], [dt_r])
        TT("dve", mag, dt_, ar, ALU.mult, [dt_r, in_r], [mag_r])
        ACT(mag, mag, AF.Exp, [mag_r], [mag_r])
        TT("dve", rfr, dt_, ai, ALU.mult, [dt_r, in_r], [rfr_r])
        TS("dve", rfr, rfr, 1.0 / TWO_PI, ALU.mult, [rfr_r], [rfr_r])
        frac_sincos(rfr, W, tf, ti, sn, cs, rfr_r, tf_r, sn_r, cs_r)
        TT("dve", cs, cs, mag, ALU.mult, [cs_r, mag_r], [cs_r])
        TT("dve", sn, sn, mag, ALU.mult, [sn_r, mag_r], [sn_r])
        out = {"mag": (mag, mag_r), "abr": (cs, cs_r), "abi": (sn, sn_r), "rfr": (rfr, rfr_r)}
        if want_f:
            fre, fre_r = nb("fre")
            fim, fim_r = nb("fim")
            den, den_r = dt_, dt_r
            TT("dve", den, ar, ar, ALU.mult, [in_r, dt_r], [den_r])
            TT("dve", tf, ai, ai, ALU.mult, [in_r, tf_r], [tf_r])
            TT("dve", den, den, tf, ALU.add, [den_r, tf_r], [den_r])
            RECIP(den, den, [den_r], [den_r])
            nre, nre_r = tf, tf_r
            TS("dve", nre, cs, -1.0, ALU.add, [cs_r, tf_r], [nre_r])
            TT("dve", fre, nre, ar, ALU.mult, [nre_r, in_r], [fre_r])
            TT("dve", fim, sn, ai, ALU.mult, [sn_r, in_r], [fim_r])
            TT("dve", fre, fre, fim, ALU.add, [fre_r, fim_r], [fre_r])
            TT("dve", fre, fre, den, ALU.mult, [fre_r, den_r], [fre_r])
            TT("dve", fim, sn, ar, ALU.mult, [sn_r, in_r, fre_r], [fim_r])
            TT("dve", nre, nre, ai, ALU.mult, [nre_r, in_r], [nre_r])
            TT("dve", fim, fim, nre, ALU.subtract, [fim_r, nre_r], [fim_r])
            TT("dve", fim, fim, den, ALU.mult, [fim_r, den_r], [fim_r])
            out["fre"] = (fre, fre_r)
            out["fim"] = (fim, fim_r)
        return out

    KS5 = int(os.environ.get("KS5", "9"))
    PENG = os.environ.get("KPENG", "pool")
    PENG2 = os.environ.get("KPENG2", "pool")

    def s5_pass(i, yc, zbuf, z_r, mark, st_sem, out_sem, smL):
        wu, wur, ju = P.wget()
        id32_ = ccol("ident32", 0, 128)
        Bbr, Bbr_r = aalloc(256, BF16, "Bbr")
        Bbi, Bbi_r = aalloc(256, BF16, "Bbi")
        Cre, Cre_r = aalloc(256, BF16, "Cre")
        nCre, nCre_r = aalloc(256, BF16, "nCre")
        nCim, nCim_r = aalloc(256, BF16, "nCim")
        csl = slice(yc * 4, (yc + 1) * 4)
        mag, mag_r = smL["mag"][0][:, csl], smL["mag"][1]
        abr, abr_r = smL["abr"][0][:, csl], smL["abr"][1]
        abi, abi_r = smL["abi"][0][:, csl], smL["abi"][1]
        rfr, rfr_r = smL["rfr"][0][:, csl], smL["rfr"][1]
        fre, fre_r = smL["fre"][0][:, csl], smL["fre"][1]
        fim, fim_r = smL["fim"][0][:, csl], smL["fim"][1]
        mark_b = ar["off"]
        stg, stg_r = aalloc(4 * 512, F32, "s5stg")
        DMA("sp", stg, s5w[i, yc][:, 0:4 * 512], [], [stg_r], st_sem)
        Bre_p, Bim_p, Cre_p, Cim_p = (stg[:, q * 512:(q + 1) * 512] for q in range(4))
        dg = [aalloc(128, F32, "dg%d" % q) for q in range(2)]
        pf, pf_r = [], []
        for q, (fq, fq_r) in enumerate(((fre, fre_r), (fim, fim_r))):
            pb, pr = P.bank()
            for sc in range(4):
                dgt, dgt_r = dg[sc % 2]
                TS("dve", dgt, id32_, fq[:, sc:sc + 1], ALU.mult, [cst_r, fq_r], [dgt_r])
                MM(pb[:, sc * 128:(sc + 1) * 128], ones32[:], dgt, True, True, [dgt_r, cst_r], [pr], signal=True)
            pf.append(pb)
            pf_r.append(pr)
        m1, m1_r = aalloc(512, F32, "m1")
        m2, m2_r = aalloc(512, F32, "m2")
        TT("dve", m1, pf[0][:, 0:512], Bre_p, ALU.mult, [pf_r[0], stg_r], [m1_r])
        TT("dve", m2, pf[1][:, 0:512], Bim_p, ALU.mult, [pf_r[1], stg_r], [m2_r])
        TT("dve", Bbr, m1, m2, ALU.subtract, [m1_r, m2_r], [Bbr_r])
        TT("dve", m1, pf[0][:, 0:512], Bim_p, ALU.mult, [pf_r[0], stg_r, m1_r], [m1_r])
        TT("dve", m2, pf[1][:, 0:512], Bre_p, ALU.mult, [pf_r[1], stg_r, m2_r], [m2_r])
        TT("dve", Bbi, m1, m2, ALU.add, [m1_r, m2_r], [Bbi_r])
        COPY("act", Cre, Cre_p, [stg_r], [Cre_r])
        TS("dve", nCre, Cre_p, -1.0, ALU.mult, [stg_r], [nCre_r])
        TS("dve", nCim, Cim_p, -1.0, ALU.mult, [stg_r], [nCim_r])
        S.barrier(io_sems)
        ar["off"] = mark_b
        cosT, cosT_r = aalloc(1024, F32, "cosT")
        sinT, sinT_r = aalloc(1024, F32, "sinT")
        ttf, ttf_r = aalloc(1024, F32, "ttf")
        tti, tti_r = aalloc(1024, F32, "tti")
        tti = tti.bitcast(I32)
        vt, vt_r = ttf, ttf_r
        vt, vt_r = aalloc(1024, F32, "vt")
        for sc in range(4):
            TS("dve", vt[:, sc * 256:(sc + 1) * 256], jvec, rfr[:, sc:sc + 1], ALU.mult, [cst_r, rfr_r], [vt_r])
        frac_sincos(vt, 1024, ttf, tti, sinT, cosT, vt_r, ttf_r, sinT_r, cosT_r)
        tt4 = tti.bitcast(BF16)
        x2, x2_r = aalloc(1024, F32, "x2")
        g2, g2_r = aalloc(512, F32, "g2")
        t2, t2_r = aalloc(512, BF16, "t2")
        AB, AB_r, GG, GG_r, TQ, TQ_r = [], [], [], [], [], []
        for st, (xb, gb, tb) in enumerate(((vt, ttf[:, 0:512], tt4[:, 0:1024]), (x2, g2, t2))):
            AB.append([xb[:, q * 256:(q + 1) * 256] for q in range(4)])
            AB_r.append([Region("ab%d_%d" % (st, q)) for q in range(4)])
            GG.append([gb[:, q * 256:(q + 1) * 256] for q in range(2)])
            GG_r.append([Region("g%d_%d" % (st, q)) for q in range(2)])
            TQ.append([tb[:, q * 256:(q + 1) * 256] for q in range(4)])
            TQ_r.append([Region("tq%d_%d" % (st, q)) for q in range(4)])
        decs, decs_r = ttf[:, 512:640], Region("decs")
        u2, u2_r = aalloc(256, F32, "u2")
        U32 = [ttf[:, 768:1024], u2]
        U32_r = [Region("u32a"), Region("u32b")]
        UBF = [tt4[:, 1024:1280], tt4[:, 1280:1536]]
        UBF_r = [Region("ubfa"), Region("ubfb")]
        ysb, ysb_r = aalloc(256, F32, "ysb")
        yt2, yt2_r = aalloc(256, F32, "yt2")
        small, small_r = aalloc(256, F32, "small")
        gL = small[:, 0:8]
        tmp4 = small[:, 8:40]
        car = small[:, 40:48]
        hL = small[:, 48:56]
        car_r = Region("car")
        gL_r = Region("gL")
        h0, h0_r = aalloc(128, F32, "h0")
        csm, csm_r = aalloc(128, F32, "csm")
        hs, hs_r = aalloc(128, F32, "hs")
        tm64, tm64_r = aalloc(256, F32, "tm64")
        S.barrier(io_sems)
        py, pyr = P.banks[0], P.bank_r[0]
        dcol = ccol("s5_d", i * 4 + yc)
        v3 = lambda a: a.rearrange("p (s j) -> p s j", j=8)

        def crot(dst_re, dst_im, g_re, g_im, cs_, sn_, W3, rd, wr):
            tA = tmp4[:, 0:W3]
            tB = tmp4[:, 8:8 + W3]
            TT("dve", tA, g_re, cs_, ALU.mult, rd, [small_r])
            TT("dve", tB, g_im, sn_, ALU.mult, rd, [small_r])
            TT("dve", dst_re, tA, tB, ALU.subtract, [small_r], wr)
            TT("dve", tA, g_re, sn_, ALU.mult, rd + [small_r], [small_r])
            TT("dve", tB, g_im, cs_, ALU.mult, rd + [small_r], [small_r])
            TT("dve", dst_im, tA, tB, ALU.add, [small_r], wr)

        w255 = small[:, 56:64]
        w255_r = Region("w255")
        crot(w255[:, 0:4], w255[:, 4:8], cosT.rearrange("p (c j) -> p c j", j=256)[:, :, 255],
             sinT.rearrange("p (c j) -> p c j", j=256)[:, :, 255], cosT.rearrange("p (c j) -> p c j", j=256)[:, :, 1],
             sinT.rearrange("p (c j) -> p c j", j=256)[:, :, 1], 4,
             [cosT_r, sinT_r, small_r], [w255_r])

        def tabs(sc, n, kind):
            if kind == "s":
                return (cosT[:, sc * 256:sc * 256 + 8].unsqueeze(1).to_broadcast([128, 16, 8]),
                        sinT[:, sc * 256:sc * 256 + 8].unsqueeze(1).to_broadcast([128, 16, 8]))
            return cosT[:, sc * 256:sc * 256 + n], sinT[:, sc * 256:sc * 256 + n]

        def vw(a, n, kind):
            return v3(a[:, 0:n]) if kind == "s" else a[:, 0:n]

        def PRE(k):
            c0, n, kind = S5T[k]
            tix = min(c0 // 512, NT - 1)
            u32, u32_r, ubf, ubf_r = U32[k % 2], U32_r[k % 2], UBF[k % 2], UBF_r[k % 2]
            pu, pur = P.bank()
            for kc in range(NCH):
                MM(pu[:, 0:n], wu[:, kc * 128:(kc + 1) * 128], hb[:, kc, c0:c0 + n], kc == 0, kc == NCH - 1,
                   [wur, h_r[kc][tix]], [pur])
            COPY("act", u32[:, 0:n], pu[:, 0:n], [pur], [u32_r])
            COPY("act", ubf[:, 0:n], u32[:, 0:n], [u32_r], [ubf_r])
            if kind == "s":
                DMA("sp", h0.rearrange("p (q c s) -> p q c s", q=2, c=4), st_s5[i, :, :, yc * 4:(yc + 1) * 4, :].rearrange(
                    "q p c s -> p q c s"), [], [h0_r], st_sem)
                h0v = h0.rearrange("p (q c s) -> p q c s", q=2, c=4)
                csv = csm.rearrange("p (q c s) -> p q c s", q=2, c=4)
                tmv = tm64.rearrange("p (q c s) -> p q c s", q=4, c=4)
                abr_b = abr.unsqueeze(2).to_broadcast([128, 4, 16])
                abi_b = abi.unsqueeze(2).to_broadcast([128, 4, 16])
                TT("dve", tmv[:, 0], h0v[:, 0], abr_b, ALU.mult, [h0_r, abr_r], [tm64_r])
                TT("dve", tmv[:, 1], h0v[:, 1], abi_b, ALU.mult, [h0_r, abi_r], [tm64_r])
                TT("dve", csv[:, 0], tmv[:, 0], tmv[:, 1], ALU.subtract, [tm64_r], [csm_r])
                TT("dve", tmv[:, 2], h0v[:, 1], abr_b, ALU.mult, [h0_r, abr_r], [tm64_r])
                TT("dve", tmv[:, 3], h0v[:, 0], abi_b, ALU.mult, [h0_r, abi_r], [tm64_r])
                TT("dve", csv[:, 1], tmv[:, 2], tmv[:, 3], ALU.add, [tm64_r], [csm_r])

        pxs = {}
        pzs = {}
        id32 = ccol("ident32", 0, 128)
        nid32 = ccol("nident32", 0, 128)

        def Fx(k, sc):
            c0, n, kind = S5T[k]
            ubf, ubf_r = UBF[k % 2], UBF_r[k % 2]
            px, pxr = P.bank()
            MM(px[:, 0:n], Bbr[:, sc * 128:(sc + 1) * 128], ubf[:, 0:n], True, True, [Bbr_r, ubf_r], [pxr], signal=True)
            MM(px[:, 256:256 + n], Bbi[:, sc * 128:(sc + 1) * 128], ubf[:, 0:n], True, True, [Bbi_r, ubf_r], [pxr],
               signal=True)
            pxs[(k, sc)] = (px, pxr)

        def Fr(k, sc):
            c0, n, kind = S5T[k]
            st = sc % 2
            a1, a2, b1, b2 = AB[st]
            a1_r, a2_r, b1_r, b2_r = AB_r[st]
            cs_, sn_ = tabs(sc, n, kind)
            px, pxr = pxs.pop((k, sc))
            xr, xi = vw(px, n, kind), vw(px[:, 256:512], n, kind)
            TT("dve", vw(a1, n, kind), xr, cs_, ALU.mult, [pxr, cosT_r], [a1_r])
            TT("dve", vw(a2, n, kind), xi, sn_, ALU.mult, [pxr, sinT_r], [a2_r])
            TT("dve", vw(b1, n, kind), xi, cs_, ALU.mult, [pxr, cosT_r], [b1_r])
            TT("dve", vw(b2, n, kind), xr, sn_, ALU.mult, [pxr, sinT_r], [b2_r])
            pz, pzr = P.bank()
            MM(pz[:, 0:n], id32, a1[:, 0:n], True, False, [a1_r, cst_r], [pzr], signal=True)
            MM(pz[:, 0:n], id32, a2[:, 0:n], False, True, [a2_r, cst_r], [pzr], signal=True)
            MM(pz[:, 256:256 + n], id32, b1[:, 0:n], True, False, [b1_r, cst_r], [pzr], signal=True)
            MM(pz[:, 256:256 + n], nid32, b2[:, 0:n], False, True, [b2_r, cst_r], [pzr], signal=True)
            pzs[(k, sc)] = (pz, pzr)

        def Gs(k, sc, first):
            c0, n, kind = S5T[k]
            st = sc % 2
            gre, gim = GG[st]
            gre_r, gim_r = GG_r[st]
            tq, tq_r = TQ[st], TQ_r[st]
            cs_, sn_ = tabs(sc, n, kind)
            pz, pzr = pzs.pop((k, sc))
            xre, xim = pz[:, 0:n], pz[:, 256:256 + n]
            if kind == "s":
                csv = csm.rearrange("p (q c s) -> p q c s", q=2, c=4)
                TT("dve", pz[:, 0:n:8], pz[:, 0:n:8], csv[:, 0, sc], ALU.add, [pzr, csm_r], [pzr])
                TT("dve", pz[:, 256:256 + n:8], pz[:, 256:256 + n:8], csv[:, 1, sc], ALU.add, [pzr, csm_r], [pzr])
                TS("dve", decs[:, 0:n], ccol("cmask4", 16, 128), mag[:, sc:sc + 1], ALU.mult, [cst_r, mag_r], [decs_r])
                dec = decs[:, 0:n]
                dec_rd = [decs_r]
            else:
                dec = mag[:, sc:sc + 1].to_broadcast([128, n])
                dec_rd = [mag_r]
            if kind == "p" and not first:
                SCAN(gre[:, 0:n], dec, xre, [pzr, car_r] + dec_rd, [gre_r], initial=car[:, sc:sc + 1])
                SCAN(gim[:, 0:n], dec, xim, [pzr, car_r] + dec_rd, [gim_r], initial=car[:, 4 + sc:5 + sc])
            else:
                SCAN(gre[:, 0:n], dec, xre, [pzr] + dec_rd, [gre_r])
                SCAN(gim[:, 0:n], dec, xim, [pzr] + dec_rd, [gim_r])
            g_re3, g_im3 = vw(gre, n, kind), vw(gim, n, kind)
            TT(PENG2, vw(tq[0], n, kind), g_re3, cs_, ALU.mult, [gre_r, cosT_r], [tq_r[0]])
            TT(PENG2, vw(tq[1], n, kind), g_im3, sn_, ALU.mult, [gim_r, sinT_r], [tq_r[1]])
            TT(PENG, vw(tq[2], n, kind), g_re3, sn_, ALU.mult, [gre_r, sinT_r], [tq_r[2]])
            TT(PENG, vw(tq[3], n, kind), g_im3, cs_, ALU.mult, [gim_r, cosT_r], [tq_r[3]])
            if kind == "p":
                COPY("act", gL[:, sc:sc + 1], gre[:, n - 1:n], [gre_r], [gL_r])
                COPY("act", gL[:, 4 + sc:5 + sc], gim[:, n - 1:n], [gim_r], [gL_r])
            else:
                tmv = tm64.rearrange("p (q c s) -> p q c s", q=4, c=4)
                COPY("act", tmv[:, 0, sc], gre[:, 7:n:8], [gre_r], [tm64_r])
                COPY("act", tmv[:, 1, sc], gim[:, 7:n:8], [gim_r], [tm64_r])

        def Gp(k, sc):
            c0, n, kind = S5T[k]
            tq, tq_r = TQ[sc % 2], TQ_r[sc % 2]
            for q, wmat, wreg in ((0, Cre, Cre_r), (1, nCre, nCre_r), (2, nCim, nCim_r), (3, nCim, nCim_r)):
                MM(py[:, 0:n], wmat[:, sc * 128:(sc + 1) * 128], tq[q][:, 0:n], sc == 0 and q == 0, sc == 3 and q == 3,
                   [wreg, tq_r[q]], [pyr], signal=True)

        def POSTC(k):
            c0, n, kind = S5T[k]
            if kind == "p" and n == 256:
                crot(car[:, 0:4], car[:, 4:8], gL[:, 0:4], gL[:, 4:8], w255[:, 0:4], w255[:, 4:8], 4,
                     [gL_r, small_r, w255_r], [car_r])
            elif kind == "p":
                L = n - 1
                csL = cosT.rearrange("p (c j) -> p c j", j=256)[:, :, L]
                snL = sinT.rearrange("p (c j) -> p c j", j=256)[:, :, L]
                crot(hL[:, 0:4], hL[:, 4:8], gL[:, 0:4], gL[:, 4:8], csL, snL, 4, [gL_r, small_r, cosT_r, sinT_r], [small_r])
                DMA("sp", o_ps5[i, :, :, yc * 4:(yc + 1) * 4].rearrange("q p c -> p q c"),
                    hL.rearrange("p (q c) -> p q c", q=2), [small_r], [], out_sem)
            else:
                hsv = hs.rearrange("p (q c s) -> p q c s", q=2, c=4)
                tmv = tm64.rearrange("p (q c s) -> p q c s", q=4, c=4)
                cs7 = cosT.rearrange("p (c j) -> p c j", j=256)[:, :, 7:8].to_broadcast([128, 4, 16])
                sn7 = sinT.rearrange("p (c j) -> p c j", j=256)[:, :, 7:8].to_broadcast([128, 4, 16])
                TT("dve", tmv[:, 2], tmv[:, 0], cs7, ALU.mult, [tm64_r, cosT_r], [tm64_r])
                TT("dve", tmv[:, 3], tmv[:, 1], sn7, ALU.mult, [tm64_r, sinT_r], [tm64_r])
                TT("dve", hsv[:, 0], tmv[:, 2], tmv[:, 3], ALU.subtract, [tm64_r], [hs_r])
                TT("dve", tmv[:, 2], tmv[:, 0], sn7, ALU.mult, [tm64_r, sinT_r], [tm64_r])
                TT("dve", tmv[:, 3], tmv[:, 1], cs7, ALU.mult, [tm64_r, cosT_r], [tm64_r])
                TT("dve", hsv[:, 1], tmv[:, 2], tmv[:, 3], ALU.add, [tm64_r], [hs_r])
                DMA("sp", o_ss5[i, :, :, yc * 4:(yc + 1) * 4, :].rearrange("q p c s -> p q c s"), hsv, [hs_r], [], out_sem)

        def POST(k):
            c0, n, kind = S5T[k]
            u32, u32_r = U32[k % 2], U32_r[k % 2]
            STT("dve", ysb[:, 0:n], u32[:, 0:n], dcol, py[:, 0:n], ALU.mult, ALU.add, [u32_r, pyr, cst_r], [ysb_r])
            TT("pool", yt2[:, 0:n], ysb[:, 0:n], ysb[:, 0:n], ALU.mult, [ysb_r], [yt2_r])
            TS("dve", yt2[:, 0:n], yt2[:, 0:n], 0.044715, ALU.mult, [yt2_r], [yt2_r], s2=1.0, op1=ALU.add)
            TT("pool", yt2[:, 0:n], yt2[:, 0:n], ysb[:, 0:n], ALU.mult, [yt2_r, ysb_r], [yt2_r])
            ACT(yt2[:, 0:n], yt2[:, 0:n], AF.Sigmoid, [yt2_r], [yt2_r], scale=2.0 * 0.7978845608028654)
            TT("dve", zbuf[:, yc * T + c0:yc * T + c0 + n], ysb[:, 0:n], yt2[:, 0:n], ALU.mult, [ysb_r, yt2_r], [z_r[yc]])

        items = [(k, sc) for k in range(len(S5T)) for sc in range(4)]
        NI = len(items)

        def emit_fx(j):
            if j < NI:
                k2, sc2 = items[j]
                if sc2 == 0:
                    PRE(k2)
                Fx(k2, sc2)

        emit_fx(0)
        emit_fx(1)
        Fr(*items[0])
        for idx in range(NI + 1):
            emit_fx(idx + 2)
            if idx + 1 < NI:
                Fr(*items[idx + 1])
            if idx < NI:
                k, sc = items[idx]
                Gs(k, sc, k == 0)
                if sc == 3:
                    POSTC(k)
            if idx >= 1:
                kp, scp = items[idx - 1]
                Gp(kp, scp)
                if scp == 3:
                    POST(kp)
        P.wrelease(ju)

    def s5_final(i, zbuf, z_r):
        wg, wgr, jg = P.wget()
        wo0, wo0r, jo0 = P.wget()
        wo1, wo1r, jo1 = P.wget()
        s5o, s5o_r = [], []
        for m in range(4):
            a, r = aalloc(256, BF16, "s5o%d" % m)
            s5o.append(a)
            s5o_r.append(r)
        sgm, sgm_r = [], []
        for m in range(2):
            a, r = aalloc(512, F32, "sgm%d" % m)
            sgm.append(a)
            sgm_r.append(r)
        for t in range(NT):
            t0, t1 = TILES[t]
            n = t1 - t0
            for m in range(4):
                pg, pgr = P.bank()
                for kc in range(4):
                    MM(pg[:, 0:n], wg[:, kc * 512 + m * 128:kc * 512 + (m + 1) * 128], zbuf[:, kc * T + t0:kc * T + t1],
                       kc == 0, kc == 3, [wgr, z_r[kc]], [pgr])
                ACT(sgm[m % 2][:, 0:n], pg[:, 0:n], AF.Sigmoid, [pgr], [sgm_r[m % 2]])
                TT("dve", s5o[m][:, 0:n], zbuf[:, m * T + t0:m * T + t1], sgm[m % 2][:, 0:n], ALU.mult,
                   [z_r[m], sgm_r[m % 2]], [s5o_r[m]])
            for m8 in range(NCH):
                pm, pmr = P.bank()
                for kc in range(4):
                    wsl, wslr = (wo0, wo0r) if kc < 2 else (wo1, wo1r)
                    MM(pm[:, 0:n], wsl[:, (kc % 2) * 1024 + m8 * 128:(kc % 2) * 1024 + (m8 + 1) * 128], s5o[kc][:, 0:n],
                       kc == 0, kc == 3, [wslr, s5o_r[kc]], [pmr])
                TT("dve", x[:, m8, t0:t1], pm[:, 0:n], x[:, m8, t0:t1], ALU.add, [pmr, x_r[m8][t]], [x_r[m8][t]])
        for j in (jg, jo0, jo1):
            P.wrelease(j)

    def ab_mixer(layer):
        i = layer // 2
        areset()
        norm_alloc()
        for t in range(NT):
            norm_tile(t, 4 + layer)
        P.rot = [4, 5, 6, 7]
        if "r" in KAB:
            for pr in range(2):
                areset_passes(pr == 0)
                ret_pass(i, pr, st_sem, out_sem)
        areset()
        if "s" in KAB:
            P.rot = [1, 2, 3, 4, 5, 6, 7]
            zbuf, _ = aalloc(4 * T // 2, BF16, "zbuf")
            z_r = [Region("z%d" % m) for m in range(4)]
            smb = 256 + 18 + (i * 3) * 16
            smL = s5_params(jv[:, smb:smb + 16], jv[:, smb + 16:smb + 32], jv[:, smb + 32:smb + 48], 16, cst_r, True, "smL")
            mark = ar["off"]
            for yc in range(4):
                s5_pass(i, yc, zbuf, z_r, mark, st_sem, out_sem, smL)
                S.barrier(io_sems)
                ar["off"] = mark
            P.rot = list(range(8))
            s5_final(i, zbuf, z_r)
            areset()
        P.rot = list(range(8))

    st_sem = S.dma_sem("st")
    out_sem = S.dma_sem("out")
    io_sems.append(st_sem)
    io_sems.append(out_sem)
    for (layer, kind) in subs:
        if kind == "ffn1":
            ffn(layer)
        elif kind == "ffn2":
            ffn(8 + layer)
        elif layer % 2 == 1:
            gla_mixer(layer)
        else:
            ab_mixer(layer)
    assert P.next_get == len(P.loads) == P.issued, (P.next_get, len(P.loads), P.issued)

    areset()
    norm_alloc()
    for t in range(NT):
        norm_tile(t, 12, final=True)
    for c in range(NCH):
        DMA("sp", yT[c * 128:(c + 1) * 128, :], x[:, c, :], [x_r[c][t] for t in range(NT)], [], out_sem)
    finals = [(s.h, s.count) for s in (out_sem, st_sem, in_sem) if s.count > 0]
    S.replay(finals)
    return P


def _ffn_pack(w_gu, w_down):
    a = w_gu.reshape(NCH, 128, 2, NFF, 128).transpose(3, 1, 2, 0, 4).reshape(NFF, 128, 2048)
    b = w_down.reshape(NFF, 128, D)
    return np.concatenate([a, b], axis=2)


def _inproj(w, cols):
    m = len(cols)
    return w[:, cols].reshape(NCH, 128, m).transpose(1, 0, 2).reshape(128, NCH * m)


def pack_weights(plan, inp):
    f = np.float32
    blocks = []
    ffn_cache = {}
    for (key, n) in plan:
        kind = key[0]
        if kind == "ffn":
            _, layer, which, j = key
            ck = (layer, which)
            if ck not in ffn_cache:
                ffn_cache.clear()
                nm = "ffn1" if which == 0 else "ffn2"
                ffn_cache[ck] = _ffn_pack(inp[nm + "_w_gu"][layer], inp[nm + "_w_down"][layer])
            blk = ffn_cache[ck][j]
        elif kind == "gla":
            _, i, hd, part = key
            w = inp["gla_w_in"][i]
            wo = inp["gla_w_out"][i]
            ar = np.arange
            if part == 0:
                blk = np.concatenate([_inproj(w, hd * 128 + ar(128)), _inproj(w, 512 + hd * 128 + ar(128)),
                                      _inproj(w, 3072 + ar(16))], axis=1)
            elif part == 1:
                blk = np.concatenate([_inproj(w, 1024 + hd * 256 + ar(256)), _inproj(w, 2048 + hd * 256 + ar(128))], axis=1)
            else:
                rows = wo[hd * 256:(hd + 1) * 256].reshape(2, 128, D).transpose(1, 0, 2).reshape(128, 2 * D)
                blk = np.concatenate([_inproj(w, 2048 + hd * 256 + 128 + ar(128)), rows], axis=1)
        else:
            blk = pack_ab(key, inp)
        assert blk.shape == (128, n), (key, blk.shape, n)
        blocks.append(blk.astype(f, copy=False))
    if not blocks:
        return np.zeros((128, 1), f)
    return np.ascontiguousarray(np.concatenate(blocks, axis=1))


def build_cst(inp):
    f = np.float32
    CL = cst_layout()
    c = np.zeros((128, CL.n), f)

    def put(name, arr):
        o, n = CL[name]
        arr = np.asarray(arr, f)
        assert arr.shape[1] == n, (name, arr.shape, n)
        c[:arr.shape[0], o:o + n] = arr

    gains = np.concatenate([inp["norm_ffn1"], inp["norm_mix"], inp["norm_ffn2"], inp["norm_final"][None]], axis=0)
    put("gains", gains.reshape(13, NCH, 128).transpose(2, 0, 1).reshape(128, 13 * NCH))
    put("gla_b", inp["gla_b_alpha"].reshape(2, 4, 128).transpose(2, 0, 1).reshape(128, 8))
    put("gla_g", inp["gla_norm"].reshape(2, 2, 128).transpose(2, 0, 1).reshape(128, 4))
    put("s5_d", inp["s5_d"].reshape(2, 4, 128).transpose(2, 0, 1).reshape(128, 8))
    put("one", np.ones((128, 1), f))
    put("hm", (np.arange(128)[:, None] // 64 == np.arange(2)[None, :]).astype(f))
    cm = np.ones((128, 512), f)
    cm[:, 0::128] = 0.0
    put("cmask", cm)
    cm4 = np.ones((128, 144), f)
    cm4[:, 0] = 0.0
    cm4[:, 16::8] = 0.0
    put("cmask4", cm4)
    put("ident32", np.eye(128, dtype=f))
    put("nident32", -np.eye(128, dtype=f))
    s_ = np.arange(128)[:, None]
    t_ = np.arange(128)[None, :]
    put("causal", (t_ >= s_).astype(f))
    put("samp", ((t_ >= s_) & (t_ // 8 == s_ // 8)).astype(f))
    put("seqm", (np.arange(128)[:, None] // 8 == np.arange(16)[None, :]).astype(f))
    put("ident", np.eye(128, dtype=f))
    w2 = np.zeros((16, 1024), f)
    w2[:, 0:512] = inp["gla_w_alpha2"][0]
    w2[:, 512:1024] = inp["gla_w_alpha2"][1]
    put("w2", w2)
    put("jvec", np.broadcast_to(np.arange(256, dtype=f)[None], (128, 256)))
    put("ret_gam", ret_gam_table())
    sm = np.zeros((128, 96), f)
    for i in range(2):
        for q, nm in enumerate(("s5_a_re", "s5_a_im")):
            sm[:, (i * 3 + q) * 16:(i * 3 + q + 1) * 16] = inp[nm][i].reshape(16, 128).T
        sm[:, (i * 3 + 2) * 16:(i * 3 + 3) * 16] = np.repeat(inp["s5_log_dt"][i], 64).reshape(16, 128).T
    put("s5sm", sm)
    return c


def ret_gam_table():
    g = np.zeros((128, 18), np.float64)
    for pr in range(2):
        for p in range(128):
            h = 2 * pr + p // 64
            gamma = 1.0 - 2.0 ** (-5.0 - h)
            for k, n in enumerate((128, 16, 8)):
                base = (pr * 3 + k) * 3
                g[p, base] = gamma ** n
                g[p, base + 1 + p // 64] = gamma ** n
    return g.astype(np.float32)


def pack_ab(key, inp):
    ar = np.arange
    kind = key[0]
    if kind == "ret":
        _, i, pr, part = key
        w = inp["ab_w_in"][i]
        sw = np.concatenate([hh * 64 + (ar(64) + 32) % 64 for hh in range(2)])
        if part == 0:
            return np.concatenate([_inproj(w, 512 + pr * 128 + ar(128)), _inproj(w, 512 + pr * 128 + sw),
                                   _inproj(w, 768 + pr * 128 + ar(128))], axis=1)
        if part == 1:
            return np.concatenate([_inproj(w, 768 + pr * 128 + sw), _inproj(w, 1024 + pr * 256 + ar(256))], axis=1)
        if part == 2:
            return np.concatenate([_inproj(w, 1536 + pr * 256 + ar(128)), _inproj(w, 1536 + pr * 256 + 128 + ar(128))], axis=1)
        wo = inp["ab_w_out"][i]
        return wo[512 + pr * 256:512 + (pr + 1) * 256].reshape(2, 128, D).transpose(1, 0, 2).reshape(128, 2 * D)
    if kind == "s5u":
        _, i, yc = key
        return _inproj(inp["ab_w_in"][i], yc * 128 + ar(128))
    if kind == "s5g":
        _, i = key
        return inp["s5_w_glu"][i].reshape(4, 128, 512).transpose(1, 0, 2).reshape(128, 2048)
    if kind == "s5o":
        _, i, half = key
        wo = inp["ab_w_out"][i]
        return wo[half * 256:(half + 1) * 256].reshape(2, 128, D).transpose(1, 0, 2).reshape(128, 2 * D)
    raise KeyError(key)


def rot_tables():
    half = 32
    inv_freq = (1.0 / (np.float32(10000.0) ** (np.arange(half, dtype=np.float32) / np.float32(half)))).astype(np.float32)
    out = np.zeros((2, NT, 128, 4, 512), np.float32)
    p = np.arange(128)
    d = p % 64
    fi = d % 32
    sign = np.where(d < 32, -1.0, 1.0)
    for t, (t0, t1) in enumerate(TILES):
        n = t1 - t0
        col = np.arange(n)
        if t < NT - 1:
            pos = t0 + col
            j = col % 128
        else:
            pos = np.where(col < 16, 2048 + col, PAST_LEN + (col - 16) % 8)
            j = np.where(col < 16, col, (col - 16) % 8)
        ang = (pos.astype(np.float32)[None, :] * inv_freq[fi][:, None]).astype(np.float32).astype(np.float64)
        cs = np.cos(ang)
        sn = np.sin(ang) * sign[:, None]
        for pr in range(2):
            h = 2 * pr + p // 64
            gamma = 1.0 - 2.0 ** (-5.0 - h)
            lg = np.log(gamma)[:, None] * (j[None, :] + 1.0)
            Eq = np.exp(lg)
            Ek = np.exp(-lg) * (64.0 ** -0.5)
            out[pr, t, :, 0, :n] = cs * Eq
            out[pr, t, :, 1, :n] = sn * Eq
            out[pr, t, :, 2, :n] = cs * Ek
            out[pr, t, :, 3, :n] = sn * Ek
    return out


def s5w_pack(inp):
    out = np.zeros((2, 4, 128, 7, 512), np.float32)
    s = np.arange(128)
    for i in range(2):
        for yc in range(4):
            for sc in range(4):
                c = yc * 4 + sc
                g = 2 * c + s // 64
                p_ = s % 64
                for n in range(16):
                    r = 32 * sc + 16 * (s // 64) + n
                    out[i, yc, r, 0, sc * 128 + s] = inp["s5_b_re"][i][g, p_, n]
                    out[i, yc, r, 1, sc * 128 + s] = inp["s5_b_im"][i][g, p_, n]
                    out[i, yc, s, 2, sc * 128 + r] = inp["s5_c_re"][i][g, n, p_]
                    out[i, yc, s, 3, sc * 128 + r] = inp["s5_c_im"][i][g, n, p_]
                out[i, yc, :, 4, sc * 128:(sc + 1) * 128] = inp["s5_a_re"][i][g, p_][None, :]
                out[i, yc, :, 5, sc * 128:(sc + 1) * 128] = inp["s5_a_im"][i][g, p_][None, :]
                out[i, yc, :, 6, sc * 128:(sc + 1) * 128] = inp["s5_log_dt"][i][g][None, :]
    return out.reshape(2, 4, 128, 7 * 512)


def unpack_ab(o, core, sl, p_s5re, p_s5im, p_ret, s_s5re, s_s5im, s_ret):
    p_ret[:, core] = o["o_pret"].reshape(2, 4, 64, 128)
    s_ret[:, sl] = o["o_sret"].reshape(2, 4, 64, NSS, 128).transpose(0, 3, 1, 2, 4)
    ps5 = o["o_ps5"].transpose(0, 1, 3, 2).reshape(2, 2, 32, 64)
    p_s5re[:, core] = ps5[:, 0]
    p_s5im[:, core] = ps5[:, 1]
    ss5 = o["o_ss5"].transpose(0, 1, 4, 3, 2).reshape(2, 2, NSS, 32, 64)
    s_s5re[:, sl] = ss5[:, 0]
    s_s5im[:, sl] = ss5[:, 1]


def prepare_inputs(inp, subs):
    f = np.float32
    plan = make_plan(subs)
    wts = pack_weights(plan, inp)
    cst = build_cst(inp)
    has_gla = any(k == "mix" and l % 2 == 1 for (l, k) in subs)
    has_ab = any(k == "mix" and l % 2 == 0 for (l, k) in subs)
    if has_ab:
        rot = rot_tables()
        s5w = s5w_pack(inp)
    maps = []
    for core in range(8):
        xp = np.concatenate([inp["meta_tokens"], inp["x_prompt"][core]], axis=0)
        xs = inp["x_sample"][core * NSS:(core + 1) * NSS].reshape(NSAMP, D)
        xT = np.ascontiguousarray(np.concatenate([xp, xs], axis=0).T).astype(f)
        m = {"xT": xT, "cst": cst, "wts": wts}
        sl = slice(core * NSS, (core + 1) * NSS)
        if has_gla:
            m["st_gla"] = np.ascontiguousarray(inp["state_gla"][:, sl].transpose(0, 2, 3, 1, 4)).astype(f)
        if has_ab:
            m["rot_d"] = rot
            m["s5w"] = s5w
            re = inp["state_s5_re"][:, sl].reshape(2, NSS, 16, 128).transpose(0, 3, 2, 1)
            im = inp["state_s5_im"][:, sl].reshape(2, NSS, 16, 128).transpose(0, 3, 2, 1)
            m["st_s5"] = np.ascontiguousarray(np.stack([re, im], axis=1)).astype(f)
            m["st_ret"] = np.ascontiguousarray(
                inp["state_ret"][:, sl].reshape(2, NSS, 2, 128, 128).transpose(0, 2, 3, 1, 4)).astype(f)
        maps.append(m)
    return maps


_CACHE = {}
DEBUG_OUT = {}


def kernel(**inputs):
    inputs = {k: np.asarray(v) for k, v in inputs.items()}
    ncores = int(os.environ.get("KCORES", "8"))
    subs = parse_subs()
    maps = prepare_inputs(inputs, subs)[:ncores]
    key = tuple(subs)
    if key not in _CACHE:
        _CACHE[key] = build_program(subs)
    P = _CACHE[key]
    if os.environ.get("KTRACE"):
        res = run_bass_kernel_spmd(P.nc, maps, core_ids=list(range(ncores)), trace=True)
        print("KTRACE exec_time_ns", res.exec_time_ns)
    else:
        res = run_bass_kernel_spmd(P.nc, maps, core_ids=list(range(ncores)))
    outs = res.results
    yp = np.zeros((8, SEQ, D), np.float32)
    ys = np.zeros((128, DEC_SEQ, D), np.float32)
    p_s5re = np.zeros((2, 8, 32, 64), np.float32)
    p_s5im = np.zeros((2, 8, 32, 64), np.float32)
    p_ret = np.zeros((2, 8, 4, 64, 128), np.float32)
    p_gla = np.zeros((2, 8, 4, 128, 256), np.float32)
    s_s5re = np.zeros((2, 128, 32, 64), np.float32)
    s_s5im = np.zeros((2, 128, 32, 64), np.float32)
    s_ret = np.zeros((2, 128, 4, 64, 128), np.float32)
    s_gla = np.zeros((2, 128, 4, 128, 256), np.float32)
    for core in range(ncores):
        o = outs[core]
        y = o["yT"].T
        yp[core] = y[N_META:LP]
        sl = slice(core * NSS, (core + 1) * NSS)
        ys[sl] = y[LP:].reshape(NSS, DEC_SEQ, D)
        if "o_pgla" in o:
            p_gla[:, core] = o["o_pgla"]
            s_gla[:, sl] = o["o_sgla"].transpose(0, 3, 1, 2, 4)
        if "o_pret" in o:
            unpack_ab(o, core, sl, p_s5re, p_s5im, p_ret, s_s5re, s_s5im, s_ret)
    DEBUG_OUT.clear()
    DEBUG_OUT.update({"p_gla": p_gla, "s_gla": s_gla, "p_ret": p_ret, "s_ret": s_ret,
                      "p_s5re": p_s5re, "p_s5im": p_s5im, "s_s5re": s_s5re, "s_s5im": s_s5im})
    return (yp, ys, p_s5re, p_s5im, p_ret, p_gla, s_s5re, s_s5im, s_ret, s_gla)
```
